# Optimizing a Trainium2 kernel written in Bass

```python
import math
import jax, jax.numpy as jnp
from jax import lax
import numpy as np

D_MODEL = 1024
BATCH = 32
SEQ = 256
DEPTH = 2
DEC_BATCH = 4
DEC_SEQ = 4096
PAST_LEN = 256

GRID_W = 64
D_A = D_MODEL
HEAD_A = 64
H_A = D_A // HEAD_A
N_DIR = 2
W_LORA = 64
A_LORA = 64
G_LORA = 128
D_B = D_MODEL
CHUNK = 128
H_B = 8
HEAD_B = D_B // H_B
D_FF = 4 * D_MODEL
C_RWKV = 3 * D_A + N_DIR * (W_LORA + A_LORA) + G_LORA
D_IN = C_RWKV + 2 * D_B + 2 * D_MODEL
N_MOD = 6
EPS = 1e-6
GN_EPS = 64e-5
DECAY_SCALE = math.exp(-0.5)

kernel_name = 'hybrid_rwkv7_gmlp_diffusion_step'


def rmsnorm(x, g):
    xf = x.astype(jnp.float32)
    y = xf * lax.rsqrt(jnp.mean(xf * xf, axis=-1, keepdims=True) + EPS)
    return (y * g).astype(x.dtype)


def seq_shift(z):
    B, T, C = z.shape
    g = z.reshape(B, T, C // 2, 2)
    prev = jnp.pad(g[:, :-1, :, 0], ((0, 0), (1, 0), (0, 0)))
    nxt = jnp.pad(g[:, 1:, :, 1], ((0, 0), (0, 1), (0, 0)))
    return jnp.stack([prev, nxt], axis=-1).reshape(B, T, C)


def grid_shift(z):
    B, T, C = z.shape
    rows = T // GRID_W
    g = z.reshape(B, rows, GRID_W, C // 4, 4)
    left = jnp.pad(g[:, :, :-1, :, 0], ((0, 0), (0, 0), (1, 0), (0, 0)))
    right = jnp.pad(g[:, :, 1:, :, 1], ((0, 0), (0, 0), (0, 1), (0, 0)))
    up = jnp.pad(g[:, :-1, :, :, 2], ((0, 0), (1, 0), (0, 0), (0, 0)))
    down = jnp.pad(g[:, 1:, :, :, 3], ((0, 0), (0, 1), (0, 0), (0, 0)))
    return jnp.stack([left, right, up, down], axis=-1).reshape(B, T, C)


def _heads(t):
    return t.reshape(t.shape[:-1] + (H_A, HEAD_A))


def _bi_shared(t):
    return jnp.stack([t, jnp.flip(t, axis=1)])


def _bi_dir(t):
    return jnp.stack([t[0], jnp.flip(t[1], axis=1)])


def _rwkv_step(S, inp):
    r, w, k, v, neg_kk, b = inp
    sa = jnp.einsum('dbhij,dbhj->dbhi', S, neg_kk)
    S = S * w[..., None, :] + sa[..., :, None] * b[..., None, :] + v[..., :, None] * k[..., None, :]
    return S, jnp.einsum('dbhij,dbhj->dbhi', S, r)


def rwkv_branch(z, S0, p, l):
    B, T, _ = z.shape
    f32 = jnp.float32
    r, k, v, wd, ad, gd = jnp.split(
        z, [D_A, 2 * D_A, 3 * D_A, 3 * D_A + N_DIR * W_LORA, 3 * D_A + N_DIR * (W_LORA + A_LORA)], axis=-1)
    r, k, v = r.astype(f32), k.astype(f32), v.astype(f32)
    wd = jnp.tanh(wd.reshape(B, T, N_DIR, W_LORA).astype(f32))
    w_logit = p['w0'][l][:, None, None, :] + jnp.einsum('btdr,drc->dbtc', wd, p['w_up'][l])
    decay = jnp.exp(-DECAY_SCALE * jax.nn.sigmoid(w_logit))
    a = jax.nn.sigmoid(p['a0'][l][:, None, None, :] + jnp.einsum(
        'btdr,drc->dbtc', ad.reshape(B, T, N_DIR, A_LORA).astype(f32), p['a_up'][l]))
    g = jax.nn.sigmoid(gd.astype(f32)) @ p['g_up'][l]
    kk = _heads(k * p['k_k'][l])
    kk = kk * lax.rsqrt(jnp.sum(kk * kk, axis=-1, keepdims=True) + 1e-12)
    k_d = k[None] * (1 + (a - 1) * p['k_a'][l])
    rh, vh, k_dh = _heads(r), _heads(v), _heads(k_d)
    xs = (_bi_shared(rh), _bi_dir(_heads(decay)), _bi_dir(k_dh), _bi_shared(vh),
          _bi_shared(-kk), _bi_dir(_heads(a)) * _bi_shared(kk))
    xs = tuple(jnp.moveaxis(t, 2, 0) for t in xs)
    S_fin, ys = lax.scan(_rwkv_step, S0.astype(f32), xs)
    ys = jnp.moveaxis(ys, 0, 2)
    y = ys[0] + jnp.flip(ys[1], axis=1)
    mu = jnp.mean(y, axis=-1, keepdims=True)
    var = jnp.mean(jnp.square(y - mu), axis=-1, keepdims=True)
    y = (y - mu) * lax.rsqrt(var + GN_EPS) * _heads(p['lnx_g'][l]) + _heads(p['lnx_b'][l])
    bonus = jnp.einsum('dbthn,bthn->bth', k_dh * p['r_k'][l], rh)[..., None] * vh
    y = (y + bonus).reshape(B, T, D_A) * g
    return y.astype(z.dtype) @ p['w_branch_a'][l], S_fin


def chunk_mlp_branch(zu, zv, p, l):
    B, T, _ = zu.shape
    u = jax.nn.gelu(zu)
    v = jax.nn.gelu(zv).astype(jnp.float32)
    mu = jnp.mean(v, axis=-1, keepdims=True)
    var = jnp.mean(jnp.square(v - mu), axis=-1, keepdims=True)
    v = (v - mu) * lax.rsqrt(var + EPS) * p['ln_v_g'][l]
    vc = v.reshape(B, T // CHUNK, CHUNK, H_B, HEAD_B)
    s = jnp.einsum('hpq,bnqhc->bnphc', p['w_s'][l], vc) + jnp.transpose(p['b_s'][l])[:, :, None]
    y = u * s.reshape(B, T, D_B).astype(u.dtype)
    return y @ p['w_branch_b'][l]


def trunk_layer(x, cond, S0, shift_fn, p, l):
    mod = jax.nn.silu(cond) @ p['w_ada'][l] + p['b_ada'][l]
    sh1, sc1, g1, sh2, sc2, g2 = jnp.split(mod[:, None, :], N_MOD, axis=-1)
    h = rmsnorm(x, p['norm1_g'][l]) * (1 + sc1) + sh1
    z = h @ p['w_in'][l]
    z_rwkv, z_u, z_v, z_ga, z_gb = jnp.split(
        z, [C_RWKV, C_RWKV + D_B, C_RWKV + 2 * D_B, C_RWKV + 2 * D_B + D_MODEL], axis=-1)
    z_rwkv = z_rwkv + p['mu_shift'][l] * (shift_fn(z_rwkv) - z_rwkv)
    y_a, S_fin = rwkv_branch(z_rwkv, S0, p, l)
    y_b = chunk_mlp_branch(z_u, z_v, p, l)
    mixed = (jax.nn.sigmoid(z_ga) * y_a + jax.nn.sigmoid(z_gb) * y_b) @ p['w_out'][l]
    x = x + g1 * mixed
    h2 = rmsnorm(x, p['norm2_g'][l]) * (1 + sc2) + sh2
    x = x + g2 * (jnp.square(jax.nn.relu(h2 @ p['w1'][l])) @ p['w2'][l])
    return x, S_fin


def setup_inputs(seed: int = 0) -> dict:
    key = jax.random.key(seed)
    ks = jax.random.split(key, 32)
    f32 = jnp.float32

    def nrm(k, shape, scale):
        return jax.random.normal(k, shape, f32) * scale

    return {
        'x_prompt': nrm(ks[0], (BATCH, SEQ, D_MODEL), 1.0),
        'x_sample': nrm(ks[1], (DEC_BATCH, DEC_SEQ, D_MODEL), 1.0),
        'state_rwkv': nrm(ks[2], (DEC_BATCH, DEPTH, N_DIR, H_A, HEAD_A, HEAD_A), 0.5),
        'c': nrm(ks[3], (DEC_BATCH, D_MODEL), 1.0),
        'c_ctx': nrm(ks[4], (D_MODEL,), 1.0),
        'w_ada': nrm(ks[5], (DEPTH, D_MODEL, N_MOD * D_MODEL), 0.5 * D_MODEL ** -0.5),
        'b_ada': nrm(ks[6], (DEPTH, N_MOD * D_MODEL), 0.02),
        'norm1_g': 1.0 + nrm(ks[7], (DEPTH, D_MODEL), 0.02),
        'norm2_g': 1.0 + nrm(ks[8], (DEPTH, D_MODEL), 0.02),
        'w_in': nrm(ks[9], (DEPTH, D_MODEL, D_IN), D_MODEL ** -0.5),
        'mu_shift': jax.random.uniform(ks[10], (DEPTH, C_RWKV), f32),
        'w0': nrm(ks[11], (DEPTH, N_DIR, D_A), 0.5),
        'w_up': nrm(ks[12], (DEPTH, N_DIR, W_LORA, D_A), 0.5 * W_LORA ** -0.5),
        'a0': nrm(ks[13], (DEPTH, N_DIR, D_A), 0.5),
        'a_up': nrm(ks[14], (DEPTH, N_DIR, A_LORA, D_A), 0.5 * A_LORA ** -0.5),
        'g_up': nrm(ks[15], (DEPTH, G_LORA, D_A), G_LORA ** -0.5),
        'k_k': 0.85 + nrm(ks[16], (DEPTH, D_A), 0.05),
        'k_a': 1.0 + nrm(ks[17], (DEPTH, D_A), 0.05),
        'r_k': nrm(ks[18], (DEPTH, H_A, HEAD_A), 0.1),
        'lnx_g': 1.0 + nrm(ks[19], (DEPTH, D_A), 0.02),
        'lnx_b': nrm(ks[20], (DEPTH, D_A), 0.02),
        'w_branch_a': nrm(ks[21], (DEPTH, D_A, D_MODEL), D_A ** -0.5),
        'ln_v_g': 1.0 + nrm(ks[22], (DEPTH, D_B), 0.02),
        'w_s': nrm(ks[23], (DEPTH, H_B, CHUNK, CHUNK), 0.5 * CHUNK ** -0.5),
        'b_s': 1.0 + nrm(ks[24], (DEPTH, H_B, CHUNK), 0.02),
        'w_branch_b': nrm(ks[25], (DEPTH, D_B, D_MODEL), D_B ** -0.5),
        'w_out': nrm(ks[26], (DEPTH, D_MODEL, D_MODEL), D_MODEL ** -0.5),
        'w1': nrm(ks[27], (DEPTH, D_MODEL, D_FF), D_MODEL ** -0.5),
        'w2': nrm(ks[28], (DEPTH, D_FF, D_MODEL), D_FF ** -0.5),
        'final_g': 1.0 + nrm(ks[29], (D_MODEL,), 0.02),
    }


def reference(x_prompt, x_sample, state_rwkv, c, c_ctx, w_ada, b_ada, norm1_g, norm2_g, w_in, mu_shift,
              w0, w_up, a0, a_up, g_up, k_k, k_a, r_k, lnx_g, lnx_b, w_branch_a, ln_v_g, w_s, b_s,
              w_branch_b, w_out, w1, w2, final_g):
    p = dict(w_ada=w_ada, b_ada=b_ada, norm1_g=norm1_g, norm2_g=norm2_g, w_in=w_in, mu_shift=mu_shift,
             w0=w0, w_up=w_up, a0=a0, a_up=a_up, g_up=g_up, k_k=k_k, k_a=k_a, r_k=r_k, lnx_g=lnx_g,
             lnx_b=lnx_b, w_branch_a=w_branch_a, ln_v_g=ln_v_g, w_s=w_s, b_s=b_s, w_branch_b=w_branch_b,
             w_out=w_out, w1=w1, w2=w2)
    B_ctx = x_prompt.shape[0]
    S_zero = jnp.zeros((N_DIR, B_ctx, H_A, HEAD_A, HEAD_A), jnp.float32)
    xp, xs = x_prompt, x_sample
    ctx_states = []
    for l in range(DEPTH):
        xp, S_ctx = trunk_layer(xp, c_ctx[None, :], S_zero, seq_shift, p, l)
        ctx_states.append(jnp.moveaxis(S_ctx, 0, 1))
        S_lat0 = jnp.moveaxis(state_rwkv[:, l], 1, 0)
        xs, _ = trunk_layer(xs, c, S_lat0, grid_shift, p, l)
    y_prompt = rmsnorm(xp, final_g)
    y_sample = rmsnorm(xs, final_g)
    new_state_rwkv = jnp.stack(ctx_states, axis=1)
    return (y_prompt, y_sample, new_state_rwkv)
```

```python
import contextlib
import os
DBG = int(os.environ.get('KDBG', '99'))
KSKIP = os.environ.get('KSKIP', '')
import numpy as np
import concourse.bass as bass
import concourse.mybir as mybir
from concourse.bass_utils import run_bass_kernel_spmd

F32 = mybir.dt.float32
BF16 = mybir.dt.bfloat16
ALU = mybir.AluOpType
AF = mybir.ActivationFunctionType
AX = mybir.AxisListType

D = 1024
CR = 3456
DIN = 7552
DFF = 4096
NHEAD = 16
EPS = 1e-6
GN_EPS = 64e-5
DSC = float(np.exp(-0.5))


class Buf:
    __slots__ = ("name", "t", "w", "r", "dsem", "dcnt")

    def __init__(self, name, t=None):
        self.name = name
        self.t = t
        self.w = None
        self.r = []
        self.dsem = None
        self.dcnt = 0

    def __getitem__(self, idx):
        return self.t[idx]


class K:
    def __init__(self, nc):
        self.nc = nc
        self.eng = {"pe": nc.tensor, "act": nc.scalar, "dve": nc.vector, "pool": nc.gpsimd, "sp": nc.sync}
        self.sem = {}
        self.cnt = {}
        for e in self.eng:
            self.sem[e] = nc.alloc_semaphore(name="s_" + e)
            self.cnt[e] = 0
        self.waited = {}
        self.dsems = {}
        self.free_dsems = []
        self.ninstr = 0
        self.uid = 0

    def _wait(self, e, tok):
        if tok is None:
            return
        key, val = tok
        if key == e and e == "pe":
            return
        kk = (e, key)
        if self.waited.get(kk, 0) >= val:
            return
        self.waited[kk] = val
        self.eng[e].wait_ge(self.sem[key], val)
        self.ninstr += 1

    def _deps(self, e, reads, writes):
        for b in reads:
            self._wait(e, b.w)
        for b in writes:
            self._wait(e, b.w)
            for tok in b.r:
                self._wait(e, tok)

    def _commit(self, tok, reads, writes):
        for b in reads:
            if b not in writes:
                b.r.append(tok)
                if len(b.r) > 10:
                    best = {}
                    for k_, v_ in b.r:
                        if best.get(k_, -1) < v_:
                            best[k_] = v_
                    b.r = list(best.items())
        for b in writes:
            b.w = tok
            b.r = []

    def op(self, e, fn, reads=(), writes=()):
        reads = [b for b in reads if b is not None]
        writes = [b for b in writes if b is not None]
        self._deps(e, reads, writes)
        ins = fn(self.eng[e])
        self.cnt[e] += 1
        ins.then_inc(self.sem[e], 1)
        self.ninstr += 1
        self._commit((e, self.cnt[e]), reads, writes)

    def dma(self, q, out_ap, in_ap, reads=(), writes=(), sembuf=None, **kw):
        reads = [b for b in reads if b is not None]
        writes = [b for b in writes if b is not None]
        if sembuf is None:
            sembuf = (writes + reads)[0]
        if sembuf.dsem is None:
            if self.free_dsems:
                key, base = self.free_dsems.pop()
                sembuf.dcnt = base
            else:
                key = "d%d" % len(self.sem)
                self.sem[key] = self.nc.alloc_semaphore(name=key)
            self.dsems[key] = sembuf
            sembuf.dsem = key
        self._deps(q, reads, writes)
        ins = self.eng[q].dma_start(out=out_ap, in_=in_ap, **kw)
        sembuf.dcnt += 16
        ins.then_inc(self.sem[sembuf.dsem], 16)
        self.ninstr += 1
        self._commit((sembuf.dsem, sembuf.dcnt), reads, writes)

    def barrier(self):
        toks = [(e, self.cnt[e]) for e in self.eng if self.cnt[e] > 0]
        toks += [(key, b.dcnt) for key, b in self.dsems.items() if b.dcnt > 0]
        for e in self.eng:
            for tok in toks:
                if tok[0] != e:
                    self._wait(e, tok)
        for key, b in list(self.dsems.items()):
            if not getattr(b, "keep", False):
                self.free_dsems.append((key, b.dcnt))
                b.dsem = None
                del self.dsems[key]


def cs(a, n):
    return slice(a, a + n)


def hc(h):
    return slice((h % 2) * 512 + (h // 2) * 64, (h % 2) * 512 + (h // 2) * 64 + 64)


def build_program(NT, upto=99):
    T = NT * 128
    NG = NT // 2
    nc = bass.Bass("TRN2", target_bir_lowering=False)
    k = K(nc)

    def din(name, shape):
        return nc.dram_tensor(name, list(shape), F32, kind="ExternalInput").ap()

    x_d = din("x", [T, D])
    cond_d = din("cond", [128, 8])
    state0_d = din("state0", [2, 2, 128, 512])
    cmask_d = din("cmask", [128, 1])
    shm_d = din("shm", [128, 12 * 128])
    ident_d = din("ident", [128, 128])
    tri4_d = din("tri4", [128, 4 * 128])
    w_ada_d = din("w_ada", [2, D, 6 * D])
    b_ada_d = din("b_ada", [2, 6 * D])
    n1g_d = din("norm1_g", [2, D])
    n2g_d = din("norm2_g", [2, D])
    w_in_d = din("w_in", [2, D, DIN])
    mu_d = din("mu_shift", [2, CR])
    wup_d = din("wup_aug", [2, 2, 65, D])
    aup_d = din("aup_aug", [2, 2, 65, D])
    gup_d = din("g_up", [2, 128, D])
    kk_d = din("k_k", [2, D])
    ka_d = din("k_a", [2, D])
    rk_d = din("r_k", [2, D])
    lnxg_d = din("lnx_g", [2, D])
    lnxb_d = din("lnx_b", [2, D])
    wa_d = din("w_branch_a", [2, D, D])
    lnvg_d = din("ln_v_g", [2, D])
    wsT_d = din("wsT", [2, 128, 8 * 128])
    bsT_d = din("bsT", [2, 128, 8])
    wb_d = din("w_branch_b", [2, D, D])
    wo_d = din("w_out", [2, D, D])
    w1_d = din("w1", [2, D, DFF])
    w2_d = din("w2", [2, DFF, D])
    fg_d = din("final_g", [D])

    y_d = nc.dram_tensor("y", [T, D], F32, kind="ExternalOutput").ap()
    st_d = nc.dram_tensor("st_out", [2, 2, NG, 128, 512], F32, kind="ExternalOutput").ap()

    def dscr(name, shape, dt):
        return nc.dram_tensor(name, list(shape), dt, kind="Internal").ap()

    modbc_d = dscr("modbc", [2, 128, 6 * D], F32)
    zr_d = dscr("zr_s", [T, CR], BF16)
    zrest_d = dscr("zrest_s", [T, 4096], BF16)
    ysc_d = dscr("ysc_s", [2, T, D], F32)
    bon_d = dscr("bon_s", [2, T, 16], F32)
    x1_d = dscr("x1_s", [T, D], F32)
    x2_d = dscr("x2_s", [T, D], F32)

    class DR:
        def __init__(self, nm):
            self.b = {}
            self.nm = nm

        def __call__(self, *key):
            if key not in self.b:
                self.b[key] = Buf(self.nm + str(key))
            return self.b[key]

    R_mod, R_zr, R_zrest, R_ysc, R_bon, R_x1, R_x2, R_y, R_st = [DR(n) for n in
        ("mod", "zr", "zrest", "ysc", "bon", "x1", "x2", "y", "st")]

    PP = [nc.alloc_psum_tensor("psum%d" % i, [128, 1024], F32) for i in range(4)]
    PB = []
    for i in range(8):
        PB.append(Buf("pb%d" % i, PP[i // 2][:, cs((i % 2) * 512, 512)]))
    pst = {"b": 0, "p": 0}

    def bank():
        b = PB[pst["b"] % 8]
        pst["b"] += 1
        return b

    def pair():
        if pst["b"] % 2:
            pst["b"] += 1
        i = (pst["b"] % 8) // 2
        pst["b"] += 2
        return PP[i], PB[2 * i], PB[2 * i + 1]

    def sbt(es, name, shape, dt):
        k.uid += 1
        t = es.enter_context(nc.sbuf_tensor("%s_%d" % (name, k.uid), list(shape), dt))
        return Buf(name, t)

    ges = contextlib.ExitStack()
    ident_f = sbt(ges, "ident_f", [128, 128], F32)
    ident_b = sbt(ges, "ident_b", [128, 128], BF16)
    tri4 = sbt(ges, "tri4", [128, 4, 128], F32)
    ones_f = sbt(ges, "ones_f", [128, 1], F32)
    cmask = sbt(ges, "cmask", [128, 1], F32)
    k.dma("sp", ident_f[:], ident_d, writes=[ident_f])
    k.dma("sp", tri4[:].rearrange("p a b -> p (a b)"), tri4_d, writes=[tri4])
    k.dma("sp", cmask[:], cmask_d, writes=[cmask])
    k.op("dve", lambda e: e.tensor_copy(out=ident_b[:], in_=ident_f[:]), reads=[ident_f], writes=[ident_b])
    k.op("dve", lambda e: e.memset(ones_f[:], 1.0), writes=[ones_f])

    def bc_load(buf, dvec):
        k.dma("sp", buf[:], dvec.partition_broadcast(128), writes=[buf])

    def cast_load_rows(buf_ap_fn, dsrc, nk, ncol, buf):
        for kc in range(nk):
            k.dma("pool", buf_ap_fn(kc), dsrc[cs(kc * 128, 128), :], writes=[buf], max_dma_last_dim=4096)

    def rstd_from(e_ss, out_rstd, scale, eps):
        k.op("act", lambda e: e.activation(out=out_rstd[:], in_=e_ss[:], func=AF.Ln, bias=eps_t(eps)[:], scale=scale),
             reads=[e_ss, eps_t(eps)], writes=[out_rstd])
        k.op("act", lambda e: e.activation(out=out_rstd[:], in_=out_rstd[:], func=AF.Exp, scale=-0.5),
             reads=[out_rstd], writes=[out_rstd])

    eps_tiles = {}

    def eps_t(v):
        if v not in eps_tiles:
            b = sbt(ges, "eps%d" % len(eps_tiles), [128, 1], F32)
            k.op("dve", lambda e: e.memset(b[:], float(v)), writes=[b])
            eps_tiles[v] = b
        return eps_tiles[v]

    for v in (EPS, GN_EPS, 1e-12):
        eps_t(v)

    def phaseP(l):
        with contextlib.ExitStack() as es:
            wad = sbt(es, "wad", [128, 8, 6 * D], BF16)
            ba = sbt(es, "ba", [128, 6 * D], F32)
            mod = sbt(es, "mod", [128, 6 * D], F32)
            n1g = sbt(es, "n1g", [128, D], F32)
            n2g = sbt(es, "n2g", [128, D], F32)
            cnd = sbt(es, "cnd", [128, 8], F32)
            scb = sbt(es, "scb", [128, 8, 128], BF16)
            cast_load_rows(lambda kc: wad[:, kc, :], w_ada_d[l], 8, 6 * D, wad)
            bc_load(ba, b_ada_d[l])
            bc_load(n1g, n1g_d[l])
            bc_load(n2g, n2g_d[l])
            k.dma("sp", cnd[:], cond_d, writes=[cnd])
            k.op("act", lambda e: e.activation(out=cnd[:], in_=cnd[:], func=AF.Silu), reads=[cnd], writes=[cnd])
            k.op("dve", lambda e: e.tensor_copy(out=scb[:], in_=cnd[:].unsqueeze(2).to_broadcast([128, 8, 128])),
                 reads=[cnd], writes=[scb])
            for n in range(12):
                pb = bank()

                def f(e):
                    for kc in range(8):
                        ins = e.matmul(pb[:], lhsT=scb[:, kc, :], rhs=wad[:, kc, cs(n * 512, 512)],
                                       start=(kc == 0), stop=(kc == 7))
                    return ins
                k.op("pe", f, reads=[scb, wad], writes=[pb])
                k.op("dve", lambda e: e.tensor_tensor(out=mod[:, cs(n * 512, 512)], in0=pb[:], in1=ba[:, cs(n * 512, 512)],
                                                      op=ALU.add), reads=[pb, ba], writes=[mod])
            k.op("dve", lambda e: e.scalar_tensor_tensor(out=mod[:, cs(D, D)], in0=mod[:, cs(D, D)], scalar=1.0, in1=n1g[:],
                                                         op0=ALU.add, op1=ALU.mult), reads=[mod, n1g], writes=[mod])
            k.op("dve", lambda e: e.scalar_tensor_tensor(out=mod[:, cs(4 * D, D)], in0=mod[:, cs(4 * D, D)], scalar=1.0,
                                                         in1=n2g[:], op0=ALU.add, op1=ALU.mult), reads=[mod, n2g], writes=[mod])
            k.dma("sp", modbc_d[l], mod[:], reads=[mod], writes=[R_mod(l)], sembuf=mod)
            k.barrier()

    def load_mod(buf, l, j):
        k.dma("sp", buf[:], modbc_d[l][:, cs(j * D, D)], reads=[R_mod(l)], writes=[buf])

    def phaseA1(l, xsrc_d, R_xsrc):
        with contextlib.ExitStack() as es:
            win = sbt(es, "win", [128, 8, DIN], BF16)
            g1 = sbt(es, "g1", [128, D], F32)
            sh1 = sbt(es, "sh1", [128, D], F32)
            mu = sbt(es, "mu", [128, CR], F32)
            shm = sbt(es, "shm", [128, 12, 128], BF16)
            xt = [sbt(es, "xt0", [128, D], F32)]
            xt.append(xt[0])
            sq = sbt(es, "sq", [128, D], F32)
            hb = sbt(es, "hb", [128, D], BF16)
            hT = sbt(es, "hT", [128, 8, 128], BF16)
            ss = sbt(es, "ss", [128, 1], F32)
            rstd = sbt(es, "rstd", [128, 1], F32)
            zb = [sbt(es, "zb%d" % i, [128, CR], BF16) for i in range(2)]
            zm = [sbt(es, "zm%d" % i, [128, CR], BF16) for i in range(3)]
            zst = sbt(es, "zst", [128, CR], BF16)
            rst = [sbt(es, "rst%d" % i, [128, 512], BF16) for i in range(4)]
            cast_load_rows(lambda kc: win[:, kc, :], w_in_d[l], 8, DIN, win)
            k.dma("pool", shm[:].rearrange("p a b -> p (a b)"), shm_d, writes=[shm], max_dma_last_dim=4096)
            load_mod(sh1, l, 0)
            load_mod(g1, l, 1)
            bc_load(mu, mu_d[l])
            rsti = [0]

            def load_x(i):
                k.dma("sp", xt[i % 2][:], xsrc_d[cs(i * 128, 128), :], reads=[R_xsrc(i)], writes=[xt[i % 2]])

            def stage1(i):
                x = xt[i % 2]
                if DBG < 2:
                    if i + 1 < NT:
                        load_x(i + 1)
                    return
                k.op("act", lambda e: e.activation(out=sq[:], in_=x[:], func=AF.Square), reads=[x], writes=[sq])
                k.op("dve", lambda e: e.tensor_reduce(out=ss[:], in_=sq[:], axis=AX.X, op=ALU.add), reads=[sq], writes=[ss])
                rstd_from(ss, rstd, 1.0 / D, EPS)
                k.op("dve", lambda e: e.scalar_tensor_tensor(out=x[:], in0=x[:], scalar=rstd[:, 0:1], in1=g1[:],
                                                             op0=ALU.mult, op1=ALU.mult), reads=[x, rstd, g1], writes=[x])
                k.op("dve", lambda e: e.tensor_tensor(out=hb[:], in0=x[:], in1=sh1[:], op=ALU.add),
                     reads=[x, sh1], writes=[hb])
                if i + 1 < NT:
                    load_x(i + 1)
                if DBG < 3:
                    return
                transpose8(hb, hT)
                if DBG < 4:
                    return
                zbi, zmi = zb[i % 2], zm[i % 3]
                col = 0
                ci = 0
                while col < DIN:
                    if col < CR:
                        n = min(512, CR - col)
                    else:
                        n = 512
                    pb = bank()

                    def f(e):
                        for kc in range(8):
                            ins = e.matmul(pb[:, 0:n], lhsT=hT[:, kc, :], rhs=win[:, kc, cs(col, n)],
                                           start=(kc == 0), stop=(kc == 7))
                        return ins
                    k.op("pe", f, reads=[hT, win], writes=[pb])
                    if 'p' in KSKIP:
                        pass
                    elif col < CR:
                        if 'z' not in KSKIP:
                            k.op("act", lambda e: e.copy(out=zbi[:, cs(col, n)], in_=pb[:, 0:n]), reads=[pb], writes=[zbi])
                        if 'm' not in KSKIP:
                            k.op("dve", lambda e: e.tensor_tensor(out=zmi[:, cs(col, n)], in0=zbi[:, cs(col, n)], in1=mu[:, cs(col, n)],
                                                                  op=ALU.mult), reads=[zbi, mu], writes=[zmi])
                    else:
                        st = rst[rsti[0] % 4]
                        rsti[0] += 1
                        if ci % 2 == 0:
                            k.op("act", lambda e: e.copy(out=st[:], in_=pb[:]), reads=[pb], writes=[st])
                        else:
                            k.op("dve", lambda e: e.tensor_copy(out=st[:], in_=pb[:]), reads=[pb], writes=[st])
                        if 'r' not in KSKIP:
                            k.dma("sp", zrest_d[cs(i * 128, 128), cs(col - CR, 512)], st[:], reads=[st],
                                  writes=[R_zrest(i, (col - CR) // 512)], sembuf=st)
                    col += n
                    ci += 1

            def stage2(i):
                if DBG < 5:
                    return
                par = i % 2
                for cls in range(4):
                    nb = i - 1 if cls in (0, 2) else i + 1
                    for hh in range(2):
                        c0 = cls + 4 * 432 * hh
                        sl = slice(c0, c0 + 4 * 431 + 1, 4)
                        pb = bank()
                        srcs = [(ident_b[:], zb[i % 2], ident_b), (shm[:, 3 * cls, :], zm[i % 3], shm)]
                        if 0 <= nb < NT:
                            srcs.append((shm[:, 3 * cls + 1 + par, :], zm[nb % 3], shm))

                        def f(e):
                            for j, (lt, rb, _) in enumerate(srcs):
                                ins = e.matmul(pb[:, 0:432], lhsT=lt, rhs=rb[:, sl], start=(j == 0), stop=(j == len(srcs) - 1))
                            return ins
                        k.op("pe", f, reads=[s[1] for s in srcs] + [ident_b, shm], writes=[pb])
                        if hh == 0:
                            k.op("act", lambda e: e.copy(out=zst[:, sl], in_=pb[:, 0:432]), reads=[pb], writes=[zst])
                        else:
                            k.op("dve", lambda e: e.tensor_copy(out=zst[:, sl], in_=pb[:, 0:432]), reads=[pb], writes=[zst])
                k.dma("sp", zr_d[cs(i * 128, 128), :], zst[:], reads=[zst], writes=[R_zr(i)], sembuf=zst)

            load_x(0)
            stage1(0)
            for i in range(NT):
                if i + 1 < NT:
                    stage1(i + 1)
                stage2(i)
            k.barrier()

    def transpose8(src, dst, nblk=8, src_off=0):
        pb = bank()
        pv = pb[:].bitcast(BF16)

        def f(e):
            for j in range(nblk):
                ins = e.transpose(out=pv[:, cs(j * 128, 128)], in_=src[:, cs(src_off + j * 128, 128)], identity=ident_b[:])
            return ins
        k.op("pe", f, reads=[src, ident_b], writes=[pb])
        k.op("act", lambda e: e.copy(out=dst[:, 0:nblk, :].rearrange("p a b -> p (a b)"), in_=pv[:, 0:nblk * 128]),
             reads=[pb], writes=[dst])

    def phaseS(l, d):
        with contextlib.ExitStack() as es:
            kkc = sbt(es, "kkc", [128, D], F32)
            kac = sbt(es, "kac", [128, D], F32)
            rkc = sbt(es, "rkc", [128, D], F32)
            wup = sbt(es, "wup", [65, D], BF16)
            aup = sbt(es, "aup", [65, D], BF16)
            maskM = sbt(es, "maskM", [128, 4, 128], F32)
            maskN = sbt(es, "maskN", [128, 4, 128], F32)
            H = sbt(es, "H", [128, 8, 64], F32)
            Hb = sbt(es, "Hb", [128, 8, 64], BF16)
            zr = [sbt(es, "zr%d" % i, [128, CR], BF16) for i in range(2)]
            ldT = sbt(es, "ldT", [65, 2, 128], BF16)
            tw = sbt(es, "tw", [128, 128], BF16)
            SG = sbt(es, "SG", [128, D], F32)
            A = sbt(es, "A", [128, D], F32)
            KX = sbt(es, "KX", [128, D], F32)
            BP = sbt(es, "BP", [128, D], F32)
            KD = sbt(es, "KD", [128, D], F32)
            S0 = sbt(es, "S0", [128, D], F32)
            S1 = sbt(es, "S1", [128, D], F32)
            st16 = sbt(es, "st16", [128, 16], F32)
            rs16 = sbt(es, "rs16", [128, 16], F32)
            bon = sbt(es, "bon", [128, 16], F32)
            gC = sbt(es, "gC", [128, 8], F32)
            TM = sbt(es, "TM", [128, 4, D], BF16)
            Bg = sbt(es, "Bg", [128, D], BF16)
            Kg = sbt(es, "Kg", [128, D], BF16)
            FM = sbt(es, "FM", [128, 8, 4, 128], BF16)
            MB = sbt(es, "MB", [128, 16, 512], BF16)
            Pm = [sbt(es, "Pm%d" % i, [128, 16, 128], BF16) for i in range(2)]
            PT = [sbt(es, "PT%d" % i, [128, 16, 128], BF16) for i in range(2)]
            Xb = [sbt(es, "Xb%d" % i, [128, D], BF16) for i in range(2)]
            ysc = sbt(es, "ysc", [128, D], F32)

            bc_load(kkc, kk_d[l])
            bc_load(kac, ka_d[l])
            bc_load(rkc, rk_d[l])
            k.dma("pool", wup[:], wup_d[l, d], writes=[wup], max_dma_last_dim=4096)
            k.dma("pool", aup[:], aup_d[l, d], writes=[aup], max_dma_last_dim=4096)
            strict_i, incl_i, nmask_i = (2, 0, 3) if d == 0 else (3, 1, 2)
            for j, src in enumerate((strict_i, incl_i, strict_i, incl_i)):
                k.op("dve", lambda e: e.tensor_copy(out=maskM[:, j, :], in_=tri4[:, src, :]), reads=[tri4], writes=[maskM])
            for j in range(4):
                k.op("dve", lambda e: e.tensor_copy(out=maskN[:, j, :], in_=tri4[:, nmask_i, :]), reads=[tri4], writes=[maskN])
            tri_incl = tri4[:, incl_i, :]
            tri_excl = tri4[:, strict_i, :]
            tri_dg = tri4[:, nmask_i, :]
            k.op("dve", lambda e: e.memset(ldT[:], 1.0), writes=[ldT])
            k.dma("sp", H[:].rearrange("p a b -> p (a b)"), state0_d[l, d], writes=[H])
            k.op("act", lambda e: e.copy(out=Hb[:], in_=H[:]), reads=[H], writes=[Hb])

            order = list(range(NT)) if d == 0 else list(range(NT - 1, -1, -1))

            def load_zr(n):
                i = order[n]
                k.dma("sp", zr[n % 2][:], zr_d[cs(i * 128, 128), :], reads=[R_zr(i)], writes=[zr[n % 2]])

            load_zr(0)
            for n, i in enumerate(order):
                z = zr[n % 2]
                if n + 1 < NT:
                    load_zr(n + 1)
                rq = z[:, 0:D]
                kq = z[:, D:2 * D]
                vq = z[:, 2 * D:3 * D]
                k.op("act", lambda e: e.activation(out=tw[:, 0:64], in_=z[:, cs(3 * D + 64 * d, 64)], func=AF.Tanh),
                     reads=[z], writes=[tw])
                k.op("dve", lambda e: e.tensor_copy(out=tw[:, 64:128], in_=z[:, cs(3 * D + 128 + 64 * d, 64)]),
                     reads=[z], writes=[tw])
                pb = bank()
                pv = pb[:].bitcast(BF16)

                def f(e):
                    e.transpose(out=pv[0:64, 0:128], in_=tw[:, 0:64], identity=ident_b[:])
                    return e.transpose(out=pv[0:64, 128:256], in_=tw[:, 64:128], identity=ident_b[:])
                k.op("pe", f, reads=[tw, ident_b], writes=[pb])
                k.op("act", lambda e: e.copy(out=ldT[0:64, :, :].rearrange("p a b -> p (a b)"), in_=pv[0:64, 0:256]),
                     reads=[pb], writes=[ldT])
                for (wmat, src_j, dst) in ((wup, 0, SG), (aup, 1, A)):
                    pp, b0, b1 = pair()

                    def f(e):
                        e.matmul(pp[:, 0:512], lhsT=ldT[:, src_j, :], rhs=wmat[:, 0:512], start=True, stop=True)
                        return e.matmul(pp[:, 512:1024], lhsT=ldT[:, src_j, :], rhs=wmat[:, 512:1024], start=True, stop=True)
                    k.op("pe", f, reads=[ldT, wmat], writes=[b0, b1])
                    k.op("act", lambda e: e.activation(out=dst[:], in_=pp[:, :], func=AF.Sigmoid), reads=[b0, b1], writes=[dst])
                if DBG < 11:
                    continue
                k.op("dve", lambda e: e.tensor_tensor(out=KX[:], in0=kq, in1=kkc[:], op=ALU.mult), reads=[z, kkc], writes=[KX])
                k.op("dve", lambda e: e.tensor_tensor(out=S0[:], in0=KX[:], in1=KX[:], op=ALU.mult), reads=[KX], writes=[S0])
                k.op("dve", lambda e: e.tensor_reduce(out=st16[:], in_=S0[:].rearrange("p (h n) -> p h n", n=64), axis=AX.X,
                                                      op=ALU.add), reads=[S0], writes=[st16])
                rstd_from(st16, rs16, 1.0, 1e-12)
                k.op("dve", lambda e: e.tensor_tensor(out=KX[:].rearrange("p (h n) -> p h n", n=64),
                                                      in0=KX[:].rearrange("p (h n) -> p h n", n=64),
                                                      in1=rs16[:].unsqueeze(2).to_broadcast([128, 16, 64]), op=ALU.mult),
                     reads=[KX, rs16], writes=[KX])
                k.op("dve", lambda e: e.scalar_tensor_tensor(out=BP[:], in0=KX[:], scalar=-1.0, in1=A[:], op0=ALU.mult,
                                                             op1=ALU.mult), reads=[KX, A], writes=[BP])
                k.op("dve", lambda e: e.tensor_tensor(out=S0[:], in0=kq, in1=kac[:], op=ALU.mult), reads=[z, kac], writes=[S0])
                k.op("dve", lambda e: e.scalar_tensor_tensor(out=S0[:], in0=A[:], scalar=-1.0, in1=S0[:], op0=ALU.add,
                                                             op1=ALU.mult), reads=[A, S0], writes=[S0])
                k.op("dve", lambda e: e.tensor_tensor(out=KD[:], in0=S0[:], in1=kq, op=ALU.add), reads=[S0, z], writes=[KD])
                k.op("dve", lambda e: e.tensor_tensor(out=S0[:], in0=KD[:], in1=rkc[:], op=ALU.mult), reads=[KD, rkc], writes=[S0])
                k.op("dve", lambda e: e.tensor_tensor(out=S0[:], in0=S0[:], in1=rq, op=ALU.mult), reads=[S0, z], writes=[S0])
                k.op("dve", lambda e: e.tensor_reduce(out=bon[:], in_=S0[:].rearrange("p (h n) -> p h n", n=64), axis=AX.X,
                                                      op=ALU.add), reads=[S0], writes=[bon])
                k.dma("sp", bon_d[d, cs(i * 128, 128), :], bon[:], reads=[bon], writes=[R_bon(d, i)], sembuf=bon)
                if DBG < 12:
                    continue
                def cum(tri_ap, scale, dstE):
                    pp, b0, b1 = pair()

                    def f(e):
                        e.matmul(pp[:, 0:512], lhsT=tri_ap, rhs=SG[:, 0:512], start=True, stop=True)
                        return e.matmul(pp[:, 512:1024], lhsT=tri_ap, rhs=SG[:, 512:1024], start=True, stop=True)
                    k.op("pe", f, reads=[tri4, SG], writes=[b0, b1])
                    k.op("act", lambda e: e.activation(out=dstE[:], in_=pp[:, :], func=AF.Exp, scale=scale),
                         reads=[b0, b1], writes=[dstE])
                    return pp, b0, b1
                pp, b0, b1 = cum(tri_incl, -DSC, S0)
                k.op("dve", lambda e: e.tensor_tensor(out=TM[:, 1, :], in0=rq, in1=S0[:], op=ALU.mult), reads=[z, S0], writes=[TM])
                k.op("act", lambda e: e.activation(out=S1[:], in_=pp[:, :], func=AF.Exp, scale=DSC), reads=[b0, b1], writes=[S1])
                k.op("dve", lambda e: e.tensor_tensor(out=TM[:, 2, :], in0=BP[:], in1=S1[:], op=ALU.mult), reads=[BP, S1], writes=[TM])
                k.op("dve", lambda e: e.tensor_tensor(out=TM[:, 3, :], in0=KD[:], in1=S1[:], op=ALU.mult), reads=[KD, S1], writes=[TM])
                cum(tri_excl, -DSC, S0)
                k.op("dve", lambda e: e.tensor_tensor(out=TM[:, 0, :], in0=KX[:], in1=S0[:], op=ALU.mult), reads=[KX, S0], writes=[TM])
                cum(tri_dg, -DSC, S1)
                k.op("dve", lambda e: e.tensor_tensor(out=Bg[:], in0=BP[:], in1=S1[:], op=ALU.mult), reads=[BP, S1], writes=[Bg])
                k.op("dve", lambda e: e.tensor_tensor(out=Kg[:], in0=KD[:], in1=S1[:], op=ALU.mult), reads=[KD, S1], writes=[Kg])
                pbg = bank()

                def f(e):
                    for ct in range(8):
                        ins = e.matmul(pbg[:, ct:ct + 1], lhsT=SG[:, cs(ct * 128, 128)], rhs=ones_f[:], start=True, stop=True)
                    return ins
                k.op("pe", f, reads=[SG, ones_f], writes=[pbg])
                k.op("act", lambda e: e.activation(out=gC[:], in_=pbg[:, 0:8], func=AF.Exp, scale=-DSC), reads=[pbg], writes=[gC])
                if DBG < 13:
                    continue
                for g4 in range(4):
                    pb = bank()
                    pv = pb[:].bitcast(BF16)

                    def f(e):
                        for c2 in range(2):
                            ct = g4 * 2 + c2
                            for q in range(4):
                                ins = e.transpose(out=pv[:, cs((c2 * 4 + q) * 128, 128)], in_=TM[:, q, cs(ct * 128, 128)],
                                                  identity=ident_b[:])
                        return ins
                    k.op("pe", f, reads=[TM, ident_b], writes=[pb])
                    eng = "act" if g4 % 2 == 0 else "dve"
                    if eng == "act":
                        k.op("act", lambda e: e.copy(out=FM[:, g4 * 2:g4 * 2 + 2, :, :].rearrange("p a b c -> p (a b c)"),
                                                     in_=pv[:, :]), reads=[pb], writes=[FM])
                    else:
                        k.op("dve", lambda e: e.tensor_copy(out=FM[:, g4 * 2:g4 * 2 + 2, :, :].rearrange("p a b c -> p (a b c)"),
                                                            in_=pv[:, :]), reads=[pb], writes=[FM])
                if DBG < 14:
                    continue
                P0, PT0 = Pm[0], PT[0]
                for h in range(NHEAD):
                    ct, p0 = h // 2, (h % 2) * 64
                    if 'o' in KSKIP and h % 2 == 1:
                        continue
                    pb = bank()

                    def f(e):
                        e.matmul(pb[:, 0:256], lhsT=FM[p0:p0 + 64, ct, 2, :],
                                 rhs=FM[p0:p0 + 64, ct, 0:2, :].rearrange("p a b -> p (a b)"), start=True, stop=True)
                        return e.matmul(pb[:, 256:512], lhsT=FM[p0:p0 + 64, ct, 3, :],
                                        rhs=FM[p0:p0 + 64, ct, 0:2, :].rearrange("p a b -> p (a b)"), start=True, stop=True)
                    k.op("pe", f, reads=[FM], writes=[pb])
                    k.op("dve", lambda e: e.tensor_tensor(out=MB[:, h, :], in0=pb[:], in1=maskM[:].rearrange("p a b -> p (a b)"),
                                                          op=ALU.mult), reads=[pb, maskM], writes=[MB])
                    k.op("act", lambda e: e.copy(out=PT0[:, h, :], in_=MB[:, h, 0:128]), reads=[MB], writes=[PT0])
                for g in range(4):
                    if 'n' in KSKIP:
                        continue
                    pb = bank()

                    def f(e):
                        for hh in range(4):
                            h = (g % 2) + 2 * (4 * (g // 2) + hh)
                            ct, p0 = h // 2, (h % 2) * 64
                            ins = e.matmul(pb[:, cs(hh * 128, 128)], lhsT=FM[p0:p0 + 64, ct, 0, :], rhs=FM[p0:p0 + 64, ct, 2, :],
                                           start=True, stop=True)
                        return ins
                    k.op("pe", f, reads=[FM], writes=[pb])
                    h0 = (g % 2) + 8 * (g // 2)
                    k.op("dve", lambda e: e.tensor_tensor(out=P0[:, h0:h0 + 7:2, :],
                                                          in0=pb[:].rearrange("p (a b) -> p a b", b=128), in1=maskN[:], op=ALU.mult),
                         reads=[pb, maskN], writes=[P0])
                if DBG < 15:
                    continue
                pp, b0, b1 = pair()

                def f(e):
                    for h in range(NHEAD):
                        ct, p0 = h // 2, (h % 2) * 64
                        e.matmul(pp[:, hc(h)], lhsT=FM[p0:p0 + 64, ct, 0, :], rhs=Hb[p0:p0 + 64, ct, :],
                                 start=True, stop=False)
                        ins = e.matmul(pp[:, hc(h)], lhsT=MB[:, h, 256:384], rhs=z[:, cs(2 * D + h * 64, 64)],
                                       start=False, stop=True)
                    return ins
                k.op("pe", f, reads=[FM, Hb, MB, z], writes=[b0, b1])
                xc = Xb[0]
                k.op("act", lambda e: e.copy(out=xc[:], in_=pp[:, :]), reads=[b0, b1], writes=[xc])
                if DBG < 16:
                    continue
                cur = 0
                for lev in range(7):
                    Pc, PTc = Pm[cur], PT[cur]
                    pp, b0, b1 = pair()
                    xc, xn = Xb[lev % 2], Xb[(lev + 1) % 2]

                    def f(e):
                        for h in range(NHEAD):
                            ins = e.matmul(pp[:, hc(h)], lhsT=PTc[:, h, :], rhs=xc[:, hc(h)], start=True, stop=True)
                        return ins
                    k.op("pe", f, reads=[PTc, xc], writes=[b0, b1])
                    k.op("dve", lambda e: e.tensor_tensor(out=xn[:], in0=pp[:, :], in1=xc[:], op=ALU.add),
                         reads=[b0, b1, xc], writes=[xn])
                    if lev < 6:
                        Pn, PTn = Pm[1 - cur], PT[1 - cur]
                        for g in range(4):
                            for which in range(2):
                                pb = bank()

                                def f(e):
                                    for hh in range(4):
                                        h = g * 4 + hh
                                        if which == 0:
                                            ins = e.matmul(pb[:, cs(hh * 128, 128)], lhsT=PTc[:, h, :], rhs=Pc[:, h, :], start=True, stop=True)
                                        else:
                                            ins = e.matmul(pb[:, cs(hh * 128, 128)], lhsT=Pc[:, h, :], rhs=PTc[:, h, :], start=True, stop=True)
                                    return ins
                                k.op("pe", f, reads=[Pc, PTc], writes=[pb])
                                dstb = Pn if which == 0 else PTn
                                if which == 0:
                                    k.op("act", lambda e: e.copy(out=dstb[:, g * 4:g * 4 + 4, :].rearrange("p a b -> p (a b)"),
                                                                 in_=pb[:]), reads=[pb], writes=[dstb])
                                else:
                                    k.op("dve", lambda e: e.tensor_copy(out=dstb[:, g * 4:g * 4 + 4, :].rearrange("p a b -> p (a b)"),
                                                                        in_=pb[:]), reads=[pb], writes=[dstb])
                        cur = 1 - cur
                U = Xb[1]
                if DBG < 17:
                    continue
                pp, b0, b1 = pair()

                def f(e):
                    for h in range(NHEAD):
                        ct, p0 = h // 2, (h % 2) * 64
                        e.matmul(pp[:, hc(h)], lhsT=FM[p0:p0 + 64, ct, 1, :], rhs=Hb[p0:p0 + 64, ct, :], start=True, stop=False)
                        e.matmul(pp[:, hc(h)], lhsT=MB[:, h, 128:256], rhs=U[:, hc(h)], start=False, stop=False)
                        ins = e.matmul(pp[:, hc(h)], lhsT=MB[:, h, 384:512], rhs=z[:, cs(2 * D + h * 64, 64)],
                                       start=False, stop=True)
                    return ins
                k.op("pe", f, reads=[FM, Hb, MB, U, z], writes=[b0, b1])
                k.op("act", lambda e: e.copy(out=ysc[:].rearrange("p (hp h2 i) -> p h2 hp i", h2=2, i=64),
                                             in_=pp[:, :].rearrange("p (h2 hp i) -> p h2 hp i", hp=8, i=64)), reads=[b0, b1], writes=[ysc])
                k.dma("sp", ysc_d[d, cs(i * 128, 128), :], ysc[:], reads=[ysc], writes=[R_ysc(d, i)], sembuf=ysc)
                if DBG < 18:
                    continue
                pp, b0, b1 = pair()

                def f(e):
                    for ct in range(8):
                        e.matmul(pp[:, cs(ct * 128, 128)], lhsT=Bg[:, cs(ct * 128, 128)],
                                 rhs=U[:].rearrange("p (a b c) -> p a b c", a=2, b=8)[:, :, ct, :], start=True, stop=False)
                        ins = e.matmul(pp[:, cs(ct * 128, 128)], lhsT=Kg[:, cs(ct * 128, 128)], rhs=z[:, cs(2 * D + ct * 128, 128)],
                                       start=False, stop=True)
                    return ins
                k.op("pe", f, reads=[Bg, Kg, U, z], writes=[b0, b1])
                k.op("dve", lambda e: e.tensor_tensor(out=H[:], in0=H[:], in1=gC[:].unsqueeze(2).to_broadcast([128, 8, 64]),
                                                      op=ALU.mult), reads=[H, gC], writes=[H])
                ppv = pp[:, :].rearrange("p (a b) -> p a b", b=128)
                k.op("dve", lambda e: e.tensor_tensor(out=H[0:64, :, :], in0=H[0:64, :, :], in1=ppv[0:64, :, 0:64], op=ALU.add),
                     reads=[H, b0, b1], writes=[H])
                k.op("dve", lambda e: e.tensor_tensor(out=H[64:128, :, :], in0=H[64:128, :, :], in1=ppv[64:128, :, 64:128], op=ALU.add),
                     reads=[H, b0, b1], writes=[H])
                if n % 2 == 1:
                    grp = i // 2
                    k.dma("sp", st_d[l, d, grp], H[:].rearrange("p a b -> p (a b)"), reads=[H], writes=[R_st(l, d, grp)], sembuf=H)
                    k.op("dve", lambda e: e.tensor_scalar(out=H[:], in0=H[:], scalar1=cmask[:, 0:1], scalar2=None, op0=ALU.mult),
                         reads=[H, cmask], writes=[H])
                k.op("act", lambda e: e.copy(out=Hb[:], in_=H[:]), reads=[H], writes=[Hb])
            k.barrier()

    def phaseM(l, xsrc_d, R_xsrc):
        with contextlib.ExitStack() as es:
            wa = sbt(es, "wa", [128, 8, D], BF16)
            wb = sbt(es, "wb", [128, 8, D], BF16)
            wo = sbt(es, "wo", [128, 8, D], BF16)
            gup = sbt(es, "gup", [128, D], BF16)
            wsT = sbt(es, "wsT", [128, 8, 128], BF16)
            bsT = sbt(es, "bsT", [128, 8], F32)
            lnxg = sbt(es, "lnxg", [128, D], F32)
            lnxb = sbt(es, "lnxb", [128, D], F32)
            lnvg = sbt(es, "lnvg", [128, D], F32)
            gate1 = sbt(es, "gate1", [128, D], F32)
            zv = sbt(es, "zv", [128, D + 128], BF16)
            zrs = sbt(es, "zrs", [128, 4096], BF16)
            yf = sbt(es, "yf", [128, D], F32)
            yb = sbt(es, "yb", [128, D], F32)
            b0t = sbt(es, "b0t", [128, 16], F32)
            b1t = sbt(es, "b1t", [128, 16], F32)
            xt = sbt(es, "xtm", [128, D], F32)
            W0 = sbt(es, "W0", [128, D], F32)
            W1 = sbt(es, "W1", [128, D], F32)
            W2 = sbt(es, "W2", [128, D], F32)
            s16a = sbt(es, "s16a", [128, 16], F32)
            s16b = sbt(es, "s16b", [128, 16], F32)
            s16c = sbt(es, "s16c", [128, 16], F32)
            bnst = sbt(es, "bnst", [128, 2, 6], F32)
            mv = sbt(es, "mv", [128, 2], F32)
            rsv = sbt(es, "rsv", [128, 1], F32)
            gsb = sbt(es, "gsb", [128, 128], BF16)
            gT = sbt(es, "gT", [128, 1, 128], BF16)
            actb = sbt(es, "actb", [128, D], BF16)
            actT = sbt(es, "actT", [128, 8, 128], BF16)
            ub = sbt(es, "ub", [128, D], BF16)
            vcb = sbt(es, "vcb", [128, D], BF16)
            cast_load_rows(lambda kc: wa[:, kc, :], wa_d[l], 8, D, wa)
            cast_load_rows(lambda kc: wb[:, kc, :], wb_d[l], 8, D, wb)
            cast_load_rows(lambda kc: wo[:, kc, :], wo_d[l], 8, D, wo)
            k.dma("pool", gup[:], gup_d[l], writes=[gup], max_dma_last_dim=4096)
            k.dma("pool", wsT[:].rearrange("p a b -> p (a b)"), wsT_d[l], writes=[wsT], max_dma_last_dim=4096)
            k.dma("sp", bsT[:], bsT_d[l], writes=[bsT])
            bc_load(lnxg, lnxg_d[l])
            bc_load(lnxb, lnxb_d[l])
            bc_load(lnvg, lnvg_d[l])
            load_mod(gate1, l, 2)

            def v3(b):
                return b[:].rearrange("p (h n) -> p h n", n=64)

            def bc16(b):
                return b[:].unsqueeze(2).to_broadcast([128, 16, 64])

            def proj(src_bf, wmat):
                transpose8(src_bf, actT)
                pp, p0, p1 = pair()

                def f(e):
                    for nn in range(2):
                        for kc in range(8):
                            ins = e.matmul(pp[:, cs(nn * 512, 512)], lhsT=actT[:, kc, :], rhs=wmat[:, kc, cs(nn * 512, 512)],
                                           start=(kc == 0), stop=(kc == 7))
                    return ins
                k.op("pe", f, reads=[actT, wmat], writes=[p0, p1])
                return pp, p0, p1

            for i in range(NT):
                rows = cs(i * 128, 128)
                k.dma("sp", zv[:, 0:D], zr_d[rows, 2 * D:3 * D], reads=[R_zr(i)], writes=[zv])
                k.dma("sp", zv[:, D:D + 128], zr_d[rows, cs(3 * D + 256, 128)], reads=[R_zr(i)], writes=[zv])
                k.dma("sp", zrs[:], zrest_d[rows, :], reads=[R_zrest(i, j) for j in range(8)], writes=[zrs])
                k.dma("sp", yf[:], ysc_d[0, rows, :], reads=[R_ysc(0, i)], writes=[yf])
                k.dma("sp", yb[:], ysc_d[1, rows, :], reads=[R_ysc(1, i)], writes=[yb])
                k.dma("sp", b0t[:], bon_d[0, rows, :], reads=[R_bon(0, i)], writes=[b0t])
                k.dma("sp", b1t[:], bon_d[1, rows, :], reads=[R_bon(1, i)], writes=[b1t])
                k.dma("sp", xt[:], xsrc_d[rows, :], reads=[R_xsrc(i)], writes=[xt])
                k.op("dve", lambda e: e.tensor_tensor(out=yf[:], in0=yf[:], in1=yb[:], op=ALU.add), reads=[yf, yb], writes=[yf])
                k.op("dve", lambda e: e.tensor_reduce(out=s16a[:], in_=v3(yf), axis=AX.X, op=ALU.add), reads=[yf], writes=[s16a])
                k.op("dve", lambda e: e.tensor_scalar(out=s16a[:], in0=s16a[:], scalar1=1.0 / 64, scalar2=None, op0=ALU.mult),
                     reads=[s16a], writes=[s16a])
                k.op("dve", lambda e: e.tensor_tensor(out=v3(yf), in0=v3(yf), in1=bc16(s16a), op=ALU.subtract), reads=[yf, s16a], writes=[yf])
                k.op("dve", lambda e: e.tensor_tensor(out=W0[:], in0=yf[:], in1=yf[:], op=ALU.mult), reads=[yf], writes=[W0])
                k.op("dve", lambda e: e.tensor_reduce(out=s16b[:], in_=v3(W0), axis=AX.X, op=ALU.add), reads=[W0], writes=[s16b])
                rstd_from(s16b, s16c, 1.0 / 64, GN_EPS)
                k.op("dve", lambda e: e.tensor_tensor(out=v3(yf), in0=v3(yf), in1=bc16(s16c), op=ALU.mult), reads=[yf, s16c], writes=[yf])
                k.op("dve", lambda e: e.tensor_tensor(out=yf[:], in0=yf[:], in1=lnxg[:], op=ALU.mult), reads=[yf, lnxg], writes=[yf])
                k.op("dve", lambda e: e.tensor_tensor(out=yf[:], in0=yf[:], in1=lnxb[:], op=ALU.add), reads=[yf, lnxb], writes=[yf])
                k.op("dve", lambda e: e.tensor_tensor(out=b0t[:], in0=b0t[:], in1=b1t[:], op=ALU.add), reads=[b0t, b1t], writes=[b0t])
                k.op("dve", lambda e: e.tensor_tensor(out=v3(W0), in0=zv[:, 0:D].rearrange("p (h n) -> p h n", n=64), in1=bc16(b0t),
                                                      op=ALU.mult), reads=[zv, b0t], writes=[W0])
                k.op("dve", lambda e: e.tensor_tensor(out=yf[:], in0=yf[:], in1=W0[:], op=ALU.add), reads=[yf, W0], writes=[yf])
                k.op("act", lambda e: e.activation(out=gsb[:], in_=zv[:, D:D + 128], func=AF.Sigmoid), reads=[zv], writes=[gsb])
                transpose8(gsb, gT, nblk=1)
                pp, p0, p1 = pair()

                def f(e):
                    e.matmul(pp[:, 0:512], lhsT=gT[:, 0, :], rhs=gup[:, 0:512], start=True, stop=True)
                    return e.matmul(pp[:, 512:1024], lhsT=gT[:, 0, :], rhs=gup[:, 512:1024], start=True, stop=True)
                k.op("pe", f, reads=[gT, gup], writes=[p0, p1])
                k.op("dve", lambda e: e.tensor_tensor(out=actb[:], in0=pp[:, :], in1=yf[:], op=ALU.mult), reads=[p0, p1, yf], writes=[actb])
                pp, p0, p1 = proj(actb, wa)
                k.op("act", lambda e: e.activation(out=W0[:], in_=zrs[:, 2048:3072], func=AF.Sigmoid), reads=[zrs], writes=[W0])
                k.op("dve", lambda e: e.tensor_tensor(out=W2[:], in0=pp[:, :], in1=W0[:], op=ALU.mult), reads=[p0, p1, W0], writes=[W2])
                k.op("act", lambda e: e.activation(out=ub[:], in_=zrs[:, 0:1024], func=AF.Gelu_apprx_tanh), reads=[zrs], writes=[ub])
                k.op("act", lambda e: e.activation(out=W1[:], in_=zrs[:, 1024:2048], func=AF.Gelu_apprx_tanh), reads=[zrs], writes=[W1])
                for c in range(2):
                    k.op("dve", lambda e: e.bn_stats(out=bnst[:, c, :], in_=W1[:, cs(c * 512, 512)]), reads=[W1], writes=[bnst])
                k.op("dve", lambda e: e.bn_aggr(out=mv[:], in_=bnst[:].rearrange("p a b -> p (a b)")), reads=[bnst], writes=[mv])
                rstd_from_ap(mv, 1, rsv, EPS)
                k.op("dve", lambda e: e.tensor_scalar(out=W1[:], in0=W1[:], scalar1=mv[:, 0:1], scalar2=rsv[:, 0:1], op0=ALU.subtract,
                                                      op1=ALU.mult), reads=[W1, mv, rsv], writes=[W1])
                k.op("dve", lambda e: e.tensor_tensor(out=vcb[:], in0=W1[:], in1=lnvg[:], op=ALU.mult), reads=[W1, lnvg], writes=[vcb])
                pp, p0, p1 = pair()

                def f(e):
                    for g in range(8):
                        ins = e.matmul(pp[:, cs(g * 128, 128)], lhsT=wsT[:, g, :], rhs=vcb[:, cs(g * 128, 128)], start=True, stop=True)
                    return ins
                k.op("pe", f, reads=[wsT, vcb], writes=[p0, p1])
                k.op("dve", lambda e: e.tensor_tensor(out=W1[:].rearrange("p (g c) -> p g c", c=128),
                                                      in0=pp[:, :].rearrange("p (g c) -> p g c", c=128),
                                                      in1=bsT[:].unsqueeze(2).to_broadcast([128, 8, 128]), op=ALU.add),
                     reads=[p0, p1, bsT], writes=[W1])
                k.op("dve", lambda e: e.tensor_tensor(out=actb[:], in0=W1[:], in1=ub[:], op=ALU.mult), reads=[W1, ub], writes=[actb])
                pp, p0, p1 = proj(actb, wb)
                k.op("act", lambda e: e.activation(out=W0[:], in_=zrs[:, 3072:4096], func=AF.Sigmoid), reads=[zrs], writes=[W0])
                k.op("dve", lambda e: e.tensor_tensor(out=W1[:], in0=pp[:, :], in1=W0[:], op=ALU.mult), reads=[p0, p1, W0], writes=[W1])
                k.op("dve", lambda e: e.tensor_tensor(out=actb[:], in0=W1[:], in1=W2[:], op=ALU.add), reads=[W1, W2], writes=[actb])
                pp, p0, p1 = proj(actb, wo)
                k.op("dve", lambda e: e.tensor_tensor(out=W0[:], in0=pp[:, :], in1=gate1[:], op=ALU.mult), reads=[p0, p1, gate1], writes=[W0])
                k.op("dve", lambda e: e.tensor_tensor(out=W0[:], in0=W0[:], in1=xt[:], op=ALU.add), reads=[W0, xt], writes=[W0])
                k.dma("sp", x1_d[rows, :], W0[:], reads=[W0], writes=[R_x1(i)], sembuf=W0)
            k.barrier()

    def rstd_from_ap(mvb, col, out_rstd, eps):
        k.op("act", lambda e: e.activation(out=out_rstd[:], in_=mvb[:, col:col + 1], func=AF.Ln, bias=eps_t(eps)[:], scale=1.0),
             reads=[mvb, eps_t(eps)], writes=[out_rstd])
        k.op("act", lambda e: e.activation(out=out_rstd[:], in_=out_rstd[:], func=AF.Exp, scale=-0.5),
             reads=[out_rstd], writes=[out_rstd])

    def phaseC(l, last):
        with contextlib.ExitStack() as es:
            w1 = sbt(es, "w1", [128, 8, DFF], BF16)
            w2 = sbt(es, "w2", [128, 32, D], BF16)
            g2 = sbt(es, "g2", [128, D], F32)
            sh2 = sbt(es, "sh2", [128, D], F32)
            gate2 = sbt(es, "gate2", [128, D], F32)
            fg = sbt(es, "fg", [128, D], F32)
            xt = [sbt(es, "xc%d" % i, [128, D], F32) for i in range(2)]
            W0 = sbt(es, "Wc0", [128, D], F32)
            hb = sbt(es, "hb2", [128, D], BF16)
            hT = sbt(es, "hT2", [128, 8, 128], BF16)
            rl = sbt(es, "rl", [128, 512], BF16)
            hid = sbt(es, "hid", [128, DFF], BF16)
            hidT = sbt(es, "hidT", [128, 32, 128], BF16)
            ss = sbt(es, "ssc", [128, 1], F32)
            rstd = sbt(es, "rstdc", [128, 1], F32)
            cast_load_rows(lambda kc: w1[:, kc, :], w1_d[l], 8, DFF, w1)
            cast_load_rows(lambda kc: w2[:, kc, :], w2_d[l], 32, D, w2)
            load_mod(sh2, l, 3)
            load_mod(g2, l, 4)
            load_mod(gate2, l, 5)
            if last:
                bc_load(fg, fg_d)

            def load_x(i):
                k.dma("sp", xt[i % 2][:], x1_d[cs(i * 128, 128), :], reads=[R_x1(i)], writes=[xt[i % 2]])
            load_x(0)
            for i in range(NT):
                x = xt[i % 2]
                if i + 1 < NT:
                    load_x(i + 1)
                k.op("act", lambda e: e.activation(out=W0[:], in_=x[:], func=AF.Square), reads=[x], writes=[W0])
                k.op("dve", lambda e: e.tensor_reduce(out=ss[:], in_=W0[:], axis=AX.X, op=ALU.add), reads=[W0], writes=[ss])
                rstd_from(ss, rstd, 1.0 / D, EPS)
                k.op("dve", lambda e: e.scalar_tensor_tensor(out=W0[:], in0=x[:], scalar=rstd[:, 0:1], in1=g2[:], op0=ALU.mult,
                                                             op1=ALU.mult), reads=[x, rstd, g2], writes=[W0])
                k.op("dve", lambda e: e.tensor_tensor(out=hb[:], in0=W0[:], in1=sh2[:], op=ALU.add), reads=[W0, sh2], writes=[hb])
                transpose8(hb, hT)
                for n in range(8):
                    pb = bank()

                    def f(e):
                        for kc in range(8):
                            ins = e.matmul(pb[:], lhsT=hT[:, kc, :], rhs=w1[:, kc, cs(n * 512, 512)], start=(kc == 0), stop=(kc == 7))
                        return ins
                    k.op("pe", f, reads=[hT, w1], writes=[pb])
                    k.op("act", lambda e: e.activation(out=rl[:], in_=pb[:], func=AF.Relu), reads=[pb], writes=[rl])
                    k.op("dve", lambda e: e.tensor_tensor(out=hid[:, cs(n * 512, 512)], in0=rl[:], in1=rl[:], op=ALU.mult),
                         reads=[rl], writes=[hid])
                for q in range(4):
                    pb = bank()
                    pv = pb[:].bitcast(BF16)

                    def f(e):
                        for j in range(8):
                            ins = e.transpose(out=pv[:, cs(j * 128, 128)], in_=hid[:, cs((q * 8 + j) * 128, 128)], identity=ident_b[:])
                        return ins
                    k.op("pe", f, reads=[hid, ident_b], writes=[pb])
                    if q % 2 == 0:
                        k.op("act", lambda e: e.copy(out=hidT[:, q * 8:q * 8 + 8, :].rearrange("p a b -> p (a b)"), in_=pv[:, :]),
                             reads=[pb], writes=[hidT])
                    else:
                        k.op("dve", lambda e: e.tensor_copy(out=hidT[:, q * 8:q * 8 + 8, :].rearrange("p a b -> p (a b)"), in_=pv[:, :]),
                             reads=[pb], writes=[hidT])
                pp, p0, p1 = pair()

                def f(e):
                    for nn in range(2):
                        for kc in range(32):
                            ins = e.matmul(pp[:, cs(nn * 512, 512)], lhsT=hidT[:, kc, :], rhs=w2[:, kc, cs(nn * 512, 512)],
                                           start=(kc == 0), stop=(kc == 31))
                    return ins
                k.op("pe", f, reads=[hidT, w2], writes=[p0, p1])
                k.op("dve", lambda e: e.tensor_tensor(out=W0[:], in0=pp[:, :], in1=gate2[:], op=ALU.mult), reads=[p0, p1, gate2], writes=[W0])
                k.op("dve", lambda e: e.tensor_tensor(out=W0[:], in0=W0[:], in1=x[:], op=ALU.add), reads=[W0, x], writes=[W0])
                rows = cs(i * 128, 128)
                if not last:
                    k.dma("sp", x2_d[rows, :], W0[:], reads=[W0], writes=[R_x2(i)], sembuf=W0)
                else:
                    k.op("act", lambda e: e.activation(out=x[:], in_=W0[:], func=AF.Square), reads=[W0], writes=[x])
                    k.op("dve", lambda e: e.tensor_reduce(out=ss[:], in_=x[:], axis=AX.X, op=ALU.add), reads=[x], writes=[ss])
                    rstd_from(ss, rstd, 1.0 / D, EPS)
                    k.op("dve", lambda e: e.scalar_tensor_tensor(out=W0[:], in0=W0[:], scalar=rstd[:, 0:1], in1=fg[:], op0=ALU.mult,
                                                                 op1=ALU.mult), reads=[W0, rstd, fg], writes=[W0])
                    k.dma("sp", y_d[rows, :], W0[:], reads=[W0], writes=[R_y(i)], sembuf=W0)
            k.barrier()

    R_xin = DR("xin")
    steps = [lambda: phaseP(0), lambda: phaseP(1)]
    for l in range(2):
        xs, Rx = (x_d, R_xin) if l == 0 else (x2_d, R_x2)
        steps += [lambda l=l, xs=xs, Rx=Rx: phaseA1(l, xs, Rx), lambda l=l: phaseS(l, 0), lambda l=l: phaseS(l, 1),
                  lambda l=l, xs=xs, Rx=Rx: phaseM(l, xs, Rx), lambda l=l: phaseC(l, last=(l == 1))]
    for st_ in steps[:upto]:
        st_()
    k.barrier()
    ges.close()
    return nc, k


def _shift_mats(kind):
    m = np.zeros((4, 3, 128, 128), np.float32)
    eye = np.eye(128, dtype=np.float32)
    t = np.arange(128)
    for cls in range(4):
        cur = np.zeros((128, 128), np.float32)
        nbe = np.zeros((128, 128), np.float32)
        nbo = np.zeros((128, 128), np.float32)
        if kind == "sample":
            if cls == 0:
                for to in t:
                    if to % 64 != 0:
                        cur[to - 1, to] = 1
            elif cls == 1:
                for to in t:
                    if to % 64 != 63:
                        cur[to + 1, to] = 1
            elif cls == 2:
                for to in t:
                    if to >= 64:
                        cur[to - 64, to] = 1
                    else:
                        nbe[to + 64, to] = 1
                        nbo[to + 64, to] = 1
            else:
                for to in t:
                    if to < 64:
                        cur[to + 64, to] = 1
                    else:
                        nbe[to - 64, to] = 1
                        nbo[to - 64, to] = 1
        else:
            if cls in (0, 2):
                for to in t:
                    if to >= 1:
                        cur[to - 1, to] = 1
                nbo[127, 0] = 1
            else:
                for to in t:
                    if to <= 126:
                        cur[to + 1, to] = 1
                nbe[0, 127] = 1
        m[cls, 0] = cur - eye
        m[cls, 1] = nbe
        m[cls, 2] = nbo
    return np.ascontiguousarray(m.reshape(12, 128, 128).transpose(1, 0, 2).reshape(128, 12 * 128))


def _tri4():
    s = np.arange(128)[:, None]
    t = np.arange(128)[None, :]
    m = np.stack([(s <= t), (s >= t), (s < t), (s > t)], axis=1).astype(np.float32)
    return np.ascontiguousarray(m.reshape(128, 512))


def _state_to_H(st):
    a = st.reshape(2, 2, 8, 2, 64, 64)
    a = a.transpose(0, 1, 3, 5, 2, 4)
    return np.ascontiguousarray(a.reshape(2, 2, 128, 512))


def _H_to_state(Hm):
    lead = Hm.shape[:-2]
    a = Hm.reshape(lead + (2, 64, 8, 64))
    nl = len(lead)
    perm = tuple(range(nl)) + (nl + 2, nl + 0, nl + 3, nl + 1)
    a = a.transpose(perm)
    return a.reshape(lead + (16, 64, 64))


def make_core_inputs(kind, x_tokens, cond_vec, state_lh, shared):
    d = dict(shared)
    d["x"] = np.ascontiguousarray(x_tokens, dtype=np.float32)
    d["cond"] = np.ascontiguousarray(cond_vec.reshape(8, 128).T, dtype=np.float32)
    d["state0"] = _state_to_H(state_lh)
    d["cmask"] = np.full((128, 1), 1.0 if kind == "sample" else 0.0, np.float32)
    d["shm"] = _shift_mats(kind)
    return d


def shared_inputs(w_ada, b_ada, norm1_g, norm2_g, w_in, mu_shift, w0, w_up, a0, a_up, g_up, k_k, k_a, r_k, lnx_g,
                  lnx_b, w_branch_a, ln_v_g, w_s, b_s, w_branch_b, w_out, w1, w2, final_g):
    f = lambda a: np.ascontiguousarray(np.asarray(a), dtype=np.float32)
    wup_aug = np.concatenate([np.asarray(w_up), np.asarray(w0)[:, :, None, :]], axis=2)
    aup_aug = np.concatenate([np.asarray(a_up), np.asarray(a0)[:, :, None, :]], axis=2)
    wsT = np.asarray(w_s).transpose(0, 3, 1, 2).reshape(2, 128, 8 * 128)
    bsT = np.asarray(b_s).transpose(0, 2, 1)
    return dict(ident=np.eye(128, dtype=np.float32), tri4=_tri4(), w_ada=f(w_ada), b_ada=f(b_ada), norm1_g=f(norm1_g),
                norm2_g=f(norm2_g), w_in=f(w_in), mu_shift=f(mu_shift), wup_aug=f(wup_aug), aup_aug=f(aup_aug), g_up=f(g_up),
                k_k=f(k_k), k_a=f(k_a), r_k=f(np.asarray(r_k).reshape(2, D)), lnx_g=f(lnx_g), lnx_b=f(lnx_b),
                w_branch_a=f(w_branch_a), ln_v_g=f(ln_v_g), wsT=f(wsT), bsT=f(bsT), w_branch_b=f(w_branch_b), w_out=f(w_out),
                w1=f(w1), w2=f(w2), final_g=f(final_g))


_PROG = {}


def kernel(x_prompt, x_sample, state_rwkv, c, c_ctx, w_ada, b_ada, norm1_g, norm2_g, w_in, mu_shift,
           w0, w_up, a0, a_up, g_up, k_k, k_a, r_k, lnx_g, lnx_b, w_branch_a, ln_v_g, w_s, b_s,
           w_branch_b, w_out, w1, w2, final_g):
    NT = 32
    x_prompt = np.asarray(x_prompt, dtype=np.float32)
    x_sample = np.asarray(x_sample, dtype=np.float32)
    state_rwkv = np.asarray(state_rwkv, dtype=np.float32)
    c = np.asarray(c, dtype=np.float32)
    c_ctx = np.asarray(c_ctx, dtype=np.float32)
    shared = shared_inputs(w_ada, b_ada, norm1_g, norm2_g, w_in, mu_shift, w0, w_up, a0, a_up, g_up, k_k, k_a, r_k,
                           lnx_g, lnx_b, w_branch_a, ln_v_g, w_s, b_s, w_branch_b, w_out, w1, w2, final_g)
    in_maps = []
    for b in range(4):
        in_maps.append(make_core_inputs("sample", x_sample[b], c[b], state_rwkv[b], shared))
    zero_state = np.zeros((2, 2, 16, 64, 64), np.float32)
    for q in range(4):
        xs = np.zeros((NT * 128, D), np.float32)
        xs[:2048] = x_prompt[8 * q:8 * q + 8].reshape(2048, D)
        xs[2048:] = xs[:2048]
        in_maps.append(make_core_inputs("prompt", xs, c_ctx, zero_state, shared))
    if NT not in _PROG:
        _PROG[NT] = build_program(NT)[0]
    res = run_bass_kernel_spmd(_PROG[NT], in_maps, core_ids=list(range(8)))
    r = res.results
    y_sample = np.stack([r[b]["y"] for b in range(4)], axis=0)
    y_prompt = np.concatenate([r[4 + q]["y"][:2048].reshape(8, 256, D) for q in range(4)], axis=0)
    sts = []
    for q in range(4):
        so = r[4 + q]["st_out"]
        so = so[:, :, :8]
        s = _H_to_state(so)
        sts.append(np.transpose(s, (2, 0, 1, 3, 4, 5)))
    new_state = np.ascontiguousarray(np.concatenate(sts, axis=0), dtype=np.float32)
    return (np.ascontiguousarray(y_prompt, dtype=np.float32), np.ascontiguousarray(y_sample, dtype=np.float32), new_state)
```

```python
import contextlib
import os
DBG = int(os.environ.get('KDBG', '99'))
KSKIP = os.environ.get('KSKIP', '')
import numpy as np
import concourse.bass as bass
import concourse.mybir as mybir
from concourse.bass_utils import run_bass_kernel_spmd

F32 = mybir.dt.float32
BF16 = mybir.dt.bfloat16
ALU = mybir.AluOpType
AF = mybir.ActivationFunctionType
AX = mybir.AxisListType

D = 1024
CR = 3456
DIN = 7552
DFF = 4096
NHEAD = 16
EPS = 1e-6
GN_EPS = 64e-5
DSC = float(np.exp(-0.5))


class Buf:
    __slots__ = ("name", "t", "w", "r", "dsem", "dcnt")

    def __init__(self, name, t=None):
        self.name = name
        self.t = t
        self.w = None
        self.r = []
        self.dsem = None
        self.dcnt = 0

    def __getitem__(self, idx):
        return self.t[idx]


class K:
    def __init__(self, nc):
        self.nc = nc
        self.eng = {"pe": nc.tensor, "act": nc.scalar, "dve": nc.vector, "pool": nc.gpsimd, "sp": nc.sync}
        self.sem = {}
        self.cnt = {}
        for e in self.eng:
            self.sem[e] = nc.alloc_semaphore(name="s_" + e)
            self.cnt[e] = 0
        self.waited = {}
        self.dsems = {}
        self.free_dsems = []
        self.ninstr = 0
        self.uid = 0

    def _wait(self, e, tok):
        if tok is None:
            return
        key, val = tok
        if key == e and e == "pe":
            return
        kk = (e, key)
        if self.waited.get(kk, 0) >= val:
            return
        self.waited[kk] = val
        self.eng[e].wait_ge(self.sem[key], val)
        self.ninstr += 1

    def _deps(self, e, reads, writes):
        for b in reads:
            self._wait(e, b.w)
        for b in writes:
            self._wait(e, b.w)
            for tok in b.r:
                self._wait(e, tok)

    def _commit(self, tok, reads, writes):
        for b in reads:
            if b not in writes:
                b.r.append(tok)
                if len(b.r) > 10:
                    best = {}
                    for k_, v_ in b.r:
                        if best.get(k_, -1) < v_:
                            best[k_] = v_
                    b.r = list(best.items())
        for b in writes:
            b.w = tok
            b.r = []

    def op(self, e, fn, reads=(), writes=()):
        reads = [b for b in reads if b is not None]
        writes = [b for b in writes if b is not None]
        self._deps(e, reads, writes)
        ins = fn(self.eng[e])
        self.cnt[e] += 1
        ins.then_inc(self.sem[e], 1)
        self.ninstr += 1
        self._commit((e, self.cnt[e]), reads, writes)

    def dma(self, q, out_ap, in_ap, reads=(), writes=(), sembuf=None, **kw):
        reads = [b for b in reads if b is not None]
        writes = [b for b in writes if b is not None]
        if sembuf is None:
            sembuf = (writes + reads)[0]
        if sembuf.dsem is None:
            if self.free_dsems:
                key, base = self.free_dsems.pop()
                sembuf.dcnt = base
            else:
                key = "d%d" % len(self.sem)
                self.sem[key] = self.nc.alloc_semaphore(name=key)
            self.dsems[key] = sembuf
            sembuf.dsem = key
        self._deps(q, reads, writes)
        ins = self.eng[q].dma_start(out=out_ap, in_=in_ap, **kw)
        sembuf.dcnt += 16
        ins.then_inc(self.sem[sembuf.dsem], 16)
        self.ninstr += 1
        self._commit((sembuf.dsem, sembuf.dcnt), reads, writes)

    def barrier(self):
        toks = [(e, self.cnt[e]) for e in self.eng if self.cnt[e] > 0]
        toks += [(key, b.dcnt) for key, b in self.dsems.items() if b.dcnt > 0]
        for e in self.eng:
            for tok in toks:
                if tok[0] != e:
                    self._wait(e, tok)
        for key, b in list(self.dsems.items()):
            if not getattr(b, "keep", False):
                self.free_dsems.append((key, b.dcnt))
                b.dsem = None
                del self.dsems[key]


def cs(a, n):
    return slice(a, a + n)


def hc(h):
    return slice((h % 2) * 512 + (h // 2) * 64, (h % 2) * 512 + (h // 2) * 64 + 64)


def build_program(NT, upto=99):
    T = NT * 128
    NG = NT // 2
    nc = bass.Bass("TRN2", target_bir_lowering=False)
    k = K(nc)

    def din(name, shape):
        return nc.dram_tensor(name, list(shape), F32, kind="ExternalInput").ap()

    x_d = din("x", [T, D])
    cond_d = din("cond", [128, 8])
    state0_d = din("state0", [2, 2, 128, 512])
    cmask_d = din("cmask", [128, 1])
    shm_d = din("shm", [128, 12 * 128])
    ident_d = din("ident", [128, 128])
    tri4_d = din("tri4", [128, 4 * 128])
    w_ada_d = din("w_ada", [2, D, 6 * D])
    b_ada_d = din("b_ada", [2, 6 * D])
    n1g_d = din("norm1_g", [2, D])
    n2g_d = din("norm2_g", [2, D])
    w_in_d = din("w_in", [2, D, DIN])
    mu_d = din("mu_shift", [2, CR])
    wup_d = din("wup_aug", [2, 2, 65, D])
    aup_d = din("aup_aug", [2, 2, 65, D])
    gup_d = din("g_up", [2, 128, D])
    kk_d = din("k_k", [2, D])
    ka_d = din("k_a", [2, D])
    rk_d = din("r_k", [2, D])
    lnxg_d = din("lnx_g", [2, D])
    lnxb_d = din("lnx_b", [2, D])
    wa_d = din("w_branch_a", [2, D, D])
    lnvg_d = din("ln_v_g", [2, D])
    wsT_d = din("wsT", [2, 128, 8 * 128])
    bsT_d = din("bsT", [2, 128, 8])
    wb_d = din("w_branch_b", [2, D, D])
    wo_d = din("w_out", [2, D, D])
    w1_d = din("w1", [2, D, DFF])
    w2_d = din("w2", [2, DFF, D])
    fg_d = din("final_g", [D])

    y_d = nc.dram_tensor("y", [T, D], F32, kind="ExternalOutput").ap()
    st_d = nc.dram_tensor("st_out", [2, 2, NG, 128, 512], F32, kind="ExternalOutput").ap()

    def dscr(name, shape, dt):
        return nc.dram_tensor(name, list(shape), dt, kind="Internal").ap()

    modbc_d = dscr("modbc", [2, 128, 6 * D], F32)
    zr_d = dscr("zr_s", [T, CR], BF16)
    zrest_d = dscr("zrest_s", [T, 4096], BF16)
    ysc_d = dscr("ysc_s", [2, T, D], F32)
    bon_d = dscr("bon_s", [2, T, 16], F32)
    x1_d = dscr("x1_s", [T, D], F32)
    x2_d = dscr("x2_s", [T, D], F32)

    class DR:
        def __init__(self, nm):
            self.b = {}
            self.nm = nm

        def __call__(self, *key):
            if key not in self.b:
                self.b[key] = Buf(self.nm + str(key))
            return self.b[key]

    R_mod, R_zr, R_zrest, R_ysc, R_bon, R_x1, R_x2, R_y, R_st = [DR(n) for n in
        ("mod", "zr", "zrest", "ysc", "bon", "x1", "x2", "y", "st")]

    PP = [nc.alloc_psum_tensor("psum%d" % i, [128, 1024], F32) for i in range(4)]
    PB = []
    for i in range(8):
        PB.append(Buf("pb%d" % i, PP[i // 2][:, cs((i % 2) * 512, 512)]))
    pst = {"b": 0, "p": 0}

    def bank():
        b = PB[pst["b"] % 8]
        pst["b"] += 1
        return b

    def pair():
        if pst["b"] % 2:
            pst["b"] += 1
        i = (pst["b"] % 8) // 2
        pst["b"] += 2
        return PP[i], PB[2 * i], PB[2 * i + 1]

    def sbt(es, name, shape, dt):
        k.uid += 1
        t = es.enter_context(nc.sbuf_tensor("%s_%d" % (name, k.uid), list(shape), dt))
        return Buf(name, t)

    ges = contextlib.ExitStack()
    ident_f = sbt(ges, "ident_f", [128, 128], F32)
    ident_b = sbt(ges, "ident_b", [128, 128], BF16)
    tri4 = sbt(ges, "tri4", [128, 4, 128], F32)
    ones_f = sbt(ges, "ones_f", [128, 1], F32)
    cmask = sbt(ges, "cmask", [128, 1], F32)
    k.dma("sp", ident_f[:], ident_d, writes=[ident_f])
    k.dma("sp", tri4[:].rearrange("p a b -> p (a b)"), tri4_d, writes=[tri4])
    k.dma("sp", cmask[:], cmask_d, writes=[cmask])
    k.op("dve", lambda e: e.tensor_copy(out=ident_b[:], in_=ident_f[:]), reads=[ident_f], writes=[ident_b])
    k.op("dve", lambda e: e.memset(ones_f[:], 1.0), writes=[ones_f])

    def bc_load(buf, dvec):
        k.dma("sp", buf[:], dvec.partition_broadcast(128), writes=[buf])

    def cast_load_rows(buf_ap_fn, dsrc, nk, ncol, buf):
        for kc in range(nk):
            k.dma("pool", buf_ap_fn(kc), dsrc[cs(kc * 128, 128), :], writes=[buf], max_dma_last_dim=4096)

    def rstd_from(e_ss, out_rstd, scale, eps):
        k.op("act", lambda e: e.activation(out=out_rstd[:], in_=e_ss[:], func=AF.Ln, bias=eps_t(eps)[:], scale=scale),
             reads=[e_ss, eps_t(eps)], writes=[out_rstd])
        k.op("act", lambda e: e.activation(out=out_rstd[:], in_=out_rstd[:], func=AF.Exp, scale=-0.5),
             reads=[out_rstd], writes=[out_rstd])

    eps_tiles = {}

    def eps_t(v):
        if v not in eps_tiles:
            b = sbt(ges, "eps%d" % len(eps_tiles), [128, 1], F32)
            k.op("dve", lambda e: e.memset(b[:], float(v)), writes=[b])
            eps_tiles[v] = b
        return eps_tiles[v]

    for v in (EPS, GN_EPS, 1e-12):
        eps_t(v)

    def phaseP(l):
        with contextlib.ExitStack() as es:
            wad = sbt(es, "wad", [128, 8, 6 * D], BF16)
            ba = sbt(es, "ba", [128, 6 * D], F32)
            mod = sbt(es, "mod", [128, 6 * D], F32)
            n1g = sbt(es, "n1g", [128, D], F32)
            n2g = sbt(es, "n2g", [128, D], F32)
            cnd = sbt(es, "cnd", [128, 8], F32)
            scb = sbt(es, "scb", [128, 8, 128], BF16)
            cast_load_rows(lambda kc: wad[:, kc, :], w_ada_d[l], 8, 6 * D, wad)
            bc_load(ba, b_ada_d[l])
            bc_load(n1g, n1g_d[l])
            bc_load(n2g, n2g_d[l])
            k.dma("sp", cnd[:], cond_d, writes=[cnd])
            k.op("act", lambda e: e.activation(out=cnd[:], in_=cnd[:], func=AF.Silu), reads=[cnd], writes=[cnd])
            k.op("dve", lambda e: e.tensor_copy(out=scb[:], in_=cnd[:].unsqueeze(2).to_broadcast([128, 8, 128])),
                 reads=[cnd], writes=[scb])
            for n in range(12):
                pb = bank()

                def f(e):
                    for kc in range(8):
                        ins = e.matmul(pb[:], lhsT=scb[:, kc, :], rhs=wad[:, kc, cs(n * 512, 512)],
                                       start=(kc == 0), stop=(kc == 7))
                    return ins
                k.op("pe", f, reads=[scb, wad], writes=[pb])
                k.op("dve", lambda e: e.tensor_tensor(out=mod[:, cs(n * 512, 512)], in0=pb[:], in1=ba[:, cs(n * 512, 512)],
                                                      op=ALU.add), reads=[pb, ba], writes=[mod])
            k.op("dve", lambda e: e.scalar_tensor_tensor(out=mod[:, cs(D, D)], in0=mod[:, cs(D, D)], scalar=1.0, in1=n1g[:],
                                                         op0=ALU.add, op1=ALU.mult), reads=[mod, n1g], writes=[mod])
            k.op("dve", lambda e: e.scalar_tensor_tensor(out=mod[:, cs(4 * D, D)], in0=mod[:, cs(4 * D, D)], scalar=1.0,
                                                         in1=n2g[:], op0=ALU.add, op1=ALU.mult), reads=[mod, n2g], writes=[mod])
            k.dma("sp", modbc_d[l], mod[:], reads=[mod], writes=[R_mod(l)], sembuf=mod)
            k.barrier()

    def load_mod(buf, l, j):
        k.dma("sp", buf[:], modbc_d[l][:, cs(j * D, D)], reads=[R_mod(l)], writes=[buf])

    def phaseA1(l, xsrc_d, R_xsrc):
        with contextlib.ExitStack() as es:
            win = sbt(es, "win", [128, 8, DIN], BF16)
            g1 = sbt(es, "g1", [128, D], F32)
            sh1 = sbt(es, "sh1", [128, D], F32)
            mu = sbt(es, "mu", [128, CR], F32)
            shm = sbt(es, "shm", [128, 12, 128], BF16)
            xt = [sbt(es, "xt0", [128, D], F32)]
            xt.append(xt[0])
            sq = sbt(es, "sq", [128, D], F32)
            hb = sbt(es, "hb", [128, D], BF16)
            hT = sbt(es, "hT", [128, 8, 128], BF16)
            ss = sbt(es, "ss", [128, 1], F32)
            rstd = sbt(es, "rstd", [128, 1], F32)
            zb = [sbt(es, "zb%d" % i, [128, CR], BF16) for i in range(2)]
            zm = [sbt(es, "zm%d" % i, [128, CR], BF16) for i in range(3)]
            zst = sbt(es, "zst", [128, CR], BF16)
            rst = [sbt(es, "rst%d" % i, [128, 512], BF16) for i in range(4)]
            cast_load_rows(lambda kc: win[:, kc, :], w_in_d[l], 8, DIN, win)
            k.dma("pool", shm[:].rearrange("p a b -> p (a b)"), shm_d, writes=[shm], max_dma_last_dim=4096)
            load_mod(sh1, l, 0)
            load_mod(g1, l, 1)
            bc_load(mu, mu_d[l])
            rsti = [0]

            def load_x(i):
                k.dma("sp", xt[i % 2][:], xsrc_d[cs(i * 128, 128), :], reads=[R_xsrc(i)], writes=[xt[i % 2]])

            def stage1(i):
                x = xt[i % 2]
                if DBG < 2:
                    if i + 1 < NT:
                        load_x(i + 1)
                    return
                k.op("act", lambda e: e.activation(out=sq[:], in_=x[:], func=AF.Square), reads=[x], writes=[sq])
                k.op("dve", lambda e: e.tensor_reduce(out=ss[:], in_=sq[:], axis=AX.X, op=ALU.add), reads=[sq], writes=[ss])
                rstd_from(ss, rstd, 1.0 / D, EPS)
                k.op("dve", lambda e: e.scalar_tensor_tensor(out=x[:], in0=x[:], scalar=rstd[:, 0:1], in1=g1[:],
                                                             op0=ALU.mult, op1=ALU.mult), reads=[x, rstd, g1], writes=[x])
                k.op("dve", lambda e: e.tensor_tensor(out=hb[:], in0=x[:], in1=sh1[:], op=ALU.add),
                     reads=[x, sh1], writes=[hb])
                if i + 1 < NT:
                    load_x(i + 1)
                if DBG < 3:
                    return
                transpose8(hb, hT)
                if DBG < 4:
                    return
                zbi, zmi = zb[i % 2], zm[i % 3]
                col = 0
                ci = 0
                while col < DIN:
                    if col < CR:
                        n = min(512, CR - col)
                    else:
                        n = 512
                    pb = bank()

                    def f(e):
                        for kc in range(8):
                            ins = e.matmul(pb[:, 0:n], lhsT=hT[:, kc, :], rhs=win[:, kc, cs(col, n)],
                                           start=(kc == 0), stop=(kc == 7))
                        return ins
                    k.op("pe", f, reads=[hT, win], writes=[pb])
                    if 'p' in KSKIP:
                        pass
                    elif col < CR:
                        if 'z' not in KSKIP:
                            k.op("act", lambda e: e.copy(out=zbi[:, cs(col, n)], in_=pb[:, 0:n]), reads=[pb], writes=[zbi])
                        if 'm' not in KSKIP:
                            k.op("dve", lambda e: e.tensor_tensor(out=zmi[:, cs(col, n)], in0=zbi[:, cs(col, n)], in1=mu[:, cs(col, n)],
                                                                  op=ALU.mult), reads=[zbi, mu], writes=[zmi])
                    else:
                        st = rst[rsti[0] % 4]
                        rsti[0] += 1
                        if ci % 2 == 0:
                            k.op("act", lambda e: e.copy(out=st[:], in_=pb[:]), reads=[pb], writes=[st])
                        else:
                            k.op("dve", lambda e: e.tensor_copy(out=st[:], in_=pb[:]), reads=[pb], writes=[st])
                        if 'r' not in KSKIP:
                            k.dma("sp", zrest_d[cs(i * 128, 128), cs(col - CR, 512)], st[:], reads=[st],
                                  writes=[R_zrest(i, (col - CR) // 512)], sembuf=st)
                    col += n
                    ci += 1

            def stage2(i):
                if DBG < 5:
                    return
                par = i % 2
                for cls in range(4):
                    nb = i - 1 if cls in (0, 2) else i + 1
                    for hh in range(2):
                        c0 = cls + 4 * 432 * hh
                        sl = slice(c0, c0 + 4 * 431 + 1, 4)
                        pb = bank()
                        srcs = [(ident_b[:], zb[i % 2], ident_b), (shm[:, 3 * cls, :], zm[i % 3], shm)]
                        if 0 <= nb < NT:
                            srcs.append((shm[:, 3 * cls + 1 + par, :], zm[nb % 3], shm))

                        def f(e):
                            for j, (lt, rb, _) in enumerate(srcs):
                                ins = e.matmul(pb[:, 0:432], lhsT=lt, rhs=rb[:, sl], start=(j == 0), stop=(j == len(srcs) - 1))
                            return ins
                        k.op("pe", f, reads=[s[1] for s in srcs] + [ident_b, shm], writes=[pb])
                        if hh == 0:
                            k.op("act", lambda e: e.copy(out=zst[:, sl], in_=pb[:, 0:432]), reads=[pb], writes=[zst])
                        else:
                            k.op("dve", lambda e: e.tensor_copy(out=zst[:, sl], in_=pb[:, 0:432]), reads=[pb], writes=[zst])
                k.dma("sp", zr_d[cs(i * 128, 128), :], zst[:], reads=[zst], writes=[R_zr(i)], sembuf=zst)

            load_x(0)
            stage1(0)
            for i in range(NT):
                if i + 1 < NT:
                    stage1(i + 1)
                stage2(i)
            k.barrier()

    def transpose8(src, dst, nblk=8, src_off=0):
        pb = bank()
        pv = pb[:].bitcast(BF16)

        def f(e):
            for j in range(nblk):
                ins = e.transpose(out=pv[:, cs(j * 128, 128)], in_=src[:, cs(src_off + j * 128, 128)], identity=ident_b[:])
            return ins
        k.op("pe", f, reads=[src, ident_b], writes=[pb])
        k.op("act", lambda e: e.copy(out=dst[:, 0:nblk, :].rearrange("p a b -> p (a b)"), in_=pv[:, 0:nblk * 128]),
             reads=[pb], writes=[dst])

    def phaseS(l, d):
        with contextlib.ExitStack() as es:
            kkc = sbt(es, "kkc", [128, D], F32)
            kac = sbt(es, "kac", [128, D], F32)
            rkc = sbt(es, "rkc", [128, D], F32)
            wup = sbt(es, "wup", [65, D], BF16)
            aup = sbt(es, "aup", [65, D], BF16)
            maskM = sbt(es, "maskM", [128, 4, 128], F32)
            maskN = sbt(es, "maskN", [128, 4, 128], F32)
            H = sbt(es, "H", [128, 8, 64], F32)
            Hb = sbt(es, "Hb", [128, 8, 64], BF16)
            zr = [sbt(es, "zr%d" % i, [128, CR], BF16) for i in range(2)]
            ldT = sbt(es, "ldT", [65, 2, 128], BF16)
            tw = sbt(es, "tw", [128, 128], BF16)
            SG = sbt(es, "SG", [128, D], F32)
            A = sbt(es, "A", [128, D], F32)
            KX = sbt(es, "KX", [128, D], F32)
            BP = sbt(es, "BP", [128, D], F32)
            KD = sbt(es, "KD", [128, D], F32)
            S0 = sbt(es, "S0", [128, D], F32)
            S1 = sbt(es, "S1", [128, D], F32)
            st16 = sbt(es, "st16", [128, 16], F32)
            rs16 = sbt(es, "rs16", [128, 16], F32)
            bon = sbt(es, "bon", [128, 16], F32)
            gC = sbt(es, "gC", [128, 8], F32)
            TM = sbt(es, "TM", [128, 4, D], BF16)
            Bg = sbt(es, "Bg", [128, D], BF16)
            Kg = sbt(es, "Kg", [128, D], BF16)
            FM = sbt(es, "FM", [128, 8, 4, 128], BF16)
            MB = sbt(es, "MB", [128, 16, 512], BF16)
            Pm = [sbt(es, "Pm%d" % i, [128, 16, 128], BF16) for i in range(2)]
            PT = [sbt(es, "PT%d" % i, [128, 16, 128], BF16) for i in range(2)]
            Xb = [sbt(es, "Xb%d" % i, [128, D], BF16) for i in range(2)]
            ysc = sbt(es, "ysc", [128, D], F32)

            bc_load(kkc, kk_d[l])
            bc_load(kac, ka_d[l])
            bc_load(rkc, rk_d[l])
            k.dma("pool", wup[:], wup_d[l, d], writes=[wup], max_dma_last_dim=4096)
            k.dma("pool", aup[:], aup_d[l, d], writes=[aup], max_dma_last_dim=4096)
            strict_i, incl_i, nmask_i = (2, 0, 3) if d == 0 else (3, 1, 2)
            for j, src in enumerate((strict_i, incl_i, strict_i, incl_i)):
                k.op("dve", lambda e: e.tensor_copy(out=maskM[:, j, :], in_=tri4[:, src, :]), reads=[tri4], writes=[maskM])
            for j in range(4):
                k.op("dve", lambda e: e.tensor_copy(out=maskN[:, j, :], in_=tri4[:, nmask_i, :]), reads=[tri4], writes=[maskN])
            tri_incl = tri4[:, incl_i, :]
            tri_excl = tri4[:, strict_i, :]
            tri_dg = tri4[:, nmask_i, :]
            k.op("dve", lambda e: e.memset(ldT[:], 1.0), writes=[ldT])
            k.dma("sp", H[:].rearrange("p a b -> p (a b)"), state0_d[l, d], writes=[H])
            k.op("act", lambda e: e.copy(out=Hb[:], in_=H[:]), reads=[H], writes=[Hb])

            order = list(range(NT)) if d == 0 else list(range(NT - 1, -1, -1))

            def load_zr(n):
                i = order[n]
                k.dma("sp", zr[n % 2][:], zr_d[cs(i * 128, 128), :], reads=[R_zr(i)], writes=[zr[n % 2]])

            load_zr(0)
            for n, i in enumerate(order):
                z = zr[n % 2]
                if n + 1 < NT:
                    load_zr(n + 1)
                rq = z[:, 0:D]
                kq = z[:, D:2 * D]
                vq = z[:, 2 * D:3 * D]
                k.op("act", lambda e: e.activation(out=tw[:, 0:64], in_=z[:, cs(3 * D + 64 * d, 64)], func=AF.Tanh),
                     reads=[z], writes=[tw])
                k.op("dve", lambda e: e.tensor_copy(out=tw[:, 64:128], in_=z[:, cs(3 * D + 128 + 64 * d, 64)]),
                     reads=[z], writes=[tw])
                pb = bank()
                pv = pb[:].bitcast(BF16)

                def f(e):
                    e.transpose(out=pv[0:64, 0:128], in_=tw[:, 0:64], identity=ident_b[:])
                    return e.transpose(out=pv[0:64, 128:256], in_=tw[:, 64:128], identity=ident_b[:])
                k.op("pe", f, reads=[tw, ident_b], writes=[pb])
                k.op("act", lambda e: e.copy(out=ldT[0:64, :, :].rearrange("p a b -> p (a b)"), in_=pv[0:64, 0:256]),
                     reads=[pb], writes=[ldT])
                for (wmat, src_j, dst) in ((wup, 0, SG), (aup, 1, A)):
                    pp, b0, b1 = pair()

                    def f(e):
                        e.matmul(pp[:, 0:512], lhsT=ldT[:, src_j, :], rhs=wmat[:, 0:512], start=True, stop=True)
                        return e.matmul(pp[:, 512:1024], lhsT=ldT[:, src_j, :], rhs=wmat[:, 512:1024], start=True, stop=True)
                    k.op("pe", f, reads=[ldT, wmat], writes=[b0, b1])
                    k.op("act", lambda e: e.activation(out=dst[:], in_=pp[:, :], func=AF.Sigmoid), reads=[b0, b1], writes=[dst])
                if DBG < 11:
                    continue
                k.op("dve", lambda e: e.tensor_tensor(out=KX[:], in0=kq, in1=kkc[:], op=ALU.mult), reads=[z, kkc], writes=[KX])
                k.op("dve", lambda e: e.tensor_tensor(out=S0[:], in0=KX[:], in1=KX[:], op=ALU.mult), reads=[KX], writes=[S0])
                k.op("dve", lambda e: e.tensor_reduce(out=st16[:], in_=S0[:].rearrange("p (h n) -> p h n", n=64), axis=AX.X,
                                                      op=ALU.add), reads=[S0], writes=[st16])
                rstd_from(st16, rs16, 1.0, 1e-12)
                k.op("dve", lambda e: e.tensor_tensor(out=KX[:].rearrange("p (h n) -> p h n", n=64),
                                                      in0=KX[:].rearrange("p (h n) -> p h n", n=64),
                                                      in1=rs16[:].unsqueeze(2).to_broadcast([128, 16, 64]), op=ALU.mult),
                     reads=[KX, rs16], writes=[KX])
                k.op("dve", lambda e: e.scalar_tensor_tensor(out=BP[:], in0=KX[:], scalar=-1.0, in1=A[:], op0=ALU.mult,
                                                             op1=ALU.mult), reads=[KX, A], writes=[BP])
                k.op("dve", lambda e: e.tensor_tensor(out=S0[:], in0=kq, in1=kac[:], op=ALU.mult), reads=[z, kac], writes=[S0])
                k.op("dve", lambda e: e.scalar_tensor_tensor(out=S0[:], in0=A[:], scalar=-1.0, in1=S0[:], op0=ALU.add,
                                                             op1=ALU.mult), reads=[A, S0], writes=[S0])
                k.op("dve", lambda e: e.tensor_tensor(out=KD[:], in0=S0[:], in1=kq, op=ALU.add), reads=[S0, z], writes=[KD])
                k.op("dve", lambda e: e.tensor_tensor(out=S0[:], in0=KD[:], in1=rkc[:], op=ALU.mult), reads=[KD, rkc], writes=[S0])
                k.op("dve", lambda e: e.tensor_tensor(out=S0[:], in0=S0[:], in1=rq, op=ALU.mult), reads=[S0, z], writes=[S0])
                k.op("dve", lambda e: e.tensor_reduce(out=bon[:], in_=S0[:].rearrange("p (h n) -> p h n", n=64), axis=AX.X,
                                                      op=ALU.add), reads=[S0], writes=[bon])
                k.dma("sp", bon_d[d, cs(i * 128, 128), :], bon[:], reads=[bon], writes=[R_bon(d, i)], sembuf=bon)
                if DBG < 12:
                    continue
                def cum(tri_ap, scale, dstE):
                    pp, b0, b1 = pair()

                    def f(e):
                        e.matmul(pp[:, 0:512], lhsT=tri_ap, rhs=SG[:, 0:512], start=True, stop=True)
                        return e.matmul(pp[:, 512:1024], lhsT=tri_ap, rhs=SG[:, 512:1024], start=True, stop=True)
                    k.op("pe", f, reads=[tri4, SG], writes=[b0, b1])
                    k.op("act", lambda e: e.activation(out=dstE[:], in_=pp[:, :], func=AF.Exp, scale=scale),
                         reads=[b0, b1], writes=[dstE])
                    return pp, b0, b1
                pp, b0, b1 = cum(tri_incl, -DSC, S0)
                k.op("dve", lambda e: e.tensor_tensor(out=TM[:, 1, :], in0=rq, in1=S0[:], op=ALU.mult), reads=[z, S0], writes=[TM])
                k.op("act", lambda e: e.activation(out=S1[:], in_=pp[:, :], func=AF.Exp, scale=DSC), reads=[b0, b1], writes=[S1])
                k.op("dve", lambda e: e.tensor_tensor(out=TM[:, 2, :], in0=BP[:], in1=S1[:], op=ALU.mult), reads=[BP, S1], writes=[TM])
                k.op("dve", lambda e: e.tensor_tensor(out=TM[:, 3, :], in0=KD[:], in1=S1[:], op=ALU.mult), reads=[KD, S1], writes=[TM])
                cum(tri_excl, -DSC, S0)
                k.op("dve", lambda e: e.tensor_tensor(out=TM[:, 0, :], in0=KX[:], in1=S0[:], op=ALU.mult), reads=[KX, S0], writes=[TM])
                cum(tri_dg, -DSC, S1)
                k.op("dve", lambda e: e.tensor_tensor(out=Bg[:], in0=BP[:], in1=S1[:], op=ALU.mult), reads=[BP, S1], writes=[Bg])
                k.op("dve", lambda e: e.tensor_tensor(out=Kg[:], in0=KD[:], in1=S1[:], op=ALU.mult), reads=[KD, S1], writes=[Kg])
                pbg = bank()

                def f(e):
                    for ct in range(8):
                        ins = e.matmul(pbg[:, ct:ct + 1], lhsT=SG[:, cs(ct * 128, 128)], rhs=ones_f[:], start=True, stop=True)
                    return ins
                k.op("pe", f, reads=[SG, ones_f], writes=[pbg])
                k.op("act", lambda e: e.activation(out=gC[:], in_=pbg[:, 0:8], func=AF.Exp, scale=-DSC), reads=[pbg], writes=[gC])
                if DBG < 13:
                    continue
                for g4 in range(4):
                    pb = bank()
                    pv = pb[:].bitcast(BF16)

                    def f(e):
                        for c2 in range(2):
                            ct = g4 * 2 + c2
                            for q in range(4):
                                ins = e.transpose(out=pv[:, cs((c2 * 4 + q) * 128, 128)], in_=TM[:, q, cs(ct * 128, 128)],
                                                  identity=ident_b[:])
                        return ins
                    k.op("pe", f, reads=[TM, ident_b], writes=[pb])
                    eng = "act" if g4 % 2 == 0 else "dve"
                    if eng == "act":
                        k.op("act", lambda e: e.copy(out=FM[:, g4 * 2:g4 * 2 + 2, :, :].rearrange("p a b c -> p (a b c)"),
                                                     in_=pv[:, :]), reads=[pb], writes=[FM])
                    else:
                        k.op("dve", lambda e: e.tensor_copy(out=FM[:, g4 * 2:g4 * 2 + 2, :, :].rearrange("p a b c -> p (a b c)"),
                                                            in_=pv[:, :]), reads=[pb], writes=[FM])
                if DBG < 14:
                    continue
                P0, PT0 = Pm[0], PT[0]
                for h in range(NHEAD):
                    ct, p0 = h // 2, (h % 2) * 64
                    if 'o' in KSKIP and h % 2 == 1:
                        continue
                    pb = bank()

                    def f(e):
                        e.matmul(pb[:, 0:256], lhsT=FM[p0:p0 + 64, ct, 2, :],
                                 rhs=FM[p0:p0 + 64, ct, 0:2, :].rearrange("p a b -> p (a b)"), start=True, stop=True)
                        return e.matmul(pb[:, 256:512], lhsT=FM[p0:p0 + 64, ct, 3, :],
                                        rhs=FM[p0:p0 + 64, ct, 0:2, :].rearrange("p a b -> p (a b)"), start=True, stop=True)
                    k.op("pe", f, reads=[FM], writes=[pb])
                    k.op("dve", lambda e: e.tensor_tensor(out=MB[:, h, :], in0=pb[:], in1=maskM[:].rearrange("p a b -> p (a b)"),
                                                          op=ALU.mult), reads=[pb, maskM], writes=[MB])
                    k.op("act", lambda e: e.copy(out=PT0[:, h, :], in_=MB[:, h, 0:128]), reads=[MB], writes=[PT0])
                for g in range(4):
                    if 'n' in KSKIP:
                        continue
                    pb = bank()

                    def f(e):
                        for hh in range(4):
                            h = (g % 2) + 2 * (4 * (g // 2) + hh)
                            ct, p0 = h // 2, (h % 2) * 64
                            ins = e.matmul(pb[:, cs(hh * 128, 128)], lhsT=FM[p0:p0 + 64, ct, 0, :], rhs=FM[p0:p0 + 64, ct, 2, :],
                                           start=True, stop=True)
                        return ins
                    k.op("pe", f, reads=[FM], writes=[pb])
                    h0 = (g % 2) + 8 * (g // 2)
                    k.op("dve", lambda e: e.tensor_tensor(out=P0[:, h0:h0 + 7:2, :],
                                                          in0=pb[:].rearrange("p (a b) -> p a b", b=128), in1=maskN[:], op=ALU.mult),
                         reads=[pb, maskN], writes=[P0])
                if DBG < 15:
                    continue
                pp, b0, b1 = pair()

                def f(e):
                    for h in range(NHEAD):
                        ct, p0 = h // 2, (h % 2) * 64
                        e.matmul(pp[:, hc(h)], lhsT=FM[p0:p0 + 64, ct, 0, :], rhs=Hb[p0:p0 + 64, ct, :],
                                 start=True, stop=False)
                        ins = e.matmul(pp[:, hc(h)], lhsT=MB[:, h, 256:384], rhs=z[:, cs(2 * D + h * 64, 64)],
                                       start=False, stop=True)
                    return ins
                k.op("pe", f, reads=[FM, Hb, MB, z], writes=[b0, b1])
                xc = Xb[0]
                k.op("act", lambda e: e.copy(out=xc[:], in_=pp[:, :]), reads=[b0, b1], writes=[xc])
                if DBG < 16:
                    continue
                cur = 0
                for lev in range(7):
                    Pc, PTc = Pm[cur], PT[cur]
                    pp, b0, b1 = pair()
                    xc, xn = Xb[lev % 2], Xb[(lev + 1) % 2]

                    def f(e):
                        for h in range(NHEAD):
                            ins = e.matmul(pp[:, hc(h)], lhsT=PTc[:, h, :], rhs=xc[:, hc(h)], start=True, stop=True)
                        return ins
                    k.op("pe", f, reads=[PTc, xc], writes=[b0, b1])
                    k.op("dve", lambda e: e.tensor_tensor(out=xn[:], in0=pp[:, :], in1=xc[:], op=ALU.add),
                         reads=[b0, b1, xc], writes=[xn])
                    if lev < 6:
                        Pn, PTn = Pm[1 - cur], PT[1 - cur]
                        for g in range(4):
                            for which in range(2):
                                pb = bank()

                                def f(e):
                                    for hh in range(4):
                                        h = g * 4 + hh
                                        if which == 0:
                                            ins = e.matmul(pb[:, cs(hh * 128, 128)], lhsT=PTc[:, h, :], rhs=Pc[:, h, :], start=True, stop=True)
                                        else:
                                            ins = e.matmul(pb[:, cs(hh * 128, 128)], lhsT=Pc[:, h, :], rhs=PTc[:, h, :], start=True, stop=True)
                                    return ins
                                k.op("pe", f, reads=[Pc, PTc], writes=[pb])
                                dstb = Pn if which == 0 else PTn
                                if which == 0:
                                    k.op("act", lambda e: e.copy(out=dstb[:, g * 4:g * 4 + 4, :].rearrange("p a b -> p (a b)"),
                                                                 in_=pb[:]), reads=[pb], writes=[dstb])
                                else:
                                    k.op("dve", lambda e: e.tensor_copy(out=dstb[:, g * 4:g * 4 + 4, :].rearrange("p a b -> p (a b)"),
                                                                        in_=pb[:]), reads=[pb], writes=[dstb])
                        cur = 1 - cur
                U = Xb[1]
                if DBG < 17:
                    continue
                pp, b0, b1 = pair()

                def f(e):
                    for h in range(NHEAD):
                        ct, p0 = h // 2, (h % 2) * 64
                        e.matmul(pp[:, hc(h)], lhsT=FM[p0:p0 + 64, ct, 1, :], rhs=Hb[p0:p0 + 64, ct, :], start=True, stop=False)
                        e.matmul(pp[:, hc(h)], lhsT=MB[:, h, 128:256], rhs=U[:, hc(h)], start=False, stop=False)
                        ins = e.matmul(pp[:, hc(h)], lhsT=MB[:, h, 384:512], rhs=z[:, cs(2 * D + h * 64, 64)],
                                       start=False, stop=True)
                    return ins
                k.op("pe", f, reads=[FM, Hb, MB, U, z], writes=[b0, b1])
                k.op("act", lambda e: e.copy(out=ysc[:].rearrange("p (hp h2 i) -> p h2 hp i", h2=2, i=64),
                                             in_=pp[:, :].rearrange("p (h2 hp i) -> p h2 hp i", hp=8, i=64)), reads=[b0, b1], writes=[ysc])
                k.dma("sp", ysc_d[d, cs(i * 128, 128), :], ysc[:], reads=[ysc], writes=[R_ysc(d, i)], sembuf=ysc)
                if DBG < 18:
                    continue
                pp, b0, b1 = pair()

                def f(e):
                    for ct in range(8):
                        e.matmul(pp[:, cs(ct * 128, 128)], lhsT=Bg[:, cs(ct * 128, 128)],
                                 rhs=U[:].rearrange("p (a b c) -> p a b c", a=2, b=8)[:, :, ct, :], start=True, stop=False)
                        ins = e.matmul(pp[:, cs(ct * 128, 128)], lhsT=Kg[:, cs(ct * 128, 128)], rhs=z[:, cs(2 * D + ct * 128, 128)],
                                       start=False, stop=True)
                    return ins
                k.op("pe", f, reads=[Bg, Kg, U, z], writes=[b0, b1])
                k.op("dve", lambda e: e.tensor_tensor(out=H[:], in0=H[:], in1=gC[:].unsqueeze(2).to_broadcast([128, 8, 64]),
                                                      op=ALU.mult), reads=[H, gC], writes=[H])
                ppv = pp[:, :].rearrange("p (a b) -> p a b", b=128)
                k.op("dve", lambda e: e.tensor_tensor(out=H[0:64, :, :], in0=H[0:64, :, :], in1=ppv[0:64, :, 0:64], op=ALU.add),
                     reads=[H, b0, b1], writes=[H])
                k.op("dve", lambda e: e.tensor_tensor(out=H[64:128, :, :], in0=H[64:128, :, :], in1=ppv[64:128, :, 64:128], op=ALU.add),
                     reads=[H, b0, b1], writes=[H])
                if n % 2 == 1:
                    grp = i // 2
                    k.dma("sp", st_d[l, d, grp], H[:].rearrange("p a b -> p (a b)"), reads=[H], writes=[R_st(l, d, grp)], sembuf=H)
                    k.op("dve", lambda e: e.tensor_scalar(out=H[:], in0=H[:], scalar1=cmask[:, 0:1], scalar2=None, op0=ALU.mult),
                         reads=[H, cmask], writes=[H])
                k.op("act", lambda e: e.copy(out=Hb[:], in_=H[:]), reads=[H], writes=[Hb])
            k.barrier()


    def phaseS2(l):
        with contextlib.ExitStack() as es:
            kkc = sbt(es, "kkc", [128, D], F32)
            kac = sbt(es, "kac", [128, D], F32)
            rkc = sbt(es, "rkc", [128, D], F32)
            bc_load(kkc, kk_d[l])
            bc_load(kac, ka_d[l])
            bc_load(rkc, rk_d[l])

            def dir_gen(d):
                sfx = "_%d" % d
                wup = sbt(es, "wup" + sfx, [65, D], BF16)
                aup = sbt(es, "aup" + sfx, [65, D], BF16)
                maskM = sbt(es, "maskM" + sfx, [128, 4, 128], BF16)
                maskN = sbt(es, "maskN" + sfx, [128, 4, 128], BF16)
                H = sbt(es, "H" + sfx, [128, 8, 64], F32)
                Hb = sbt(es, "Hb" + sfx, [128, 8, 64], BF16)
                z = sbt(es, "zr" + sfx, [128, CR], BF16)
                ldT = sbt(es, "ldT" + sfx, [65, 2, 128], BF16)
                tw = sbt(es, "tw" + sfx, [128, 128], BF16)
                SG = sbt(es, "SG" + sfx, [128, D], F32)
                A = sbt(es, "A" + sfx, [128, D], F32)
                KX = sbt(es, "KX" + sfx, [128, D], F32)
                BP = sbt(es, "BP" + sfx, [128, D], F32)
                KD = sbt(es, "KD" + sfx, [128, D], F32)
                S0 = sbt(es, "S0" + sfx, [128, D], F32)
                S1 = A
                ysc = S0
                st16 = sbt(es, "st16" + sfx, [128, 16], F32)
                rs16 = sbt(es, "rs16" + sfx, [128, 16], F32)
                bon = sbt(es, "bon" + sfx, [128, 16], F32)
                gC = sbt(es, "gC" + sfx, [128, 8], F32)
                TM = sbt(es, "TM" + sfx, [128, 4, D], BF16)
                FM = sbt(es, "FM" + sfx, [128, 8, 4, 128], BF16)
                MB = sbt(es, "MB" + sfx, [128, 16, 512], BF16)
                Pm = [sbt(es, "Pm%d" % i + sfx, [128, 16, 128], BF16) for i in range(2)]
                PT = [sbt(es, "PT%d" % i + sfx, [128, 16, 128], BF16) for i in range(2)]
                Xb = [sbt(es, "Xb%d" % i + sfx, [128, D], BF16) for i in range(2)]

                k.dma("pool", wup[:], wup_d[l, d], writes=[wup], max_dma_last_dim=4096)
                k.dma("pool", aup[:], aup_d[l, d], writes=[aup], max_dma_last_dim=4096)
                strict_i, incl_i, nmask_i = (2, 0, 3) if d == 0 else (3, 1, 2)
                for j, src in enumerate((strict_i, incl_i, strict_i, incl_i)):
                    k.op("dve", lambda e: e.tensor_copy(out=maskM[:, j, :], in_=tri4[:, src, :]), reads=[tri4], writes=[maskM])
                for j in range(4):
                    k.op("dve", lambda e: e.tensor_copy(out=maskN[:, j, :], in_=tri4[:, nmask_i, :]), reads=[tri4], writes=[maskN])
                tri_incl = tri4[:, incl_i, :]
                tri_excl = tri4[:, strict_i, :]
                tri_dg = tri4[:, nmask_i, :]
                k.op("dve", lambda e: e.memset(ldT[:], 1.0), writes=[ldT])
                k.dma("sp", H[:].rearrange("p a b -> p (a b)"), state0_d[l, d], writes=[H])
                k.op("act", lambda e: e.copy(out=Hb[:], in_=H[:]), reads=[H], writes=[Hb])
                order = list(range(NT)) if d == 0 else list(range(NT - 1, -1, -1))
                yield

                for n, i in enumerate(order):
                    k.dma("sp", z[:], zr_d[cs(i * 128, 128), :], reads=[R_zr(i)], writes=[z])
                    rq = z[:, 0:D]
                    kq = z[:, D:2 * D]
                    k.op("act", lambda e: e.activation(out=tw[:, 0:64], in_=z[:, cs(3 * D + 64 * d, 64)], func=AF.Tanh),
                         reads=[z], writes=[tw])
                    k.op("pool", lambda e: e.tensor_copy(out=tw[:, 64:128], in_=z[:, cs(3 * D + 128 + 64 * d, 64)]),
                         reads=[z], writes=[tw])
                    pb = bank()
                    pv = pb[:].bitcast(BF16)

                    def f(e):
                        e.transpose(out=pv[0:64, 0:128], in_=tw[:, 0:64], identity=ident_b[:])
                        return e.transpose(out=pv[0:64, 128:256], in_=tw[:, 64:128], identity=ident_b[:])
                    k.op("pe", f, reads=[tw, ident_b], writes=[pb])
                    k.op("act", lambda e: e.copy(out=ldT[0:64, :, :].rearrange("p a b -> p (a b)"), in_=pv[0:64, 0:256]),
                         reads=[pb], writes=[ldT])
                    for (wmat, src_j, dst) in ((wup, 0, SG), (aup, 1, A)):
                        pp, b0, b1 = pair()

                        def f(e):
                            e.matmul(pp[:, 0:512], lhsT=ldT[:, src_j, :], rhs=wmat[:, 0:512], start=True, stop=True)
                            return e.matmul(pp[:, 512:1024], lhsT=ldT[:, src_j, :], rhs=wmat[:, 512:1024], start=True, stop=True)
                        k.op("pe", f, reads=[ldT, wmat], writes=[b0, b1])
                        k.op("act", lambda e: e.activation(out=dst[:], in_=pp[:, :], func=AF.Sigmoid), reads=[b0, b1], writes=[dst])
                    k.op("pool", lambda e: e.tensor_tensor(out=KX[:], in0=kq, in1=kkc[:], op=ALU.mult), reads=[z, kkc], writes=[KX])
                    k.op("pool", lambda e: e.tensor_tensor(out=S0[:], in0=KX[:], in1=KX[:], op=ALU.mult), reads=[KX], writes=[S0])
                    k.op("dve", lambda e: e.tensor_reduce(out=st16[:], in_=S0[:].rearrange("p (h n) -> p h n", n=64), axis=AX.X,
                                                          op=ALU.add), reads=[S0], writes=[st16])
                    rstd_from(st16, rs16, 1.0, 1e-12)
                    k.op("dve", lambda e: e.tensor_tensor(out=KX[:].rearrange("p (h n) -> p h n", n=64),
                                                          in0=KX[:].rearrange("p (h n) -> p h n", n=64),
                                                          in1=rs16[:].unsqueeze(2).to_broadcast([128, 16, 64]), op=ALU.mult),
                         reads=[KX, rs16], writes=[KX])
                    yield
                    k.op("dve", lambda e: e.scalar_tensor_tensor(out=BP[:], in0=KX[:], scalar=-1.0, in1=A[:], op0=ALU.mult,
                                                                 op1=ALU.mult), reads=[KX, A], writes=[BP])
                    k.op("pool", lambda e: e.tensor_tensor(out=S0[:], in0=kq, in1=kac[:], op=ALU.mult), reads=[z, kac], writes=[S0])
                    k.op("dve", lambda e: e.scalar_tensor_tensor(out=S0[:], in0=A[:], scalar=-1.0, in1=S0[:], op0=ALU.add,
                                                                 op1=ALU.mult), reads=[A, S0], writes=[S0])
                    k.op("pool", lambda e: e.tensor_tensor(out=KD[:], in0=S0[:], in1=kq, op=ALU.add), reads=[S0, z], writes=[KD])
                    k.op("pool", lambda e: e.tensor_tensor(out=S0[:], in0=KD[:], in1=rkc[:], op=ALU.mult), reads=[KD, rkc], writes=[S0])
                    k.op("pool", lambda e: e.tensor_tensor(out=S0[:], in0=S0[:], in1=rq, op=ALU.mult), reads=[S0, z], writes=[S0])
                    k.op("dve", lambda e: e.tensor_reduce(out=bon[:], in_=S0[:].rearrange("p (h n) -> p h n", n=64), axis=AX.X,
                                                          op=ALU.add), reads=[S0], writes=[bon])
                    k.dma("sp", bon_d[d, cs(i * 128, 128), :], bon[:], reads=[bon], writes=[R_bon(d, i)], sembuf=bon)
                    yield

                    def cum(tri_ap, scale, dstE):
                        pp, b0, b1 = pair()

                        def f(e):
                            e.matmul(pp[:, 0:512], lhsT=tri_ap, rhs=SG[:, 0:512], start=True, stop=True)
                            return e.matmul(pp[:, 512:1024], lhsT=tri_ap, rhs=SG[:, 512:1024], start=True, stop=True)
                        k.op("pe", f, reads=[tri4, SG], writes=[b0, b1])
                        k.op("act", lambda e: e.activation(out=dstE[:], in_=pp[:, :], func=AF.Exp, scale=scale),
                             reads=[b0, b1], writes=[dstE])
                        return pp, b0, b1
                    pp, b0, b1 = cum(tri_incl, -DSC, S0)
                    k.op("dve", lambda e: e.tensor_tensor(out=TM[:, 1, :], in0=rq, in1=S0[:], op=ALU.mult), reads=[z, S0], writes=[TM])
                    k.op("act", lambda e: e.activation(out=S1[:], in_=pp[:, :], func=AF.Exp, scale=DSC), reads=[b0, b1], writes=[S1])
                    k.op("dve", lambda e: e.tensor_tensor(out=TM[:, 2, :], in0=BP[:], in1=S1[:], op=ALU.mult), reads=[BP, S1], writes=[TM])
                    k.op("pool", lambda e: e.tensor_tensor(out=TM[:, 3, :], in0=KD[:], in1=S1[:], op=ALU.mult), reads=[KD, S1], writes=[TM])
                    cum(tri_excl, -DSC, S0)
                    k.op("dve", lambda e: e.tensor_tensor(out=TM[:, 0, :], in0=KX[:], in1=S0[:], op=ALU.mult), reads=[KX, S0], writes=[TM])
                    pbg = bank()

                    def f(e):
                        for ct in range(8):
                            ins = e.matmul(pbg[:, ct:ct + 1], lhsT=SG[:, cs(ct * 128, 128)], rhs=ones_f[:], start=True, stop=True)
                        return ins
                    k.op("pe", f, reads=[SG, ones_f], writes=[pbg])
                    k.op("act", lambda e: e.activation(out=gC[:], in_=pbg[:, 0:8], func=AF.Exp, scale=-DSC), reads=[pbg], writes=[gC])
                    yield
                    for g4 in range(4):
                        pb = bank()
                        pv = pb[:].bitcast(BF16)

                        def f(e):
                            for c2 in range(2):
                                ct = g4 * 2 + c2
                                for q in range(4):
                                    ins = e.transpose(out=pv[:, cs((c2 * 4 + q) * 128, 128)], in_=TM[:, q, cs(ct * 128, 128)],
                                                      identity=ident_b[:])
                            return ins
                        k.op("pe", f, reads=[TM, ident_b], writes=[pb])
                        if g4 % 2 == 0:
                            k.op("act", lambda e: e.copy(out=FM[:, g4 * 2:g4 * 2 + 2, :, :].rearrange("p a b c -> p (a b c)"),
                                                         in_=pv[:, :]), reads=[pb], writes=[FM])
                        else:
                            k.op("dve", lambda e: e.tensor_copy(out=FM[:, g4 * 2:g4 * 2 + 2, :, :].rearrange("p a b c -> p (a b c)"),
                                                                in_=pv[:, :]), reads=[pb], writes=[FM])
                    cum(tri_dg, -DSC, S1)
                    k.op("dve", lambda e: e.tensor_tensor(out=TM[:, 2, :], in0=BP[:], in1=S1[:], op=ALU.mult), reads=[BP, S1], writes=[TM])
                    k.op("pool", lambda e: e.tensor_tensor(out=TM[:, 3, :], in0=KD[:], in1=S1[:], op=ALU.mult), reads=[KD, S1], writes=[TM])
                    yield
                    P0 = Pm[0]
                    for h in range(NHEAD):
                        ct, p0 = h // 2, (h % 2) * 64
                        pb = bank()

                        def f(e):
                            e.matmul(pb[:, 0:256], lhsT=FM[p0:p0 + 64, ct, 2, :],
                                     rhs=FM[p0:p0 + 64, ct, 0:2, :].rearrange("p a b -> p (a b)"), start=True, stop=True)
                            return e.matmul(pb[:, 256:512], lhsT=FM[p0:p0 + 64, ct, 3, :],
                                            rhs=FM[p0:p0 + 64, ct, 0:2, :].rearrange("p a b -> p (a b)"), start=True, stop=True)
                        k.op("pe", f, reads=[FM], writes=[pb])
                        k.op("dve", lambda e: e.tensor_tensor(out=MB[:, h, :], in0=pb[:], in1=maskM[:].rearrange("p a b -> p (a b)"),
                                                              op=ALU.mult), reads=[pb, maskM], writes=[MB])
                        if h % 4 == 3:
                            yield
                    for g in range(4):
                        pb = bank()

                        def f(e):
                            for hh in range(4):
                                h = (g % 2) + 2 * (4 * (g // 2) + hh)
                                ct, p0 = h // 2, (h % 2) * 64
                                ins = e.matmul(pb[:, cs(hh * 128, 128)], lhsT=FM[p0:p0 + 64, ct, 0, :], rhs=FM[p0:p0 + 64, ct, 2, :],
                                               start=True, stop=True)
                            return ins
                        k.op("pe", f, reads=[FM], writes=[pb])
                        h0 = (g % 2) + 8 * (g // 2)
                        k.op("dve", lambda e: e.tensor_tensor(out=P0[:, h0:h0 + 7:2, :],
                                                              in0=pb[:].rearrange("p (a b) -> p a b", b=128), in1=maskN[:], op=ALU.mult),
                             reads=[pb, maskN], writes=[P0])
                    yield
                    pp, b0, b1 = pair()

                    def f(e):
                        for h in range(NHEAD):
                            ct, p0 = h // 2, (h % 2) * 64
                            e.matmul(pp[:, hc(h)], lhsT=FM[p0:p0 + 64, ct, 0, :], rhs=Hb[p0:p0 + 64, ct, :],
                                     start=True, stop=False)
                            ins = e.matmul(pp[:, hc(h)], lhsT=MB[:, h, 256:384], rhs=z[:, cs(2 * D + h * 64, 64)],
                                           start=False, stop=True)
                        return ins
                    k.op("pe", f, reads=[FM, Hb, MB, z], writes=[b0, b1])
                    xc = Xb[0]
                    k.op("act", lambda e: e.copy(out=xc[:], in_=pp[:, :]), reads=[b0, b1], writes=[xc])
                    yield
                    cur = 0
                    for lev in range(7):
                        Pc = Pm[cur]
                        pp, b0, b1 = pair()
                        xc, xn = Xb[lev % 2], Xb[(lev + 1) % 2]
                        if lev == 0:
                            ptv = lambda h: MB[:, h, 0:128]
                            ptb = MB
                        else:
                            ptv = lambda h, PTc=PT[cur]: PTc[:, h, :]
                            ptb = PT[cur]

                        def f(e):
                            for h in range(NHEAD):
                                ins = e.matmul(pp[:, hc(h)], lhsT=ptv(h), rhs=xc[:, hc(h)], start=True, stop=True)
                            return ins
                        k.op("pe", f, reads=[ptb, xc], writes=[b0, b1])
                        k.op("dve", lambda e: e.tensor_tensor(out=xn[:], in0=pp[:, :], in1=xc[:], op=ALU.add),
                             reads=[b0, b1, xc], writes=[xn])
                        if lev < 6:
                            Pn, PTn = Pm[1 - cur], PT[1 - cur]
                            for g in range(4):
                                for which in range(2):
                                    pb = bank()

                                    def f(e):
                                        for hh in range(4):
                                            h = g * 4 + hh
                                            if which == 0:
                                                ins = e.matmul(pb[:, cs(hh * 128, 128)], lhsT=ptv(h), rhs=Pc[:, h, :], start=True, stop=True)
                                            else:
                                                ins = e.matmul(pb[:, cs(hh * 128, 128)], lhsT=Pc[:, h, :], rhs=ptv(h), start=True, stop=True)
                                        return ins
                                    k.op("pe", f, reads=[Pc, ptb], writes=[pb])
                                    dstb = Pn if which == 0 else PTn
                                    if which == 0:
                                        k.op("act", lambda e: e.copy(out=dstb[:, g * 4:g * 4 + 4, :].rearrange("p a b -> p (a b)"),
                                                                     in_=pb[:]), reads=[pb], writes=[dstb])
                                    else:
                                        k.op("dve", lambda e: e.tensor_copy(out=dstb[:, g * 4:g * 4 + 4, :].rearrange("p a b -> p (a b)"),
                                                                            in_=pb[:]), reads=[pb], writes=[dstb])
                                if g % 2 == 1:
                                    yield
                            cur = 1 - cur
                    U = Xb[1]
                    pp, b0, b1 = pair()

                    def f(e):
                        for h in range(NHEAD):
                            ct, p0 = h // 2, (h % 2) * 64
                            e.matmul(pp[:, hc(h)], lhsT=FM[p0:p0 + 64, ct, 1, :], rhs=Hb[p0:p0 + 64, ct, :], start=True, stop=False)
                            e.matmul(pp[:, hc(h)], lhsT=MB[:, h, 128:256], rhs=U[:, hc(h)], start=False, stop=False)
                            ins = e.matmul(pp[:, hc(h)], lhsT=MB[:, h, 384:512], rhs=z[:, cs(2 * D + h * 64, 64)],
                                           start=False, stop=True)
                        return ins
                    k.op("pe", f, reads=[FM, Hb, MB, U, z], writes=[b0, b1])
                    k.op("act", lambda e: e.copy(out=ysc[:].rearrange("p (hp h2 i) -> p h2 hp i", h2=2, i=64),
                                                 in_=pp[:, :].rearrange("p (h2 hp i) -> p h2 hp i", hp=8, i=64)), reads=[b0, b1], writes=[ysc])
                    k.dma("sp", ysc_d[d, cs(i * 128, 128), :], ysc[:], reads=[ysc], writes=[R_ysc(d, i)], sembuf=ysc)
                    pp, b0, b1 = pair()

                    def f(e):
                        for ct in range(8):
                            e.matmul(pp[:, cs(ct * 128, 128)], lhsT=TM[:, 2, cs(ct * 128, 128)],
                                     rhs=U[:].rearrange("p (a b c) -> p a b c", a=2, b=8)[:, :, ct, :], start=True, stop=False)
                            ins = e.matmul(pp[:, cs(ct * 128, 128)], lhsT=TM[:, 3, cs(ct * 128, 128)], rhs=z[:, cs(2 * D + ct * 128, 128)],
                                           start=False, stop=True)
                        return ins
                    k.op("pe", f, reads=[TM, U, z], writes=[b0, b1])
                    k.op("dve", lambda e: e.tensor_tensor(out=H[:], in0=H[:], in1=gC[:].unsqueeze(2).to_broadcast([128, 8, 64]),
                                                          op=ALU.mult), reads=[H, gC], writes=[H])
                    ppv = pp[:, :].rearrange("p (a b) -> p a b", b=128)
                    k.op("dve", lambda e: e.tensor_tensor(out=H[0:64, :, :], in0=H[0:64, :, :], in1=ppv[0:64, :, 0:64], op=ALU.add),
                         reads=[H, b0, b1], writes=[H])
                    k.op("dve", lambda e: e.tensor_tensor(out=H[64:128, :, :], in0=H[64:128, :, :], in1=ppv[64:128, :, 64:128], op=ALU.add),
                         reads=[H, b0, b1], writes=[H])
                    if n % 2 == 1:
                        grp = i // 2
                        k.dma("sp", st_d[l, d, grp], H[:].rearrange("p a b -> p (a b)"), reads=[H], writes=[R_st(l, d, grp)], sembuf=H)
                        k.op("dve", lambda e: e.tensor_scalar(out=H[:], in0=H[:], scalar1=cmask[:, 0:1], scalar2=None, op0=ALU.mult),
                             reads=[H, cmask], writes=[H])
                    k.op("act", lambda e: e.copy(out=Hb[:], in_=H[:]), reads=[H], writes=[Hb])
                    yield

            gens = [dir_gen(0), dir_gen(1)]
            next(gens[0])
            next(gens[1])
            for _ in range(6):
                next(gens[0])
            live = list(gens)
            while live:
                for g in list(live):
                    try:
                        next(g)
                    except StopIteration:
                        live.remove(g)
            k.barrier()

    def phaseM(l, xsrc_d, R_xsrc):
        with contextlib.ExitStack() as es:
            wa = sbt(es, "wa", [128, 8, D], BF16)
            wb = sbt(es, "wb", [128, 8, D], BF16)
            wo = sbt(es, "wo", [128, 8, D], BF16)
            gup = sbt(es, "gup", [128, D], BF16)
            wsT = sbt(es, "wsT", [128, 8, 128], BF16)
            bsT = sbt(es, "bsT", [128, 8], F32)
            lnxg = sbt(es, "lnxg", [128, D], F32)
            lnxb = sbt(es, "lnxb", [128, D], F32)
            lnvg = sbt(es, "lnvg", [128, D], F32)
            gate1 = sbt(es, "gate1", [128, D], F32)
            zv = sbt(es, "zv", [128, D + 128], BF16)
            zrs = sbt(es, "zrs", [128, 4096], BF16)
            yf = sbt(es, "yf", [128, D], F32)
            yb = sbt(es, "yb", [128, D], F32)
            b0t = sbt(es, "b0t", [128, 16], F32)
            b1t = sbt(es, "b1t", [128, 16], F32)
            xt = sbt(es, "xtm", [128, D], F32)
            W0 = sbt(es, "W0", [128, D], F32)
            W1 = sbt(es, "W1", [128, D], F32)
            W2 = sbt(es, "W2", [128, D], F32)
            s16a = sbt(es, "s16a", [128, 16], F32)
            s16b = sbt(es, "s16b", [128, 16], F32)
            s16c = sbt(es, "s16c", [128, 16], F32)
            bnst = sbt(es, "bnst", [128, 2, 6], F32)
            mv = sbt(es, "mv", [128, 2], F32)
            rsv = sbt(es, "rsv", [128, 1], F32)
            gsb = sbt(es, "gsb", [128, 128], BF16)
            gT = sbt(es, "gT", [128, 1, 128], BF16)
            actb = sbt(es, "actb", [128, D], BF16)
            actT = sbt(es, "actT", [128, 8, 128], BF16)
            ub = sbt(es, "ub", [128, D], BF16)
            vcb = sbt(es, "vcb", [128, D], BF16)
            cast_load_rows(lambda kc: wa[:, kc, :], wa_d[l], 8, D, wa)
            cast_load_rows(lambda kc: wb[:, kc, :], wb_d[l], 8, D, wb)
            cast_load_rows(lambda kc: wo[:, kc, :], wo_d[l], 8, D, wo)
            k.dma("pool", gup[:], gup_d[l], writes=[gup], max_dma_last_dim=4096)
            k.dma("pool", wsT[:].rearrange("p a b -> p (a b)"), wsT_d[l], writes=[wsT], max_dma_last_dim=4096)
            k.dma("sp", bsT[:], bsT_d[l], writes=[bsT])
            bc_load(lnxg, lnxg_d[l])
            bc_load(lnxb, lnxb_d[l])
            bc_load(lnvg, lnvg_d[l])
            load_mod(gate1, l, 2)

            def v3(b):
                return b[:].rearrange("p (h n) -> p h n", n=64)

            def bc16(b):
                return b[:].unsqueeze(2).to_broadcast([128, 16, 64])

            def proj(src_bf, wmat):
                transpose8(src_bf, actT)
                pp, p0, p1 = pair()

                def f(e):
                    for nn in range(2):
                        for kc in range(8):
                            ins = e.matmul(pp[:, cs(nn * 512, 512)], lhsT=actT[:, kc, :], rhs=wmat[:, kc, cs(nn * 512, 512)],
                                           start=(kc == 0), stop=(kc == 7))
                    return ins
                k.op("pe", f, reads=[actT, wmat], writes=[p0, p1])
                return pp, p0, p1

            for i in range(NT):
                rows = cs(i * 128, 128)
                k.dma("sp", zv[:, 0:D], zr_d[rows, 2 * D:3 * D], reads=[R_zr(i)], writes=[zv])
                k.dma("sp", zv[:, D:D + 128], zr_d[rows, cs(3 * D + 256, 128)], reads=[R_zr(i)], writes=[zv])
                k.dma("sp", zrs[:], zrest_d[rows, :], reads=[R_zrest(i, j) for j in range(8)], writes=[zrs])
                k.dma("sp", yf[:], ysc_d[0, rows, :], reads=[R_ysc(0, i)], writes=[yf])
                k.dma("sp", yb[:], ysc_d[1, rows, :], reads=[R_ysc(1, i)], writes=[yb])
                k.dma("sp", b0t[:], bon_d[0, rows, :], reads=[R_bon(0, i)], writes=[b0t])
                k.dma("sp", b1t[:], bon_d[1, rows, :], reads=[R_bon(1, i)], writes=[b1t])
                k.dma("sp", xt[:], xsrc_d[rows, :], reads=[R_xsrc(i)], writes=[xt])
                k.op("dve", lambda e: e.tensor_tensor(out=yf[:], in0=yf[:], in1=yb[:], op=ALU.add), reads=[yf, yb], writes=[yf])
                k.op("dve", lambda e: e.tensor_reduce(out=s16a[:], in_=v3(yf), axis=AX.X, op=ALU.add), reads=[yf], writes=[s16a])
                k.op("dve", lambda e: e.tensor_scalar(out=s16a[:], in0=s16a[:], scalar1=1.0 / 64, scalar2=None, op0=ALU.mult),
                     reads=[s16a], writes=[s16a])
                k.op("dve", lambda e: e.tensor_tensor(out=v3(yf), in0=v3(yf), in1=bc16(s16a), op=ALU.subtract), reads=[yf, s16a], writes=[yf])
                k.op("dve", lambda e: e.tensor_tensor(out=W0[:], in0=yf[:], in1=yf[:], op=ALU.mult), reads=[yf], writes=[W0])
                k.op("dve", lambda e: e.tensor_reduce(out=s16b[:], in_=v3(W0), axis=AX.X, op=ALU.add), reads=[W0], writes=[s16b])
                rstd_from(s16b, s16c, 1.0 / 64, GN_EPS)
                k.op("dve", lambda e: e.tensor_tensor(out=v3(yf), in0=v3(yf), in1=bc16(s16c), op=ALU.mult), reads=[yf, s16c], writes=[yf])
                k.op("dve", lambda e: e.tensor_tensor(out=yf[:], in0=yf[:], in1=lnxg[:], op=ALU.mult), reads=[yf, lnxg], writes=[yf])
                k.op("dve", lambda e: e.tensor_tensor(out=yf[:], in0=yf[:], in1=lnxb[:], op=ALU.add), reads=[yf, lnxb], writes=[yf])
                k.op("dve", lambda e: e.tensor_tensor(out=b0t[:], in0=b0t[:], in1=b1t[:], op=ALU.add), reads=[b0t, b1t], writes=[b0t])
                k.op("dve", lambda e: e.tensor_tensor(out=v3(W0), in0=zv[:, 0:D].rearrange("p (h n) -> p h n", n=64), in1=bc16(b0t),
                                                      op=ALU.mult), reads=[zv, b0t], writes=[W0])
                k.op("dve", lambda e: e.tensor_tensor(out=yf[:], in0=yf[:], in1=W0[:], op=ALU.add), reads=[yf, W0], writes=[yf])
                k.op("act", lambda e: e.activation(out=gsb[:], in_=zv[:, D:D + 128], func=AF.Sigmoid), reads=[zv], writes=[gsb])
                transpose8(gsb, gT, nblk=1)
                pp, p0, p1 = pair()

                def f(e):
                    e.matmul(pp[:, 0:512], lhsT=gT[:, 0, :], rhs=gup[:, 0:512], start=True, stop=True)
                    return e.matmul(pp[:, 512:1024], lhsT=gT[:, 0, :], rhs=gup[:, 512:1024], start=True, stop=True)
                k.op("pe", f, reads=[gT, gup], writes=[p0, p1])
                k.op("dve", lambda e: e.tensor_tensor(out=actb[:], in0=pp[:, :], in1=yf[:], op=ALU.mult), reads=[p0, p1, yf], writes=[actb])
                pp, p0, p1 = proj(actb, wa)
                k.op("act", lambda e: e.activation(out=W0[:], in_=zrs[:, 2048:3072], func=AF.Sigmoid), reads=[zrs], writes=[W0])
                k.op("dve", lambda e: e.tensor_tensor(out=W2[:], in0=pp[:, :], in1=W0[:], op=ALU.mult), reads=[p0, p1, W0], writes=[W2])
                k.op("act", lambda e: e.activation(out=ub[:], in_=zrs[:, 0:1024], func=AF.Gelu_apprx_tanh), reads=[zrs], writes=[ub])
                k.op("act", lambda e: e.activation(out=W1[:], in_=zrs[:, 1024:2048], func=AF.Gelu_apprx_tanh), reads=[zrs], writes=[W1])
                for c in range(2):
                    k.op("dve", lambda e: e.bn_stats(out=bnst[:, c, :], in_=W1[:, cs(c * 512, 512)]), reads=[W1], writes=[bnst])
                k.op("dve", lambda e: e.bn_aggr(out=mv[:], in_=bnst[:].rearrange("p a b -> p (a b)")), reads=[bnst], writes=[mv])
                rstd_from_ap(mv, 1, rsv, EPS)
                k.op("dve", lambda e: e.tensor_scalar(out=W1[:], in0=W1[:], scalar1=mv[:, 0:1], scalar2=rsv[:, 0:1], op0=ALU.subtract,
                                                      op1=ALU.mult), reads=[W1, mv, rsv], writes=[W1])
                k.op("dve", lambda e: e.tensor_tensor(out=vcb[:], in0=W1[:], in1=lnvg[:], op=ALU.mult), reads=[W1, lnvg], writes=[vcb])
                pp, p0, p1 = pair()

                def f(e):
                    for g in range(8):
                        ins = e.matmul(pp[:, cs(g * 128, 128)], lhsT=wsT[:, g, :], rhs=vcb[:, cs(g * 128, 128)], start=True, stop=True)
                    return ins
                k.op("pe", f, reads=[wsT, vcb], writes=[p0, p1])
                k.op("dve", lambda e: e.tensor_tensor(out=W1[:].rearrange("p (g c) -> p g c", c=128),
                                                      in0=pp[:, :].rearrange("p (g c) -> p g c", c=128),
                                                      in1=bsT[:].unsqueeze(2).to_broadcast([128, 8, 128]), op=ALU.add),
                     reads=[p0, p1, bsT], writes=[W1])
                k.op("dve", lambda e: e.tensor_tensor(out=actb[:], in0=W1[:], in1=ub[:], op=ALU.mult), reads=[W1, ub], writes=[actb])
                pp, p0, p1 = proj(actb, wb)
                k.op("act", lambda e: e.activation(out=W0[:], in_=zrs[:, 3072:4096], func=AF.Sigmoid), reads=[zrs], writes=[W0])
                k.op("dve", lambda e: e.tensor_tensor(out=W1[:], in0=pp[:, :], in1=W0[:], op=ALU.mult), reads=[p0, p1, W0], writes=[W1])
                k.op("dve", lambda e: e.tensor_tensor(out=actb[:], in0=W1[:], in1=W2[:], op=ALU.add), reads=[W1, W2], writes=[actb])
                pp, p0, p1 = proj(actb, wo)
                k.op("dve", lambda e: e.tensor_tensor(out=W0[:], in0=pp[:, :], in1=gate1[:], op=ALU.mult), reads=[p0, p1, gate1], writes=[W0])
                k.op("dve", lambda e: e.tensor_tensor(out=W0[:], in0=W0[:], in1=xt[:], op=ALU.add), reads=[W0, xt], writes=[W0])
                k.dma("sp", x1_d[rows, :], W0[:], reads=[W0], writes=[R_x1(i)], sembuf=W0)
            k.barrier()

    def rstd_from_ap(mvb, col, out_rstd, eps):
        k.op("act", lambda e: e.activation(out=out_rstd[:], in_=mvb[:, col:col + 1], func=AF.Ln, bias=eps_t(eps)[:], scale=1.0),
             reads=[mvb, eps_t(eps)], writes=[out_rstd])
        k.op("act", lambda e: e.activation(out=out_rstd[:], in_=out_rstd[:], func=AF.Exp, scale=-0.5),
             reads=[out_rstd], writes=[out_rstd])

    def phaseC(l, last):
        with contextlib.ExitStack() as es:
            w1 = sbt(es, "w1", [128, 8, DFF], BF16)
            w2 = sbt(es, "w2", [128, 32, D], BF16)
            g2 = sbt(es, "g2", [128, D], F32)
            sh2 = sbt(es, "sh2", [128, D], F32)
            gate2 = sbt(es, "gate2", [128, D], F32)
            fg = sbt(es, "fg", [128, D], F32)
            xt = [sbt(es, "xc%d" % i, [128, D], F32) for i in range(2)]
            W0 = sbt(es, "Wc0", [128, D], F32)
            hb = sbt(es, "hb2", [128, D], BF16)
            hT = sbt(es, "hT2", [128, 8, 128], BF16)
            rl = sbt(es, "rl", [128, 512], BF16)
            hid = sbt(es, "hid", [128, DFF], BF16)
            hidT = sbt(es, "hidT", [128, 32, 128], BF16)
            ss = sbt(es, "ssc", [128, 1], F32)
            rstd = sbt(es, "rstdc", [128, 1], F32)
            cast_load_rows(lambda kc: w1[:, kc, :], w1_d[l], 8, DFF, w1)
            cast_load_rows(lambda kc: w2[:, kc, :], w2_d[l], 32, D, w2)
            load_mod(sh2, l, 3)
            load_mod(g2, l, 4)
            load_mod(gate2, l, 5)
            if last:
                bc_load(fg, fg_d)

            def load_x(i):
                k.dma("sp", xt[i % 2][:], x1_d[cs(i * 128, 128), :], reads=[R_x1(i)], writes=[xt[i % 2]])
            load_x(0)
            for i in range(NT):
                x = xt[i % 2]
                if i + 1 < NT:
                    load_x(i + 1)
                k.op("act", lambda e: e.activation(out=W0[:], in_=x[:], func=AF.Square), reads=[x], writes=[W0])
                k.op("dve", lambda e: e.tensor_reduce(out=ss[:], in_=W0[:], axis=AX.X, op=ALU.add), reads=[W0], writes=[ss])
                rstd_from(ss, rstd, 1.0 / D, EPS)
                k.op("dve", lambda e: e.scalar_tensor_tensor(out=W0[:], in0=x[:], scalar=rstd[:, 0:1], in1=g2[:], op0=ALU.mult,
                                                             op1=ALU.mult), reads=[x, rstd, g2], writes=[W0])
                k.op("dve", lambda e: e.tensor_tensor(out=hb[:], in0=W0[:], in1=sh2[:], op=ALU.add), reads=[W0, sh2], writes=[hb])
                transpose8(hb, hT)
                for n in range(8):
                    pb = bank()

                    def f(e):
                        for kc in range(8):
                            ins = e.matmul(pb[:], lhsT=hT[:, kc, :], rhs=w1[:, kc, cs(n * 512, 512)], start=(kc == 0), stop=(kc == 7))
                        return ins
                    k.op("pe", f, reads=[hT, w1], writes=[pb])
                    k.op("act", lambda e: e.activation(out=rl[:], in_=pb[:], func=AF.Relu), reads=[pb], writes=[rl])
                    k.op("dve", lambda e: e.tensor_tensor(out=hid[:, cs(n * 512, 512)], in0=rl[:], in1=rl[:], op=ALU.mult),
                         reads=[rl], writes=[hid])
                for q in range(4):
                    pb = bank()
                    pv = pb[:].bitcast(BF16)

                    def f(e):
                        for j in range(8):
                            ins = e.transpose(out=pv[:, cs(j * 128, 128)], in_=hid[:, cs((q * 8 + j) * 128, 128)], identity=ident_b[:])
                        return ins
                    k.op("pe", f, reads=[hid, ident_b], writes=[pb])
                    if q % 2 == 0:
                        k.op("act", lambda e: e.copy(out=hidT[:, q * 8:q * 8 + 8, :].rearrange("p a b -> p (a b)"), in_=pv[:, :]),
                             reads=[pb], writes=[hidT])
                    else:
                        k.op("dve", lambda e: e.tensor_copy(out=hidT[:, q * 8:q * 8 + 8, :].rearrange("p a b -> p (a b)"), in_=pv[:, :]),
                             reads=[pb], writes=[hidT])
                pp, p0, p1 = pair()

                def f(e):
                    for nn in range(2):
                        for kc in range(32):
                            ins = e.matmul(pp[:, cs(nn * 512, 512)], lhsT=hidT[:, kc, :], rhs=w2[:, kc, cs(nn * 512, 512)],
                                           start=(kc == 0), stop=(kc == 31))
                    return ins
                k.op("pe", f, reads=[hidT, w2], writes=[p0, p1])
                k.op("dve", lambda e: e.tensor_tensor(out=W0[:], in0=pp[:, :], in1=gate2[:], op=ALU.mult), reads=[p0, p1, gate2], writes=[W0])
                k.op("dve", lambda e: e.tensor_tensor(out=W0[:], in0=W0[:], in1=x[:], op=ALU.add), reads=[W0, x], writes=[W0])
                rows = cs(i * 128, 128)
                if not last:
                    k.dma("sp", x2_d[rows, :], W0[:], reads=[W0], writes=[R_x2(i)], sembuf=W0)
                else:
                    k.op("act", lambda e: e.activation(out=x[:], in_=W0[:], func=AF.Square), reads=[W0], writes=[x])
                    k.op("dve", lambda e: e.tensor_reduce(out=ss[:], in_=x[:], axis=AX.X, op=ALU.add), reads=[x], writes=[ss])
                    rstd_from(ss, rstd, 1.0 / D, EPS)
                    k.op("dve", lambda e: e.scalar_tensor_tensor(out=W0[:], in0=W0[:], scalar=rstd[:, 0:1], in1=fg[:], op0=ALU.mult,
                                                                 op1=ALU.mult), reads=[W0, rstd, fg], writes=[W0])
                    k.dma("sp", y_d[rows, :], W0[:], reads=[W0], writes=[R_y(i)], sembuf=W0)
            k.barrier()

    R_xin = DR("xin")
    steps = [lambda: phaseP(0), lambda: phaseP(1)]
    for l in range(2):
        xs, Rx = (x_d, R_xin) if l == 0 else (x2_d, R_x2)
        steps += [lambda l=l, xs=xs, Rx=Rx: phaseA1(l, xs, Rx), lambda l=l: phaseS2(l),
                  lambda l=l, xs=xs, Rx=Rx: phaseM(l, xs, Rx), lambda l=l: phaseC(l, last=(l == 1))]
    for st_ in steps[:upto]:
        st_()
    k.barrier()
    ges.close()
    return nc, k


def _shift_mats(kind):
    m = np.zeros((4, 3, 128, 128), np.float32)
    eye = np.eye(128, dtype=np.float32)
    t = np.arange(128)
    for cls in range(4):
        cur = np.zeros((128, 128), np.float32)
        nbe = np.zeros((128, 128), np.float32)
        nbo = np.zeros((128, 128), np.float32)
        if kind == "sample":
            if cls == 0:
                for to in t:
                    if to % 64 != 0:
                        cur[to - 1, to] = 1
            elif cls == 1:
                for to in t:
                    if to % 64 != 63:
                        cur[to + 1, to] = 1
            elif cls == 2:
                for to in t:
                    if to >= 64:
                        cur[to - 64, to] = 1
                    else:
                        nbe[to + 64, to] = 1
                        nbo[to + 64, to] = 1
            else:
                for to in t:
                    if to < 64:
                        cur[to + 64, to] = 1
                    else:
                        nbe[to - 64, to] = 1
                        nbo[to - 64, to] = 1
        else:
            if cls in (0, 2):
                for to in t:
                    if to >= 1:
                        cur[to - 1, to] = 1
                nbo[127, 0] = 1
            else:
                for to in t:
                    if to <= 126:
                        cur[to + 1, to] = 1
                nbe[0, 127] = 1
        m[cls, 0] = cur - eye
        m[cls, 1] = nbe
        m[cls, 2] = nbo
    return np.ascontiguousarray(m.reshape(12, 128, 128).transpose(1, 0, 2).reshape(128, 12 * 128))


def _tri4():
    s = np.arange(128)[:, None]
    t = np.arange(128)[None, :]
    m = np.stack([(s <= t), (s >= t), (s < t), (s > t)], axis=1).astype(np.float32)
    return np.ascontiguousarray(m.reshape(128, 512))


def _state_to_H(st):
    a = st.reshape(2, 2, 8, 2, 64, 64)
    a = a.transpose(0, 1, 3, 5, 2, 4)
    return np.ascontiguousarray(a.reshape(2, 2, 128, 512))


def _H_to_state(Hm):
    lead = Hm.shape[:-2]
    a = Hm.reshape(lead + (2, 64, 8, 64))
    nl = len(lead)
    perm = tuple(range(nl)) + (nl + 2, nl + 0, nl + 3, nl + 1)
    a = a.transpose(perm)
    return a.reshape(lead + (16, 64, 64))


def make_core_inputs(kind, x_tokens, cond_vec, state_lh, shared):
    d = dict(shared)
    d["x"] = np.ascontiguousarray(x_tokens, dtype=np.float32)
    d["cond"] = np.ascontiguousarray(cond_vec.reshape(8, 128).T, dtype=np.float32)
    d["state0"] = _state_to_H(state_lh)
    d["cmask"] = np.full((128, 1), 1.0 if kind == "sample" else 0.0, np.float32)
    d["shm"] = _shift_mats(kind)
    return d


def shared_inputs(w_ada, b_ada, norm1_g, norm2_g, w_in, mu_shift, w0, w_up, a0, a_up, g_up, k_k, k_a, r_k, lnx_g,
                  lnx_b, w_branch_a, ln_v_g, w_s, b_s, w_branch_b, w_out, w1, w2, final_g):
    f = lambda a: np.ascontiguousarray(np.asarray(a), dtype=np.float32)
    wup_aug = np.concatenate([np.asarray(w_up), np.asarray(w0)[:, :, None, :]], axis=2)
    aup_aug = np.concatenate([np.asarray(a_up), np.asarray(a0)[:, :, None, :]], axis=2)
    wsT = np.asarray(w_s).transpose(0, 3, 1, 2).reshape(2, 128, 8 * 128)
    bsT = np.asarray(b_s).transpose(0, 2, 1)
    return dict(ident=np.eye(128, dtype=np.float32), tri4=_tri4(), w_ada=f(w_ada), b_ada=f(b_ada), norm1_g=f(norm1_g),
                norm2_g=f(norm2_g), w_in=f(w_in), mu_shift=f(mu_shift), wup_aug=f(wup_aug), aup_aug=f(aup_aug), g_up=f(g_up),
                k_k=f(k_k), k_a=f(k_a), r_k=f(np.asarray(r_k).reshape(2, D)), lnx_g=f(lnx_g), lnx_b=f(lnx_b),
                w_branch_a=f(w_branch_a), ln_v_g=f(ln_v_g), wsT=f(wsT), bsT=f(bsT), w_branch_b=f(w_branch_b), w_out=f(w_out),
                w1=f(w1), w2=f(w2), final_g=f(final_g))


_PROG = {}


def kernel(x_prompt, x_sample, state_rwkv, c, c_ctx, w_ada, b_ada, norm1_g, norm2_g, w_in, mu_shift,
           w0, w_up, a0, a_up, g_up, k_k, k_a, r_k, lnx_g, lnx_b, w_branch_a, ln_v_g, w_s, b_s,
           w_branch_b, w_out, w1, w2, final_g):
    NT = 32
    x_prompt = np.asarray(x_prompt, dtype=np.float32)
    x_sample = np.asarray(x_sample, dtype=np.float32)
    state_rwkv = np.asarray(state_rwkv, dtype=np.float32)
    c = np.asarray(c, dtype=np.float32)
    c_ctx = np.asarray(c_ctx, dtype=np.float32)
    shared = shared_inputs(w_ada, b_ada, norm1_g, norm2_g, w_in, mu_shift, w0, w_up, a0, a_up, g_up, k_k, k_a, r_k,
                           lnx_g, lnx_b, w_branch_a, ln_v_g, w_s, b_s, w_branch_b, w_out, w1, w2, final_g)
    in_maps = []
    for b in range(4):
        in_maps.append(make_core_inputs("sample", x_sample[b], c[b], state_rwkv[b], shared))
    zero_state = np.zeros((2, 2, 16, 64, 64), np.float32)
    for q in range(4):
        xs = np.zeros((NT * 128, D), np.float32)
        xs[:2048] = x_prompt[8 * q:8 * q + 8].reshape(2048, D)
        xs[2048:] = xs[:2048]
        in_maps.append(make_core_inputs("prompt", xs, c_ctx, zero_state, shared))
    if NT not in _PROG:
        _PROG[NT] = build_program(NT)[0]
    res = run_bass_kernel_spmd(_PROG[NT], in_maps, core_ids=list(range(8)))
    r = res.results
    y_sample = np.stack([r[b]["y"] for b in range(4)], axis=0)
    y_prompt = np.concatenate([r[4 + q]["y"][:2048].reshape(8, 256, D) for q in range(4)], axis=0)
    sts = []
    for q in range(4):
        so = r[4 + q]["st_out"]
        so = so[:, :, :8]
        s = _H_to_state(so)
        sts.append(np.transpose(s, (2, 0, 1, 3, 4, 5)))
    new_state = np.ascontiguousarray(np.concatenate(sts, axis=0), dtype=np.float32)
    return (np.ascontiguousarray(y_prompt, dtype=np.float32), np.ascontiguousarray(y_sample, dtype=np.float32), new_state)
```

```python
import contextlib
import os
DBG = int(os.environ.get('KDBG', '99'))
KSKIP = os.environ.get('KSKIP', '')
import numpy as np
import concourse.bass as bass
import concourse.mybir as mybir
from concourse.bass_utils import run_bass_kernel_spmd

F32 = mybir.dt.float32
BF16 = mybir.dt.bfloat16
ALU = mybir.AluOpType
AF = mybir.ActivationFunctionType
AX = mybir.AxisListType

D = 1024
CR = 3456
DIN = 7552
DFF = 4096
NHEAD = 16
EPS = 1e-6
GN_EPS = 64e-5
DSC = float(np.exp(-0.5))


class Buf:
    __slots__ = ("name", "t", "w", "r", "dsem", "dcnt")

    def __init__(self, name, t=None):
        self.name = name
        self.t = t
        self.w = None
        self.r = []
        self.dsem = None
        self.dcnt = 0

    def __getitem__(self, idx):
        return self.t[idx]


class K:
    def __init__(self, nc):
        self.nc = nc
        self.eng = {"pe": nc.tensor, "act": nc.scalar, "dve": nc.vector, "pool": nc.gpsimd, "sp": nc.sync}
        self.sem = {}
        self.cnt = {}
        for e in self.eng:
            self.sem[e] = nc.alloc_semaphore(name="s_" + e)
            self.cnt[e] = 0
        self.waited = {}
        self.dsems = {}
        self.free_dsems = []
        self.ninstr = 0
        self.uid = 0

    def _wait(self, e, tok):
        if tok is None:
            return
        key, val = tok
        if key == e and e == "pe":
            return
        kk = (e, key)
        if self.waited.get(kk, 0) >= val:
            return
        self.waited[kk] = val
        self.eng[e].wait_ge(self.sem[key], val)
        self.ninstr += 1

    def _deps(self, e, reads, writes):
        for b in reads:
            self._wait(e, b.w)
        for b in writes:
            self._wait(e, b.w)
            for tok in b.r:
                self._wait(e, tok)

    def _commit(self, tok, reads, writes):
        for b in reads:
            if b not in writes:
                b.r.append(tok)
                if len(b.r) > 10:
                    best = {}
                    for k_, v_ in b.r:
                        if best.get(k_, -1) < v_:
                            best[k_] = v_
                    b.r = list(best.items())
        for b in writes:
            b.w = tok
            b.r = []

    def op(self, e, fn, reads=(), writes=()):
        reads = [b for b in reads if b is not None]
        writes = [b for b in writes if b is not None]
        self._deps(e, reads, writes)
        ins = fn(self.eng[e])
        self.cnt[e] += 1
        ins.then_inc(self.sem[e], 1)
        self.ninstr += 1
        self._commit((e, self.cnt[e]), reads, writes)

    def dma(self, q, out_ap, in_ap, reads=(), writes=(), sembuf=None, **kw):
        reads = [b for b in reads if b is not None]
        writes = [b for b in writes if b is not None]
        if sembuf is None:
            sembuf = (writes + reads)[0]
        if sembuf.dsem is None:
            if self.free_dsems:
                key, base = self.free_dsems.pop()
                sembuf.dcnt = base
            else:
                key = "d%d" % len(self.sem)
                self.sem[key] = self.nc.alloc_semaphore(name=key)
            self.dsems[key] = sembuf
            sembuf.dsem = key
        self._deps(q, reads, writes)
        ins = self.eng[q].dma_start(out=out_ap, in_=in_ap, **kw)
        sembuf.dcnt += 16
        ins.then_inc(self.sem[sembuf.dsem], 16)
        self.ninstr += 1
        self._commit((sembuf.dsem, sembuf.dcnt), reads, writes)

    def barrier(self):
        toks = [(e, self.cnt[e]) for e in self.eng if self.cnt[e] > 0]
        toks += [(key, b.dcnt) for key, b in self.dsems.items() if b.dcnt > 0]
        for e in self.eng:
            for tok in toks:
                if tok[0] != e:
                    self._wait(e, tok)
        for key, b in list(self.dsems.items()):
            if not getattr(b, "keep", False):
                self.free_dsems.append((key, b.dcnt))
                b.dsem = None
                del self.dsems[key]


def cs(a, n):
    return slice(a, a + n)


def hc(h):
    return slice((h % 2) * 512 + (h // 2) * 64, (h % 2) * 512 + (h // 2) * 64 + 64)


def build_program(NT, upto=99):
    T = NT * 128
    NG = NT // 2
    nc = bass.Bass("TRN2", target_bir_lowering=False)
    k = K(nc)

    def din(name, shape):
        return nc.dram_tensor(name, list(shape), F32, kind="ExternalInput").ap()

    x_d = din("x", [T, D])
    cond_d = din("cond", [128, 8])
    state0_d = din("state0", [2, 2, 128, 512])
    cmask_d = din("cmask", [128, 1])
    shm_d = din("shm", [128, 12 * 128])
    ident_d = din("ident", [128, 128])
    tri4_d = din("tri4", [128, 4 * 128])
    w_ada_d = din("w_ada", [2, D, 6 * D])
    b_ada_d = din("b_ada", [2, 6 * D])
    n1g_d = din("norm1_g", [2, D])
    n2g_d = din("norm2_g", [2, D])
    w_in_d = din("w_in", [2, D, DIN])
    mu_d = din("mu_shift", [2, CR])
    wup_d = din("wup_aug", [2, 2, 65, D])
    aup_d = din("aup_aug", [2, 2, 65, D])
    gup_d = din("g_up", [2, 128, D])
    kk_d = din("k_k", [2, D])
    ka_d = din("k_a", [2, D])
    rk_d = din("r_k", [2, D])
    lnxg_d = din("lnx_g", [2, D])
    lnxb_d = din("lnx_b", [2, D])
    wa_d = din("w_branch_a", [2, D, D])
    lnvg_d = din("ln_v_g", [2, D])
    wsT_d = din("wsT", [2, 128, 8 * 128])
    bsT_d = din("bsT", [2, 128, 8])
    wb_d = din("w_branch_b", [2, D, D])
    wo_d = din("w_out", [2, D, D])
    w1_d = din("w1", [2, D, DFF])
    w2_d = din("w2", [2, DFF, D])
    fg_d = din("final_g", [D])

    y_d = nc.dram_tensor("y", [T, D], F32, kind="ExternalOutput").ap()
    st_d = nc.dram_tensor("st_out", [2, 2, NG, 128, 512], F32, kind="ExternalOutput").ap()

    def dscr(name, shape, dt):
        return nc.dram_tensor(name, list(shape), dt, kind="Internal").ap()

    modbc_d = dscr("modbc", [2, 128, 6 * D], F32)
    zr_d = dscr("zr_s", [T, CR], BF16)
    zrest_d = dscr("zrest_s", [T, 4096], BF16)
    ysc_d = dscr("ysc_s", [2, T, D], F32)
    bon_d = dscr("bon_s", [2, T, 16], F32)
    x1_d = dscr("x1_s", [T, D], F32)
    x2_d = dscr("x2_s", [T, D], F32)

    class DR:
        def __init__(self, nm):
            self.b = {}
            self.nm = nm

        def __call__(self, *key):
            if key not in self.b:
                self.b[key] = Buf(self.nm + str(key))
            return self.b[key]

    R_mod, R_zr, R_zrest, R_ysc, R_bon, R_x1, R_x2, R_y, R_st = [DR(n) for n in
        ("mod", "zr", "zrest", "ysc", "bon", "x1", "x2", "y", "st")]

    PP = [nc.alloc_psum_tensor("psum%d" % i, [128, 1024], F32) for i in range(4)]
    PB = []
    for i in range(8):
        PB.append(Buf("pb%d" % i, PP[i // 2][:, cs((i % 2) * 512, 512)]))
    pst = {"b": 0, "p": 0}

    def bank():
        b = PB[pst["b"] % 8]
        pst["b"] += 1
        return b

    def pair():
        if pst["b"] % 2:
            pst["b"] += 1
        i = (pst["b"] % 8) // 2
        pst["b"] += 2
        return PP[i], PB[2 * i], PB[2 * i + 1]

    def sbt(es, name, shape, dt):
        k.uid += 1
        t = es.enter_context(nc.sbuf_tensor("%s_%d" % (name, k.uid), list(shape), dt))
        return Buf(name, t)

    ges = contextlib.ExitStack()
    ident_f = sbt(ges, "ident_f", [128, 128], F32)
    ident_b = sbt(ges, "ident_b", [128, 128], BF16)
    tri4 = sbt(ges, "tri4", [128, 4, 128], F32)
    ones_f = sbt(ges, "ones_f", [128, 1], F32)
    cmask = sbt(ges, "cmask", [128, 1], F32)
    k.dma("sp", ident_f[:], ident_d, writes=[ident_f])
    k.dma("sp", tri4[:].rearrange("p a b -> p (a b)"), tri4_d, writes=[tri4])
    k.dma("sp", cmask[:], cmask_d, writes=[cmask])
    k.op("dve", lambda e: e.tensor_copy(out=ident_b[:], in_=ident_f[:]), reads=[ident_f], writes=[ident_b])
    k.op("dve", lambda e: e.memset(ones_f[:], 1.0), writes=[ones_f])

    def bc_load(buf, dvec):
        k.dma("sp", buf[:], dvec.partition_broadcast(128), writes=[buf])

    def cast_load_rows(buf_ap_fn, dsrc, nk, ncol, buf):
        for kc in range(nk):
            k.dma("pool", buf_ap_fn(kc), dsrc[cs(kc * 128, 128), :], writes=[buf], max_dma_last_dim=4096)

    def rstd_from(e_ss, out_rstd, scale, eps):
        k.op("act", lambda e: e.activation(out=out_rstd[:], in_=e_ss[:], func=AF.Ln, bias=eps_t(eps)[:], scale=scale),
             reads=[e_ss, eps_t(eps)], writes=[out_rstd])
        k.op("act", lambda e: e.activation(out=out_rstd[:], in_=out_rstd[:], func=AF.Exp, scale=-0.5),
             reads=[out_rstd], writes=[out_rstd])

    eps_tiles = {}

    def eps_t(v):
        if v not in eps_tiles:
            b = sbt(ges, "eps%d" % len(eps_tiles), [128, 1], F32)
            k.op("dve", lambda e: e.memset(b[:], float(v)), writes=[b])
            eps_tiles[v] = b
        return eps_tiles[v]

    for v in (EPS, GN_EPS, 1e-12):
        eps_t(v)

    def phaseP(l):
        with contextlib.ExitStack() as es:
            wad = sbt(es, "wad", [128, 8, 6 * D], BF16)
            ba = sbt(es, "ba", [128, 6 * D], F32)
            mod = sbt(es, "mod", [128, 6 * D], F32)
            n1g = sbt(es, "n1g", [128, D], F32)
            n2g = sbt(es, "n2g", [128, D], F32)
            cnd = sbt(es, "cnd", [128, 8], F32)
            scb = sbt(es, "scb", [128, 8, 128], BF16)
            cast_load_rows(lambda kc: wad[:, kc, :], w_ada_d[l], 8, 6 * D, wad)
            bc_load(ba, b_ada_d[l])
            bc_load(n1g, n1g_d[l])
            bc_load(n2g, n2g_d[l])
            k.dma("sp", cnd[:], cond_d, writes=[cnd])
            k.op("act", lambda e: e.activation(out=cnd[:], in_=cnd[:], func=AF.Silu), reads=[cnd], writes=[cnd])
            k.op("dve", lambda e: e.tensor_copy(out=scb[:], in_=cnd[:].unsqueeze(2).to_broadcast([128, 8, 128])),
                 reads=[cnd], writes=[scb])
            for n in range(12):
                pb = bank()

                def f(e):
                    for kc in range(8):
                        ins = e.matmul(pb[:], lhsT=scb[:, kc, :], rhs=wad[:, kc, cs(n * 512, 512)],
                                       start=(kc == 0), stop=(kc == 7))
                    return ins
                k.op("pe", f, reads=[scb, wad], writes=[pb])
                k.op("dve", lambda e: e.tensor_tensor(out=mod[:, cs(n * 512, 512)], in0=pb[:], in1=ba[:, cs(n * 512, 512)],
                                                      op=ALU.add), reads=[pb, ba], writes=[mod])
            k.op("dve", lambda e: e.scalar_tensor_tensor(out=mod[:, cs(D, D)], in0=mod[:, cs(D, D)], scalar=1.0, in1=n1g[:],
                                                         op0=ALU.add, op1=ALU.mult), reads=[mod, n1g], writes=[mod])
            k.op("dve", lambda e: e.scalar_tensor_tensor(out=mod[:, cs(4 * D, D)], in0=mod[:, cs(4 * D, D)], scalar=1.0,
                                                         in1=n2g[:], op0=ALU.add, op1=ALU.mult), reads=[mod, n2g], writes=[mod])
            k.dma("sp", modbc_d[l], mod[:], reads=[mod], writes=[R_mod(l)], sembuf=mod)
            k.barrier()

    def load_mod(buf, l, j):
        k.dma("sp", buf[:], modbc_d[l][:, cs(j * D, D)], reads=[R_mod(l)], writes=[buf])

    def phaseA1(l, xsrc_d, R_xsrc):
        with contextlib.ExitStack() as es:
            win = sbt(es, "win", [128, 8, DIN], BF16)
            g1 = sbt(es, "g1", [128, D], F32)
            sh1 = sbt(es, "sh1", [128, D], F32)
            mu = sbt(es, "mu", [128, CR], F32)
            shm = sbt(es, "shm", [128, 12, 128], BF16)
            xt = [sbt(es, "xt0", [128, D], F32)]
            xt.append(xt[0])
            sq = sbt(es, "sq", [128, D], F32)
            hb = sbt(es, "hb", [128, D], BF16)
            hT = sbt(es, "hT", [128, 8, 128], BF16)
            ss = sbt(es, "ss", [128, 1], F32)
            rstd = sbt(es, "rstd", [128, 1], F32)
            zb = [sbt(es, "zb%d" % i, [128, CR], BF16) for i in range(2)]
            zm = [sbt(es, "zm%d" % i, [128, CR], BF16) for i in range(3)]
            zst = sbt(es, "zst", [128, CR], BF16)
            rst = [sbt(es, "rst%d" % i, [128, 512], BF16) for i in range(4)]
            cast_load_rows(lambda kc: win[:, kc, :], w_in_d[l], 8, DIN, win)
            k.dma("pool", shm[:].rearrange("p a b -> p (a b)"), shm_d, writes=[shm], max_dma_last_dim=4096)
            load_mod(sh1, l, 0)
            load_mod(g1, l, 1)
            bc_load(mu, mu_d[l])
            rsti = [0]

            def load_x(i):
                k.dma("sp", xt[i % 2][:], xsrc_d[cs(i * 128, 128), :], reads=[R_xsrc(i)], writes=[xt[i % 2]])

            def stage1(i):
                x = xt[i % 2]
                if DBG < 2:
                    if i + 1 < NT:
                        load_x(i + 1)
                    return
                k.op("act", lambda e: e.activation(out=sq[:], in_=x[:], func=AF.Square), reads=[x], writes=[sq])
                k.op("dve", lambda e: e.tensor_reduce(out=ss[:], in_=sq[:], axis=AX.X, op=ALU.add), reads=[sq], writes=[ss])
                rstd_from(ss, rstd, 1.0 / D, EPS)
                k.op("dve", lambda e: e.scalar_tensor_tensor(out=x[:], in0=x[:], scalar=rstd[:, 0:1], in1=g1[:],
                                                             op0=ALU.mult, op1=ALU.mult), reads=[x, rstd, g1], writes=[x])
                k.op("dve", lambda e: e.tensor_tensor(out=hb[:], in0=x[:], in1=sh1[:], op=ALU.add),
                     reads=[x, sh1], writes=[hb])
                if i + 1 < NT:
                    load_x(i + 1)
                if DBG < 3:
                    return
                transpose8(hb, hT)
                if DBG < 4:
                    return
                zbi, zmi = zb[i % 2], zm[i % 3]
                col = 0
                ci = 0
                while col < DIN:
                    if col < CR:
                        n = min(512, CR - col)
                    else:
                        n = 512
                    pb = bank()

                    def f(e):
                        for kc in range(8):
                            ins = e.matmul(pb[:, 0:n], lhsT=hT[:, kc, :], rhs=win[:, kc, cs(col, n)],
                                           start=(kc == 0), stop=(kc == 7))
                        return ins
                    k.op("pe", f, reads=[hT, win], writes=[pb])
                    if 'p' in KSKIP:
                        pass
                    elif col < CR:
                        if 'z' not in KSKIP:
                            k.op("act", lambda e: e.copy(out=zbi[:, cs(col, n)], in_=pb[:, 0:n]), reads=[pb], writes=[zbi])
                        if 'm' not in KSKIP:
                            k.op("dve", lambda e: e.tensor_tensor(out=zmi[:, cs(col, n)], in0=zbi[:, cs(col, n)], in1=mu[:, cs(col, n)],
                                                                  op=ALU.mult), reads=[zbi, mu], writes=[zmi])
                    else:
                        st = rst[rsti[0] % 4]
                        rsti[0] += 1
                        if ci % 2 == 0:
                            k.op("act", lambda e: e.copy(out=st[:], in_=pb[:]), reads=[pb], writes=[st])
                        else:
                            k.op("dve", lambda e: e.tensor_copy(out=st[:], in_=pb[:]), reads=[pb], writes=[st])
                        if 'r' not in KSKIP:
                            k.dma("sp", zrest_d[cs(i * 128, 128), cs(col - CR, 512)], st[:], reads=[st],
                                  writes=[R_zrest(i, (col - CR) // 512)], sembuf=st)
                    col += n
                    ci += 1

            def stage2(i):
                if DBG < 5:
                    return
                par = i % 2
                for cls in range(4):
                    nb = i - 1 if cls in (0, 2) else i + 1
                    for hh in range(2):
                        c0 = cls + 4 * 432 * hh
                        sl = slice(c0, c0 + 4 * 431 + 1, 4)
                        pb = bank()
                        srcs = [(ident_b[:], zb[i % 2], ident_b), (shm[:, 3 * cls, :], zm[i % 3], shm)]
                        if 0 <= nb < NT:
                            srcs.append((shm[:, 3 * cls + 1 + par, :], zm[nb % 3], shm))

                        def f(e):
                            for j, (lt, rb, _) in enumerate(srcs):
                                ins = e.matmul(pb[:, 0:432], lhsT=lt, rhs=rb[:, sl], start=(j == 0), stop=(j == len(srcs) - 1))
                            return ins
                        k.op("pe", f, reads=[s[1] for s in srcs] + [ident_b, shm], writes=[pb])
                        if hh == 0:
                            k.op("act", lambda e: e.copy(out=zst[:, sl], in_=pb[:, 0:432]), reads=[pb], writes=[zst])
                        else:
                            k.op("dve", lambda e: e.tensor_copy(out=zst[:, sl], in_=pb[:, 0:432]), reads=[pb], writes=[zst])
                k.dma("sp", zr_d[cs(i * 128, 128), :], zst[:], reads=[zst], writes=[R_zr(i)], sembuf=zst)

            load_x(0)
            stage1(0)
            for i in range(NT):
                if i + 1 < NT:
                    stage1(i + 1)
                stage2(i)
            k.barrier()

    def transpose8(src, dst, nblk=8, src_off=0):
        pb = bank()
        pv = pb[:].bitcast(BF16)

        def f(e):
            for j in range(nblk):
                ins = e.transpose(out=pv[:, cs(j * 128, 128)], in_=src[:, cs(src_off + j * 128, 128)], identity=ident_b[:])
            return ins
        k.op("pe", f, reads=[src, ident_b], writes=[pb])
        k.op("act", lambda e: e.copy(out=dst[:, 0:nblk, :].rearrange("p a b -> p (a b)"), in_=pv[:, 0:nblk * 128]),
             reads=[pb], writes=[dst])

    def phaseS(l, d):
        with contextlib.ExitStack() as es:
            kkc = sbt(es, "kkc", [128, D], F32)
            kac = sbt(es, "kac", [128, D], F32)
            rkc = sbt(es, "rkc", [128, D], F32)
            wup = sbt(es, "wup", [65, D], BF16)
            aup = sbt(es, "aup", [65, D], BF16)
            maskM = sbt(es, "maskM", [128, 4, 128], F32)
            maskN = sbt(es, "maskN", [128, 4, 128], F32)
            H = sbt(es, "H", [128, 8, 64], F32)
            Hb = sbt(es, "Hb", [128, 8, 64], BF16)
            zr = [sbt(es, "zr%d" % i, [128, CR], BF16) for i in range(2)]
            ldT = sbt(es, "ldT", [65, 2, 128], BF16)
            tw = sbt(es, "tw", [128, 128], BF16)
            SG = sbt(es, "SG", [128, D], F32)
            A = sbt(es, "A", [128, D], F32)
            KX = sbt(es, "KX", [128, D], F32)
            BP = sbt(es, "BP", [128, D], F32)
            KD = sbt(es, "KD", [128, D], F32)
            S0 = sbt(es, "S0", [128, D], F32)
            S1 = sbt(es, "S1", [128, D], F32)
            st16 = sbt(es, "st16", [128, 16], F32)
            rs16 = sbt(es, "rs16", [128, 16], F32)
            bon = sbt(es, "bon", [128, 16], F32)
            gC = sbt(es, "gC", [128, 8], F32)
            TM = sbt(es, "TM", [128, 4, D], BF16)
            Bg = sbt(es, "Bg", [128, D], BF16)
            Kg = sbt(es, "Kg", [128, D], BF16)
            FM = sbt(es, "FM", [128, 8, 4, 128], BF16)
            MB = sbt(es, "MB", [128, 16, 512], BF16)
            Pm = [sbt(es, "Pm%d" % i, [128, 16, 128], BF16) for i in range(2)]
            PT = [sbt(es, "PT%d" % i, [128, 16, 128], BF16) for i in range(2)]
            Xb = [sbt(es, "Xb%d" % i, [128, D], BF16) for i in range(2)]
            ysc = sbt(es, "ysc", [128, D], F32)

            bc_load(kkc, kk_d[l])
            bc_load(kac, ka_d[l])
            bc_load(rkc, rk_d[l])
            k.dma("pool", wup[:], wup_d[l, d], writes=[wup], max_dma_last_dim=4096)
            k.dma("pool", aup[:], aup_d[l, d], writes=[aup], max_dma_last_dim=4096)
            strict_i, incl_i, nmask_i = (2, 0, 3) if d == 0 else (3, 1, 2)
            for j, src in enumerate((strict_i, incl_i, strict_i, incl_i)):
                k.op("dve", lambda e: e.tensor_copy(out=maskM[:, j, :], in_=tri4[:, src, :]), reads=[tri4], writes=[maskM])
            for j in range(4):
                k.op("dve", lambda e: e.tensor_copy(out=maskN[:, j, :], in_=tri4[:, nmask_i, :]), reads=[tri4], writes=[maskN])
            tri_incl = tri4[:, incl_i, :]
            tri_excl = tri4[:, strict_i, :]
            tri_dg = tri4[:, nmask_i, :]
            k.op("dve", lambda e: e.memset(ldT[:], 1.0), writes=[ldT])
            k.dma("sp", H[:].rearrange("p a b -> p (a b)"), state0_d[l, d], writes=[H])
            k.op("act", lambda e: e.copy(out=Hb[:], in_=H[:]), reads=[H], writes=[Hb])

            order = list(range(NT)) if d == 0 else list(range(NT - 1, -1, -1))

            def load_zr(n):
                i = order[n]
                k.dma("sp", zr[n % 2][:], zr_d[cs(i * 128, 128), :], reads=[R_zr(i)], writes=[zr[n % 2]])

            load_zr(0)
            for n, i in enumerate(order):
                z = zr[n % 2]
                if n + 1 < NT:
                    load_zr(n + 1)
                rq = z[:, 0:D]
                kq = z[:, D:2 * D]
                vq = z[:, 2 * D:3 * D]
                k.op("act", lambda e: e.activation(out=tw[:, 0:64], in_=z[:, cs(3 * D + 64 * d, 64)], func=AF.Tanh),
                     reads=[z], writes=[tw])
                k.op("dve", lambda e: e.tensor_copy(out=tw[:, 64:128], in_=z[:, cs(3 * D + 128 + 64 * d, 64)]),
                     reads=[z], writes=[tw])
                pb = bank()
                pv = pb[:].bitcast(BF16)

                def f(e):
                    e.transpose(out=pv[0:64, 0:128], in_=tw[:, 0:64], identity=ident_b[:])
                    return e.transpose(out=pv[0:64, 128:256], in_=tw[:, 64:128], identity=ident_b[:])
                k.op("pe", f, reads=[tw, ident_b], writes=[pb])
                k.op("act", lambda e: e.copy(out=ldT[0:64, :, :].rearrange("p a b -> p (a b)"), in_=pv[0:64, 0:256]),
                     reads=[pb], writes=[ldT])
                for (wmat, src_j, dst) in ((wup, 0, SG), (aup, 1, A)):
                    pp, b0, b1 = pair()

                    def f(e):
                        e.matmul(pp[:, 0:512], lhsT=ldT[:, src_j, :], rhs=wmat[:, 0:512], start=True, stop=True)
                        return e.matmul(pp[:, 512:1024], lhsT=ldT[:, src_j, :], rhs=wmat[:, 512:1024], start=True, stop=True)
                    k.op("pe", f, reads=[ldT, wmat], writes=[b0, b1])
                    k.op("act", lambda e: e.activation(out=dst[:], in_=pp[:, :], func=AF.Sigmoid), reads=[b0, b1], writes=[dst])
                if DBG < 11:
                    continue
                k.op("dve", lambda e: e.tensor_tensor(out=KX[:], in0=kq, in1=kkc[:], op=ALU.mult), reads=[z, kkc], writes=[KX])
                k.op("dve", lambda e: e.tensor_tensor(out=S0[:], in0=KX[:], in1=KX[:], op=ALU.mult), reads=[KX], writes=[S0])
                k.op("dve", lambda e: e.tensor_reduce(out=st16[:], in_=S0[:].rearrange("p (h n) -> p h n", n=64), axis=AX.X,
                                                      op=ALU.add), reads=[S0], writes=[st16])
                rstd_from(st16, rs16, 1.0, 1e-12)
                k.op("dve", lambda e: e.tensor_tensor(out=KX[:].rearrange("p (h n) -> p h n", n=64),
                                                      in0=KX[:].rearrange("p (h n) -> p h n", n=64),
                                                      in1=rs16[:].unsqueeze(2).to_broadcast([128, 16, 64]), op=ALU.mult),
                     reads=[KX, rs16], writes=[KX])
                k.op("dve", lambda e: e.scalar_tensor_tensor(out=BP[:], in0=KX[:], scalar=-1.0, in1=A[:], op0=ALU.mult,
                                                             op1=ALU.mult), reads=[KX, A], writes=[BP])
                k.op("dve", lambda e: e.tensor_tensor(out=S0[:], in0=kq, in1=kac[:], op=ALU.mult), reads=[z, kac], writes=[S0])
                k.op("dve", lambda e: e.scalar_tensor_tensor(out=S0[:], in0=A[:], scalar=-1.0, in1=S0[:], op0=ALU.add,
                                                             op1=ALU.mult), reads=[A, S0], writes=[S0])
                k.op("dve", lambda e: e.tensor_tensor(out=KD[:], in0=S0[:], in1=kq, op=ALU.add), reads=[S0, z], writes=[KD])
                k.op("dve", lambda e: e.tensor_tensor(out=S0[:], in0=KD[:], in1=rkc[:], op=ALU.mult), reads=[KD, rkc], writes=[S0])
                k.op("dve", lambda e: e.tensor_tensor(out=S0[:], in0=S0[:], in1=rq, op=ALU.mult), reads=[S0, z], writes=[S0])
                k.op("dve", lambda e: e.tensor_reduce(out=bon[:], in_=S0[:].rearrange("p (h n) -> p h n", n=64), axis=AX.X,
                                                      op=ALU.add), reads=[S0], writes=[bon])
                k.dma("sp", bon_d[d, cs(i * 128, 128), :], bon[:], reads=[bon], writes=[R_bon(d, i)], sembuf=bon)
                if DBG < 12:
                    continue
                def cum(tri_ap, scale, dstE):
                    pp, b0, b1 = pair()

                    def f(e):
                        e.matmul(pp[:, 0:512], lhsT=tri_ap, rhs=SG[:, 0:512], start=True, stop=True)
                        return e.matmul(pp[:, 512:1024], lhsT=tri_ap, rhs=SG[:, 512:1024], start=True, stop=True)
                    k.op("pe", f, reads=[tri4, SG], writes=[b0, b1])
                    k.op("act", lambda e: e.activation(out=dstE[:], in_=pp[:, :], func=AF.Exp, scale=scale),
                         reads=[b0, b1], writes=[dstE])
                    return pp, b0, b1
                pp, b0, b1 = cum(tri_incl, -DSC, S0)
                k.op("dve", lambda e: e.tensor_tensor(out=TM[:, 1, :], in0=rq, in1=S0[:], op=ALU.mult), reads=[z, S0], writes=[TM])
                k.op("act", lambda e: e.activation(out=S1[:], in_=pp[:, :], func=AF.Exp, scale=DSC), reads=[b0, b1], writes=[S1])
                k.op("dve", lambda e: e.tensor_tensor(out=TM[:, 2, :], in0=BP[:], in1=S1[:], op=ALU.mult), reads=[BP, S1], writes=[TM])
                k.op("dve", lambda e: e.tensor_tensor(out=TM[:, 3, :], in0=KD[:], in1=S1[:], op=ALU.mult), reads=[KD, S1], writes=[TM])
                cum(tri_excl, -DSC, S0)
                k.op("dve", lambda e: e.tensor_tensor(out=TM[:, 0, :], in0=KX[:], in1=S0[:], op=ALU.mult), reads=[KX, S0], writes=[TM])
                cum(tri_dg, -DSC, S1)
                k.op("dve", lambda e: e.tensor_tensor(out=Bg[:], in0=BP[:], in1=S1[:], op=ALU.mult), reads=[BP, S1], writes=[Bg])
                k.op("dve", lambda e: e.tensor_tensor(out=Kg[:], in0=KD[:], in1=S1[:], op=ALU.mult), reads=[KD, S1], writes=[Kg])
                pbg = bank()

                def f(e):
                    for ct in range(8):
                        ins = e.matmul(pbg[:, ct:ct + 1], lhsT=SG[:, cs(ct * 128, 128)], rhs=ones_f[:], start=True, stop=True)
                    return ins
                k.op("pe", f, reads=[SG, ones_f], writes=[pbg])
                k.op("act", lambda e: e.activation(out=gC[:], in_=pbg[:, 0:8], func=AF.Exp, scale=-DSC), reads=[pbg], writes=[gC])
                if DBG < 13:
                    continue
                for g4 in range(4):
                    pb = bank()
                    pv = pb[:].bitcast(BF16)

                    def f(e):
                        for c2 in range(2):
                            ct = g4 * 2 + c2
                            for q in range(4):
                                ins = e.transpose(out=pv[:, cs((c2 * 4 + q) * 128, 128)], in_=TM[:, q, cs(ct * 128, 128)],
                                                  identity=ident_b[:])
                        return ins
                    k.op("pe", f, reads=[TM, ident_b], writes=[pb])
                    eng = "act" if g4 % 2 == 0 else "dve"
                    if eng == "act":
                        k.op("act", lambda e: e.copy(out=FM[:, g4 * 2:g4 * 2 + 2, :, :].rearrange("p a b c -> p (a b c)"),
                                                     in_=pv[:, :]), reads=[pb], writes=[FM])
                    else:
                        k.op("dve", lambda e: e.tensor_copy(out=FM[:, g4 * 2:g4 * 2 + 2, :, :].rearrange("p a b c -> p (a b c)"),
                                                            in_=pv[:, :]), reads=[pb], writes=[FM])
                if DBG < 14:
                    continue
                P0, PT0 = Pm[0], PT[0]
                for h in range(NHEAD):
                    ct, p0 = h // 2, (h % 2) * 64
                    if 'o' in KSKIP and h % 2 == 1:
                        continue
                    pb = bank()

                    def f(e):
                        e.matmul(pb[:, 0:256], lhsT=FM[p0:p0 + 64, ct, 2, :],
                                 rhs=FM[p0:p0 + 64, ct, 0:2, :].rearrange("p a b -> p (a b)"), start=True, stop=True)
                        return e.matmul(pb[:, 256:512], lhsT=FM[p0:p0 + 64, ct, 3, :],
                                        rhs=FM[p0:p0 + 64, ct, 0:2, :].rearrange("p a b -> p (a b)"), start=True, stop=True)
                    k.op("pe", f, reads=[FM], writes=[pb])
                    k.op("dve", lambda e: e.tensor_tensor(out=MB[:, h, :], in0=pb[:], in1=maskM[:].rearrange("p a b -> p (a b)"),
                                                          op=ALU.mult), reads=[pb, maskM], writes=[MB])
                    k.op("act", lambda e: e.copy(out=PT0[:, h, :], in_=MB[:, h, 0:128]), reads=[MB], writes=[PT0])
                for g in range(4):
                    if 'n' in KSKIP:
                        continue
                    pb = bank()

                    def f(e):
                        for hh in range(4):
                            h = (g % 2) + 2 * (4 * (g // 2) + hh)
                            ct, p0 = h // 2, (h % 2) * 64
                            ins = e.matmul(pb[:, cs(hh * 128, 128)], lhsT=FM[p0:p0 + 64, ct, 0, :], rhs=FM[p0:p0 + 64, ct, 2, :],
                                           start=True, stop=True)
                        return ins
                    k.op("pe", f, reads=[FM], writes=[pb])
                    h0 = (g % 2) + 8 * (g // 2)
                    k.op("dve", lambda e: e.tensor_tensor(out=P0[:, h0:h0 + 7:2, :],
                                                          in0=pb[:].rearrange("p (a b) -> p a b", b=128), in1=maskN[:], op=ALU.mult),
                         reads=[pb, maskN], writes=[P0])
                if DBG < 15:
                    continue
                pp, b0, b1 = pair()

                def f(e):
                    for h in range(NHEAD):
                        ct, p0 = h // 2, (h % 2) * 64
                        e.matmul(pp[:, hc(h)], lhsT=FM[p0:p0 + 64, ct, 0, :], rhs=Hb[p0:p0 + 64, ct, :],
                                 start=True, stop=False)
                        ins = e.matmul(pp[:, hc(h)], lhsT=MB[:, h, 256:384], rhs=z[:, cs(2 * D + h * 64, 64)],
                                       start=False, stop=True)
                    return ins
                k.op("pe", f, reads=[FM, Hb, MB, z], writes=[b0, b1])
                xc = Xb[0]
                k.op("act", lambda e: e.copy(out=xc[:], in_=pp[:, :]), reads=[b0, b1], writes=[xc])
                if DBG < 16:
                    continue
                cur = 0
                for lev in range(7):
                    Pc, PTc = Pm[cur], PT[cur]
                    pp, b0, b1 = pair()
                    xc, xn = Xb[lev % 2], Xb[(lev + 1) % 2]

                    def f(e):
                        for h in range(NHEAD):
                            ins = e.matmul(pp[:, hc(h)], lhsT=PTc[:, h, :], rhs=xc[:, hc(h)], start=True, stop=True)
                        return ins
                    k.op("pe", f, reads=[PTc, xc], writes=[b0, b1])
                    k.op("dve", lambda e: e.tensor_tensor(out=xn[:], in0=pp[:, :], in1=xc[:], op=ALU.add),
                         reads=[b0, b1, xc], writes=[xn])
                    if lev < 6:
                        Pn, PTn = Pm[1 - cur], PT[1 - cur]
                        for g in range(4):
                            for which in range(2):
                                pb = bank()

                                def f(e):
                                    for hh in range(4):
                                        h = g * 4 + hh
                                        if which == 0:
                                            ins = e.matmul(pb[:, cs(hh * 128, 128)], lhsT=PTc[:, h, :], rhs=Pc[:, h, :], start=True, stop=True)
                                        else:
                                            ins = e.matmul(pb[:, cs(hh * 128, 128)], lhsT=Pc[:, h, :], rhs=PTc[:, h, :], start=True, stop=True)
                                    return ins
                                k.op("pe", f, reads=[Pc, PTc], writes=[pb])
                                dstb = Pn if which == 0 else PTn
                                if which == 0:
                                    k.op("act", lambda e: e.copy(out=dstb[:, g * 4:g * 4 + 4, :].rearrange("p a b -> p (a b)"),
                                                                 in_=pb[:]), reads=[pb], writes=[dstb])
                                else:
                                    k.op("dve", lambda e: e.tensor_copy(out=dstb[:, g * 4:g * 4 + 4, :].rearrange("p a b -> p (a b)"),
                                                                        in_=pb[:]), reads=[pb], writes=[dstb])
                        cur = 1 - cur
                U = Xb[1]
                if DBG < 17:
                    continue
                pp, b0, b1 = pair()

                def f(e):
                    for h in range(NHEAD):
                        ct, p0 = h // 2, (h % 2) * 64
                        e.matmul(pp[:, hc(h)], lhsT=FM[p0:p0 + 64, ct, 1, :], rhs=Hb[p0:p0 + 64, ct, :], start=True, stop=False)
                        e.matmul(pp[:, hc(h)], lhsT=MB[:, h, 128:256], rhs=U[:, hc(h)], start=False, stop=False)
                        ins = e.matmul(pp[:, hc(h)], lhsT=MB[:, h, 384:512], rhs=z[:, cs(2 * D + h * 64, 64)],
                                       start=False, stop=True)
                    return ins
                k.op("pe", f, reads=[FM, Hb, MB, U, z], writes=[b0, b1])
                k.op("act", lambda e: e.copy(out=ysc[:].rearrange("p (hp h2 i) -> p h2 hp i", h2=2, i=64),
                                             in_=pp[:, :].rearrange("p (h2 hp i) -> p h2 hp i", hp=8, i=64)), reads=[b0, b1], writes=[ysc])
                k.dma("sp", ysc_d[d, cs(i * 128, 128), :], ysc[:], reads=[ysc], writes=[R_ysc(d, i)], sembuf=ysc)
                if DBG < 18:
                    continue
                pp, b0, b1 = pair()

                def f(e):
                    for ct in range(8):
                        e.matmul(pp[:, cs(ct * 128, 128)], lhsT=Bg[:, cs(ct * 128, 128)],
                                 rhs=U[:].rearrange("p (a b c) -> p a b c", a=2, b=8)[:, :, ct, :], start=True, stop=False)
                        ins = e.matmul(pp[:, cs(ct * 128, 128)], lhsT=Kg[:, cs(ct * 128, 128)], rhs=z[:, cs(2 * D + ct * 128, 128)],
                                       start=False, stop=True)
                    return ins
                k.op("pe", f, reads=[Bg, Kg, U, z], writes=[b0, b1])
                k.op("dve", lambda e: e.tensor_tensor(out=H[:], in0=H[:], in1=gC[:].unsqueeze(2).to_broadcast([128, 8, 64]),
                                                      op=ALU.mult), reads=[H, gC], writes=[H])
                ppv = pp[:, :].rearrange("p (a b) -> p a b", b=128)
                k.op("dve", lambda e: e.tensor_tensor(out=H[0:64, :, :], in0=H[0:64, :, :], in1=ppv[0:64, :, 0:64], op=ALU.add),
                     reads=[H, b0, b1], writes=[H])
                k.op("dve", lambda e: e.tensor_tensor(out=H[64:128, :, :], in0=H[64:128, :, :], in1=ppv[64:128, :, 64:128], op=ALU.add),
                     reads=[H, b0, b1], writes=[H])
                if n % 2 == 1:
                    grp = i // 2
                    k.dma("sp", st_d[l, d, grp], H[:].rearrange("p a b -> p (a b)"), reads=[H], writes=[R_st(l, d, grp)], sembuf=H)
                    k.op("dve", lambda e: e.tensor_scalar(out=H[:], in0=H[:], scalar1=cmask[:, 0:1], scalar2=None, op0=ALU.mult),
                         reads=[H, cmask], writes=[H])
                k.op("act", lambda e: e.copy(out=Hb[:], in_=H[:]), reads=[H], writes=[Hb])
            k.barrier()


    def phaseS2(l):
        with contextlib.ExitStack() as es:
            kkc = sbt(es, "kkc", [128, D], F32)
            kac = sbt(es, "kac", [128, D], F32)
            rkc = sbt(es, "rkc", [128, D], F32)
            bc_load(kkc, kk_d[l])
            bc_load(kac, ka_d[l])
            bc_load(rkc, rk_d[l])

            def dir_gen(d):
                sfx = "_%d" % d
                wup = sbt(es, "wup" + sfx, [65, D], BF16)
                aup = sbt(es, "aup" + sfx, [65, D], BF16)
                maskM = sbt(es, "maskM" + sfx, [128, 4, 128], BF16)
                maskN = sbt(es, "maskN" + sfx, [128, 4, 128], BF16)
                H = sbt(es, "H" + sfx, [128, 8, 64], F32)
                Hb = sbt(es, "Hb" + sfx, [128, 8, 64], BF16)
                z = sbt(es, "zr" + sfx, [128, CR], BF16)
                ldT = sbt(es, "ldT" + sfx, [65, 2, 128], BF16)
                tw = sbt(es, "tw" + sfx, [128, 128], BF16)
                SG = sbt(es, "SG" + sfx, [128, D], F32)
                A = sbt(es, "A" + sfx, [128, D], F32)
                KX = sbt(es, "KX" + sfx, [128, D], F32)
                BP = sbt(es, "BP" + sfx, [128, D], F32)
                KD = sbt(es, "KD" + sfx, [128, D], F32)
                S0 = sbt(es, "S0" + sfx, [128, D], F32)
                S1 = A
                ysc = S0
                st16 = sbt(es, "st16" + sfx, [128, 16], F32)
                rs16 = sbt(es, "rs16" + sfx, [128, 16], F32)
                bon = sbt(es, "bon" + sfx, [128, 16], F32)
                gC = sbt(es, "gC" + sfx, [128, 8], F32)
                TM = sbt(es, "TM" + sfx, [128, 4, D], BF16)
                FM = sbt(es, "FM" + sfx, [128, 8, 4, 128], BF16)
                MB = sbt(es, "MB" + sfx, [128, 16, 512], BF16)
                Pm = [sbt(es, "Pm%d" % i + sfx, [128, 16, 128], BF16) for i in range(2)]
                PT = [sbt(es, "PT%d" % i + sfx, [128, 16, 128], BF16) for i in range(2)]
                Xb = [sbt(es, "Xb%d" % i + sfx, [128, D], BF16) for i in range(2)]

                k.dma("pool", wup[:], wup_d[l, d], writes=[wup], max_dma_last_dim=4096)
                k.dma("pool", aup[:], aup_d[l, d], writes=[aup], max_dma_last_dim=4096)
                strict_i, incl_i, nmask_i = (2, 0, 3) if d == 0 else (3, 1, 2)
                for j, src in enumerate((strict_i, incl_i, strict_i, incl_i)):
                    k.op("dve", lambda e: e.tensor_copy(out=maskM[:, j, :], in_=tri4[:, src, :]), reads=[tri4], writes=[maskM])
                for j in range(4):
                    k.op("dve", lambda e: e.tensor_copy(out=maskN[:, j, :], in_=tri4[:, nmask_i, :]), reads=[tri4], writes=[maskN])
                tri_incl = tri4[:, incl_i, :]
                tri_excl = tri4[:, strict_i, :]
                tri_dg = tri4[:, nmask_i, :]
                k.op("dve", lambda e: e.memset(ldT[:], 1.0), writes=[ldT])
                k.dma("sp", H[:].rearrange("p a b -> p (a b)"), state0_d[l, d], writes=[H])
                k.op("act", lambda e: e.copy(out=Hb[:], in_=H[:]), reads=[H], writes=[Hb])
                order = list(range(NT)) if d == 0 else list(range(NT - 1, -1, -1))
                yield

                for n, i in enumerate(order):
                    k.dma("sp", z[:], zr_d[cs(i * 128, 128), :], reads=[R_zr(i)], writes=[z])
                    rq = z[:, 0:D]
                    kq = z[:, D:2 * D]
                    k.op("act", lambda e: e.activation(out=tw[:, 0:64], in_=z[:, cs(3 * D + 64 * d, 64)], func=AF.Tanh),
                         reads=[z], writes=[tw])
                    k.op("pool", lambda e: e.tensor_copy(out=tw[:, 64:128], in_=z[:, cs(3 * D + 128 + 64 * d, 64)]),
                         reads=[z], writes=[tw])
                    pb = bank()
                    pv = pb[:].bitcast(BF16)

                    def f(e):
                        e.transpose(out=pv[0:64, 0:128], in_=tw[:, 0:64], identity=ident_b[:])
                        return e.transpose(out=pv[0:64, 128:256], in_=tw[:, 64:128], identity=ident_b[:])
                    k.op("pe", f, reads=[tw, ident_b], writes=[pb])
                    k.op("act", lambda e: e.copy(out=ldT[0:64, :, :].rearrange("p a b -> p (a b)"), in_=pv[0:64, 0:256]),
                         reads=[pb], writes=[ldT])
                    for (wmat, src_j, dst) in ((wup, 0, SG), (aup, 1, A)):
                        pp, b0, b1 = pair()

                        def f(e):
                            e.matmul(pp[:, 0:512], lhsT=ldT[:, src_j, :], rhs=wmat[:, 0:512], start=True, stop=True)
                            return e.matmul(pp[:, 512:1024], lhsT=ldT[:, src_j, :], rhs=wmat[:, 512:1024], start=True, stop=True)
                        k.op("pe", f, reads=[ldT, wmat], writes=[b0, b1])
                        k.op("act", lambda e: e.activation(out=dst[:], in_=pp[:, :], func=AF.Sigmoid), reads=[b0, b1], writes=[dst])
                    k.op("pool", lambda e: e.tensor_tensor(out=KX[:], in0=kq, in1=kkc[:], op=ALU.mult), reads=[z, kkc], writes=[KX])
                    k.op("pool", lambda e: e.tensor_tensor(out=S0[:], in0=KX[:], in1=KX[:], op=ALU.mult), reads=[KX], writes=[S0])
                    k.op("dve", lambda e: e.tensor_reduce(out=st16[:], in_=S0[:].rearrange("p (h n) -> p h n", n=64), axis=AX.X,
                                                          op=ALU.add), reads=[S0], writes=[st16])
                    rstd_from(st16, rs16, 1.0, 1e-12)
                    k.op("dve", lambda e: e.tensor_tensor(out=KX[:].rearrange("p (h n) -> p h n", n=64),
                                                          in0=KX[:].rearrange("p (h n) -> p h n", n=64),
                                                          in1=rs16[:].unsqueeze(2).to_broadcast([128, 16, 64]), op=ALU.mult),
                         reads=[KX, rs16], writes=[KX])
                    yield
                    k.op("dve", lambda e: e.scalar_tensor_tensor(out=BP[:], in0=KX[:], scalar=-1.0, in1=A[:], op0=ALU.mult,
                                                                 op1=ALU.mult), reads=[KX, A], writes=[BP])
                    k.op("pool", lambda e: e.tensor_tensor(out=S0[:], in0=kq, in1=kac[:], op=ALU.mult), reads=[z, kac], writes=[S0])
                    k.op("dve", lambda e: e.scalar_tensor_tensor(out=S0[:], in0=A[:], scalar=-1.0, in1=S0[:], op0=ALU.add,
                                                                 op1=ALU.mult), reads=[A, S0], writes=[S0])
                    k.op("pool", lambda e: e.tensor_tensor(out=KD[:], in0=S0[:], in1=kq, op=ALU.add), reads=[S0, z], writes=[KD])
                    k.op("pool", lambda e: e.tensor_tensor(out=S0[:], in0=KD[:], in1=rkc[:], op=ALU.mult), reads=[KD, rkc], writes=[S0])
                    k.op("pool", lambda e: e.tensor_tensor(out=S0[:], in0=S0[:], in1=rq, op=ALU.mult), reads=[S0, z], writes=[S0])
                    k.op("dve", lambda e: e.tensor_reduce(out=bon[:], in_=S0[:].rearrange("p (h n) -> p h n", n=64), axis=AX.X,
                                                          op=ALU.add), reads=[S0], writes=[bon])
                    k.dma("sp", bon_d[d, cs(i * 128, 128), :], bon[:], reads=[bon], writes=[R_bon(d, i)], sembuf=bon)
                    yield

                    def cum(tri_ap, scale, dstE):
                        pp, b0, b1 = pair()

                        def f(e):
                            e.matmul(pp[:, 0:512], lhsT=tri_ap, rhs=SG[:, 0:512], start=True, stop=True)
                            return e.matmul(pp[:, 512:1024], lhsT=tri_ap, rhs=SG[:, 512:1024], start=True, stop=True)
                        k.op("pe", f, reads=[tri4, SG], writes=[b0, b1])
                        k.op("act", lambda e: e.activation(out=dstE[:], in_=pp[:, :], func=AF.Exp, scale=scale),
                             reads=[b0, b1], writes=[dstE])
                        return pp, b0, b1
                    pp, b0, b1 = cum(tri_incl, -DSC, S0)
                    k.op("dve", lambda e: e.tensor_tensor(out=TM[:, 1, :], in0=rq, in1=S0[:], op=ALU.mult), reads=[z, S0], writes=[TM])
                    k.op("act", lambda e: e.activation(out=S1[:], in_=pp[:, :], func=AF.Exp, scale=DSC), reads=[b0, b1], writes=[S1])
                    k.op("dve", lambda e: e.tensor_tensor(out=TM[:, 2, :], in0=BP[:], in1=S1[:], op=ALU.mult), reads=[BP, S1], writes=[TM])
                    k.op("pool", lambda e: e.tensor_tensor(out=TM[:, 3, :], in0=KD[:], in1=S1[:], op=ALU.mult), reads=[KD, S1], writes=[TM])
                    cum(tri_excl, -DSC, S0)
                    k.op("dve", lambda e: e.tensor_tensor(out=TM[:, 0, :], in0=KX[:], in1=S0[:], op=ALU.mult), reads=[KX, S0], writes=[TM])
                    pbg = bank()

                    def f(e):
                        for ct in range(8):
                            ins = e.matmul(pbg[:, ct:ct + 1], lhsT=SG[:, cs(ct * 128, 128)], rhs=ones_f[:], start=True, stop=True)
                        return ins
                    k.op("pe", f, reads=[SG, ones_f], writes=[pbg])
                    k.op("act", lambda e: e.activation(out=gC[:], in_=pbg[:, 0:8], func=AF.Exp, scale=-DSC), reads=[pbg], writes=[gC])
                    yield
                    for g4 in range(4):
                        pb = bank()
                        pv = pb[:].bitcast(BF16)

                        def f(e):
                            for c2 in range(2):
                                ct = g4 * 2 + c2
                                for q in range(4):
                                    ins = e.transpose(out=pv[:, cs((c2 * 4 + q) * 128, 128)], in_=TM[:, q, cs(ct * 128, 128)],
                                                      identity=ident_b[:])
                            return ins
                        k.op("pe", f, reads=[TM, ident_b], writes=[pb])
                        if g4 % 2 == 0:
                            k.op("act", lambda e: e.copy(out=FM[:, g4 * 2:g4 * 2 + 2, :, :].rearrange("p a b c -> p (a b c)"),
                                                         in_=pv[:, :]), reads=[pb], writes=[FM])
                        else:
                            k.op("dve", lambda e: e.tensor_copy(out=FM[:, g4 * 2:g4 * 2 + 2, :, :].rearrange("p a b c -> p (a b c)"),
                                                                in_=pv[:, :]), reads=[pb], writes=[FM])
                    cum(tri_dg, -DSC, S1)
                    k.op("dve", lambda e: e.tensor_tensor(out=TM[:, 2, :], in0=BP[:], in1=S1[:], op=ALU.mult), reads=[BP, S1], writes=[TM])
                    k.op("pool", lambda e: e.tensor_tensor(out=TM[:, 3, :], in0=KD[:], in1=S1[:], op=ALU.mult), reads=[KD, S1], writes=[TM])
                    yield
                    P0 = Pm[0]
                    for h in range(NHEAD):
                        ct, p0 = h // 2, (h % 2) * 64
                        pb = bank()

                        def f(e):
                            e.matmul(pb[:, 0:256], lhsT=FM[p0:p0 + 64, ct, 2, :],
                                     rhs=FM[p0:p0 + 64, ct, 0:2, :].rearrange("p a b -> p (a b)"), start=True, stop=True)
                            return e.matmul(pb[:, 256:512], lhsT=FM[p0:p0 + 64, ct, 3, :],
                                            rhs=FM[p0:p0 + 64, ct, 0:2, :].rearrange("p a b -> p (a b)"), start=True, stop=True)
                        k.op("pe", f, reads=[FM], writes=[pb])
                        k.op("dve", lambda e: e.tensor_tensor(out=MB[:, h, :], in0=pb[:], in1=maskM[:].rearrange("p a b -> p (a b)"),
                                                              op=ALU.mult), reads=[pb, maskM], writes=[MB])
                        if h % 4 == 3:
                            yield
                    for g in range(4):
                        pb = bank()

                        def f(e):
                            for hh in range(4):
                                h = (g % 2) + 2 * (4 * (g // 2) + hh)
                                ct, p0 = h // 2, (h % 2) * 64
                                ins = e.matmul(pb[:, cs(hh * 128, 128)], lhsT=FM[p0:p0 + 64, ct, 0, :], rhs=FM[p0:p0 + 64, ct, 2, :],
                                               start=True, stop=True)
                            return ins
                        k.op("pe", f, reads=[FM], writes=[pb])
                        h0 = (g % 2) + 8 * (g // 2)
                        k.op("dve", lambda e: e.tensor_tensor(out=P0[:, h0:h0 + 7:2, :],
                                                              in0=pb[:].rearrange("p (a b) -> p a b", b=128), in1=maskN[:], op=ALU.mult),
                             reads=[pb, maskN], writes=[P0])
                    yield
                    pp, b0, b1 = pair()

                    def f(e):
                        for h in range(NHEAD):
                            ct, p0 = h // 2, (h % 2) * 64
                            e.matmul(pp[:, hc(h)], lhsT=FM[p0:p0 + 64, ct, 0, :], rhs=Hb[p0:p0 + 64, ct, :],
                                     start=True, stop=False)
                            ins = e.matmul(pp[:, hc(h)], lhsT=MB[:, h, 256:384], rhs=z[:, cs(2 * D + h * 64, 64)],
                                           start=False, stop=True)
                        return ins
                    k.op("pe", f, reads=[FM, Hb, MB, z], writes=[b0, b1])
                    xc = Xb[0]
                    k.op("act", lambda e: e.copy(out=xc[:], in_=pp[:, :]), reads=[b0, b1], writes=[xc])
                    yield
                    cur = 0
                    for lev in range(7):
                        Pc = Pm[cur]
                        pp, b0, b1 = pair()
                        xc, xn = Xb[lev % 2], Xb[(lev + 1) % 2]
                        if lev == 0:
                            ptv = lambda h: MB[:, h, 0:128]
                            ptb = MB
                        else:
                            ptv = lambda h, PTc=PT[cur]: PTc[:, h, :]
                            ptb = PT[cur]

                        def f(e):
                            e.matmul(pp[:, 0:512], lhsT=ident_b[:], rhs=xc[:, 0:512], start=True, stop=False)
                            e.matmul(pp[:, 512:1024], lhsT=ident_b[:], rhs=xc[:, 512:1024], start=True, stop=False)
                            for h in range(NHEAD):
                                ins = e.matmul(pp[:, hc(h)], lhsT=ptv(h), rhs=xc[:, hc(h)], start=False, stop=(h >= NHEAD - 2))
                            return ins
                        k.op("pe", f, reads=[ptb, xc, ident_b], writes=[b0, b1])
                        k.op("act", lambda e: e.copy(out=xn[:], in_=pp[:, :]), reads=[b0, b1], writes=[xn])
                        if lev < 6:
                            Pn, PTn = Pm[1 - cur], PT[1 - cur]
                            for g in range(4):
                                for which in range(2):
                                    pb = bank()

                                    def f(e):
                                        for hh in range(4):
                                            h = g * 4 + hh
                                            if which == 0:
                                                ins = e.matmul(pb[:, cs(hh * 128, 128)], lhsT=ptv(h), rhs=Pc[:, h, :], start=True, stop=True)
                                            else:
                                                ins = e.matmul(pb[:, cs(hh * 128, 128)], lhsT=Pc[:, h, :], rhs=ptv(h), start=True, stop=True)
                                        return ins
                                    k.op("pe", f, reads=[Pc, ptb], writes=[pb])
                                    dstb = Pn if which == 0 else PTn
                                    if which == 0 or g % 2 == 0:
                                        k.op("act", lambda e: e.copy(out=dstb[:, g * 4:g * 4 + 4, :].rearrange("p a b -> p (a b)"),
                                                                     in_=pb[:]), reads=[pb], writes=[dstb])
                                    else:
                                        k.op("dve", lambda e: e.tensor_copy(out=dstb[:, g * 4:g * 4 + 4, :].rearrange("p a b -> p (a b)"),
                                                                            in_=pb[:]), reads=[pb], writes=[dstb])
                                if g % 2 == 1:
                                    yield
                            cur = 1 - cur
                    U = Xb[1]
                    pp, b0, b1 = pair()

                    def f(e):
                        for h in range(NHEAD):
                            ct, p0 = h // 2, (h % 2) * 64
                            e.matmul(pp[:, hc(h)], lhsT=FM[p0:p0 + 64, ct, 1, :], rhs=Hb[p0:p0 + 64, ct, :], start=True, stop=False)
                            e.matmul(pp[:, hc(h)], lhsT=MB[:, h, 128:256], rhs=U[:, hc(h)], start=False, stop=False)
                            ins = e.matmul(pp[:, hc(h)], lhsT=MB[:, h, 384:512], rhs=z[:, cs(2 * D + h * 64, 64)],
                                           start=False, stop=True)
                        return ins
                    k.op("pe", f, reads=[FM, Hb, MB, U, z], writes=[b0, b1])
                    k.op("act", lambda e: e.copy(out=ysc[:].rearrange("p (hp h2 i) -> p h2 hp i", h2=2, i=64),
                                                 in_=pp[:, :].rearrange("p (h2 hp i) -> p h2 hp i", hp=8, i=64)), reads=[b0, b1], writes=[ysc])
                    k.dma("sp", ysc_d[d, cs(i * 128, 128), :], ysc[:], reads=[ysc], writes=[R_ysc(d, i)], sembuf=ysc)
                    pp, b0, b1 = pair()

                    def f(e):
                        for ct in range(8):
                            e.matmul(pp[:, cs(ct * 128, 128)], lhsT=TM[:, 2, cs(ct * 128, 128)],
                                     rhs=U[:].rearrange("p (a b c) -> p a b c", a=2, b=8)[:, :, ct, :], start=True, stop=False)
                            ins = e.matmul(pp[:, cs(ct * 128, 128)], lhsT=TM[:, 3, cs(ct * 128, 128)], rhs=z[:, cs(2 * D + ct * 128, 128)],
                                           start=False, stop=True)
                        return ins
                    k.op("pe", f, reads=[TM, U, z], writes=[b0, b1])
                    k.op("dve", lambda e: e.tensor_tensor(out=H[:], in0=H[:], in1=gC[:].unsqueeze(2).to_broadcast([128, 8, 64]),
                                                          op=ALU.mult), reads=[H, gC], writes=[H])
                    ppv = pp[:, :].rearrange("p (a b) -> p a b", b=128)
                    k.op("dve", lambda e: e.tensor_tensor(out=H[0:64, :, :], in0=H[0:64, :, :], in1=ppv[0:64, :, 0:64], op=ALU.add),
                         reads=[H, b0, b1], writes=[H])
                    k.op("dve", lambda e: e.tensor_tensor(out=H[64:128, :, :], in0=H[64:128, :, :], in1=ppv[64:128, :, 64:128], op=ALU.add),
                         reads=[H, b0, b1], writes=[H])
                    if n % 2 == 1:
                        grp = i // 2
                        k.dma("sp", st_d[l, d, grp], H[:].rearrange("p a b -> p (a b)"), reads=[H], writes=[R_st(l, d, grp)], sembuf=H)
                        k.op("dve", lambda e: e.tensor_scalar(out=H[:], in0=H[:], scalar1=cmask[:, 0:1], scalar2=None, op0=ALU.mult),
                             reads=[H, cmask], writes=[H])
                    k.op("act", lambda e: e.copy(out=Hb[:], in_=H[:]), reads=[H], writes=[Hb])
                    yield

            gens = [dir_gen(0), dir_gen(1)]
            next(gens[0])
            next(gens[1])
            for _ in range(6):
                next(gens[0])
            live = list(gens)
            while live:
                for g in list(live):
                    try:
                        next(g)
                    except StopIteration:
                        live.remove(g)
            k.barrier()

    def phaseM(l, xsrc_d, R_xsrc):
        with contextlib.ExitStack() as es:
            wa = sbt(es, "wa", [128, 8, D], BF16)
            wb = sbt(es, "wb", [128, 8, D], BF16)
            wo = sbt(es, "wo", [128, 8, D], BF16)
            gup = sbt(es, "gup", [128, D], BF16)
            wsT = sbt(es, "wsT", [128, 8, 128], BF16)
            bsT = sbt(es, "bsT", [128, 8], F32)
            lnxg = sbt(es, "lnxg", [128, D], F32)
            lnxb = sbt(es, "lnxb", [128, D], F32)
            lnvg = sbt(es, "lnvg", [128, D], F32)
            gate1 = sbt(es, "gate1", [128, D], F32)
            zv = sbt(es, "zv", [128, D + 128], BF16)
            zrs = sbt(es, "zrs", [128, 4096], BF16)
            yf = sbt(es, "yf", [128, D], F32)
            yb = sbt(es, "yb", [128, D], F32)
            b0t = sbt(es, "b0t", [128, 16], F32)
            b1t = sbt(es, "b1t", [128, 16], F32)
            xt = sbt(es, "xtm", [128, D], F32)
            W0 = sbt(es, "W0", [128, D], F32)
            W1 = sbt(es, "W1", [128, D], F32)
            W2 = sbt(es, "W2", [128, D], F32)
            s16a = sbt(es, "s16a", [128, 16], F32)
            s16b = sbt(es, "s16b", [128, 16], F32)
            s16c = sbt(es, "s16c", [128, 16], F32)
            bnst = sbt(es, "bnst", [128, 2, 6], F32)
            mv = sbt(es, "mv", [128, 2], F32)
            rsv = sbt(es, "rsv", [128, 1], F32)
            gsb = sbt(es, "gsb", [128, 128], BF16)
            gT = sbt(es, "gT", [128, 1, 128], BF16)
            actb = sbt(es, "actb", [128, D], BF16)
            actT = sbt(es, "actT", [128, 8, 128], BF16)
            ub = sbt(es, "ub", [128, D], BF16)
            vcb = sbt(es, "vcb", [128, D], BF16)
            cast_load_rows(lambda kc: wa[:, kc, :], wa_d[l], 8, D, wa)
            cast_load_rows(lambda kc: wb[:, kc, :], wb_d[l], 8, D, wb)
            cast_load_rows(lambda kc: wo[:, kc, :], wo_d[l], 8, D, wo)
            k.dma("pool", gup[:], gup_d[l], writes=[gup], max_dma_last_dim=4096)
            k.dma("pool", wsT[:].rearrange("p a b -> p (a b)"), wsT_d[l], writes=[wsT], max_dma_last_dim=4096)
            k.dma("sp", bsT[:], bsT_d[l], writes=[bsT])
            bc_load(lnxg, lnxg_d[l])
            bc_load(lnxb, lnxb_d[l])
            bc_load(lnvg, lnvg_d[l])
            load_mod(gate1, l, 2)

            def v3(b):
                return b[:].rearrange("p (h n) -> p h n", n=64)

            def bc16(b):
                return b[:].unsqueeze(2).to_broadcast([128, 16, 64])

            def proj(src_bf, wmat):
                transpose8(src_bf, actT)
                pp, p0, p1 = pair()

                def f(e):
                    for nn in range(2):
                        for kc in range(8):
                            ins = e.matmul(pp[:, cs(nn * 512, 512)], lhsT=actT[:, kc, :], rhs=wmat[:, kc, cs(nn * 512, 512)],
                                           start=(kc == 0), stop=(kc == 7))
                    return ins
                k.op("pe", f, reads=[actT, wmat], writes=[p0, p1])
                return pp, p0, p1

            for i in range(NT):
                rows = cs(i * 128, 128)
                k.dma("sp", zv[:, 0:D], zr_d[rows, 2 * D:3 * D], reads=[R_zr(i)], writes=[zv])
                k.dma("sp", zv[:, D:D + 128], zr_d[rows, cs(3 * D + 256, 128)], reads=[R_zr(i)], writes=[zv])
                k.dma("sp", zrs[:], zrest_d[rows, :], reads=[R_zrest(i, j) for j in range(8)], writes=[zrs])
                k.dma("sp", yf[:], ysc_d[0, rows, :], reads=[R_ysc(0, i)], writes=[yf])
                k.dma("sp", yb[:], ysc_d[1, rows, :], reads=[R_ysc(1, i)], writes=[yb])
                k.dma("sp", b0t[:], bon_d[0, rows, :], reads=[R_bon(0, i)], writes=[b0t])
                k.dma("sp", b1t[:], bon_d[1, rows, :], reads=[R_bon(1, i)], writes=[b1t])
                k.dma("sp", xt[:], xsrc_d[rows, :], reads=[R_xsrc(i)], writes=[xt])
                k.op("dve", lambda e: e.tensor_tensor(out=yf[:], in0=yf[:], in1=yb[:], op=ALU.add), reads=[yf, yb], writes=[yf])
                k.op("dve", lambda e: e.tensor_reduce(out=s16a[:], in_=v3(yf), axis=AX.X, op=ALU.add), reads=[yf], writes=[s16a])
                k.op("dve", lambda e: e.tensor_scalar(out=s16a[:], in0=s16a[:], scalar1=1.0 / 64, scalar2=None, op0=ALU.mult),
                     reads=[s16a], writes=[s16a])
                k.op("dve", lambda e: e.tensor_tensor(out=v3(yf), in0=v3(yf), in1=bc16(s16a), op=ALU.subtract), reads=[yf, s16a], writes=[yf])
                k.op("dve", lambda e: e.tensor_tensor(out=W0[:], in0=yf[:], in1=yf[:], op=ALU.mult), reads=[yf], writes=[W0])
                k.op("dve", lambda e: e.tensor_reduce(out=s16b[:], in_=v3(W0), axis=AX.X, op=ALU.add), reads=[W0], writes=[s16b])
                rstd_from(s16b, s16c, 1.0 / 64, GN_EPS)
                k.op("dve", lambda e: e.tensor_tensor(out=v3(yf), in0=v3(yf), in1=bc16(s16c), op=ALU.mult), reads=[yf, s16c], writes=[yf])
                k.op("dve", lambda e: e.tensor_tensor(out=yf[:], in0=yf[:], in1=lnxg[:], op=ALU.mult), reads=[yf, lnxg], writes=[yf])
                k.op("dve", lambda e: e.tensor_tensor(out=yf[:], in0=yf[:], in1=lnxb[:], op=ALU.add), reads=[yf, lnxb], writes=[yf])
                k.op("dve", lambda e: e.tensor_tensor(out=b0t[:], in0=b0t[:], in1=b1t[:], op=ALU.add), reads=[b0t, b1t], writes=[b0t])
                k.op("dve", lambda e: e.tensor_tensor(out=v3(W0), in0=zv[:, 0:D].rearrange("p (h n) -> p h n", n=64), in1=bc16(b0t),
                                                      op=ALU.mult), reads=[zv, b0t], writes=[W0])
                k.op("dve", lambda e: e.tensor_tensor(out=yf[:], in0=yf[:], in1=W0[:], op=ALU.add), reads=[yf, W0], writes=[yf])
                k.op("act", lambda e: e.activation(out=gsb[:], in_=zv[:, D:D + 128], func=AF.Sigmoid), reads=[zv], writes=[gsb])
                transpose8(gsb, gT, nblk=1)
                pp, p0, p1 = pair()

                def f(e):
                    e.matmul(pp[:, 0:512], lhsT=gT[:, 0, :], rhs=gup[:, 0:512], start=True, stop=True)
                    return e.matmul(pp[:, 512:1024], lhsT=gT[:, 0, :], rhs=gup[:, 512:1024], start=True, stop=True)
                k.op("pe", f, reads=[gT, gup], writes=[p0, p1])
                k.op("dve", lambda e: e.tensor_tensor(out=actb[:], in0=pp[:, :], in1=yf[:], op=ALU.mult), reads=[p0, p1, yf], writes=[actb])
                pp, p0, p1 = proj(actb, wa)
                k.op("act", lambda e: e.activation(out=W0[:], in_=zrs[:, 2048:3072], func=AF.Sigmoid), reads=[zrs], writes=[W0])
                k.op("dve", lambda e: e.tensor_tensor(out=W2[:], in0=pp[:, :], in1=W0[:], op=ALU.mult), reads=[p0, p1, W0], writes=[W2])
                k.op("act", lambda e: e.activation(out=ub[:], in_=zrs[:, 0:1024], func=AF.Gelu_apprx_tanh), reads=[zrs], writes=[ub])
                k.op("act", lambda e: e.activation(out=W1[:], in_=zrs[:, 1024:2048], func=AF.Gelu_apprx_tanh), reads=[zrs], writes=[W1])
                for c in range(2):
                    k.op("dve", lambda e: e.bn_stats(out=bnst[:, c, :], in_=W1[:, cs(c * 512, 512)]), reads=[W1], writes=[bnst])
                k.op("dve", lambda e: e.bn_aggr(out=mv[:], in_=bnst[:].rearrange("p a b -> p (a b)")), reads=[bnst], writes=[mv])
                rstd_from_ap(mv, 1, rsv, EPS)
                k.op("dve", lambda e: e.tensor_scalar(out=W1[:], in0=W1[:], scalar1=mv[:, 0:1], scalar2=rsv[:, 0:1], op0=ALU.subtract,
                                                      op1=ALU.mult), reads=[W1, mv, rsv], writes=[W1])
                k.op("dve", lambda e: e.tensor_tensor(out=vcb[:], in0=W1[:], in1=lnvg[:], op=ALU.mult), reads=[W1, lnvg], writes=[vcb])
                pp, p0, p1 = pair()

                def f(e):
                    for g in range(8):
                        ins = e.matmul(pp[:, cs(g * 128, 128)], lhsT=wsT[:, g, :], rhs=vcb[:, cs(g * 128, 128)], start=True, stop=True)
                    return ins
                k.op("pe", f, reads=[wsT, vcb], writes=[p0, p1])
                k.op("dve", lambda e: e.tensor_tensor(out=W1[:].rearrange("p (g c) -> p g c", c=128),
                                                      in0=pp[:, :].rearrange("p (g c) -> p g c", c=128),
                                                      in1=bsT[:].unsqueeze(2).to_broadcast([128, 8, 128]), op=ALU.add),
                     reads=[p0, p1, bsT], writes=[W1])
                k.op("dve", lambda e: e.tensor_tensor(out=actb[:], in0=W1[:], in1=ub[:], op=ALU.mult), reads=[W1, ub], writes=[actb])
                pp, p0, p1 = proj(actb, wb)
                k.op("act", lambda e: e.activation(out=W0[:], in_=zrs[:, 3072:4096], func=AF.Sigmoid), reads=[zrs], writes=[W0])
                k.op("dve", lambda e: e.tensor_tensor(out=W1[:], in0=pp[:, :], in1=W0[:], op=ALU.mult), reads=[p0, p1, W0], writes=[W1])
                k.op("dve", lambda e: e.tensor_tensor(out=actb[:], in0=W1[:], in1=W2[:], op=ALU.add), reads=[W1, W2], writes=[actb])
                pp, p0, p1 = proj(actb, wo)
                k.op("dve", lambda e: e.tensor_tensor(out=W0[:], in0=pp[:, :], in1=gate1[:], op=ALU.mult), reads=[p0, p1, gate1], writes=[W0])
                k.op("dve", lambda e: e.tensor_tensor(out=W0[:], in0=W0[:], in1=xt[:], op=ALU.add), reads=[W0, xt], writes=[W0])
                k.dma("sp", x1_d[rows, :], W0[:], reads=[W0], writes=[R_x1(i)], sembuf=W0)
            k.barrier()

    def rstd_from_ap(mvb, col, out_rstd, eps):
        k.op("act", lambda e: e.activation(out=out_rstd[:], in_=mvb[:, col:col + 1], func=AF.Ln, bias=eps_t(eps)[:], scale=1.0),
             reads=[mvb, eps_t(eps)], writes=[out_rstd])
        k.op("act", lambda e: e.activation(out=out_rstd[:], in_=out_rstd[:], func=AF.Exp, scale=-0.5),
             reads=[out_rstd], writes=[out_rstd])

    def phaseC(l, last):
        with contextlib.ExitStack() as es:
            w1 = sbt(es, "w1", [128, 8, DFF], BF16)
            w2 = sbt(es, "w2", [128, 32, D], BF16)
            g2 = sbt(es, "g2", [128, D], F32)
            sh2 = sbt(es, "sh2", [128, D], F32)
            gate2 = sbt(es, "gate2", [128, D], F32)
            fg = sbt(es, "fg", [128, D], F32)
            xt = [sbt(es, "xc%d" % i, [128, D], F32) for i in range(2)]
            W0 = sbt(es, "Wc0", [128, D], F32)
            hb = sbt(es, "hb2", [128, D], BF16)
            hT = sbt(es, "hT2", [128, 8, 128], BF16)
            rl = sbt(es, "rl", [128, 512], BF16)
            hid = sbt(es, "hid", [128, DFF], BF16)
            hidT = sbt(es, "hidT", [128, 32, 128], BF16)
            ss = sbt(es, "ssc", [128, 1], F32)
            rstd = sbt(es, "rstdc", [128, 1], F32)
            cast_load_rows(lambda kc: w1[:, kc, :], w1_d[l], 8, DFF, w1)
            cast_load_rows(lambda kc: w2[:, kc, :], w2_d[l], 32, D, w2)
            load_mod(sh2, l, 3)
            load_mod(g2, l, 4)
            load_mod(gate2, l, 5)
            if last:
                bc_load(fg, fg_d)

            def load_x(i):
                k.dma("sp", xt[i % 2][:], x1_d[cs(i * 128, 128), :], reads=[R_x1(i)], writes=[xt[i % 2]])
            load_x(0)
            for i in range(NT):
                x = xt[i % 2]
                if i + 1 < NT:
                    load_x(i + 1)
                k.op("act", lambda e: e.activation(out=W0[:], in_=x[:], func=AF.Square), reads=[x], writes=[W0])
                k.op("dve", lambda e: e.tensor_reduce(out=ss[:], in_=W0[:], axis=AX.X, op=ALU.add), reads=[W0], writes=[ss])
                rstd_from(ss, rstd, 1.0 / D, EPS)
                k.op("dve", lambda e: e.scalar_tensor_tensor(out=W0[:], in0=x[:], scalar=rstd[:, 0:1], in1=g2[:], op0=ALU.mult,
                                                             op1=ALU.mult), reads=[x, rstd, g2], writes=[W0])
                k.op("dve", lambda e: e.tensor_tensor(out=hb[:], in0=W0[:], in1=sh2[:], op=ALU.add), reads=[W0, sh2], writes=[hb])
                transpose8(hb, hT)
                for n in range(8):
                    pb = bank()

                    def f(e):
                        for kc in range(8):
                            ins = e.matmul(pb[:], lhsT=hT[:, kc, :], rhs=w1[:, kc, cs(n * 512, 512)], start=(kc == 0), stop=(kc == 7))
                        return ins
                    k.op("pe", f, reads=[hT, w1], writes=[pb])
                    k.op("act", lambda e: e.activation(out=rl[:], in_=pb[:], func=AF.Relu), reads=[pb], writes=[rl])
                    k.op("dve", lambda e: e.tensor_tensor(out=hid[:, cs(n * 512, 512)], in0=rl[:], in1=rl[:], op=ALU.mult),
                         reads=[rl], writes=[hid])
                for q in range(4):
                    pb = bank()
                    pv = pb[:].bitcast(BF16)

                    def f(e):
                        for j in range(8):
                            ins = e.transpose(out=pv[:, cs(j * 128, 128)], in_=hid[:, cs((q * 8 + j) * 128, 128)], identity=ident_b[:])
                        return ins
                    k.op("pe", f, reads=[hid, ident_b], writes=[pb])
                    if q % 2 == 0:
                        k.op("act", lambda e: e.copy(out=hidT[:, q * 8:q * 8 + 8, :].rearrange("p a b -> p (a b)"), in_=pv[:, :]),
                             reads=[pb], writes=[hidT])
                    else:
                        k.op("dve", lambda e: e.tensor_copy(out=hidT[:, q * 8:q * 8 + 8, :].rearrange("p a b -> p (a b)"), in_=pv[:, :]),
                             reads=[pb], writes=[hidT])
                pp, p0, p1 = pair()

                def f(e):
                    for nn in range(2):
                        for kc in range(32):
                            ins = e.matmul(pp[:, cs(nn * 512, 512)], lhsT=hidT[:, kc, :], rhs=w2[:, kc, cs(nn * 512, 512)],
                                           start=(kc == 0), stop=(kc == 31))
                    return ins
                k.op("pe", f, reads=[hidT, w2], writes=[p0, p1])
                k.op("dve", lambda e: e.tensor_tensor(out=W0[:], in0=pp[:, :], in1=gate2[:], op=ALU.mult), reads=[p0, p1, gate2], writes=[W0])
                k.op("dve", lambda e: e.tensor_tensor(out=W0[:], in0=W0[:], in1=x[:], op=ALU.add), reads=[W0, x], writes=[W0])
                rows = cs(i * 128, 128)
                if not last:
                    k.dma("sp", x2_d[rows, :], W0[:], reads=[W0], writes=[R_x2(i)], sembuf=W0)
                else:
                    k.op("act", lambda e: e.activation(out=x[:], in_=W0[:], func=AF.Square), reads=[W0], writes=[x])
                    k.op("dve", lambda e: e.tensor_reduce(out=ss[:], in_=x[:], axis=AX.X, op=ALU.add), reads=[x], writes=[ss])
                    rstd_from(ss, rstd, 1.0 / D, EPS)
                    k.op("dve", lambda e: e.scalar_tensor_tensor(out=W0[:], in0=W0[:], scalar=rstd[:, 0:1], in1=fg[:], op0=ALU.mult,
                                                                 op1=ALU.mult), reads=[W0, rstd, fg], writes=[W0])
                    k.dma("sp", y_d[rows, :], W0[:], reads=[W0], writes=[R_y(i)], sembuf=W0)
            k.barrier()

    R_xin = DR("xin")
    steps = [lambda: phaseP(0), lambda: phaseP(1)]
    for l in range(2):
        xs, Rx = (x_d, R_xin) if l == 0 else (x2_d, R_x2)
        steps += [lambda l=l, xs=xs, Rx=Rx: phaseA1(l, xs, Rx), lambda l=l: phaseS2(l),
                  lambda l=l, xs=xs, Rx=Rx: phaseM(l, xs, Rx), lambda l=l: phaseC(l, last=(l == 1))]
    for st_ in steps[:upto]:
        st_()
    k.barrier()
    ges.close()
    return nc, k


def _shift_mats(kind):
    m = np.zeros((4, 3, 128, 128), np.float32)
    eye = np.eye(128, dtype=np.float32)
    t = np.arange(128)
    for cls in range(4):
        cur = np.zeros((128, 128), np.float32)
        nbe = np.zeros((128, 128), np.float32)
        nbo = np.zeros((128, 128), np.float32)
        if kind == "sample":
            if cls == 0:
                for to in t:
                    if to % 64 != 0:
                        cur[to - 1, to] = 1
            elif cls == 1:
                for to in t:
                    if to % 64 != 63:
                        cur[to + 1, to] = 1
            elif cls == 2:
                for to in t:
                    if to >= 64:
                        cur[to - 64, to] = 1
                    else:
                        nbe[to + 64, to] = 1
                        nbo[to + 64, to] = 1
            else:
                for to in t:
                    if to < 64:
                        cur[to + 64, to] = 1
                    else:
                        nbe[to - 64, to] = 1
                        nbo[to - 64, to] = 1
        else:
            if cls in (0, 2):
                for to in t:
                    if to >= 1:
                        cur[to - 1, to] = 1
                nbo[127, 0] = 1
            else:
                for to in t:
                    if to <= 126:
                        cur[to + 1, to] = 1
                nbe[0, 127] = 1
        m[cls, 0] = cur - eye
        m[cls, 1] = nbe
        m[cls, 2] = nbo
    return np.ascontiguousarray(m.reshape(12, 128, 128).transpose(1, 0, 2).reshape(128, 12 * 128))


def _tri4():
    s = np.arange(128)[:, None]
    t = np.arange(128)[None, :]
    m = np.stack([(s <= t), (s >= t), (s < t), (s > t)], axis=1).astype(np.float32)
    return np.ascontiguousarray(m.reshape(128, 512))


def _state_to_H(st):
    a = st.reshape(2, 2, 8, 2, 64, 64)
    a = a.transpose(0, 1, 3, 5, 2, 4)
    return np.ascontiguousarray(a.reshape(2, 2, 128, 512))


def _H_to_state(Hm):
    lead = Hm.shape[:-2]
    a = Hm.reshape(lead + (2, 64, 8, 64))
    nl = len(lead)
    perm = tuple(range(nl)) + (nl + 2, nl + 0, nl + 3, nl + 1)
    a = a.transpose(perm)
    return a.reshape(lead + (16, 64, 64))


def make_core_inputs(kind, x_tokens, cond_vec, state_lh, shared):
    d = dict(shared)
    d["x"] = np.ascontiguousarray(x_tokens, dtype=np.float32)
    d["cond"] = np.ascontiguousarray(cond_vec.reshape(8, 128).T, dtype=np.float32)
    d["state0"] = _state_to_H(state_lh)
    d["cmask"] = np.full((128, 1), 1.0 if kind == "sample" else 0.0, np.float32)
    d["shm"] = _shift_mats(kind)
    return d


def shared_inputs(w_ada, b_ada, norm1_g, norm2_g, w_in, mu_shift, w0, w_up, a0, a_up, g_up, k_k, k_a, r_k, lnx_g,
                  lnx_b, w_branch_a, ln_v_g, w_s, b_s, w_branch_b, w_out, w1, w2, final_g):
    f = lambda a: np.ascontiguousarray(np.asarray(a), dtype=np.float32)
    wup_aug = np.concatenate([np.asarray(w_up), np.asarray(w0)[:, :, None, :]], axis=2)
    aup_aug = np.concatenate([np.asarray(a_up), np.asarray(a0)[:, :, None, :]], axis=2)
    wsT = np.asarray(w_s).transpose(0, 3, 1, 2).reshape(2, 128, 8 * 128)
    bsT = np.asarray(b_s).transpose(0, 2, 1)
    return dict(ident=np.eye(128, dtype=np.float32), tri4=_tri4(), w_ada=f(w_ada), b_ada=f(b_ada), norm1_g=f(norm1_g),
                norm2_g=f(norm2_g), w_in=f(w_in), mu_shift=f(mu_shift), wup_aug=f(wup_aug), aup_aug=f(aup_aug), g_up=f(g_up),
                k_k=f(k_k), k_a=f(k_a), r_k=f(np.asarray(r_k).reshape(2, D)), lnx_g=f(lnx_g), lnx_b=f(lnx_b),
                w_branch_a=f(w_branch_a), ln_v_g=f(ln_v_g), wsT=f(wsT), bsT=f(bsT), w_branch_b=f(w_branch_b), w_out=f(w_out),
                w1=f(w1), w2=f(w2), final_g=f(final_g))


_PROG = {}


def kernel(x_prompt, x_sample, state_rwkv, c, c_ctx, w_ada, b_ada, norm1_g, norm2_g, w_in, mu_shift,
           w0, w_up, a0, a_up, g_up, k_k, k_a, r_k, lnx_g, lnx_b, w_branch_a, ln_v_g, w_s, b_s,
           w_branch_b, w_out, w1, w2, final_g):
    NT = 32
    x_prompt = np.asarray(x_prompt, dtype=np.float32)
    x_sample = np.asarray(x_sample, dtype=np.float32)
    state_rwkv = np.asarray(state_rwkv, dtype=np.float32)
    c = np.asarray(c, dtype=np.float32)
    c_ctx = np.asarray(c_ctx, dtype=np.float32)
    shared = shared_inputs(w_ada, b_ada, norm1_g, norm2_g, w_in, mu_shift, w0, w_up, a0, a_up, g_up, k_k, k_a, r_k,
                           lnx_g, lnx_b, w_branch_a, ln_v_g, w_s, b_s, w_branch_b, w_out, w1, w2, final_g)
    in_maps = []
    for b in range(4):
        in_maps.append(make_core_inputs("sample", x_sample[b], c[b], state_rwkv[b], shared))
    zero_state = np.zeros((2, 2, 16, 64, 64), np.float32)
    for q in range(4):
        xs = np.zeros((NT * 128, D), np.float32)
        xs[:2048] = x_prompt[8 * q:8 * q + 8].reshape(2048, D)
        xs[2048:] = xs[:2048]
        in_maps.append(make_core_inputs("prompt", xs, c_ctx, zero_state, shared))
    if NT not in _PROG:
        _PROG[NT] = build_program(NT)[0]
    res = run_bass_kernel_spmd(_PROG[NT], in_maps, core_ids=list(range(8)))
    r = res.results
    y_sample = np.stack([r[b]["y"] for b in range(4)], axis=0)
    y_prompt = np.concatenate([r[4 + q]["y"][:2048].reshape(8, 256, D) for q in range(4)], axis=0)
    sts = []
    for q in range(4):
        so = r[4 + q]["st_out"]
        so = so[:, :, :8]
        s = _H_to_state(so)
        sts.append(np.transpose(s, (2, 0, 1, 3, 4, 5)))
    new_state = np.ascontiguousarray(np.concatenate(sts, axis=0), dtype=np.float32)
    return (np.ascontiguousarray(y_prompt, dtype=np.float32), np.ascontiguousarray(y_sample, dtype=np.float32), new_state)
```

```python
import contextlib
import os
DBG = int(os.environ.get('KDBG', '99'))
KSKIP = os.environ.get('KSKIP', '')
import numpy as np
import concourse.bass as bass
import concourse.mybir as mybir
from concourse.bass_utils import run_bass_kernel_spmd

F32 = mybir.dt.float32
BF16 = mybir.dt.bfloat16
ALU = mybir.AluOpType
AF = mybir.ActivationFunctionType
AX = mybir.AxisListType

D = 1024
CR = 3456
DIN = 7552
DFF = 4096
NHEAD = 16
EPS = 1e-6
GN_EPS = 64e-5
DSC = float(np.exp(-0.5))


class Buf:
    __slots__ = ("name", "t", "w", "r", "dsem", "dcnt")

    def __init__(self, name, t=None):
        self.name = name
        self.t = t
        self.w = None
        self.r = []
        self.dsem = None
        self.dcnt = 0

    def __getitem__(self, idx):
        return self.t[idx]


class K:
    def __init__(self, nc):
        self.nc = nc
        self.eng = {"pe": nc.tensor, "act": nc.scalar, "dve": nc.vector, "pool": nc.gpsimd, "sp": nc.sync}
        self.sem = {}
        self.cnt = {}
        for e in self.eng:
            self.sem[e] = nc.alloc_semaphore(name="s_" + e)
            self.cnt[e] = 0
        self.waited = {}
        self.dsems = {}
        self.free_dsems = []
        self.ninstr = 0
        self.uid = 0

    def _wait(self, e, tok):
        if tok is None:
            return
        key, val = tok
        if key == e and e == "pe":
            return
        kk = (e, key)
        if self.waited.get(kk, 0) >= val:
            return
        self.waited[kk] = val
        self.eng[e].wait_ge(self.sem[key], val)
        self.ninstr += 1

    def _deps(self, e, reads, writes):
        for b in reads:
            self._wait(e, b.w)
        for b in writes:
            self._wait(e, b.w)
            for tok in b.r:
                self._wait(e, tok)

    def _commit(self, tok, reads, writes):
        for b in reads:
            if b not in writes:
                b.r.append(tok)
                if len(b.r) > 10:
                    best = {}
                    for k_, v_ in b.r:
                        if best.get(k_, -1) < v_:
                            best[k_] = v_
                    b.r = list(best.items())
        for b in writes:
            b.w = tok
            b.r = []

    def op(self, e, fn, reads=(), writes=()):
        reads = [b for b in reads if b is not None]
        writes = [b for b in writes if b is not None]
        self._deps(e, reads, writes)
        ins = fn(self.eng[e])
        self.cnt[e] += 1
        ins.then_inc(self.sem[e], 1)
        self.ninstr += 1
        self._commit((e, self.cnt[e]), reads, writes)

    def dma(self, q, out_ap, in_ap, reads=(), writes=(), sembuf=None, **kw):
        reads = [b for b in reads if b is not None]
        writes = [b for b in writes if b is not None]
        if sembuf is None:
            sembuf = (writes + reads)[0]
        if sembuf.dsem is None:
            if self.free_dsems:
                key, base = self.free_dsems.pop()
                sembuf.dcnt = base
            else:
                key = "d%d" % len(self.sem)
                self.sem[key] = self.nc.alloc_semaphore(name=key)
            self.dsems[key] = sembuf
            sembuf.dsem = key
        self._deps(q, reads, writes)
        ins = self.eng[q].dma_start(out=out_ap, in_=in_ap, **kw)
        sembuf.dcnt += 16
        ins.then_inc(self.sem[sembuf.dsem], 16)
        self.ninstr += 1
        self._commit((sembuf.dsem, sembuf.dcnt), reads, writes)

    def barrier(self):
        toks = [(e, self.cnt[e]) for e in self.eng if self.cnt[e] > 0]
        toks += [(key, b.dcnt) for key, b in self.dsems.items() if b.dcnt > 0]
        for e in self.eng:
            for tok in toks:
                if tok[0] != e:
                    self._wait(e, tok)
        for key, b in list(self.dsems.items()):
            if not getattr(b, "keep", False):
                self.free_dsems.append((key, b.dcnt))
                b.dsem = None
                del self.dsems[key]


def cs(a, n):
    return slice(a, a + n)


def hc(h):
    return slice((h % 2) * 512 + (h // 2) * 64, (h % 2) * 512 + (h // 2) * 64 + 64)


def build_program(NT, upto=99):
    T = NT * 128
    NG = NT // 2
    nc = bass.Bass("TRN2", target_bir_lowering=False)
    k = K(nc)

    def din(name, shape):
        return nc.dram_tensor(name, list(shape), F32, kind="ExternalInput").ap()

    x_d = din("x", [T, D])
    cond_d = din("cond", [128, 8])
    state0_d = din("state0", [2, 2, 128, 512])
    cmask_d = din("cmask", [128, 1])
    shm_d = din("shm", [128, 12 * 128])
    ident_d = din("ident", [128, 128])
    tri4_d = din("tri4", [128, 4 * 128])
    w_ada_d = din("w_ada", [2, D, 6 * D])
    b_ada_d = din("b_ada", [2, 6 * D])
    n1g_d = din("norm1_g", [2, D])
    n2g_d = din("norm2_g", [2, D])
    w_in_d = din("w_in", [2, D, DIN])
    mu_d = din("mu_shift", [2, CR])
    wup_d = din("wup_aug", [2, 2, 65, D])
    aup_d = din("aup_aug", [2, 2, 65, D])
    gup_d = din("g_up", [2, 128, D])
    kk_d = din("k_k", [2, D])
    ka_d = din("k_a", [2, D])
    rk_d = din("r_k", [2, D])
    lnxg_d = din("lnx_g", [2, D])
    lnxb_d = din("lnx_b", [2, D])
    wa_d = din("w_branch_a", [2, D, D])
    lnvg_d = din("ln_v_g", [2, D])
    wsT_d = din("wsT", [2, 128, 8 * 128])
    bsT_d = din("bsT", [2, 128, 8])
    wb_d = din("w_branch_b", [2, D, D])
    wo_d = din("w_out", [2, D, D])
    w1_d = din("w1", [2, D, DFF])
    w2_d = din("w2", [2, DFF, D])
    fg_d = din("final_g", [D])

    y_d = nc.dram_tensor("y", [T, D], F32, kind="ExternalOutput").ap()
    st_d = nc.dram_tensor("st_out", [2, 2, NG, 128, 512], F32, kind="ExternalOutput").ap()

    def dscr(name, shape, dt):
        return nc.dram_tensor(name, list(shape), dt, kind="Internal").ap()

    modbc_d = dscr("modbc", [2, 128, 6 * D], F32)
    zr_d = dscr("zr_s", [T, CR], BF16)
    zrest_d = dscr("zrest_s", [T, 4096], BF16)
    ysc_d = dscr("ysc_s", [2, T, D], F32)
    bon_d = dscr("bon_s", [2, T, 16], F32)
    x1_d = dscr("x1_s", [T, D], F32)
    x2_d = dscr("x2_s", [T, D], F32)

    class DR:
        def __init__(self, nm):
            self.b = {}
            self.nm = nm

        def __call__(self, *key):
            if key not in self.b:
                self.b[key] = Buf(self.nm + str(key))
            return self.b[key]

    R_mod, R_zr, R_zrest, R_ysc, R_bon, R_x1, R_x2, R_y, R_st = [DR(n) for n in
        ("mod", "zr", "zrest", "ysc", "bon", "x1", "x2", "y", "st")]

    PP = [nc.alloc_psum_tensor("psum%d" % i, [128, 1024], F32) for i in range(4)]
    PB = []
    for i in range(8):
        PB.append(Buf("pb%d" % i, PP[i // 2][:, cs((i % 2) * 512, 512)]))
    pst = {"b": 0, "p": 0}

    def bank():
        b = PB[pst["b"] % 8]
        pst["b"] += 1
        return b

    def pair():
        if pst["b"] % 2:
            pst["b"] += 1
        i = (pst["b"] % 8) // 2
        pst["b"] += 2
        return PP[i], PB[2 * i], PB[2 * i + 1]

    def sbt(es, name, shape, dt):
        k.uid += 1
        t = es.enter_context(nc.sbuf_tensor("%s_%d" % (name, k.uid), list(shape), dt))
        return Buf(name, t)

    ges = contextlib.ExitStack()
    ident_f = sbt(ges, "ident_f", [128, 128], F32)
    ident_b = sbt(ges, "ident_b", [128, 128], BF16)
    tri4 = sbt(ges, "tri4", [128, 4, 128], F32)
    ones_f = sbt(ges, "ones_f", [128, 1], F32)
    cmask = sbt(ges, "cmask", [128, 1], F32)
    k.dma("sp", ident_f[:], ident_d, writes=[ident_f])
    k.dma("sp", tri4[:].rearrange("p a b -> p (a b)"), tri4_d, writes=[tri4])
    k.dma("sp", cmask[:], cmask_d, writes=[cmask])
    k.op("dve", lambda e: e.tensor_copy(out=ident_b[:], in_=ident_f[:]), reads=[ident_f], writes=[ident_b])
    k.op("dve", lambda e: e.memset(ones_f[:], 1.0), writes=[ones_f])

    def bc_load(buf, dvec):
        k.dma("sp", buf[:], dvec.partition_broadcast(128), writes=[buf])

    def cast_load_rows(buf_ap_fn, dsrc, nk, ncol, buf):
        for kc in range(nk):
            k.dma("pool", buf_ap_fn(kc), dsrc[cs(kc * 128, 128), :], writes=[buf], max_dma_last_dim=4096)

    def rstd_from(e_ss, out_rstd, scale, eps):
        k.op("act", lambda e: e.activation(out=out_rstd[:], in_=e_ss[:], func=AF.Ln, bias=eps_t(eps)[:], scale=scale),
             reads=[e_ss, eps_t(eps)], writes=[out_rstd])
        k.op("act", lambda e: e.activation(out=out_rstd[:], in_=out_rstd[:], func=AF.Exp, scale=-0.5),
             reads=[out_rstd], writes=[out_rstd])

    eps_tiles = {}

    def eps_t(v):
        if v not in eps_tiles:
            b = sbt(ges, "eps%d" % len(eps_tiles), [128, 1], F32)
            k.op("dve", lambda e: e.memset(b[:], float(v)), writes=[b])
            eps_tiles[v] = b
        return eps_tiles[v]

    for v in (EPS, GN_EPS, 1e-12):
        eps_t(v)

    def phaseP(l):
        with contextlib.ExitStack() as es:
            wad = sbt(es, "wad", [128, 8, 6 * D], BF16)
            ba = sbt(es, "ba", [128, 6 * D], F32)
            mod = sbt(es, "mod", [128, 6 * D], F32)
            n1g = sbt(es, "n1g", [128, D], F32)
            n2g = sbt(es, "n2g", [128, D], F32)
            cnd = sbt(es, "cnd", [128, 8], F32)
            scb = sbt(es, "scb", [128, 8, 128], BF16)
            cast_load_rows(lambda kc: wad[:, kc, :], w_ada_d[l], 8, 6 * D, wad)
            bc_load(ba, b_ada_d[l])
            bc_load(n1g, n1g_d[l])
            bc_load(n2g, n2g_d[l])
            k.dma("sp", cnd[:], cond_d, writes=[cnd])
            k.op("act", lambda e: e.activation(out=cnd[:], in_=cnd[:], func=AF.Silu), reads=[cnd], writes=[cnd])
            k.op("dve", lambda e: e.tensor_copy(out=scb[:], in_=cnd[:].unsqueeze(2).to_broadcast([128, 8, 128])),
                 reads=[cnd], writes=[scb])
            for n in range(12):
                pb = bank()

                def f(e):
                    for kc in range(8):
                        ins = e.matmul(pb[:], lhsT=scb[:, kc, :], rhs=wad[:, kc, cs(n * 512, 512)],
                                       start=(kc == 0), stop=(kc == 7))
                    return ins
                k.op("pe", f, reads=[scb, wad], writes=[pb])
                k.op("dve", lambda e: e.tensor_tensor(out=mod[:, cs(n * 512, 512)], in0=pb[:], in1=ba[:, cs(n * 512, 512)],
                                                      op=ALU.add), reads=[pb, ba], writes=[mod])
            k.op("dve", lambda e: e.scalar_tensor_tensor(out=mod[:, cs(D, D)], in0=mod[:, cs(D, D)], scalar=1.0, in1=n1g[:],
                                                         op0=ALU.add, op1=ALU.mult), reads=[mod, n1g], writes=[mod])
            k.op("dve", lambda e: e.scalar_tensor_tensor(out=mod[:, cs(4 * D, D)], in0=mod[:, cs(4 * D, D)], scalar=1.0,
                                                         in1=n2g[:], op0=ALU.add, op1=ALU.mult), reads=[mod, n2g], writes=[mod])
            k.dma("sp", modbc_d[l], mod[:], reads=[mod], writes=[R_mod(l)], sembuf=mod)
            k.barrier()

    def load_mod(buf, l, j):
        k.dma("sp", buf[:], modbc_d[l][:, cs(j * D, D)], reads=[R_mod(l)], writes=[buf])

    def phaseA1(l, xsrc_d, R_xsrc):
        with contextlib.ExitStack() as es:
            win = sbt(es, "win", [128, 8, DIN], BF16)
            g1 = sbt(es, "g1", [128, D], F32)
            sh1 = sbt(es, "sh1", [128, D], F32)
            mu = sbt(es, "mu", [128, CR], F32)
            shm = sbt(es, "shm", [128, 12, 128], BF16)
            xt = [sbt(es, "xt0", [128, D], F32)]
            xt.append(xt[0])
            sq = sbt(es, "sq", [128, D], F32)
            hb = sbt(es, "hb", [128, D], BF16)
            hT = sbt(es, "hT", [128, 8, 128], BF16)
            ss = sbt(es, "ss", [128, 1], F32)
            rstd = sbt(es, "rstd", [128, 1], F32)
            zb = [sbt(es, "zb%d" % i, [128, CR], BF16) for i in range(2)]
            zm = [sbt(es, "zm%d" % i, [128, CR], BF16) for i in range(3)]
            zst = sbt(es, "zst", [128, CR], BF16)
            rst = [sbt(es, "rst%d" % i, [128, 512], BF16) for i in range(4)]
            cast_load_rows(lambda kc: win[:, kc, :], w_in_d[l], 8, DIN, win)
            k.dma("pool", shm[:].rearrange("p a b -> p (a b)"), shm_d, writes=[shm], max_dma_last_dim=4096)
            load_mod(sh1, l, 0)
            load_mod(g1, l, 1)
            bc_load(mu, mu_d[l])
            rsti = [0]

            def load_x(i):
                k.dma("sp", xt[i % 2][:], xsrc_d[cs(i * 128, 128), :], reads=[R_xsrc(i)], writes=[xt[i % 2]])

            def stage1(i):
                x = xt[i % 2]
                if DBG < 2:
                    if i + 1 < NT:
                        load_x(i + 1)
                    return
                k.op("act", lambda e: e.activation(out=sq[:], in_=x[:], func=AF.Square), reads=[x], writes=[sq])
                k.op("dve", lambda e: e.tensor_reduce(out=ss[:], in_=sq[:], axis=AX.X, op=ALU.add), reads=[sq], writes=[ss])
                rstd_from(ss, rstd, 1.0 / D, EPS)
                k.op("dve", lambda e: e.scalar_tensor_tensor(out=x[:], in0=x[:], scalar=rstd[:, 0:1], in1=g1[:],
                                                             op0=ALU.mult, op1=ALU.mult), reads=[x, rstd, g1], writes=[x])
                k.op("dve", lambda e: e.tensor_tensor(out=hb[:], in0=x[:], in1=sh1[:], op=ALU.add),
                     reads=[x, sh1], writes=[hb])
                if i + 1 < NT:
                    load_x(i + 1)
                if DBG < 3:
                    return
                transpose8(hb, hT)
                if DBG < 4:
                    return
                zbi, zmi = zb[i % 2], zm[i % 3]
                col = 0
                ci = 0
                while col < DIN:
                    if col < CR:
                        n = min(512, CR - col)
                    else:
                        n = 512
                    pb = bank()

                    def f(e):
                        for kc in range(8):
                            ins = e.matmul(pb[:, 0:n], lhsT=hT[:, kc, :], rhs=win[:, kc, cs(col, n)],
                                           start=(kc == 0), stop=(kc == 7))
                        return ins
                    k.op("pe", f, reads=[hT, win], writes=[pb])
                    if 'p' in KSKIP:
                        pass
                    elif col < CR:
                        if 'z' not in KSKIP:
                            k.op("act", lambda e: e.copy(out=zbi[:, cs(col, n)], in_=pb[:, 0:n]), reads=[pb], writes=[zbi])
                        if 'm' not in KSKIP:
                            k.op("dve", lambda e: e.tensor_tensor(out=zmi[:, cs(col, n)], in0=zbi[:, cs(col, n)], in1=mu[:, cs(col, n)],
                                                                  op=ALU.mult), reads=[zbi, mu], writes=[zmi])
                    else:
                        st = rst[rsti[0] % 4]
                        rsti[0] += 1
                        if ci % 2 == 0:
                            k.op("act", lambda e: e.copy(out=st[:], in_=pb[:]), reads=[pb], writes=[st])
                        else:
                            k.op("dve", lambda e: e.tensor_copy(out=st[:], in_=pb[:]), reads=[pb], writes=[st])
                        if 'r' not in KSKIP:
                            k.dma("sp", zrest_d[cs(i * 128, 128), cs(col - CR, 512)], st[:], reads=[st],
                                  writes=[R_zrest(i, (col - CR) // 512)], sembuf=st)
                    col += n
                    ci += 1

            def stage2(i):
                if DBG < 5:
                    return
                par = i % 2
                for cls in range(4):
                    nb = i - 1 if cls in (0, 2) else i + 1
                    for hh in range(2):
                        c0 = cls + 4 * 432 * hh
                        sl = slice(c0, c0 + 4 * 431 + 1, 4)
                        pb = bank()
                        srcs = [(ident_b[:], zb[i % 2], ident_b), (shm[:, 3 * cls, :], zm[i % 3], shm)]
                        if 0 <= nb < NT:
                            srcs.append((shm[:, 3 * cls + 1 + par, :], zm[nb % 3], shm))

                        def f(e):
                            for j, (lt, rb, _) in enumerate(srcs):
                                ins = e.matmul(pb[:, 0:432], lhsT=lt, rhs=rb[:, sl], start=(j == 0), stop=(j == len(srcs) - 1))
                            return ins
                        k.op("pe", f, reads=[s[1] for s in srcs] + [ident_b, shm], writes=[pb])
                        if hh == 0:
                            k.op("act", lambda e: e.copy(out=zst[:, sl], in_=pb[:, 0:432]), reads=[pb], writes=[zst])
                        else:
                            k.op("dve", lambda e: e.tensor_copy(out=zst[:, sl], in_=pb[:, 0:432]), reads=[pb], writes=[zst])
                k.dma("sp", zr_d[cs(i * 128, 128), :], zst[:], reads=[zst], writes=[R_zr(i)], sembuf=zst)

            load_x(0)
            stage1(0)
            for i in range(NT):
                if i + 1 < NT:
                    stage1(i + 1)
                stage2(i)
            k.barrier()

    def transpose8(src, dst, nblk=8, src_off=0):
        pb = bank()
        pv = pb[:].bitcast(BF16)

        def f(e):
            for j in range(nblk):
                ins = e.transpose(out=pv[:, cs(j * 128, 128)], in_=src[:, cs(src_off + j * 128, 128)], identity=ident_b[:])
            return ins
        k.op("pe", f, reads=[src, ident_b], writes=[pb])
        k.op("act", lambda e: e.copy(out=dst[:, 0:nblk, :].rearrange("p a b -> p (a b)"), in_=pv[:, 0:nblk * 128]),
             reads=[pb], writes=[dst])

    def phaseS(l, d):
        with contextlib.ExitStack() as es:
            kkc = sbt(es, "kkc", [128, D], F32)
            kac = sbt(es, "kac", [128, D], F32)
            rkc = sbt(es, "rkc", [128, D], F32)
            wup = sbt(es, "wup", [65, D], BF16)
            aup = sbt(es, "aup", [65, D], BF16)
            maskM = sbt(es, "maskM", [128, 4, 128], F32)
            maskN = sbt(es, "maskN", [128, 4, 128], F32)
            H = sbt(es, "H", [128, 8, 64], F32)
            Hb = sbt(es, "Hb", [128, 8, 64], BF16)
            zr = [sbt(es, "zr%d" % i, [128, CR], BF16) for i in range(2)]
            ldT = sbt(es, "ldT", [65, 2, 128], BF16)
            tw = sbt(es, "tw", [128, 128], BF16)
            SG = sbt(es, "SG", [128, D], F32)
            A = sbt(es, "A", [128, D], F32)
            KX = sbt(es, "KX", [128, D], F32)
            BP = sbt(es, "BP", [128, D], F32)
            KD = sbt(es, "KD", [128, D], F32)
            S0 = sbt(es, "S0", [128, D], F32)
            S1 = sbt(es, "S1", [128, D], F32)
            st16 = sbt(es, "st16", [128, 16], F32)
            rs16 = sbt(es, "rs16", [128, 16], F32)
            bon = sbt(es, "bon", [128, 16], F32)
            gC = sbt(es, "gC", [128, 8], F32)
            TM = sbt(es, "TM", [128, 4, D], BF16)
            Bg = sbt(es, "Bg", [128, D], BF16)
            Kg = sbt(es, "Kg", [128, D], BF16)
            FM = sbt(es, "FM", [128, 8, 4, 128], BF16)
            MB = sbt(es, "MB", [128, 16, 512], BF16)
            Pm = [sbt(es, "Pm%d" % i, [128, 16, 128], BF16) for i in range(2)]
            PT = [sbt(es, "PT%d" % i, [128, 16, 128], BF16) for i in range(2)]
            Xb = [sbt(es, "Xb%d" % i, [128, D], BF16) for i in range(2)]
            ysc = sbt(es, "ysc", [128, D], F32)

            bc_load(kkc, kk_d[l])
            bc_load(kac, ka_d[l])
            bc_load(rkc, rk_d[l])
            k.dma("pool", wup[:], wup_d[l, d], writes=[wup], max_dma_last_dim=4096)
            k.dma("pool", aup[:], aup_d[l, d], writes=[aup], max_dma_last_dim=4096)
            strict_i, incl_i, nmask_i = (2, 0, 3) if d == 0 else (3, 1, 2)
            for j, src in enumerate((strict_i, incl_i, strict_i, incl_i)):
                k.op("dve", lambda e: e.tensor_copy(out=maskM[:, j, :], in_=tri4[:, src, :]), reads=[tri4], writes=[maskM])
            for j in range(4):
                k.op("dve", lambda e: e.tensor_copy(out=maskN[:, j, :], in_=tri4[:, nmask_i, :]), reads=[tri4], writes=[maskN])
            tri_incl = tri4[:, incl_i, :]
            tri_excl = tri4[:, strict_i, :]
            tri_dg = tri4[:, nmask_i, :]
            k.op("dve", lambda e: e.memset(ldT[:], 1.0), writes=[ldT])
            k.dma("sp", H[:].rearrange("p a b -> p (a b)"), state0_d[l, d], writes=[H])
            k.op("act", lambda e: e.copy(out=Hb[:], in_=H[:]), reads=[H], writes=[Hb])

            order = list(range(NT)) if d == 0 else list(range(NT - 1, -1, -1))

            def load_zr(n):
                i = order[n]
                k.dma("sp", zr[n % 2][:], zr_d[cs(i * 128, 128), :], reads=[R_zr(i)], writes=[zr[n % 2]])

            load_zr(0)
            for n, i in enumerate(order):
                z = zr[n % 2]
                if n + 1 < NT:
                    load_zr(n + 1)
                rq = z[:, 0:D]
                kq = z[:, D:2 * D]
                vq = z[:, 2 * D:3 * D]
                k.op("act", lambda e: e.activation(out=tw[:, 0:64], in_=z[:, cs(3 * D + 64 * d, 64)], func=AF.Tanh),
                     reads=[z], writes=[tw])
                k.op("dve", lambda e: e.tensor_copy(out=tw[:, 64:128], in_=z[:, cs(3 * D + 128 + 64 * d, 64)]),
                     reads=[z], writes=[tw])
                pb = bank()
                pv = pb[:].bitcast(BF16)

                def f(e):
                    e.transpose(out=pv[0:64, 0:128], in_=tw[:, 0:64], identity=ident_b[:])
                    return e.transpose(out=pv[0:64, 128:256], in_=tw[:, 64:128], identity=ident_b[:])
                k.op("pe", f, reads=[tw, ident_b], writes=[pb])
                k.op("act", lambda e: e.copy(out=ldT[0:64, :, :].rearrange("p a b -> p (a b)"), in_=pv[0:64, 0:256]),
                     reads=[pb], writes=[ldT])
                for (wmat, src_j, dst) in ((wup, 0, SG), (aup, 1, A)):
                    pp, b0, b1 = pair()

                    def f(e):
                        e.matmul(pp[:, 0:512], lhsT=ldT[:, src_j, :], rhs=wmat[:, 0:512], start=True, stop=True)
                        return e.matmul(pp[:, 512:1024], lhsT=ldT[:, src_j, :], rhs=wmat[:, 512:1024], start=True, stop=True)
                    k.op("pe", f, reads=[ldT, wmat], writes=[b0, b1])
                    k.op("act", lambda e: e.activation(out=dst[:], in_=pp[:, :], func=AF.Sigmoid), reads=[b0, b1], writes=[dst])
                if DBG < 11:
                    continue
                k.op("dve", lambda e: e.tensor_tensor(out=KX[:], in0=kq, in1=kkc[:], op=ALU.mult), reads=[z, kkc], writes=[KX])
                k.op("dve", lambda e: e.tensor_tensor(out=S0[:], in0=KX[:], in1=KX[:], op=ALU.mult), reads=[KX], writes=[S0])
                k.op("dve", lambda e: e.tensor_reduce(out=st16[:], in_=S0[:].rearrange("p (h n) -> p h n", n=64), axis=AX.X,
                                                      op=ALU.add), reads=[S0], writes=[st16])
                rstd_from(st16, rs16, 1.0, 1e-12)
                k.op("dve", lambda e: e.tensor_tensor(out=KX[:].rearrange("p (h n) -> p h n", n=64),
                                                      in0=KX[:].rearrange("p (h n) -> p h n", n=64),
                                                      in1=rs16[:].unsqueeze(2).to_broadcast([128, 16, 64]), op=ALU.mult),
                     reads=[KX, rs16], writes=[KX])
                k.op("dve", lambda e: e.scalar_tensor_tensor(out=BP[:], in0=KX[:], scalar=-1.0, in1=A[:], op0=ALU.mult,
                                                             op1=ALU.mult), reads=[KX, A], writes=[BP])
                k.op("dve", lambda e: e.tensor_tensor(out=S0[:], in0=kq, in1=kac[:], op=ALU.mult), reads=[z, kac], writes=[S0])
                k.op("dve", lambda e: e.scalar_tensor_tensor(out=S0[:], in0=A[:], scalar=-1.0, in1=S0[:], op0=ALU.add,
                                                             op1=ALU.mult), reads=[A, S0], writes=[S0])
                k.op("dve", lambda e: e.tensor_tensor(out=KD[:], in0=S0[:], in1=kq, op=ALU.add), reads=[S0, z], writes=[KD])
                k.op("dve", lambda e: e.tensor_tensor(out=S0[:], in0=KD[:], in1=rkc[:], op=ALU.mult), reads=[KD, rkc], writes=[S0])
                k.op("dve", lambda e: e.tensor_tensor(out=S0[:], in0=S0[:], in1=rq, op=ALU.mult), reads=[S0, z], writes=[S0])
                k.op("dve", lambda e: e.tensor_reduce(out=bon[:], in_=S0[:].rearrange("p (h n) -> p h n", n=64), axis=AX.X,
                                                      op=ALU.add), reads=[S0], writes=[bon])
                k.dma("sp", bon_d[d, cs(i * 128, 128), :], bon[:], reads=[bon], writes=[R_bon(d, i)], sembuf=bon)
                if DBG < 12:
                    continue
                def cum(tri_ap, scale, dstE):
                    pp, b0, b1 = pair()

                    def f(e):
                        e.matmul(pp[:, 0:512], lhsT=tri_ap, rhs=SG[:, 0:512], start=True, stop=True)
                        return e.matmul(pp[:, 512:1024], lhsT=tri_ap, rhs=SG[:, 512:1024], start=True, stop=True)
                    k.op("pe", f, reads=[tri4, SG], writes=[b0, b1])
                    k.op("act", lambda e: e.activation(out=dstE[:], in_=pp[:, :], func=AF.Exp, scale=scale),
                         reads=[b0, b1], writes=[dstE])
                    return pp, b0, b1
                pp, b0, b1 = cum(tri_incl, -DSC, S0)
                k.op("dve", lambda e: e.tensor_tensor(out=TM[:, 1, :], in0=rq, in1=S0[:], op=ALU.mult), reads=[z, S0], writes=[TM])
                k.op("act", lambda e: e.activation(out=S1[:], in_=pp[:, :], func=AF.Exp, scale=DSC), reads=[b0, b1], writes=[S1])
                k.op("dve", lambda e: e.tensor_tensor(out=TM[:, 2, :], in0=BP[:], in1=S1[:], op=ALU.mult), reads=[BP, S1], writes=[TM])
                k.op("dve", lambda e: e.tensor_tensor(out=TM[:, 3, :], in0=KD[:], in1=S1[:], op=ALU.mult), reads=[KD, S1], writes=[TM])
                cum(tri_excl, -DSC, S0)
                k.op("dve", lambda e: e.tensor_tensor(out=TM[:, 0, :], in0=KX[:], in1=S0[:], op=ALU.mult), reads=[KX, S0], writes=[TM])
                cum(tri_dg, -DSC, S1)
                k.op("dve", lambda e: e.tensor_tensor(out=Bg[:], in0=BP[:], in1=S1[:], op=ALU.mult), reads=[BP, S1], writes=[Bg])
                k.op("dve", lambda e: e.tensor_tensor(out=Kg[:], in0=KD[:], in1=S1[:], op=ALU.mult), reads=[KD, S1], writes=[Kg])
                pbg = bank()

                def f(e):
                    for ct in range(8):
                        ins = e.matmul(pbg[:, ct:ct + 1], lhsT=SG[:, cs(ct * 128, 128)], rhs=ones_f[:], start=True, stop=True)
                    return ins
                k.op("pe", f, reads=[SG, ones_f], writes=[pbg])
                k.op("act", lambda e: e.activation(out=gC[:], in_=pbg[:, 0:8], func=AF.Exp, scale=-DSC), reads=[pbg], writes=[gC])
                if DBG < 13:
                    continue
                for g4 in range(4):
                    pb = bank()
                    pv = pb[:].bitcast(BF16)

                    def f(e):
                        for c2 in range(2):
                            ct = g4 * 2 + c2
                            for q in range(4):
                                ins = e.transpose(out=pv[:, cs((c2 * 4 + q) * 128, 128)], in_=TM[:, q, cs(ct * 128, 128)],
                                                  identity=ident_b[:])
                        return ins
                    k.op("pe", f, reads=[TM, ident_b], writes=[pb])
                    eng = "act" if g4 % 2 == 0 else "dve"
                    if eng == "act":
                        k.op("act", lambda e: e.copy(out=FM[:, g4 * 2:g4 * 2 + 2, :, :].rearrange("p a b c -> p (a b c)"),
                                                     in_=pv[:, :]), reads=[pb], writes=[FM])
                    else:
                        k.op("dve", lambda e: e.tensor_copy(out=FM[:, g4 * 2:g4 * 2 + 2, :, :].rearrange("p a b c -> p (a b c)"),
                                                            in_=pv[:, :]), reads=[pb], writes=[FM])
                if DBG < 14:
                    continue
                P0, PT0 = Pm[0], PT[0]
                for h in range(NHEAD):
                    ct, p0 = h // 2, (h % 2) * 64
                    if 'o' in KSKIP and h % 2 == 1:
                        continue
                    pb = bank()

                    def f(e):
                        e.matmul(pb[:, 0:256], lhsT=FM[p0:p0 + 64, ct, 2, :],
                                 rhs=FM[p0:p0 + 64, ct, 0:2, :].rearrange("p a b -> p (a b)"), start=True, stop=True)
                        return e.matmul(pb[:, 256:512], lhsT=FM[p0:p0 + 64, ct, 3, :],
                                        rhs=FM[p0:p0 + 64, ct, 0:2, :].rearrange("p a b -> p (a b)"), start=True, stop=True)
                    k.op("pe", f, reads=[FM], writes=[pb])
                    k.op("dve", lambda e: e.tensor_tensor(out=MB[:, h, :], in0=pb[:], in1=maskM[:].rearrange("p a b -> p (a b)"),
                                                          op=ALU.mult), reads=[pb, maskM], writes=[MB])
                    k.op("act", lambda e: e.copy(out=PT0[:, h, :], in_=MB[:, h, 0:128]), reads=[MB], writes=[PT0])
                for g in range(4):
                    if 'n' in KSKIP:
                        continue
                    pb = bank()

                    def f(e):
                        for hh in range(4):
                            h = (g % 2) + 2 * (4 * (g // 2) + hh)
                            ct, p0 = h // 2, (h % 2) * 64
                            ins = e.matmul(pb[:, cs(hh * 128, 128)], lhsT=FM[p0:p0 + 64, ct, 0, :], rhs=FM[p0:p0 + 64, ct, 2, :],
                                           start=True, stop=True)
                        return ins
                    k.op("pe", f, reads=[FM], writes=[pb])
                    h0 = (g % 2) + 8 * (g // 2)
                    k.op("dve", lambda e: e.tensor_tensor(out=P0[:, h0:h0 + 7:2, :],
                                                          in0=pb[:].rearrange("p (a b) -> p a b", b=128), in1=maskN[:], op=ALU.mult),
                         reads=[pb, maskN], writes=[P0])
                if DBG < 15:
                    continue
                pp, b0, b1 = pair()

                def f(e):
                    for h in range(NHEAD):
                        ct, p0 = h // 2, (h % 2) * 64
                        e.matmul(pp[:, hc(h)], lhsT=FM[p0:p0 + 64, ct, 0, :], rhs=Hb[p0:p0 + 64, ct, :],
                                 start=True, stop=False)
                        ins = e.matmul(pp[:, hc(h)], lhsT=MB[:, h, 256:384], rhs=z[:, cs(2 * D + h * 64, 64)],
                                       start=False, stop=True)
                    return ins
                k.op("pe", f, reads=[FM, Hb, MB, z], writes=[b0, b1])
                xc = Xb[0]
                k.op("act", lambda e: e.copy(out=xc[:], in_=pp[:, :]), reads=[b0, b1], writes=[xc])
                if DBG < 16:
                    continue
                cur = 0
                for lev in range(7):
                    Pc, PTc = Pm[cur], PT[cur]
                    pp, b0, b1 = pair()
                    xc, xn = Xb[lev % 2], Xb[(lev + 1) % 2]

                    def f(e):
                        for h in range(NHEAD):
                            ins = e.matmul(pp[:, hc(h)], lhsT=PTc[:, h, :], rhs=xc[:, hc(h)], start=True, stop=True)
                        return ins
                    k.op("pe", f, reads=[PTc, xc], writes=[b0, b1])
                    k.op("dve", lambda e: e.tensor_tensor(out=xn[:], in0=pp[:, :], in1=xc[:], op=ALU.add),
                         reads=[b0, b1, xc], writes=[xn])
                    if lev < 6:
                        Pn, PTn = Pm[1 - cur], PT[1 - cur]
                        for g in range(4):
                            for which in range(2):
                                pb = bank()

                                def f(e):
                                    for hh in range(4):
                                        h = g * 4 + hh
                                        if which == 0:
                                            ins = e.matmul(pb[:, cs(hh * 128, 128)], lhsT=PTc[:, h, :], rhs=Pc[:, h, :], start=True, stop=True)
                                        else:
                                            ins = e.matmul(pb[:, cs(hh * 128, 128)], lhsT=Pc[:, h, :], rhs=PTc[:, h, :], start=True, stop=True)
                                    return ins
                                k.op("pe", f, reads=[Pc, PTc], writes=[pb])
                                dstb = Pn if which == 0 else PTn
                                if which == 0:
                                    k.op("act", lambda e: e.copy(out=dstb[:, g * 4:g * 4 + 4, :].rearrange("p a b -> p (a b)"),
                                                                 in_=pb[:]), reads=[pb], writes=[dstb])
                                else:
                                    k.op("dve", lambda e: e.tensor_copy(out=dstb[:, g * 4:g * 4 + 4, :].rearrange("p a b -> p (a b)"),
                                                                        in_=pb[:]), reads=[pb], writes=[dstb])
                        cur = 1 - cur
                U = Xb[1]
                if DBG < 17:
                    continue
                pp, b0, b1 = pair()

                def f(e):
                    for h in range(NHEAD):
                        ct, p0 = h // 2, (h % 2) * 64
                        e.matmul(pp[:, hc(h)], lhsT=FM[p0:p0 + 64, ct, 1, :], rhs=Hb[p0:p0 + 64, ct, :], start=True, stop=False)
                        e.matmul(pp[:, hc(h)], lhsT=MB[:, h, 128:256], rhs=U[:, hc(h)], start=False, stop=False)
                        ins = e.matmul(pp[:, hc(h)], lhsT=MB[:, h, 384:512], rhs=z[:, cs(2 * D + h * 64, 64)],
                                       start=False, stop=True)
                    return ins
                k.op("pe", f, reads=[FM, Hb, MB, U, z], writes=[b0, b1])
                k.op("act", lambda e: e.copy(out=ysc[:].rearrange("p (hp h2 i) -> p h2 hp i", h2=2, i=64),
                                             in_=pp[:, :].rearrange("p (h2 hp i) -> p h2 hp i", hp=8, i=64)), reads=[b0, b1], writes=[ysc])
                k.dma("sp", ysc_d[d, cs(i * 128, 128), :], ysc[:], reads=[ysc], writes=[R_ysc(d, i)], sembuf=ysc)
                if DBG < 18:
                    continue
                pp, b0, b1 = pair()

                def f(e):
                    for ct in range(8):
                        e.matmul(pp[:, cs(ct * 128, 128)], lhsT=Bg[:, cs(ct * 128, 128)],
                                 rhs=U[:].rearrange("p (a b c) -> p a b c", a=2, b=8)[:, :, ct, :], start=True, stop=False)
                        ins = e.matmul(pp[:, cs(ct * 128, 128)], lhsT=Kg[:, cs(ct * 128, 128)], rhs=z[:, cs(2 * D + ct * 128, 128)],
                                       start=False, stop=True)
                    return ins
                k.op("pe", f, reads=[Bg, Kg, U, z], writes=[b0, b1])
                k.op("dve", lambda e: e.tensor_tensor(out=H[:], in0=H[:], in1=gC[:].unsqueeze(2).to_broadcast([128, 8, 64]),
                                                      op=ALU.mult), reads=[H, gC], writes=[H])
                ppv = pp[:, :].rearrange("p (a b) -> p a b", b=128)
                k.op("dve", lambda e: e.tensor_tensor(out=H[0:64, :, :], in0=H[0:64, :, :], in1=ppv[0:64, :, 0:64], op=ALU.add),
                     reads=[H, b0, b1], writes=[H])
                k.op("dve", lambda e: e.tensor_tensor(out=H[64:128, :, :], in0=H[64:128, :, :], in1=ppv[64:128, :, 64:128], op=ALU.add),
                     reads=[H, b0, b1], writes=[H])
                if n % 2 == 1:
                    grp = i // 2
                    k.dma("sp", st_d[l, d, grp], H[:].rearrange("p a b -> p (a b)"), reads=[H], writes=[R_st(l, d, grp)], sembuf=H)
                    k.op("dve", lambda e: e.tensor_scalar(out=H[:], in0=H[:], scalar1=cmask[:, 0:1], scalar2=None, op0=ALU.mult),
                         reads=[H, cmask], writes=[H])
                k.op("act", lambda e: e.copy(out=Hb[:], in_=H[:]), reads=[H], writes=[Hb])
            k.barrier()


    def phaseS2(l):
        with contextlib.ExitStack() as es:
            kkc = sbt(es, "kkc", [128, D], F32)
            kac = sbt(es, "kac", [128, D], F32)
            rkc = sbt(es, "rkc", [128, D], F32)
            bc_load(kkc, kk_d[l])
            bc_load(kac, ka_d[l])
            bc_load(rkc, rk_d[l])

            def dir_gen(d):
                sfx = "_%d" % d
                wup = sbt(es, "wup" + sfx, [65, D], BF16)
                aup = sbt(es, "aup" + sfx, [65, D], BF16)
                maskM = sbt(es, "maskM" + sfx, [128, 4, 128], BF16)
                maskN = sbt(es, "maskN" + sfx, [128, 4, 128], BF16)
                H = sbt(es, "H" + sfx, [128, 8, 64], F32)
                Hb = sbt(es, "Hb" + sfx, [128, 8, 64], BF16)
                z = sbt(es, "zr" + sfx, [128, CR], BF16)
                ldT = sbt(es, "ldT" + sfx, [65, 2, 128], BF16)
                tw = sbt(es, "tw" + sfx, [128, 128], BF16)
                SG = sbt(es, "SG" + sfx, [128, D], F32)
                A = sbt(es, "A" + sfx, [128, D], F32)
                KX = sbt(es, "KX" + sfx, [128, D], F32)
                BP = sbt(es, "BP" + sfx, [128, D], F32)
                KD = sbt(es, "KD" + sfx, [128, D], F32)
                S0 = sbt(es, "S0" + sfx, [128, D], F32)
                S1 = A
                ysc = S0
                st16 = sbt(es, "st16" + sfx, [128, 16], F32)
                rs16 = sbt(es, "rs16" + sfx, [128, 16], F32)
                bon = sbt(es, "bon" + sfx, [128, 16], F32)
                gC = sbt(es, "gC" + sfx, [128, 8], F32)
                TM = sbt(es, "TM" + sfx, [128, 4, D], BF16)
                FM = sbt(es, "FM" + sfx, [128, 8, 4, 128], BF16)
                MB = sbt(es, "MB" + sfx, [128, 16, 512], BF16)
                Pm = [sbt(es, "Pm%d" % i + sfx, [128, 16, 128], BF16) for i in range(2)]
                PT = [sbt(es, "PT%d" % i + sfx, [128, 16, 128], BF16) for i in range(2)]
                Xb = [sbt(es, "Xb%d" % i + sfx, [128, D], BF16) for i in range(2)]

                k.dma("pool", wup[:], wup_d[l, d], writes=[wup], max_dma_last_dim=4096)
                k.dma("pool", aup[:], aup_d[l, d], writes=[aup], max_dma_last_dim=4096)
                strict_i, incl_i, nmask_i = (2, 0, 3) if d == 0 else (3, 1, 2)
                for j, src in enumerate((strict_i, incl_i, strict_i, incl_i)):
                    k.op("dve", lambda e: e.tensor_copy(out=maskM[:, j, :], in_=tri4[:, src, :]), reads=[tri4], writes=[maskM])
                for j in range(4):
                    k.op("dve", lambda e: e.tensor_copy(out=maskN[:, j, :], in_=tri4[:, nmask_i, :]), reads=[tri4], writes=[maskN])
                tri_incl = tri4[:, incl_i, :]
                tri_excl = tri4[:, strict_i, :]
                tri_dg = tri4[:, nmask_i, :]
                k.op("dve", lambda e: e.memset(ldT[:], 1.0), writes=[ldT])
                k.dma("sp", H[:].rearrange("p a b -> p (a b)"), state0_d[l, d], writes=[H])
                k.op("act", lambda e: e.copy(out=Hb[:], in_=H[:]), reads=[H], writes=[Hb])
                order = list(range(NT)) if d == 0 else list(range(NT - 1, -1, -1))
                yield

                for n, i in enumerate(order):
                    k.dma("sp", z[:], zr_d[cs(i * 128, 128), :], reads=[R_zr(i)], writes=[z])
                    rq = z[:, 0:D]
                    kq = z[:, D:2 * D]
                    k.op("act", lambda e: e.activation(out=tw[:, 0:64], in_=z[:, cs(3 * D + 64 * d, 64)], func=AF.Tanh),
                         reads=[z], writes=[tw])
                    k.op("pool", lambda e: e.tensor_copy(out=tw[:, 64:128], in_=z[:, cs(3 * D + 128 + 64 * d, 64)]),
                         reads=[z], writes=[tw])
                    pb = bank()
                    pv = pb[:].bitcast(BF16)

                    def f(e):
                        e.transpose(out=pv[0:64, 0:128], in_=tw[:, 0:64], identity=ident_b[:])
                        return e.transpose(out=pv[0:64, 128:256], in_=tw[:, 64:128], identity=ident_b[:])
                    k.op("pe", f, reads=[tw, ident_b], writes=[pb])
                    k.op("act", lambda e: e.copy(out=ldT[0:64, :, :].rearrange("p a b -> p (a b)"), in_=pv[0:64, 0:256]),
                         reads=[pb], writes=[ldT])
                    for (wmat, src_j, dst) in ((wup, 0, SG), (aup, 1, A)):
                        pp, b0, b1 = pair()

                        def f(e):
                            e.matmul(pp[:, 0:512], lhsT=ldT[:, src_j, :], rhs=wmat[:, 0:512], start=True, stop=True)
                            return e.matmul(pp[:, 512:1024], lhsT=ldT[:, src_j, :], rhs=wmat[:, 512:1024], start=True, stop=True)
                        k.op("pe", f, reads=[ldT, wmat], writes=[b0, b1])
                        k.op("act", lambda e: e.activation(out=dst[:], in_=pp[:, :], func=AF.Sigmoid), reads=[b0, b1], writes=[dst])
                    k.op("pool", lambda e: e.tensor_tensor(out=KX[:], in0=kq, in1=kkc[:], op=ALU.mult), reads=[z, kkc], writes=[KX])
                    k.op("pool", lambda e: e.tensor_tensor(out=S0[:], in0=KX[:], in1=KX[:], op=ALU.mult), reads=[KX], writes=[S0])
                    k.op("dve", lambda e: e.tensor_reduce(out=st16[:], in_=S0[:].rearrange("p (h n) -> p h n", n=64), axis=AX.X,
                                                          op=ALU.add), reads=[S0], writes=[st16])
                    rstd_from(st16, rs16, 1.0, 1e-12)
                    k.op("dve", lambda e: e.tensor_tensor(out=KX[:].rearrange("p (h n) -> p h n", n=64),
                                                          in0=KX[:].rearrange("p (h n) -> p h n", n=64),
                                                          in1=rs16[:].unsqueeze(2).to_broadcast([128, 16, 64]), op=ALU.mult),
                         reads=[KX, rs16], writes=[KX])
                    yield
                    k.op("dve", lambda e: e.scalar_tensor_tensor(out=BP[:], in0=KX[:], scalar=-1.0, in1=A[:], op0=ALU.mult,
                                                                 op1=ALU.mult), reads=[KX, A], writes=[BP])
                    k.op("pool", lambda e: e.tensor_tensor(out=S0[:], in0=kq, in1=kac[:], op=ALU.mult), reads=[z, kac], writes=[S0])
                    k.op("dve", lambda e: e.scalar_tensor_tensor(out=S0[:], in0=A[:], scalar=-1.0, in1=S0[:], op0=ALU.add,
                                                                 op1=ALU.mult), reads=[A, S0], writes=[S0])
                    k.op("pool", lambda e: e.tensor_tensor(out=KD[:], in0=S0[:], in1=kq, op=ALU.add), reads=[S0, z], writes=[KD])
                    k.op("pool", lambda e: e.tensor_tensor(out=S0[:], in0=KD[:], in1=rkc[:], op=ALU.mult), reads=[KD, rkc], writes=[S0])
                    k.op("pool", lambda e: e.tensor_tensor(out=S0[:], in0=S0[:], in1=rq, op=ALU.mult), reads=[S0, z], writes=[S0])
                    k.op("dve", lambda e: e.tensor_reduce(out=bon[:], in_=S0[:].rearrange("p (h n) -> p h n", n=64), axis=AX.X,
                                                          op=ALU.add), reads=[S0], writes=[bon])
                    k.dma("sp", bon_d[d, cs(i * 128, 128), :], bon[:], reads=[bon], writes=[R_bon(d, i)], sembuf=bon)
                    yield

                    def cum(tri_ap, scale, dstE):
                        pp, b0, b1 = pair()

                        def f(e):
                            e.matmul(pp[:, 0:512], lhsT=tri_ap, rhs=SG[:, 0:512], start=True, stop=True)
                            return e.matmul(pp[:, 512:1024], lhsT=tri_ap, rhs=SG[:, 512:1024], start=True, stop=True)
                        k.op("pe", f, reads=[tri4, SG], writes=[b0, b1])
                        k.op("act", lambda e: e.activation(out=dstE[:], in_=pp[:, :], func=AF.Exp, scale=scale),
                             reads=[b0, b1], writes=[dstE])
                        return pp, b0, b1
                    pp, b0, b1 = cum(tri_incl, -DSC, S0)
                    k.op("dve", lambda e: e.tensor_tensor(out=TM[:, 1, :], in0=rq, in1=S0[:], op=ALU.mult), reads=[z, S0], writes=[TM])
                    k.op("act", lambda e: e.activation(out=S1[:], in_=pp[:, :], func=AF.Exp, scale=DSC), reads=[b0, b1], writes=[S1])
                    k.op("dve", lambda e: e.tensor_tensor(out=TM[:, 2, :], in0=BP[:], in1=S1[:], op=ALU.mult), reads=[BP, S1], writes=[TM])
                    k.op("pool", lambda e: e.tensor_tensor(out=TM[:, 3, :], in0=KD[:], in1=S1[:], op=ALU.mult), reads=[KD, S1], writes=[TM])
                    cum(tri_excl, -DSC, S0)
                    k.op("dve", lambda e: e.tensor_tensor(out=TM[:, 0, :], in0=KX[:], in1=S0[:], op=ALU.mult), reads=[KX, S0], writes=[TM])
                    pbg = bank()

                    def f(e):
                        for ct in range(8):
                            ins = e.matmul(pbg[:, ct:ct + 1], lhsT=SG[:, cs(ct * 128, 128)], rhs=ones_f[:], start=True, stop=True)
                        return ins
                    k.op("pe", f, reads=[SG, ones_f], writes=[pbg])
                    k.op("act", lambda e: e.activation(out=gC[:], in_=pbg[:, 0:8], func=AF.Exp, scale=-DSC), reads=[pbg], writes=[gC])
                    yield
                    for g4 in range(4):
                        pb = bank()
                        pv = pb[:].bitcast(BF16)

                        def f(e):
                            for c2 in range(2):
                                ct = g4 * 2 + c2
                                for q in range(4):
                                    ins = e.transpose(out=pv[:, cs((c2 * 4 + q) * 128, 128)], in_=TM[:, q, cs(ct * 128, 128)],
                                                      identity=ident_b[:])
                            return ins
                        k.op("pe", f, reads=[TM, ident_b], writes=[pb])
                        if g4 % 2 == 0:
                            k.op("act", lambda e: e.copy(out=FM[:, g4 * 2:g4 * 2 + 2, :, :].rearrange("p a b c -> p (a b c)"),
                                                         in_=pv[:, :]), reads=[pb], writes=[FM])
                        else:
                            k.op("dve", lambda e: e.tensor_copy(out=FM[:, g4 * 2:g4 * 2 + 2, :, :].rearrange("p a b c -> p (a b c)"),
                                                                in_=pv[:, :]), reads=[pb], writes=[FM])
                    cum(tri_dg, -DSC, S1)
                    k.op("dve", lambda e: e.tensor_tensor(out=TM[:, 2, :], in0=BP[:], in1=S1[:], op=ALU.mult), reads=[BP, S1], writes=[TM])
                    k.op("pool", lambda e: e.tensor_tensor(out=TM[:, 3, :], in0=KD[:], in1=S1[:], op=ALU.mult), reads=[KD, S1], writes=[TM])
                    yield
                    P0 = Pm[0]
                    for h in range(NHEAD):
                        ct, p0 = h // 2, (h % 2) * 64
                        pb = bank()

                        def f(e):
                            e.matmul(pb[:, 0:256], lhsT=FM[p0:p0 + 64, ct, 2, :],
                                     rhs=FM[p0:p0 + 64, ct, 0:2, :].rearrange("p a b -> p (a b)"), start=True, stop=True)
                            return e.matmul(pb[:, 256:512], lhsT=FM[p0:p0 + 64, ct, 3, :],
                                            rhs=FM[p0:p0 + 64, ct, 0:2, :].rearrange("p a b -> p (a b)"), start=True, stop=True)
                        k.op("pe", f, reads=[FM], writes=[pb])
                        k.op("dve", lambda e: e.tensor_tensor(out=MB[:, h, :], in0=pb[:], in1=maskM[:].rearrange("p a b -> p (a b)"),
                                                              op=ALU.mult), reads=[pb, maskM], writes=[MB])
                        if h % 4 == 3:
                            yield
                    for g in range(4):
                        pb = bank()

                        def f(e):
                            for hh in range(4):
                                h = (g % 2) + 2 * (4 * (g // 2) + hh)
                                ct, p0 = h // 2, (h % 2) * 64
                                ins = e.matmul(pb[:, cs(hh * 128, 128)], lhsT=FM[p0:p0 + 64, ct, 0, :], rhs=FM[p0:p0 + 64, ct, 2, :],
                                               start=True, stop=True)
                            return ins
                        k.op("pe", f, reads=[FM], writes=[pb])
                        h0 = (g % 2) + 8 * (g // 2)
                        k.op("dve", lambda e: e.tensor_tensor(out=P0[:, h0:h0 + 7:2, :],
                                                              in0=pb[:].rearrange("p (a b) -> p a b", b=128), in1=maskN[:], op=ALU.mult),
                             reads=[pb, maskN], writes=[P0])
                    yield
                    pp, b0, b1 = pair()

                    def f(e):
                        for h in range(NHEAD):
                            ct, p0 = h // 2, (h % 2) * 64
                            e.matmul(pp[:, hc(h)], lhsT=FM[p0:p0 + 64, ct, 0, :], rhs=Hb[p0:p0 + 64, ct, :],
                                     start=True, stop=False)
                            ins = e.matmul(pp[:, hc(h)], lhsT=MB[:, h, 256:384], rhs=z[:, cs(2 * D + h * 64, 64)],
                                           start=False, stop=True)
                        return ins
                    k.op("pe", f, reads=[FM, Hb, MB, z], writes=[b0, b1])
                    xc = Xb[0]
                    k.op("act", lambda e: e.copy(out=xc[:], in_=pp[:, :]), reads=[b0, b1], writes=[xc])
                    yield
                    cur = 0
                    for lev in range(7):
                        Pc = Pm[cur]
                        pp, b0, b1 = pair()
                        xc, xn = Xb[lev % 2], Xb[(lev + 1) % 2]
                        if lev == 0:
                            ptv = lambda h: MB[:, h, 0:128]
                            ptb = MB
                        else:
                            ptv = lambda h, PTc=PT[cur]: PTc[:, h, :]
                            ptb = PT[cur]

                        def f(e):
                            e.matmul(pp[:, 0:512], lhsT=ident_b[:], rhs=xc[:, 0:512], start=True, stop=False)
                            e.matmul(pp[:, 512:1024], lhsT=ident_b[:], rhs=xc[:, 512:1024], start=True, stop=False)
                            for h in range(NHEAD):
                                ins = e.matmul(pp[:, hc(h)], lhsT=ptv(h), rhs=xc[:, hc(h)], start=False, stop=(h >= NHEAD - 2))
                            return ins
                        k.op("pe", f, reads=[ptb, xc, ident_b], writes=[b0, b1])
                        k.op("act", lambda e: e.copy(out=xn[:], in_=pp[:, :]), reads=[b0, b1], writes=[xn])
                        if lev < 6:
                            Pn, PTn = Pm[1 - cur], PT[1 - cur]
                            for g in range(4):
                                for which in range(2):
                                    pb = bank()

                                    def f(e):
                                        for hh in range(4):
                                            h = g * 4 + hh
                                            if which == 0:
                                                ins = e.matmul(pb[:, cs(hh * 128, 128)], lhsT=ptv(h), rhs=Pc[:, h, :], start=True, stop=True)
                                            else:
                                                ins = e.matmul(pb[:, cs(hh * 128, 128)], lhsT=Pc[:, h, :], rhs=ptv(h), start=True, stop=True)
                                        return ins
                                    k.op("pe", f, reads=[Pc, ptb], writes=[pb])
                                    dstb = Pn if which == 0 else PTn
                                    if which == 0 or g % 2 == 0:
                                        k.op("act", lambda e: e.copy(out=dstb[:, g * 4:g * 4 + 4, :].rearrange("p a b -> p (a b)"),
                                                                     in_=pb[:]), reads=[pb], writes=[dstb])
                                    else:
                                        k.op("dve", lambda e: e.tensor_copy(out=dstb[:, g * 4:g * 4 + 4, :].rearrange("p a b -> p (a b)"),
                                                                            in_=pb[:]), reads=[pb], writes=[dstb])
                                if g % 2 == 1:
                                    yield
                            cur = 1 - cur
                    U = Xb[1]
                    pp, b0, b1 = pair()

                    def f(e):
                        for h in range(NHEAD):
                            ct, p0 = h // 2, (h % 2) * 64
                            e.matmul(pp[:, hc(h)], lhsT=FM[p0:p0 + 64, ct, 1, :], rhs=Hb[p0:p0 + 64, ct, :], start=True, stop=False)
                            e.matmul(pp[:, hc(h)], lhsT=MB[:, h, 128:256], rhs=U[:, hc(h)], start=False, stop=False)
                            ins = e.matmul(pp[:, hc(h)], lhsT=MB[:, h, 384:512], rhs=z[:, cs(2 * D + h * 64, 64)],
                                           start=False, stop=True)
                        return ins
                    k.op("pe", f, reads=[FM, Hb, MB, U, z], writes=[b0, b1])
                    k.op("act", lambda e: e.copy(out=ysc[:].rearrange("p (hp h2 i) -> p h2 hp i", h2=2, i=64),
                                                 in_=pp[:, :].rearrange("p (h2 hp i) -> p h2 hp i", hp=8, i=64)), reads=[b0, b1], writes=[ysc])
                    k.dma("sp", ysc_d[d, cs(i * 128, 128), :], ysc[:], reads=[ysc], writes=[R_ysc(d, i)], sembuf=ysc)
                    pp, b0, b1 = pair()

                    def f(e):
                        for ct in range(8):
                            e.matmul(pp[:, cs(ct * 128, 128)], lhsT=TM[:, 2, cs(ct * 128, 128)],
                                     rhs=U[:].rearrange("p (a b c) -> p a b c", a=2, b=8)[:, :, ct, :], start=True, stop=False)
                            ins = e.matmul(pp[:, cs(ct * 128, 128)], lhsT=TM[:, 3, cs(ct * 128, 128)], rhs=z[:, cs(2 * D + ct * 128, 128)],
                                           start=False, stop=True)
                        return ins
                    k.op("pe", f, reads=[TM, U, z], writes=[b0, b1])
                    k.op("dve", lambda e: e.tensor_tensor(out=H[:], in0=H[:], in1=gC[:].unsqueeze(2).to_broadcast([128, 8, 64]),
                                                          op=ALU.mult), reads=[H, gC], writes=[H])
                    ppv = pp[:, :].rearrange("p (a b) -> p a b", b=128)
                    k.op("dve", lambda e: e.tensor_tensor(out=H[0:64, :, :], in0=H[0:64, :, :], in1=ppv[0:64, :, 0:64], op=ALU.add),
                         reads=[H, b0, b1], writes=[H])
                    k.op("dve", lambda e: e.tensor_tensor(out=H[64:128, :, :], in0=H[64:128, :, :], in1=ppv[64:128, :, 64:128], op=ALU.add),
                         reads=[H, b0, b1], writes=[H])
                    if n % 2 == 1:
                        grp = i // 2
                        k.dma("sp", st_d[l, d, grp], H[:].rearrange("p a b -> p (a b)"), reads=[H], writes=[R_st(l, d, grp)], sembuf=H)
                        k.op("dve", lambda e: e.tensor_scalar(out=H[:], in0=H[:], scalar1=cmask[:, 0:1], scalar2=None, op0=ALU.mult),
                             reads=[H, cmask], writes=[H])
                    k.op("act", lambda e: e.copy(out=Hb[:], in_=H[:]), reads=[H], writes=[Hb])
                    yield

            gens = [dir_gen(0), dir_gen(1)]
            next(gens[0])
            next(gens[1])
            for _ in range(6):
                next(gens[0])
            live = list(gens)
            while live:
                for g in list(live):
                    try:
                        next(g)
                    except StopIteration:
                        live.remove(g)
            k.barrier()

    def phaseM(l, xsrc_d, R_xsrc):
        with contextlib.ExitStack() as es:
            wa = sbt(es, "wa", [128, 8, D], BF16)
            wb = sbt(es, "wb", [128, 8, D], BF16)
            wo = sbt(es, "wo", [128, 8, D], BF16)
            gup = sbt(es, "gup", [128, D], BF16)
            wsT = sbt(es, "wsT", [128, 8, 128], BF16)
            bsT = sbt(es, "bsT", [128, 8], F32)
            lnxg = sbt(es, "lnxg", [128, D], F32)
            lnxb = sbt(es, "lnxb", [128, D], F32)
            lnvg = sbt(es, "lnvg", [128, D], F32)
            gate1 = sbt(es, "gate1", [128, D], F32)
            def mkset(pi):
                B = {}
                B['zv'] = sbt(es, "p%d_" % pi + "zv", [128, D + 128], BF16)
                B['zrs'] = sbt(es, "p%d_" % pi + "zrs", [128, 4096], BF16)
                B['yf'] = sbt(es, "p%d_" % pi + "yf", [128, D], F32)
                B['yb'] = sbt(es, "p%d_" % pi + "yb", [128, D], F32)
                B['b0t'] = sbt(es, "p%d_" % pi + "b0t", [128, 16], F32)
                B['b1t'] = sbt(es, "p%d_" % pi + "b1t", [128, 16], F32)
                B['xt'] = sbt(es, "p%d_" % pi + "xtm", [128, D], F32)
                B['W0'] = sbt(es, "p%d_" % pi + "W0", [128, D], F32)
                B['W1'] = sbt(es, "p%d_" % pi + "W1", [128, D], F32)
                B['W2'] = sbt(es, "p%d_" % pi + "W2", [128, D], F32)
                B['s16a'] = sbt(es, "p%d_" % pi + "s16a", [128, 16], F32)
                B['s16b'] = sbt(es, "p%d_" % pi + "s16b", [128, 16], F32)
                B['s16c'] = sbt(es, "p%d_" % pi + "s16c", [128, 16], F32)
                B['bnst'] = sbt(es, "p%d_" % pi + "bnst", [128, 2, 6], F32)
                B['mv'] = sbt(es, "p%d_" % pi + "mv", [128, 2], F32)
                B['rsv'] = sbt(es, "p%d_" % pi + "rsv", [128, 1], F32)
                B['gsb'] = sbt(es, "p%d_" % pi + "gsb", [128, 128], BF16)
                B['gT'] = sbt(es, "p%d_" % pi + "gT", [128, 1, 128], BF16)
                B['actb'] = sbt(es, "p%d_" % pi + "actb", [128, D], BF16)
                B['actT'] = sbt(es, "p%d_" % pi + "actT", [128, 8, 128], BF16)
                B['ub'] = sbt(es, "p%d_" % pi + "ub", [128, D], BF16)
                B['vcb'] = sbt(es, "p%d_" % pi + "vcb", [128, D], BF16)
                return B
            sets = [mkset(0), mkset(1)]
            cast_load_rows(lambda kc: wa[:, kc, :], wa_d[l], 8, D, wa)
            cast_load_rows(lambda kc: wb[:, kc, :], wb_d[l], 8, D, wb)
            cast_load_rows(lambda kc: wo[:, kc, :], wo_d[l], 8, D, wo)
            k.dma("pool", gup[:], gup_d[l], writes=[gup], max_dma_last_dim=4096)
            k.dma("pool", wsT[:].rearrange("p a b -> p (a b)"), wsT_d[l], writes=[wsT], max_dma_last_dim=4096)
            k.dma("sp", bsT[:], bsT_d[l], writes=[bsT])
            bc_load(lnxg, lnxg_d[l])
            bc_load(lnxb, lnxb_d[l])
            bc_load(lnvg, lnvg_d[l])
            load_mod(gate1, l, 2)

            def v3(b):
                return b[:].rearrange("p (h n) -> p h n", n=64)

            def bc16(b):
                return b[:].unsqueeze(2).to_broadcast([128, 16, 64])

            def proj(src_bf, wmat, actT):
                transpose8(src_bf, actT)
                pp, p0, p1 = pair()

                def f(e):
                    for nn in range(2):
                        for kc in range(8):
                            ins = e.matmul(pp[:, cs(nn * 512, 512)], lhsT=actT[:, kc, :], rhs=wmat[:, kc, cs(nn * 512, 512)],
                                           start=(kc == 0), stop=(kc == 7))
                    return ins
                k.op("pe", f, reads=[actT, wmat], writes=[p0, p1])
                return pp, p0, p1

            def tile_gen(i, B):
                zv = B['zv']
                zrs = B['zrs']
                yf = B['yf']
                yb = B['yb']
                b0t = B['b0t']
                b1t = B['b1t']
                xt = B['xt']
                W0 = B['W0']
                W1 = B['W1']
                W2 = B['W2']
                s16a = B['s16a']
                s16b = B['s16b']
                s16c = B['s16c']
                bnst = B['bnst']
                mv = B['mv']
                rsv = B['rsv']
                gsb = B['gsb']
                gT = B['gT']
                actb = B['actb']
                actT = B['actT']
                ub = B['ub']
                vcb = B['vcb']
                rows = cs(i * 128, 128)
                k.dma("sp", zv[:, 0:D], zr_d[rows, 2 * D:3 * D], reads=[R_zr(i)], writes=[zv])
                k.dma("sp", zv[:, D:D + 128], zr_d[rows, cs(3 * D + 256, 128)], reads=[R_zr(i)], writes=[zv])
                k.dma("sp", zrs[:], zrest_d[rows, :], reads=[R_zrest(i, j) for j in range(8)], writes=[zrs])
                k.dma("sp", yf[:], ysc_d[0, rows, :], reads=[R_ysc(0, i)], writes=[yf])
                k.dma("sp", yb[:], ysc_d[1, rows, :], reads=[R_ysc(1, i)], writes=[yb])
                k.dma("sp", b0t[:], bon_d[0, rows, :], reads=[R_bon(0, i)], writes=[b0t])
                k.dma("sp", b1t[:], bon_d[1, rows, :], reads=[R_bon(1, i)], writes=[b1t])
                k.dma("sp", xt[:], xsrc_d[rows, :], reads=[R_xsrc(i)], writes=[xt])
                yield
                k.op("dve", lambda e: e.tensor_tensor(out=yf[:], in0=yf[:], in1=yb[:], op=ALU.add), reads=[yf, yb], writes=[yf])
                k.op("dve", lambda e: e.tensor_reduce(out=s16a[:], in_=v3(yf), axis=AX.X, op=ALU.add), reads=[yf], writes=[s16a])
                k.op("dve", lambda e: e.tensor_scalar(out=s16a[:], in0=s16a[:], scalar1=1.0 / 64, scalar2=None, op0=ALU.mult),
                     reads=[s16a], writes=[s16a])
                k.op("dve", lambda e: e.tensor_tensor(out=v3(yf), in0=v3(yf), in1=bc16(s16a), op=ALU.subtract), reads=[yf, s16a], writes=[yf])
                k.op("dve", lambda e: e.tensor_tensor(out=W0[:], in0=yf[:], in1=yf[:], op=ALU.mult), reads=[yf], writes=[W0])
                k.op("dve", lambda e: e.tensor_reduce(out=s16b[:], in_=v3(W0), axis=AX.X, op=ALU.add), reads=[W0], writes=[s16b])
                rstd_from(s16b, s16c, 1.0 / 64, GN_EPS)
                k.op("dve", lambda e: e.tensor_tensor(out=v3(yf), in0=v3(yf), in1=bc16(s16c), op=ALU.mult), reads=[yf, s16c], writes=[yf])
                k.op("dve", lambda e: e.tensor_tensor(out=yf[:], in0=yf[:], in1=lnxg[:], op=ALU.mult), reads=[yf, lnxg], writes=[yf])
                k.op("dve", lambda e: e.tensor_tensor(out=yf[:], in0=yf[:], in1=lnxb[:], op=ALU.add), reads=[yf, lnxb], writes=[yf])
                yield
                k.op("dve", lambda e: e.tensor_tensor(out=b0t[:], in0=b0t[:], in1=b1t[:], op=ALU.add), reads=[b0t, b1t], writes=[b0t])
                k.op("dve", lambda e: e.tensor_tensor(out=v3(W0), in0=zv[:, 0:D].rearrange("p (h n) -> p h n", n=64), in1=bc16(b0t),
                                                      op=ALU.mult), reads=[zv, b0t], writes=[W0])
                k.op("dve", lambda e: e.tensor_tensor(out=yf[:], in0=yf[:], in1=W0[:], op=ALU.add), reads=[yf, W0], writes=[yf])
                k.op("act", lambda e: e.activation(out=gsb[:], in_=zv[:, D:D + 128], func=AF.Sigmoid), reads=[zv], writes=[gsb])
                transpose8(gsb, gT, nblk=1)
                pp, p0, p1 = pair()

                def f(e):
                    e.matmul(pp[:, 0:512], lhsT=gT[:, 0, :], rhs=gup[:, 0:512], start=True, stop=True)
                    return e.matmul(pp[:, 512:1024], lhsT=gT[:, 0, :], rhs=gup[:, 512:1024], start=True, stop=True)
                k.op("pe", f, reads=[gT, gup], writes=[p0, p1])
                k.op("dve", lambda e: e.tensor_tensor(out=actb[:], in0=pp[:, :], in1=yf[:], op=ALU.mult), reads=[p0, p1, yf], writes=[actb])
                yield
                pp, p0, p1 = proj(actb, wa, actT)
                yield
                k.op("act", lambda e: e.activation(out=W0[:], in_=zrs[:, 2048:3072], func=AF.Sigmoid), reads=[zrs], writes=[W0])
                k.op("dve", lambda e: e.tensor_tensor(out=W2[:], in0=pp[:, :], in1=W0[:], op=ALU.mult), reads=[p0, p1, W0], writes=[W2])
                yield
                k.op("act", lambda e: e.activation(out=ub[:], in_=zrs[:, 0:1024], func=AF.Gelu_apprx_tanh), reads=[zrs], writes=[ub])
                k.op("act", lambda e: e.activation(out=W1[:], in_=zrs[:, 1024:2048], func=AF.Gelu_apprx_tanh), reads=[zrs], writes=[W1])
                for c in range(2):
                    k.op("dve", lambda e: e.bn_stats(out=bnst[:, c, :], in_=W1[:, cs(c * 512, 512)]), reads=[W1], writes=[bnst])
                k.op("dve", lambda e: e.bn_aggr(out=mv[:], in_=bnst[:].rearrange("p a b -> p (a b)")), reads=[bnst], writes=[mv])
                rstd_from_ap(mv, 1, rsv, EPS)
                k.op("dve", lambda e: e.tensor_scalar(out=W1[:], in0=W1[:], scalar1=mv[:, 0:1], scalar2=rsv[:, 0:1], op0=ALU.subtract,
                                                      op1=ALU.mult), reads=[W1, mv, rsv], writes=[W1])
                k.op("dve", lambda e: e.tensor_tensor(out=vcb[:], in0=W1[:], in1=lnvg[:], op=ALU.mult), reads=[W1, lnvg], writes=[vcb])
                yield
                pp, p0, p1 = pair()

                def f(e):
                    for g in range(8):
                        ins = e.matmul(pp[:, cs(g * 128, 128)], lhsT=wsT[:, g, :], rhs=vcb[:, cs(g * 128, 128)], start=True, stop=True)
                    return ins
                k.op("pe", f, reads=[wsT, vcb], writes=[p0, p1])
                k.op("dve", lambda e: e.tensor_tensor(out=W1[:].rearrange("p (g c) -> p g c", c=128),
                                                      in0=pp[:, :].rearrange("p (g c) -> p g c", c=128),
                                                      in1=bsT[:].unsqueeze(2).to_broadcast([128, 8, 128]), op=ALU.add),
                     reads=[p0, p1, bsT], writes=[W1])
                k.op("dve", lambda e: e.tensor_tensor(out=actb[:], in0=W1[:], in1=ub[:], op=ALU.mult), reads=[W1, ub], writes=[actb])
                yield
                pp, p0, p1 = proj(actb, wb, actT)
                yield
                k.op("act", lambda e: e.activation(out=W0[:], in_=zrs[:, 3072:4096], func=AF.Sigmoid), reads=[zrs], writes=[W0])
                k.op("dve", lambda e: e.tensor_tensor(out=W1[:], in0=pp[:, :], in1=W0[:], op=ALU.mult), reads=[p0, p1, W0], writes=[W1])
                k.op("dve", lambda e: e.tensor_tensor(out=actb[:], in0=W1[:], in1=W2[:], op=ALU.add), reads=[W1, W2], writes=[actb])
                yield
                pp, p0, p1 = proj(actb, wo, actT)
                yield
                k.op("dve", lambda e: e.tensor_tensor(out=W0[:], in0=pp[:, :], in1=gate1[:], op=ALU.mult), reads=[p0, p1, gate1], writes=[W0])
                k.op("dve", lambda e: e.tensor_tensor(out=W0[:], in0=W0[:], in1=xt[:], op=ALU.add), reads=[W0, xt], writes=[W0])
                k.dma("sp", x1_d[rows, :], W0[:], reads=[W0], writes=[R_x1(i)], sembuf=W0)
                yield

            pending = list(range(NT))
            live = []
            while pending or live:
                if len(live) < 2 and pending:
                    ti = pending.pop(0)
                    live.append(tile_gen(ti, sets[ti % 2]))
                    if len(live) == 2 and ti == 1:
                        for _ in range(5):
                            next(live[0])
                for g in list(live):
                    try:
                        next(g)
                    except StopIteration:
                        live.remove(g)
            k.barrier()

    def rstd_from_ap(mvb, col, out_rstd, eps):
        k.op("act", lambda e: e.activation(out=out_rstd[:], in_=mvb[:, col:col + 1], func=AF.Ln, bias=eps_t(eps)[:], scale=1.0),
             reads=[mvb, eps_t(eps)], writes=[out_rstd])
        k.op("act", lambda e: e.activation(out=out_rstd[:], in_=out_rstd[:], func=AF.Exp, scale=-0.5),
             reads=[out_rstd], writes=[out_rstd])

    def phaseC(l, last):
        with contextlib.ExitStack() as es:
            w1 = sbt(es, "w1", [128, 8, DFF], BF16)
            w2 = sbt(es, "w2", [128, 32, D], BF16)
            g2 = sbt(es, "g2", [128, D], F32)
            sh2 = sbt(es, "sh2", [128, D], F32)
            gate2 = sbt(es, "gate2", [128, D], F32)
            fg = sbt(es, "fg", [128, D], F32)
            xt = [sbt(es, "xc%d" % i, [128, D], F32) for i in range(2)]
            W0 = sbt(es, "Wc0", [128, D], F32)
            hb = sbt(es, "hb2", [128, D], BF16)
            hT = sbt(es, "hT2", [128, 8, 128], BF16)
            rl = sbt(es, "rl", [128, 512], BF16)
            hid = sbt(es, "hid", [128, DFF], BF16)
            hidT = sbt(es, "hidT", [128, 32, 128], BF16)
            ss = sbt(es, "ssc", [128, 1], F32)
            rstd = sbt(es, "rstdc", [128, 1], F32)
            cast_load_rows(lambda kc: w1[:, kc, :], w1_d[l], 8, DFF, w1)
            cast_load_rows(lambda kc: w2[:, kc, :], w2_d[l], 32, D, w2)
            load_mod(sh2, l, 3)
            load_mod(g2, l, 4)
            load_mod(gate2, l, 5)
            if last:
                bc_load(fg, fg_d)

            def load_x(i):
                k.dma("sp", xt[i % 2][:], x1_d[cs(i * 128, 128), :], reads=[R_x1(i)], writes=[xt[i % 2]])
            load_x(0)
            for i in range(NT):
                x = xt[i % 2]
                if i + 1 < NT:
                    load_x(i + 1)
                k.op("act", lambda e: e.activation(out=W0[:], in_=x[:], func=AF.Square), reads=[x], writes=[W0])
                k.op("dve", lambda e: e.tensor_reduce(out=ss[:], in_=W0[:], axis=AX.X, op=ALU.add), reads=[W0], writes=[ss])
                rstd_from(ss, rstd, 1.0 / D, EPS)
                k.op("dve", lambda e: e.scalar_tensor_tensor(out=W0[:], in0=x[:], scalar=rstd[:, 0:1], in1=g2[:], op0=ALU.mult,
                                                             op1=ALU.mult), reads=[x, rstd, g2], writes=[W0])
                k.op("dve", lambda e: e.tensor_tensor(out=hb[:], in0=W0[:], in1=sh2[:], op=ALU.add), reads=[W0, sh2], writes=[hb])
                transpose8(hb, hT)
                for n in range(8):
                    pb = bank()

                    def f(e):
                        for kc in range(8):
                            ins = e.matmul(pb[:], lhsT=hT[:, kc, :], rhs=w1[:, kc, cs(n * 512, 512)], start=(kc == 0), stop=(kc == 7))
                        return ins
                    k.op("pe", f, reads=[hT, w1], writes=[pb])
                    k.op("act", lambda e: e.activation(out=rl[:], in_=pb[:], func=AF.Relu), reads=[pb], writes=[rl])
                    k.op("dve", lambda e: e.tensor_tensor(out=hid[:, cs(n * 512, 512)], in0=rl[:], in1=rl[:], op=ALU.mult),
                         reads=[rl], writes=[hid])
                for q in range(4):
                    pb = bank()
                    pv = pb[:].bitcast(BF16)

                    def f(e):
                        for j in range(8):
                            ins = e.transpose(out=pv[:, cs(j * 128, 128)], in_=hid[:, cs((q * 8 + j) * 128, 128)], identity=ident_b[:])
                        return ins
                    k.op("pe", f, reads=[hid, ident_b], writes=[pb])
                    if q % 2 == 0:
                        k.op("act", lambda e: e.copy(out=hidT[:, q * 8:q * 8 + 8, :].rearrange("p a b -> p (a b)"), in_=pv[:, :]),
                             reads=[pb], writes=[hidT])
                    else:
                        k.op("dve", lambda e: e.tensor_copy(out=hidT[:, q * 8:q * 8 + 8, :].rearrange("p a b -> p (a b)"), in_=pv[:, :]),
                             reads=[pb], writes=[hidT])
                pp, p0, p1 = pair()

                def f(e):
                    for nn in range(2):
                        for kc in range(32):
                            ins = e.matmul(pp[:, cs(nn * 512, 512)], lhsT=hidT[:, kc, :], rhs=w2[:, kc, cs(nn * 512, 512)],
                                           start=(kc == 0), stop=(kc == 31))
                    return ins
                k.op("pe", f, reads=[hidT, w2], writes=[p0, p1])
                k.op("dve", lambda e: e.tensor_tensor(out=W0[:], in0=pp[:, :], in1=gate2[:], op=ALU.mult), reads=[p0, p1, gate2], writes=[W0])
                k.op("dve", lambda e: e.tensor_tensor(out=W0[:], in0=W0[:], in1=x[:], op=ALU.add), reads=[W0, x], writes=[W0])
                rows = cs(i * 128, 128)
                if not last:
                    k.dma("sp", x2_d[rows, :], W0[:], reads=[W0], writes=[R_x2(i)], sembuf=W0)
                else:
                    k.op("act", lambda e: e.activation(out=x[:], in_=W0[:], func=AF.Square), reads=[W0], writes=[x])
                    k.op("dve", lambda e: e.tensor_reduce(out=ss[:], in_=x[:], axis=AX.X, op=ALU.add), reads=[x], writes=[ss])
                    rstd_from(ss, rstd, 1.0 / D, EPS)
                    k.op("dve", lambda e: e.scalar_tensor_tensor(out=W0[:], in0=W0[:], scalar=rstd[:, 0:1], in1=fg[:], op0=ALU.mult,
                                                                 op1=ALU.mult), reads=[W0, rstd, fg], writes=[W0])
                    k.dma("sp", y_d[rows, :], W0[:], reads=[W0], writes=[R_y(i)], sembuf=W0)
            k.barrier()

    R_xin = DR("xin")
    steps = [lambda: phaseP(0), lambda: phaseP(1)]
    for l in range(2):
        xs, Rx = (x_d, R_xin) if l == 0 else (x2_d, R_x2)
        steps += [lambda l=l, xs=xs, Rx=Rx: phaseA1(l, xs, Rx), lambda l=l: phaseS2(l),
                  lambda l=l, xs=xs, Rx=Rx: phaseM(l, xs, Rx), lambda l=l: phaseC(l, last=(l == 1))]
    for st_ in steps[:upto]:
        st_()
    k.barrier()
    ges.close()
    return nc, k


def _shift_mats(kind):
    m = np.zeros((4, 3, 128, 128), np.float32)
    eye = np.eye(128, dtype=np.float32)
    t = np.arange(128)
    for cls in range(4):
        cur = np.zeros((128, 128), np.float32)
        nbe = np.zeros((128, 128), np.float32)
        nbo = np.zeros((128, 128), np.float32)
        if kind == "sample":
            if cls == 0:
                for to in t:
                    if to % 64 != 0:
                        cur[to - 1, to] = 1
            elif cls == 1:
                for to in t:
                    if to % 64 != 63:
                        cur[to + 1, to] = 1
            elif cls == 2:
                for to in t:
                    if to >= 64:
                        cur[to - 64, to] = 1
                    else:
                        nbe[to + 64, to] = 1
                        nbo[to + 64, to] = 1
            else:
                for to in t:
                    if to < 64:
                        cur[to + 64, to] = 1
                    else:
                        nbe[to - 64, to] = 1
                        nbo[to - 64, to] = 1
        else:
            if cls in (0, 2):
                for to in t:
                    if to >= 1:
                        cur[to - 1, to] = 1
                nbo[127, 0] = 1
            else:
                for to in t:
                    if to <= 126:
                        cur[to + 1, to] = 1
                nbe[0, 127] = 1
        m[cls, 0] = cur - eye
        m[cls, 1] = nbe
        m[cls, 2] = nbo
    return np.ascontiguousarray(m.reshape(12, 128, 128).transpose(1, 0, 2).reshape(128, 12 * 128))


def _tri4():
    s = np.arange(128)[:, None]
    t = np.arange(128)[None, :]
    m = np.stack([(s <= t), (s >= t), (s < t), (s > t)], axis=1).astype(np.float32)
    return np.ascontiguousarray(m.reshape(128, 512))


def _state_to_H(st):
    a = st.reshape(2, 2, 8, 2, 64, 64)
    a = a.transpose(0, 1, 3, 5, 2, 4)
    return np.ascontiguousarray(a.reshape(2, 2, 128, 512))


def _H_to_state(Hm):
    lead = Hm.shape[:-2]
    a = Hm.reshape(lead + (2, 64, 8, 64))
    nl = len(lead)
    perm = tuple(range(nl)) + (nl + 2, nl + 0, nl + 3, nl + 1)
    a = a.transpose(perm)
    return a.reshape(lead + (16, 64, 64))


def make_core_inputs(kind, x_tokens, cond_vec, state_lh, shared):
    d = dict(shared)
    d["x"] = np.ascontiguousarray(x_tokens, dtype=np.float32)
    d["cond"] = np.ascontiguousarray(cond_vec.reshape(8, 128).T, dtype=np.float32)
    d["state0"] = _state_to_H(state_lh)
    d["cmask"] = np.full((128, 1), 1.0 if kind == "sample" else 0.0, np.float32)
    d["shm"] = _shift_mats(kind)
    return d


def shared_inputs(w_ada, b_ada, norm1_g, norm2_g, w_in, mu_shift, w0, w_up, a0, a_up, g_up, k_k, k_a, r_k, lnx_g,
                  lnx_b, w_branch_a, ln_v_g, w_s, b_s, w_branch_b, w_out, w1, w2, final_g):
    f = lambda a: np.ascontiguousarray(np.asarray(a), dtype=np.float32)
    wup_aug = np.concatenate([np.asarray(w_up), np.asarray(w0)[:, :, None, :]], axis=2)
    aup_aug = np.concatenate([np.asarray(a_up), np.asarray(a0)[:, :, None, :]], axis=2)
    wsT = np.asarray(w_s).transpose(0, 3, 1, 2).reshape(2, 128, 8 * 128)
    bsT = np.asarray(b_s).transpose(0, 2, 1)
    return dict(ident=np.eye(128, dtype=np.float32), tri4=_tri4(), w_ada=f(w_ada), b_ada=f(b_ada), norm1_g=f(norm1_g),
                norm2_g=f(norm2_g), w_in=f(w_in), mu_shift=f(mu_shift), wup_aug=f(wup_aug), aup_aug=f(aup_aug), g_up=f(g_up),
                k_k=f(k_k), k_a=f(k_a), r_k=f(np.asarray(r_k).reshape(2, D)), lnx_g=f(lnx_g), lnx_b=f(lnx_b),
                w_branch_a=f(w_branch_a), ln_v_g=f(ln_v_g), wsT=f(wsT), bsT=f(bsT), w_branch_b=f(w_branch_b), w_out=f(w_out),
                w1=f(w1), w2=f(w2), final_g=f(final_g))


_PROG = {}


def kernel(x_prompt, x_sample, state_rwkv, c, c_ctx, w_ada, b_ada, norm1_g, norm2_g, w_in, mu_shift,
           w0, w_up, a0, a_up, g_up, k_k, k_a, r_k, lnx_g, lnx_b, w_branch_a, ln_v_g, w_s, b_s,
           w_branch_b, w_out, w1, w2, final_g):
    NT = 32
    x_prompt = np.asarray(x_prompt, dtype=np.float32)
    x_sample = np.asarray(x_sample, dtype=np.float32)
    state_rwkv = np.asarray(state_rwkv, dtype=np.float32)
    c = np.asarray(c, dtype=np.float32)
    c_ctx = np.asarray(c_ctx, dtype=np.float32)
    shared = shared_inputs(w_ada, b_ada, norm1_g, norm2_g, w_in, mu_shift, w0, w_up, a0, a_up, g_up, k_k, k_a, r_k,
                           lnx_g, lnx_b, w_branch_a, ln_v_g, w_s, b_s, w_branch_b, w_out, w1, w2, final_g)
    in_maps = []
    for b in range(4):
        in_maps.append(make_core_inputs("sample", x_sample[b], c[b], state_rwkv[b], shared))
    zero_state = np.zeros((2, 2, 16, 64, 64), np.float32)
    for q in range(4):
        xs = np.zeros((NT * 128, D), np.float32)
        xs[:2048] = x_prompt[8 * q:8 * q + 8].reshape(2048, D)
        xs[2048:] = xs[:2048]
        in_maps.append(make_core_inputs("prompt", xs, c_ctx, zero_state, shared))
    if NT not in _PROG:
        _PROG[NT] = build_program(NT)[0]
    res = run_bass_kernel_spmd(_PROG[NT], in_maps, core_ids=list(range(8)))
    r = res.results
    y_sample = np.stack([r[b]["y"] for b in range(4)], axis=0)
    y_prompt = np.concatenate([r[4 + q]["y"][:2048].reshape(8, 256, D) for q in range(4)], axis=0)
    sts = []
    for q in range(4):
        so = r[4 + q]["st_out"]
        so = so[:, :, :8]
        s = _H_to_state(so)
        sts.append(np.transpose(s, (2, 0, 1, 3, 4, 5)))
    new_state = np.ascontiguousarray(np.concatenate(sts, axis=0), dtype=np.float32)
    return (np.ascontiguousarray(y_prompt, dtype=np.float32), np.ascontiguousarray(y_sample, dtype=np.float32), new_state)
```

```python
import contextlib
import os
DBG = int(os.environ.get('KDBG', '99'))
KSKIP = os.environ.get('KSKIP', '')
import numpy as np
import concourse.bass as bass
import concourse.mybir as mybir
from concourse.bass_utils import run_bass_kernel_spmd

F32 = mybir.dt.float32
BF16 = mybir.dt.bfloat16
ALU = mybir.AluOpType
AF = mybir.ActivationFunctionType
AX = mybir.AxisListType

D = 1024
CR = 3456
DIN = 7552
DFF = 4096
NHEAD = 16
EPS = 1e-6
GN_EPS = 64e-5
DSC = float(np.exp(-0.5))


class Buf:
    __slots__ = ("name", "t", "w", "r", "dsem", "dcnt")

    def __init__(self, name, t=None):
        self.name = name
        self.t = t
        self.w = None
        self.r = []
        self.dsem = None
        self.dcnt = 0

    def __getitem__(self, idx):
        return self.t[idx]


class K:
    def __init__(self, nc):
        self.nc = nc
        self.eng = {"pe": nc.tensor, "act": nc.scalar, "dve": nc.vector, "pool": nc.gpsimd, "sp": nc.sync}
        self.sem = {}
        self.cnt = {}
        for e in self.eng:
            self.sem[e] = nc.alloc_semaphore(name="s_" + e)
            self.cnt[e] = 0
        self.waited = {}
        self.dsems = {}
        self.free_dsems = []
        self.ninstr = 0
        self.uid = 0

    def _wait(self, e, tok):
        if tok is None:
            return
        key, val = tok
        if key == e and e == "pe":
            return
        kk = (e, key)
        if self.waited.get(kk, 0) >= val:
            return
        self.waited[kk] = val
        self.eng[e].wait_ge(self.sem[key], val)
        self.ninstr += 1

    def _deps(self, e, reads, writes):
        for b in reads:
            self._wait(e, b.w)
        for b in writes:
            self._wait(e, b.w)
            for tok in b.r:
                self._wait(e, tok)

    def _commit(self, tok, reads, writes):
        for b in reads:
            if b not in writes:
                b.r.append(tok)
                if len(b.r) > 10:
                    best = {}
                    for k_, v_ in b.r:
                        if best.get(k_, -1) < v_:
                            best[k_] = v_
                    b.r = list(best.items())
        for b in writes:
            b.w = tok
            b.r = []

    def op(self, e, fn, reads=(), writes=()):
        reads = [b for b in reads if b is not None]
        writes = [b for b in writes if b is not None]
        self._deps(e, reads, writes)
        ins = fn(self.eng[e])
        self.cnt[e] += 1
        ins.then_inc(self.sem[e], 1)
        self.ninstr += 1
        self._commit((e, self.cnt[e]), reads, writes)

    def dma(self, q, out_ap, in_ap, reads=(), writes=(), sembuf=None, **kw):
        reads = [b for b in reads if b is not None]
        writes = [b for b in writes if b is not None]
        if sembuf is None:
            sembuf = (writes + reads)[0]
        if sembuf.dsem is None:
            if self.free_dsems:
                key, base = self.free_dsems.pop()
                sembuf.dcnt = base
            else:
                key = "d%d" % len(self.sem)
                self.sem[key] = self.nc.alloc_semaphore(name=key)
            self.dsems[key] = sembuf
            sembuf.dsem = key
        self._deps(q, reads, writes)
        ins = self.eng[q].dma_start(out=out_ap, in_=in_ap, **kw)
        sembuf.dcnt += 16
        ins.then_inc(self.sem[sembuf.dsem], 16)
        self.ninstr += 1
        self._commit((sembuf.dsem, sembuf.dcnt), reads, writes)

    def barrier(self):
        toks = [(e, self.cnt[e]) for e in self.eng if self.cnt[e] > 0]
        toks += [(key, b.dcnt) for key, b in self.dsems.items() if b.dcnt > 0]
        for e in self.eng:
            for tok in toks:
                if tok[0] != e:
                    self._wait(e, tok)
        for key, b in list(self.dsems.items()):
            if not getattr(b, "keep", False):
                self.free_dsems.append((key, b.dcnt))
                b.dsem = None
                del self.dsems[key]


def cs(a, n):
    return slice(a, a + n)


def hc(h):
    return slice((h % 2) * 512 + (h // 2) * 64, (h % 2) * 512 + (h // 2) * 64 + 64)


def build_program(NT, upto=99):
    T = NT * 128
    NG = NT // 2
    nc = bass.Bass("TRN2", target_bir_lowering=False)
    k = K(nc)

    def din(name, shape):
        return nc.dram_tensor(name, list(shape), F32, kind="ExternalInput").ap()

    x_d = din("x", [T, D])
    cond_d = din("cond", [128, 8])
    state0_d = din("state0", [2, 2, 128, 512])
    cmask_d = din("cmask", [128, 1])
    shm_d = din("shm", [128, 12 * 128])
    ident_d = din("ident", [128, 128])
    tri4_d = din("tri4", [128, 4 * 128])
    w_ada_d = din("w_ada", [2, D, 6 * D])
    b_ada_d = din("b_ada", [2, 6 * D])
    n1g_d = din("norm1_g", [2, D])
    n2g_d = din("norm2_g", [2, D])
    w_in_d = din("w_in", [2, D, DIN])
    mu_d = din("mu_shift", [2, CR])
    wup_d = din("wup_aug", [2, 2, 65, D])
    aup_d = din("aup_aug", [2, 2, 65, D])
    gup_d = din("g_up", [2, 128, D])
    kk_d = din("k_k", [2, D])
    ka_d = din("k_a", [2, D])
    rk_d = din("r_k", [2, D])
    lnxg_d = din("lnx_g", [2, D])
    lnxb_d = din("lnx_b", [2, D])
    wa_d = din("w_branch_a", [2, D, D])
    lnvg_d = din("ln_v_g", [2, D])
    wsT_d = din("wsT", [2, 128, 8 * 128])
    bsT_d = din("bsT", [2, 128, 8])
    wb_d = din("w_branch_b", [2, D, D])
    wo_d = din("w_out", [2, D, D])
    w1_d = din("w1", [2, D, DFF])
    w2_d = din("w2", [2, DFF, D])
    fg_d = din("final_g", [D])

    y_d = nc.dram_tensor("y", [T, D], F32, kind="ExternalOutput").ap()
    st_d = nc.dram_tensor("st_out", [2, 2, NG, 128, 512], F32, kind="ExternalOutput").ap()

    def dscr(name, shape, dt):
        return nc.dram_tensor(name, list(shape), dt, kind="Internal").ap()

    modbc_d = dscr("modbc", [2, 128, 6 * D], F32)
    zr_d = dscr("zr_s", [T, CR], BF16)
    zrest_d = dscr("zrest_s", [T, 4096], BF16)
    ysc_d = dscr("ysc_s", [2, T, D], F32)
    bon_d = dscr("bon_s", [2, T, 16], F32)
    x1_d = dscr("x1_s", [T, D], F32)
    x2_d = dscr("x2_s", [T, D], F32)

    class DR:
        def __init__(self, nm):
            self.b = {}
            self.nm = nm

        def __call__(self, *key):
            if key not in self.b:
                self.b[key] = Buf(self.nm + str(key))
            return self.b[key]

    R_mod, R_zr, R_zrest, R_ysc, R_bon, R_x1, R_x2, R_y, R_st = [DR(n) for n in
        ("mod", "zr", "zrest", "ysc", "bon", "x1", "x2", "y", "st")]

    PP = [nc.alloc_psum_tensor("psum%d" % i, [128, 1024], F32) for i in range(4)]
    PB = []
    for i in range(8):
        PB.append(Buf("pb%d" % i, PP[i // 2][:, cs((i % 2) * 512, 512)]))
    pst = {"b": 0, "p": 0}

    def bank():
        b = PB[pst["b"] % 8]
        pst["b"] += 1
        return b

    def pair():
        if pst["b"] % 2:
            pst["b"] += 1
        i = (pst["b"] % 8) // 2
        pst["b"] += 2
        return PP[i], PB[2 * i], PB[2 * i + 1]

    def sbt(es, name, shape, dt):
        k.uid += 1
        t = es.enter_context(nc.sbuf_tensor("%s_%d" % (name, k.uid), list(shape), dt))
        return Buf(name, t)

    ges = contextlib.ExitStack()
    ident_f = sbt(ges, "ident_f", [128, 128], F32)
    ident_b = sbt(ges, "ident_b", [128, 128], BF16)
    tri4 = sbt(ges, "tri4", [128, 4, 128], F32)
    ones_f = sbt(ges, "ones_f", [128, 1], F32)
    cmask = sbt(ges, "cmask", [128, 1], F32)
    k.dma("sp", ident_f[:], ident_d, writes=[ident_f])
    k.dma("sp", tri4[:].rearrange("p a b -> p (a b)"), tri4_d, writes=[tri4])
    k.dma("sp", cmask[:], cmask_d, writes=[cmask])
    k.op("dve", lambda e: e.tensor_copy(out=ident_b[:], in_=ident_f[:]), reads=[ident_f], writes=[ident_b])
    k.op("dve", lambda e: e.memset(ones_f[:], 1.0), writes=[ones_f])

    def bc_load(buf, dvec):
        k.dma("sp", buf[:], dvec.partition_broadcast(128), writes=[buf])

    def cast_load_rows(buf_ap_fn, dsrc, nk, ncol, buf):
        for kc in range(nk):
            k.dma("pool", buf_ap_fn(kc), dsrc[cs(kc * 128, 128), :], writes=[buf], max_dma_last_dim=4096)

    def rstd_from(e_ss, out_rstd, scale, eps):
        k.op("act", lambda e: e.activation(out=out_rstd[:], in_=e_ss[:], func=AF.Ln, bias=eps_t(eps)[:], scale=scale),
             reads=[e_ss, eps_t(eps)], writes=[out_rstd])
        k.op("act", lambda e: e.activation(out=out_rstd[:], in_=out_rstd[:], func=AF.Exp, scale=-0.5),
             reads=[out_rstd], writes=[out_rstd])

    eps_tiles = {}

    def eps_t(v):
        if v not in eps_tiles:
            b = sbt(ges, "eps%d" % len(eps_tiles), [128, 1], F32)
            k.op("dve", lambda e: e.memset(b[:], float(v)), writes=[b])
            eps_tiles[v] = b
        return eps_tiles[v]

    for v in (EPS, GN_EPS, 1e-12):
        eps_t(v)

    def phaseP(l):
        with contextlib.ExitStack() as es:
            wad = sbt(es, "wad", [128, 8, 6 * D], BF16)
            ba = sbt(es, "ba", [128, 6 * D], F32)
            mod = sbt(es, "mod", [128, 6 * D], F32)
            n1g = sbt(es, "n1g", [128, D], F32)
            n2g = sbt(es, "n2g", [128, D], F32)
            cnd = sbt(es, "cnd", [128, 8], F32)
            scb = sbt(es, "scb", [128, 8, 128], BF16)
            cast_load_rows(lambda kc: wad[:, kc, :], w_ada_d[l], 8, 6 * D, wad)
            bc_load(ba, b_ada_d[l])
            bc_load(n1g, n1g_d[l])
            bc_load(n2g, n2g_d[l])
            k.dma("sp", cnd[:], cond_d, writes=[cnd])
            k.op("act", lambda e: e.activation(out=cnd[:], in_=cnd[:], func=AF.Silu), reads=[cnd], writes=[cnd])
            k.op("dve", lambda e: e.tensor_copy(out=scb[:], in_=cnd[:].unsqueeze(2).to_broadcast([128, 8, 128])),
                 reads=[cnd], writes=[scb])
            for n in range(12):
                pb = bank()

                def f(e):
                    for kc in range(8):
                        ins = e.matmul(pb[:], lhsT=scb[:, kc, :], rhs=wad[:, kc, cs(n * 512, 512)],
                                       start=(kc == 0), stop=(kc == 7))
                    return ins
                k.op("pe", f, reads=[scb, wad], writes=[pb])
                k.op("dve", lambda e: e.tensor_tensor(out=mod[:, cs(n * 512, 512)], in0=pb[:], in1=ba[:, cs(n * 512, 512)],
                                                      op=ALU.add), reads=[pb, ba], writes=[mod])
            k.op("dve", lambda e: e.scalar_tensor_tensor(out=mod[:, cs(D, D)], in0=mod[:, cs(D, D)], scalar=1.0, in1=n1g[:],
                                                         op0=ALU.add, op1=ALU.mult), reads=[mod, n1g], writes=[mod])
            k.op("dve", lambda e: e.scalar_tensor_tensor(out=mod[:, cs(4 * D, D)], in0=mod[:, cs(4 * D, D)], scalar=1.0,
                                                         in1=n2g[:], op0=ALU.add, op1=ALU.mult), reads=[mod, n2g], writes=[mod])
            k.dma("sp", modbc_d[l], mod[:], reads=[mod], writes=[R_mod(l)], sembuf=mod)
            k.barrier()

    def load_mod(buf, l, j):
        k.dma("sp", buf[:], modbc_d[l][:, cs(j * D, D)], reads=[R_mod(l)], writes=[buf])

    def phaseA1(l, xsrc_d, R_xsrc):
        with contextlib.ExitStack() as es:
            win = sbt(es, "win", [128, 8, DIN], BF16)
            g1 = sbt(es, "g1", [128, D], F32)
            sh1 = sbt(es, "sh1", [128, D], F32)
            mu = sbt(es, "mu", [128, CR], F32)
            shm = sbt(es, "shm", [128, 12, 128], BF16)
            xt = [sbt(es, "xt0", [128, D], F32)]
            xt.append(xt[0])
            sq = sbt(es, "sq", [128, D], F32)
            hb = sbt(es, "hb", [128, D], BF16)
            hT = sbt(es, "hT", [128, 8, 128], BF16)
            ss = sbt(es, "ss", [128, 1], F32)
            rstd = sbt(es, "rstd", [128, 1], F32)
            zb = [sbt(es, "zb%d" % i, [128, CR], BF16) for i in range(2)]
            zm = [sbt(es, "zm%d" % i, [128, CR], BF16) for i in range(3)]
            zst = sbt(es, "zst", [128, CR], BF16)
            rst = [sbt(es, "rst%d" % i, [128, 512], BF16) for i in range(4)]
            cast_load_rows(lambda kc: win[:, kc, :], w_in_d[l], 8, DIN, win)
            k.dma("pool", shm[:].rearrange("p a b -> p (a b)"), shm_d, writes=[shm], max_dma_last_dim=4096)
            load_mod(sh1, l, 0)
            load_mod(g1, l, 1)
            bc_load(mu, mu_d[l])
            rsti = [0]

            def load_x(i):
                k.dma("sp", xt[i % 2][:], xsrc_d[cs(i * 128, 128), :], reads=[R_xsrc(i)], writes=[xt[i % 2]])

            def stage1(i):
                x = xt[i % 2]
                if DBG < 2:
                    if i + 1 < NT:
                        load_x(i + 1)
                    return
                k.op("act", lambda e: e.activation(out=sq[:], in_=x[:], func=AF.Square), reads=[x], writes=[sq])
                k.op("dve", lambda e: e.tensor_reduce(out=ss[:], in_=sq[:], axis=AX.X, op=ALU.add), reads=[sq], writes=[ss])
                rstd_from(ss, rstd, 1.0 / D, EPS)
                k.op("dve", lambda e: e.scalar_tensor_tensor(out=x[:], in0=x[:], scalar=rstd[:, 0:1], in1=g1[:],
                                                             op0=ALU.mult, op1=ALU.mult), reads=[x, rstd, g1], writes=[x])
                k.op("dve", lambda e: e.tensor_tensor(out=hb[:], in0=x[:], in1=sh1[:], op=ALU.add),
                     reads=[x, sh1], writes=[hb])
                if i + 1 < NT:
                    load_x(i + 1)
                if DBG < 3:
                    return
                transpose8(hb, hT)
                if DBG < 4:
                    return
                zbi, zmi = zb[i % 2], zm[i % 3]
                col = 0
                ci = 0
                while col < DIN:
                    if col < CR:
                        n = min(512, CR - col)
                    else:
                        n = 512
                    pb = bank()

                    def f(e):
                        for kc in range(8):
                            ins = e.matmul(pb[:, 0:n], lhsT=hT[:, kc, :], rhs=win[:, kc, cs(col, n)],
                                           start=(kc == 0), stop=(kc == 7))
                        return ins
                    k.op("pe", f, reads=[hT, win], writes=[pb])
                    if 'p' in KSKIP:
                        pass
                    elif col < CR:
                        if 'z' not in KSKIP:
                            k.op("act", lambda e: e.copy(out=zbi[:, cs(col, n)], in_=pb[:, 0:n]), reads=[pb], writes=[zbi])
                        if 'm' not in KSKIP:
                            k.op("dve", lambda e: e.tensor_tensor(out=zmi[:, cs(col, n)], in0=zbi[:, cs(col, n)], in1=mu[:, cs(col, n)],
                                                                  op=ALU.mult), reads=[zbi, mu], writes=[zmi])
                    else:
                        st = rst[rsti[0] % 4]
                        rsti[0] += 1
                        if ci % 2 == 0:
                            k.op("act", lambda e: e.copy(out=st[:], in_=pb[:]), reads=[pb], writes=[st])
                        else:
                            k.op("dve", lambda e: e.tensor_copy(out=st[:], in_=pb[:]), reads=[pb], writes=[st])
                        if 'r' not in KSKIP:
                            k.dma("sp", zrest_d[cs(i * 128, 128), cs(col - CR, 512)], st[:], reads=[st],
                                  writes=[R_zrest(i, (col - CR) // 512)], sembuf=st)
                    col += n
                    ci += 1

            def stage2(i):
                if DBG < 5:
                    return
                par = i % 2
                for cls in range(4):
                    nb = i - 1 if cls in (0, 2) else i + 1
                    for hh in range(2):
                        c0 = cls + 4 * 432 * hh
                        sl = slice(c0, c0 + 4 * 431 + 1, 4)
                        pb = bank()
                        srcs = [(ident_b[:], zb[i % 2], ident_b), (shm[:, 3 * cls, :], zm[i % 3], shm)]
                        if 0 <= nb < NT:
                            srcs.append((shm[:, 3 * cls + 1 + par, :], zm[nb % 3], shm))

                        def f(e):
                            for j, (lt, rb, _) in enumerate(srcs):
                                ins = e.matmul(pb[:, 0:432], lhsT=lt, rhs=rb[:, sl], start=(j == 0), stop=(j == len(srcs) - 1))
                            return ins
                        k.op("pe", f, reads=[s[1] for s in srcs] + [ident_b, shm], writes=[pb])
                        if hh == 0:
                            k.op("act", lambda e: e.copy(out=zst[:, sl], in_=pb[:, 0:432]), reads=[pb], writes=[zst])
                        else:
                            k.op("dve", lambda e: e.tensor_copy(out=zst[:, sl], in_=pb[:, 0:432]), reads=[pb], writes=[zst])
                k.dma("sp", zr_d[cs(i * 128, 128), :], zst[:], reads=[zst], writes=[R_zr(i)], sembuf=zst)

            load_x(0)
            stage1(0)
            for i in range(NT):
                if i + 1 < NT:
                    stage1(i + 1)
                stage2(i)
            k.barrier()

    def transpose8(src, dst, nblk=8, src_off=0):
        pb = bank()
        pv = pb[:].bitcast(BF16)

        def f(e):
            for j in range(nblk):
                ins = e.transpose(out=pv[:, cs(j * 128, 128)], in_=src[:, cs(src_off + j * 128, 128)], identity=ident_b[:])
            return ins
        k.op("pe", f, reads=[src, ident_b], writes=[pb])
        k.op("act", lambda e: e.copy(out=dst[:, 0:nblk, :].rearrange("p a b -> p (a b)"), in_=pv[:, 0:nblk * 128]),
             reads=[pb], writes=[dst])

    def phaseS(l, d):
        with contextlib.ExitStack() as es:
            kkc = sbt(es, "kkc", [128, D], F32)
            kac = sbt(es, "kac", [128, D], F32)
            rkc = sbt(es, "rkc", [128, D], F32)
            wup = sbt(es, "wup", [65, D], BF16)
            aup = sbt(es, "aup", [65, D], BF16)
            maskM = sbt(es, "maskM", [128, 4, 128], F32)
            maskN = sbt(es, "maskN", [128, 4, 128], F32)
            H = sbt(es, "H", [128, 8, 64], F32)
            Hb = sbt(es, "Hb", [128, 8, 64], BF16)
            zr = [sbt(es, "zr%d" % i, [128, CR], BF16) for i in range(2)]
            ldT = sbt(es, "ldT", [65, 2, 128], BF16)
            tw = sbt(es, "tw", [128, 128], BF16)
            SG = sbt(es, "SG", [128, D], F32)
            A = sbt(es, "A", [128, D], F32)
            KX = sbt(es, "KX", [128, D], F32)
            BP = sbt(es, "BP", [128, D], F32)
            KD = sbt(es, "KD", [128, D], F32)
            S0 = sbt(es, "S0", [128, D], F32)
            S1 = sbt(es, "S1", [128, D], F32)
            st16 = sbt(es, "st16", [128, 16], F32)
            rs16 = sbt(es, "rs16", [128, 16], F32)
            bon = sbt(es, "bon", [128, 16], F32)
            gC = sbt(es, "gC", [128, 8], F32)
            TM = sbt(es, "TM", [128, 4, D], BF16)
            Bg = sbt(es, "Bg", [128, D], BF16)
            Kg = sbt(es, "Kg", [128, D], BF16)
            FM = sbt(es, "FM", [128, 8, 4, 128], BF16)
            MB = sbt(es, "MB", [128, 16, 512], BF16)
            Pm = [sbt(es, "Pm%d" % i, [128, 16, 128], BF16) for i in range(2)]
            PT = [sbt(es, "PT%d" % i, [128, 16, 128], BF16) for i in range(2)]
            Xb = [sbt(es, "Xb%d" % i, [128, D], BF16) for i in range(2)]
            ysc = sbt(es, "ysc", [128, D], F32)

            bc_load(kkc, kk_d[l])
            bc_load(kac, ka_d[l])
            bc_load(rkc, rk_d[l])
            k.dma("pool", wup[:], wup_d[l, d], writes=[wup], max_dma_last_dim=4096)
            k.dma("pool", aup[:], aup_d[l, d], writes=[aup], max_dma_last_dim=4096)
            strict_i, incl_i, nmask_i = (2, 0, 3) if d == 0 else (3, 1, 2)
            for j, src in enumerate((strict_i, incl_i, strict_i, incl_i)):
                k.op("dve", lambda e: e.tensor_copy(out=maskM[:, j, :], in_=tri4[:, src, :]), reads=[tri4], writes=[maskM])
            for j in range(4):
                k.op("dve", lambda e: e.tensor_copy(out=maskN[:, j, :], in_=tri4[:, nmask_i, :]), reads=[tri4], writes=[maskN])
            tri_incl = tri4[:, incl_i, :]
            tri_excl = tri4[:, strict_i, :]
            tri_dg = tri4[:, nmask_i, :]
            k.op("dve", lambda e: e.memset(ldT[:], 1.0), writes=[ldT])
            k.dma("sp", H[:].rearrange("p a b -> p (a b)"), state0_d[l, d], writes=[H])
            k.op("act", lambda e: e.copy(out=Hb[:], in_=H[:]), reads=[H], writes=[Hb])

            order = list(range(NT)) if d == 0 else list(range(NT - 1, -1, -1))

            def load_zr(n):
                i = order[n]
                k.dma("sp", zr[n % 2][:], zr_d[cs(i * 128, 128), :], reads=[R_zr(i)], writes=[zr[n % 2]])

            load_zr(0)
            for n, i in enumerate(order):
                z = zr[n % 2]
                if n + 1 < NT:
                    load_zr(n + 1)
                rq = z[:, 0:D]
                kq = z[:, D:2 * D]
                vq = z[:, 2 * D:3 * D]
                k.op("act", lambda e: e.activation(out=tw[:, 0:64], in_=z[:, cs(3 * D + 64 * d, 64)], func=AF.Tanh),
                     reads=[z], writes=[tw])
                k.op("dve", lambda e: e.tensor_copy(out=tw[:, 64:128], in_=z[:, cs(3 * D + 128 + 64 * d, 64)]),
                     reads=[z], writes=[tw])
                pb = bank()
                pv = pb[:].bitcast(BF16)

                def f(e):
                    e.transpose(out=pv[0:64, 0:128], in_=tw[:, 0:64], identity=ident_b[:])
                    return e.transpose(out=pv[0:64, 128:256], in_=tw[:, 64:128], identity=ident_b[:])
                k.op("pe", f, reads=[tw, ident_b], writes=[pb])
                k.op("act", lambda e: e.copy(out=ldT[0:64, :, :].rearrange("p a b -> p (a b)"), in_=pv[0:64, 0:256]),
                     reads=[pb], writes=[ldT])
                for (wmat, src_j, dst) in ((wup, 0, SG), (aup, 1, A)):
                    pp, b0, b1 = pair()

                    def f(e):
                        e.matmul(pp[:, 0:512], lhsT=ldT[:, src_j, :], rhs=wmat[:, 0:512], start=True, stop=True)
                        return e.matmul(pp[:, 512:1024], lhsT=ldT[:, src_j, :], rhs=wmat[:, 512:1024], start=True, stop=True)
                    k.op("pe", f, reads=[ldT, wmat], writes=[b0, b1])
                    k.op("act", lambda e: e.activation(out=dst[:], in_=pp[:, :], func=AF.Sigmoid), reads=[b0, b1], writes=[dst])
                if DBG < 11:
                    continue
                k.op("dve", lambda e: e.tensor_tensor(out=KX[:], in0=kq, in1=kkc[:], op=ALU.mult), reads=[z, kkc], writes=[KX])
                k.op("dve", lambda e: e.tensor_tensor(out=S0[:], in0=KX[:], in1=KX[:], op=ALU.mult), reads=[KX], writes=[S0])
                k.op("dve", lambda e: e.tensor_reduce(out=st16[:], in_=S0[:].rearrange("p (h n) -> p h n", n=64), axis=AX.X,
                                                      op=ALU.add), reads=[S0], writes=[st16])
                rstd_from(st16, rs16, 1.0, 1e-12)
                k.op("dve", lambda e: e.tensor_tensor(out=KX[:].rearrange("p (h n) -> p h n", n=64),
                                                      in0=KX[:].rearrange("p (h n) -> p h n", n=64),
                                                      in1=rs16[:].unsqueeze(2).to_broadcast([128, 16, 64]), op=ALU.mult),
                     reads=[KX, rs16], writes=[KX])
                k.op("dve", lambda e: e.scalar_tensor_tensor(out=BP[:], in0=KX[:], scalar=-1.0, in1=A[:], op0=ALU.mult,
                                                             op1=ALU.mult), reads=[KX, A], writes=[BP])
                k.op("dve", lambda e: e.tensor_tensor(out=S0[:], in0=kq, in1=kac[:], op=ALU.mult), reads=[z, kac], writes=[S0])
                k.op("dve", lambda e: e.scalar_tensor_tensor(out=S0[:], in0=A[:], scalar=-1.0, in1=S0[:], op0=ALU.add,
                                                             op1=ALU.mult), reads=[A, S0], writes=[S0])
                k.op("dve", lambda e: e.tensor_tensor(out=KD[:], in0=S0[:], in1=kq, op=ALU.add), reads=[S0, z], writes=[KD])
                k.op("dve", lambda e: e.tensor_tensor(out=S0[:], in0=KD[:], in1=rkc[:], op=ALU.mult), reads=[KD, rkc], writes=[S0])
                k.op("dve", lambda e: e.tensor_tensor(out=S0[:], in0=S0[:], in1=rq, op=ALU.mult), reads=[S0, z], writes=[S0])
                k.op("dve", lambda e: e.tensor_reduce(out=bon[:], in_=S0[:].rearrange("p (h n) -> p h n", n=64), axis=AX.X,
                                                      op=ALU.add), reads=[S0], writes=[bon])
                k.dma("sp", bon_d[d, cs(i * 128, 128), :], bon[:], reads=[bon], writes=[R_bon(d, i)], sembuf=bon)
                if DBG < 12:
                    continue
                def cum(tri_ap, scale, dstE):
                    pp, b0, b1 = pair()

                    def f(e):
                        e.matmul(pp[:, 0:512], lhsT=tri_ap, rhs=SG[:, 0:512], start=True, stop=True)
                        return e.matmul(pp[:, 512:1024], lhsT=tri_ap, rhs=SG[:, 512:1024], start=True, stop=True)
                    k.op("pe", f, reads=[tri4, SG], writes=[b0, b1])
                    k.op("act", lambda e: e.activation(out=dstE[:], in_=pp[:, :], func=AF.Exp, scale=scale),
                         reads=[b0, b1], writes=[dstE])
                    return pp, b0, b1
                pp, b0, b1 = cum(tri_incl, -DSC, S0)
                k.op("dve", lambda e: e.tensor_tensor(out=TM[:, 1, :], in0=rq, in1=S0[:], op=ALU.mult), reads=[z, S0], writes=[TM])
                k.op("act", lambda e: e.activation(out=S1[:], in_=pp[:, :], func=AF.Exp, scale=DSC), reads=[b0, b1], writes=[S1])
                k.op("dve", lambda e: e.tensor_tensor(out=TM[:, 2, :], in0=BP[:], in1=S1[:], op=ALU.mult), reads=[BP, S1], writes=[TM])
                k.op("dve", lambda e: e.tensor_tensor(out=TM[:, 3, :], in0=KD[:], in1=S1[:], op=ALU.mult), reads=[KD, S1], writes=[TM])
                cum(tri_excl, -DSC, S0)
                k.op("dve", lambda e: e.tensor_tensor(out=TM[:, 0, :], in0=KX[:], in1=S0[:], op=ALU.mult), reads=[KX, S0], writes=[TM])
                cum(tri_dg, -DSC, S1)
                k.op("dve", lambda e: e.tensor_tensor(out=Bg[:], in0=BP[:], in1=S1[:], op=ALU.mult), reads=[BP, S1], writes=[Bg])
                k.op("dve", lambda e: e.tensor_tensor(out=Kg[:], in0=KD[:], in1=S1[:], op=ALU.mult), reads=[KD, S1], writes=[Kg])
                pbg = bank()

                def f(e):
                    for ct in range(8):
                        ins = e.matmul(pbg[:, ct:ct + 1], lhsT=SG[:, cs(ct * 128, 128)], rhs=ones_f[:], start=True, stop=True)
                    return ins
                k.op("pe", f, reads=[SG, ones_f], writes=[pbg])
                k.op("act", lambda e: e.activation(out=gC[:], in_=pbg[:, 0:8], func=AF.Exp, scale=-DSC), reads=[pbg], writes=[gC])
                if DBG < 13:
                    continue
                for g4 in range(4):
                    pb = bank()
                    pv = pb[:].bitcast(BF16)

                    def f(e):
                        for c2 in range(2):
                            ct = g4 * 2 + c2
                            for q in range(4):
                                ins = e.transpose(out=pv[:, cs((c2 * 4 + q) * 128, 128)], in_=TM[:, q, cs(ct * 128, 128)],
                                                  identity=ident_b[:])
                        return ins
                    k.op("pe", f, reads=[TM, ident_b], writes=[pb])
                    eng = "act" if g4 % 2 == 0 else "dve"
                    if eng == "act":
                        k.op("act", lambda e: e.copy(out=FM[:, g4 * 2:g4 * 2 + 2, :, :].rearrange("p a b c -> p (a b c)"),
                                                     in_=pv[:, :]), reads=[pb], writes=[FM])
                    else:
                        k.op("dve", lambda e: e.tensor_copy(out=FM[:, g4 * 2:g4 * 2 + 2, :, :].rearrange("p a b c -> p (a b c)"),
                                                            in_=pv[:, :]), reads=[pb], writes=[FM])
                if DBG < 14:
                    continue
                P0, PT0 = Pm[0], PT[0]
                for h in range(NHEAD):
                    ct, p0 = h // 2, (h % 2) * 64
                    if 'o' in KSKIP and h % 2 == 1:
                        continue
                    pb = bank()

                    def f(e):
                        e.matmul(pb[:, 0:256], lhsT=FM[p0:p0 + 64, ct, 2, :],
                                 rhs=FM[p0:p0 + 64, ct, 0:2, :].rearrange("p a b -> p (a b)"), start=True, stop=True)
                        return e.matmul(pb[:, 256:512], lhsT=FM[p0:p0 + 64, ct, 3, :],
                                        rhs=FM[p0:p0 + 64, ct, 0:2, :].rearrange("p a b -> p (a b)"), start=True, stop=True)
                    k.op("pe", f, reads=[FM], writes=[pb])
                    k.op("dve", lambda e: e.tensor_tensor(out=MB[:, h, :], in0=pb[:], in1=maskM[:].rearrange("p a b -> p (a b)"),
                                                          op=ALU.mult), reads=[pb, maskM], writes=[MB])
                    k.op("act", lambda e: e.copy(out=PT0[:, h, :], in_=MB[:, h, 0:128]), reads=[MB], writes=[PT0])
                for g in range(4):
                    if 'n' in KSKIP:
                        continue
                    pb = bank()

                    def f(e):
                        for hh in range(4):
                            h = (g % 2) + 2 * (4 * (g // 2) + hh)
                            ct, p0 = h // 2, (h % 2) * 64
                            ins = e.matmul(pb[:, cs(hh * 128, 128)], lhsT=FM[p0:p0 + 64, ct, 0, :], rhs=FM[p0:p0 + 64, ct, 2, :],
                                           start=True, stop=True)
                        return ins
                    k.op("pe", f, reads=[FM], writes=[pb])
                    h0 = (g % 2) + 8 * (g // 2)
                    k.op("dve", lambda e: e.tensor_tensor(out=P0[:, h0:h0 + 7:2, :],
                                                          in0=pb[:].rearrange("p (a b) -> p a b", b=128), in1=maskN[:], op=ALU.mult),
                         reads=[pb, maskN], writes=[P0])
                if DBG < 15:
                    continue
                pp, b0, b1 = pair()

                def f(e):
                    for h in range(NHEAD):
                        ct, p0 = h // 2, (h % 2) * 64
                        e.matmul(pp[:, hc(h)], lhsT=FM[p0:p0 + 64, ct, 0, :], rhs=Hb[p0:p0 + 64, ct, :],
                                 start=True, stop=False)
                        ins = e.matmul(pp[:, hc(h)], lhsT=MB[:, h, 256:384], rhs=z[:, cs(2 * D + h * 64, 64)],
                                       start=False, stop=True)
                    return ins
                k.op("pe", f, reads=[FM, Hb, MB, z], writes=[b0, b1])
                xc = Xb[0]
                k.op("act", lambda e: e.copy(out=xc[:], in_=pp[:, :]), reads=[b0, b1], writes=[xc])
                if DBG < 16:
                    continue
                cur = 0
                for lev in range(7):
                    Pc, PTc = Pm[cur], PT[cur]
                    pp, b0, b1 = pair()
                    xc, xn = Xb[lev % 2], Xb[(lev + 1) % 2]

                    def f(e):
                        for h in range(NHEAD):
                            ins = e.matmul(pp[:, hc(h)], lhsT=PTc[:, h, :], rhs=xc[:, hc(h)], start=True, stop=True)
                        return ins
                    k.op("pe", f, reads=[PTc, xc], writes=[b0, b1])
                    k.op("dve", lambda e: e.tensor_tensor(out=xn[:], in0=pp[:, :], in1=xc[:], op=ALU.add),
                         reads=[b0, b1, xc], writes=[xn])
                    if lev < 6:
                        Pn, PTn = Pm[1 - cur], PT[1 - cur]
                        for g in range(4):
                            for which in range(2):
                                pb = bank()

                                def f(e):
                                    for hh in range(4):
                                        h = g * 4 + hh
                                        if which == 0:
                                            ins = e.matmul(pb[:, cs(hh * 128, 128)], lhsT=PTc[:, h, :], rhs=Pc[:, h, :], start=True, stop=True)
                                        else:
                                            ins = e.matmul(pb[:, cs(hh * 128, 128)], lhsT=Pc[:, h, :], rhs=PTc[:, h, :], start=True, stop=True)
                                    return ins
                                k.op("pe", f, reads=[Pc, PTc], writes=[pb])
                                dstb = Pn if which == 0 else PTn
                                if which == 0:
                                    k.op("act", lambda e: e.copy(out=dstb[:, g * 4:g * 4 + 4, :].rearrange("p a b -> p (a b)"),
                                                                 in_=pb[:]), reads=[pb], writes=[dstb])
                                else:
                                    k.op("dve", lambda e: e.tensor_copy(out=dstb[:, g * 4:g * 4 + 4, :].rearrange("p a b -> p (a b)"),
                                                                        in_=pb[:]), reads=[pb], writes=[dstb])
                        cur = 1 - cur
                U = Xb[1]
                if DBG < 17:
                    continue
                pp, b0, b1 = pair()

                def f(e):
                    for h in range(NHEAD):
                        ct, p0 = h // 2, (h % 2) * 64
                        e.matmul(pp[:, hc(h)], lhsT=FM[p0:p0 + 64, ct, 1, :], rhs=Hb[p0:p0 + 64, ct, :], start=True, stop=False)
                        e.matmul(pp[:, hc(h)], lhsT=MB[:, h, 128:256], rhs=U[:, hc(h)], start=False, stop=False)
                        ins = e.matmul(pp[:, hc(h)], lhsT=MB[:, h, 384:512], rhs=z[:, cs(2 * D + h * 64, 64)],
                                       start=False, stop=True)
                    return ins
                k.op("pe", f, reads=[FM, Hb, MB, U, z], writes=[b0, b1])
                k.op("act", lambda e: e.copy(out=ysc[:].rearrange("p (hp h2 i) -> p h2 hp i", h2=2, i=64),
                                             in_=pp[:, :].rearrange("p (h2 hp i) -> p h2 hp i", hp=8, i=64)), reads=[b0, b1], writes=[ysc])
                k.dma("sp", ysc_d[d, cs(i * 128, 128), :], ysc[:], reads=[ysc], writes=[R_ysc(d, i)], sembuf=ysc)
                if DBG < 18:
                    continue
                pp, b0, b1 = pair()

                def f(e):
                    for ct in range(8):
                        e.matmul(pp[:, cs(ct * 128, 128)], lhsT=Bg[:, cs(ct * 128, 128)],
                                 rhs=U[:].rearrange("p (a b c) -> p a b c", a=2, b=8)[:, :, ct, :], start=True, stop=False)
                        ins = e.matmul(pp[:, cs(ct * 128, 128)], lhsT=Kg[:, cs(ct * 128, 128)], rhs=z[:, cs(2 * D + ct * 128, 128)],
                                       start=False, stop=True)
                    return ins
                k.op("pe", f, reads=[Bg, Kg, U, z], writes=[b0, b1])
                k.op("dve", lambda e: e.tensor_tensor(out=H[:], in0=H[:], in1=gC[:].unsqueeze(2).to_broadcast([128, 8, 64]),
                                                      op=ALU.mult), reads=[H, gC], writes=[H])
                ppv = pp[:, :].rearrange("p (a b) -> p a b", b=128)
                k.op("dve", lambda e: e.tensor_tensor(out=H[0:64, :, :], in0=H[0:64, :, :], in1=ppv[0:64, :, 0:64], op=ALU.add),
                     reads=[H, b0, b1], writes=[H])
                k.op("dve", lambda e: e.tensor_tensor(out=H[64:128, :, :], in0=H[64:128, :, :], in1=ppv[64:128, :, 64:128], op=ALU.add),
                     reads=[H, b0, b1], writes=[H])
                if n % 2 == 1:
                    grp = i // 2
                    k.dma("sp", st_d[l, d, grp], H[:].rearrange("p a b -> p (a b)"), reads=[H], writes=[R_st(l, d, grp)], sembuf=H)
                    k.op("dve", lambda e: e.tensor_scalar(out=H[:], in0=H[:], scalar1=cmask[:, 0:1], scalar2=None, op0=ALU.mult),
                         reads=[H, cmask], writes=[H])
                k.op("act", lambda e: e.copy(out=Hb[:], in_=H[:]), reads=[H], writes=[Hb])
            k.barrier()


    def phaseS2(l):
        with contextlib.ExitStack() as es:
            kkc = sbt(es, "kkc", [128, D], F32)
            kac = sbt(es, "kac", [128, D], F32)
            rkc = sbt(es, "rkc", [128, D], F32)
            bc_load(kkc, kk_d[l])
            bc_load(kac, ka_d[l])
            bc_load(rkc, rk_d[l])

            def dir_gen(d):
                sfx = "_%d" % d
                wup = sbt(es, "wup" + sfx, [65, D], BF16)
                aup = sbt(es, "aup" + sfx, [65, D], BF16)
                maskM = sbt(es, "maskM" + sfx, [128, 4, 128], BF16)
                maskN = sbt(es, "maskN" + sfx, [128, 4, 128], BF16)
                H = sbt(es, "H" + sfx, [128, 8, 64], F32)
                Hb = sbt(es, "Hb" + sfx, [128, 8, 64], BF16)
                z = sbt(es, "zr" + sfx, [128, CR], BF16)
                ldT = sbt(es, "ldT" + sfx, [65, 2, 128], BF16)
                tw = sbt(es, "tw" + sfx, [128, 128], BF16)
                SG = sbt(es, "SG" + sfx, [128, D], F32)
                A = sbt(es, "A" + sfx, [128, D], F32)
                KX = sbt(es, "KX" + sfx, [128, D], F32)
                BP = sbt(es, "BP" + sfx, [128, D], F32)
                KD = sbt(es, "KD" + sfx, [128, D], F32)
                S0 = sbt(es, "S0" + sfx, [128, D], F32)
                S1 = A
                ysc = S0
                st16 = sbt(es, "st16" + sfx, [128, 16], F32)
                rs16 = sbt(es, "rs16" + sfx, [128, 16], F32)
                bon = sbt(es, "bon" + sfx, [128, 16], F32)
                gC = sbt(es, "gC" + sfx, [128, 8], F32)
                TM = sbt(es, "TM" + sfx, [128, 4, D], BF16)
                FM = sbt(es, "FM" + sfx, [128, 8, 4, 128], BF16)
                MB = sbt(es, "MB" + sfx, [128, 16, 512], BF16)
                Pm = [sbt(es, "Pm%d" % i + sfx, [128, 16, 128], BF16) for i in range(2)]
                PT = [sbt(es, "PT%d" % i + sfx, [128, 16, 128], BF16) for i in range(2)]
                Xb = [sbt(es, "Xb%d" % i + sfx, [128, D], BF16) for i in range(2)]

                k.dma("pool", wup[:], wup_d[l, d], writes=[wup], max_dma_last_dim=4096)
                k.dma("pool", aup[:], aup_d[l, d], writes=[aup], max_dma_last_dim=4096)
                strict_i, incl_i, nmask_i = (2, 0, 3) if d == 0 else (3, 1, 2)
                for j, src in enumerate((strict_i, incl_i, strict_i, incl_i)):
                    k.op("dve", lambda e: e.tensor_copy(out=maskM[:, j, :], in_=tri4[:, src, :]), reads=[tri4], writes=[maskM])
                for j in range(4):
                    k.op("dve", lambda e: e.tensor_copy(out=maskN[:, j, :], in_=tri4[:, nmask_i, :]), reads=[tri4], writes=[maskN])
                tri_incl = tri4[:, incl_i, :]
                tri_excl = tri4[:, strict_i, :]
                tri_dg = tri4[:, nmask_i, :]
                k.op("dve", lambda e: e.memset(ldT[:], 1.0), writes=[ldT])
                k.dma("sp", H[:].rearrange("p a b -> p (a b)"), state0_d[l, d], writes=[H])
                k.op("act", lambda e: e.copy(out=Hb[:], in_=H[:]), reads=[H], writes=[Hb])
                order = list(range(NT)) if d == 0 else list(range(NT - 1, -1, -1))
                yield

                for n, i in enumerate(order):
                    k.dma("sp", z[:], zr_d[cs(i * 128, 128), :], reads=[R_zr(i)], writes=[z])
                    rq = z[:, 0:D]
                    kq = z[:, D:2 * D]
                    k.op("act", lambda e: e.activation(out=tw[:, 0:64], in_=z[:, cs(3 * D + 64 * d, 64)], func=AF.Tanh),
                         reads=[z], writes=[tw])
                    k.op("pool", lambda e: e.tensor_copy(out=tw[:, 64:128], in_=z[:, cs(3 * D + 128 + 64 * d, 64)]),
                         reads=[z], writes=[tw])
                    pb = bank()
                    pv = pb[:].bitcast(BF16)

                    def f(e):
                        e.transpose(out=pv[0:64, 0:128], in_=tw[:, 0:64], identity=ident_b[:])
                        return e.transpose(out=pv[0:64, 128:256], in_=tw[:, 64:128], identity=ident_b[:])
                    k.op("pe", f, reads=[tw, ident_b], writes=[pb])
                    k.op("act", lambda e: e.copy(out=ldT[0:64, :, :].rearrange("p a b -> p (a b)"), in_=pv[0:64, 0:256]),
                         reads=[pb], writes=[ldT])
                    for (wmat, src_j, dst) in ((wup, 0, SG), (aup, 1, A)):
                        pp, b0, b1 = pair()

                        def f(e):
                            e.matmul(pp[:, 0:512], lhsT=ldT[:, src_j, :], rhs=wmat[:, 0:512], start=True, stop=True)
                            return e.matmul(pp[:, 512:1024], lhsT=ldT[:, src_j, :], rhs=wmat[:, 512:1024], start=True, stop=True)
                        k.op("pe", f, reads=[ldT, wmat], writes=[b0, b1])
                        k.op("act", lambda e: e.activation(out=dst[:], in_=pp[:, :], func=AF.Sigmoid), reads=[b0, b1], writes=[dst])
                    k.op("pool", lambda e: e.tensor_tensor(out=KX[:], in0=kq, in1=kkc[:], op=ALU.mult), reads=[z, kkc], writes=[KX])
                    k.op("pool", lambda e: e.tensor_tensor(out=S0[:], in0=KX[:], in1=KX[:], op=ALU.mult), reads=[KX], writes=[S0])
                    k.op("dve", lambda e: e.tensor_reduce(out=st16[:], in_=S0[:].rearrange("p (h n) -> p h n", n=64), axis=AX.X,
                                                          op=ALU.add), reads=[S0], writes=[st16])
                    rstd_from(st16, rs16, 1.0, 1e-12)
                    k.op("dve", lambda e: e.tensor_tensor(out=KX[:].rearrange("p (h n) -> p h n", n=64),
                                                          in0=KX[:].rearrange("p (h n) -> p h n", n=64),
                                                          in1=rs16[:].unsqueeze(2).to_broadcast([128, 16, 64]), op=ALU.mult),
                         reads=[KX, rs16], writes=[KX])
                    yield
                    k.op("dve", lambda e: e.scalar_tensor_tensor(out=BP[:], in0=KX[:], scalar=-1.0, in1=A[:], op0=ALU.mult,
                                                                 op1=ALU.mult), reads=[KX, A], writes=[BP])
                    k.op("pool", lambda e: e.tensor_tensor(out=S0[:], in0=kq, in1=kac[:], op=ALU.mult), reads=[z, kac], writes=[S0])
                    k.op("dve", lambda e: e.scalar_tensor_tensor(out=S0[:], in0=A[:], scalar=-1.0, in1=S0[:], op0=ALU.add,
                                                                 op1=ALU.mult), reads=[A, S0], writes=[S0])
                    k.op("pool", lambda e: e.tensor_tensor(out=KD[:], in0=S0[:], in1=kq, op=ALU.add), reads=[S0, z], writes=[KD])
                    k.op("pool", lambda e: e.tensor_tensor(out=S0[:], in0=KD[:], in1=rkc[:], op=ALU.mult), reads=[KD, rkc], writes=[S0])
                    k.op("pool", lambda e: e.tensor_tensor(out=S0[:], in0=S0[:], in1=rq, op=ALU.mult), reads=[S0, z], writes=[S0])
                    k.op("dve", lambda e: e.tensor_reduce(out=bon[:], in_=S0[:].rearrange("p (h n) -> p h n", n=64), axis=AX.X,
                                                          op=ALU.add), reads=[S0], writes=[bon])
                    k.dma("sp", bon_d[d, cs(i * 128, 128), :], bon[:], reads=[bon], writes=[R_bon(d, i)], sembuf=bon)
                    yield

                    def cum(tri_ap, scale, dstE):
                        pp, b0, b1 = pair()

                        def f(e):
                            e.matmul(pp[:, 0:512], lhsT=tri_ap, rhs=SG[:, 0:512], start=True, stop=True)
                            return e.matmul(pp[:, 512:1024], lhsT=tri_ap, rhs=SG[:, 512:1024], start=True, stop=True)
                        k.op("pe", f, reads=[tri4, SG], writes=[b0, b1])
                        k.op("act", lambda e: e.activation(out=dstE[:], in_=pp[:, :], func=AF.Exp, scale=scale),
                             reads=[b0, b1], writes=[dstE])
                        return pp, b0, b1
                    pp, b0, b1 = cum(tri_incl, -DSC, S0)
                    k.op("dve", lambda e: e.tensor_tensor(out=TM[:, 1, :], in0=rq, in1=S0[:], op=ALU.mult), reads=[z, S0], writes=[TM])
                    k.op("act", lambda e: e.activation(out=S1[:], in_=pp[:, :], func=AF.Exp, scale=DSC), reads=[b0, b1], writes=[S1])
                    k.op("dve", lambda e: e.tensor_tensor(out=TM[:, 2, :], in0=BP[:], in1=S1[:], op=ALU.mult), reads=[BP, S1], writes=[TM])
                    k.op("pool", lambda e: e.tensor_tensor(out=TM[:, 3, :], in0=KD[:], in1=S1[:], op=ALU.mult), reads=[KD, S1], writes=[TM])
                    cum(tri_excl, -DSC, S0)
                    k.op("dve", lambda e: e.tensor_tensor(out=TM[:, 0, :], in0=KX[:], in1=S0[:], op=ALU.mult), reads=[KX, S0], writes=[TM])
                    pbg = bank()

                    def f(e):
                        for ct in range(8):
                            ins = e.matmul(pbg[:, ct:ct + 1], lhsT=SG[:, cs(ct * 128, 128)], rhs=ones_f[:], start=True, stop=True)
                        return ins
                    k.op("pe", f, reads=[SG, ones_f], writes=[pbg])
                    k.op("act", lambda e: e.activation(out=gC[:], in_=pbg[:, 0:8], func=AF.Exp, scale=-DSC), reads=[pbg], writes=[gC])
                    yield
                    for g4 in range(4):
                        pb = bank()
                        pv = pb[:].bitcast(BF16)

                        def f(e):
                            for c2 in range(2):
                                ct = g4 * 2 + c2
                                for q in range(4):
                                    ins = e.transpose(out=pv[:, cs((c2 * 4 + q) * 128, 128)], in_=TM[:, q, cs(ct * 128, 128)],
                                                      identity=ident_b[:])
                            return ins
                        k.op("pe", f, reads=[TM, ident_b], writes=[pb])
                        if g4 % 2 == 0:
                            k.op("act", lambda e: e.copy(out=FM[:, g4 * 2:g4 * 2 + 2, :, :].rearrange("p a b c -> p (a b c)"),
                                                         in_=pv[:, :]), reads=[pb], writes=[FM])
                        else:
                            k.op("dve", lambda e: e.tensor_copy(out=FM[:, g4 * 2:g4 * 2 + 2, :, :].rearrange("p a b c -> p (a b c)"),
                                                                in_=pv[:, :]), reads=[pb], writes=[FM])
                    cum(tri_dg, -DSC, S1)
                    k.op("dve", lambda e: e.tensor_tensor(out=TM[:, 2, :], in0=BP[:], in1=S1[:], op=ALU.mult), reads=[BP, S1], writes=[TM])
                    k.op("pool", lambda e: e.tensor_tensor(out=TM[:, 3, :], in0=KD[:], in1=S1[:], op=ALU.mult), reads=[KD, S1], writes=[TM])
                    yield
                    P0 = Pm[0]
                    for h in range(NHEAD):
                        ct, p0 = h // 2, (h % 2) * 64
                        pb = bank()

                        def f(e):
                            e.matmul(pb[:, 0:256], lhsT=FM[p0:p0 + 64, ct, 2, :],
                                     rhs=FM[p0:p0 + 64, ct, 0:2, :].rearrange("p a b -> p (a b)"), start=True, stop=True)
                            return e.matmul(pb[:, 256:512], lhsT=FM[p0:p0 + 64, ct, 3, :],
                                            rhs=FM[p0:p0 + 64, ct, 0:2, :].rearrange("p a b -> p (a b)"), start=True, stop=True)
                        k.op("pe", f, reads=[FM], writes=[pb])
                        k.op("dve", lambda e: e.tensor_tensor(out=MB[:, h, :], in0=pb[:], in1=maskM[:].rearrange("p a b -> p (a b)"),
                                                              op=ALU.mult), reads=[pb, maskM], writes=[MB])
                        if h % 4 == 3:
                            yield
                    for g in range(4):
                        pb = bank()

                        def f(e):
                            for hh in range(4):
                                h = (g % 2) + 2 * (4 * (g // 2) + hh)
                                ct, p0 = h // 2, (h % 2) * 64
                                ins = e.matmul(pb[:, cs(hh * 128, 128)], lhsT=FM[p0:p0 + 64, ct, 0, :], rhs=FM[p0:p0 + 64, ct, 2, :],
                                               start=True, stop=True)
                            return ins
                        k.op("pe", f, reads=[FM], writes=[pb])
                        h0 = (g % 2) + 8 * (g // 2)
                        k.op("dve", lambda e: e.tensor_tensor(out=P0[:, h0:h0 + 7:2, :],
                                                              in0=pb[:].rearrange("p (a b) -> p a b", b=128), in1=maskN[:], op=ALU.mult),
                             reads=[pb, maskN], writes=[P0])
                    yield
                    pp, b0, b1 = pair()

                    def f(e):
                        for h in range(NHEAD):
                            ct, p0 = h // 2, (h % 2) * 64
                            e.matmul(pp[:, hc(h)], lhsT=FM[p0:p0 + 64, ct, 0, :], rhs=Hb[p0:p0 + 64, ct, :],
                                     start=True, stop=False)
                            ins = e.matmul(pp[:, hc(h)], lhsT=MB[:, h, 256:384], rhs=z[:, cs(2 * D + h * 64, 64)],
                                           start=False, stop=True)
                        return ins
                    k.op("pe", f, reads=[FM, Hb, MB, z], writes=[b0, b1])
                    xc = Xb[0]
                    k.op("act", lambda e: e.copy(out=xc[:], in_=pp[:, :]), reads=[b0, b1], writes=[xc])
                    yield
                    cur = 0
                    for lev in range(7):
                        Pc = Pm[cur]
                        pp, b0, b1 = pair()
                        xc, xn = Xb[lev % 2], Xb[(lev + 1) % 2]
                        if lev == 0:
                            ptv = lambda h: MB[:, h, 0:128]
                            ptb = MB
                        else:
                            ptv = lambda h, PTc=PT[cur]: PTc[:, h, :]
                            ptb = PT[cur]

                        def f(e):
                            e.matmul(pp[:, 0:512], lhsT=ident_b[:], rhs=xc[:, 0:512], start=True, stop=False)
                            e.matmul(pp[:, 512:1024], lhsT=ident_b[:], rhs=xc[:, 512:1024], start=True, stop=False)
                            for h in range(NHEAD):
                                ins = e.matmul(pp[:, hc(h)], lhsT=ptv(h), rhs=xc[:, hc(h)], start=False, stop=(h >= NHEAD - 2))
                            return ins
                        k.op("pe", f, reads=[ptb, xc, ident_b], writes=[b0, b1])
                        k.op("act", lambda e: e.copy(out=xn[:], in_=pp[:, :]), reads=[b0, b1], writes=[xn])
                        if lev < 6:
                            Pn, PTn = Pm[1 - cur], PT[1 - cur]
                            for g in range(4):
                                for which in range(2):
                                    if which == 0 and lev == 5:
                                        continue
                                    pb = bank()

                                    def f(e):
                                        for hh in range(4):
                                            h = g * 4 + hh
                                            if which == 0:
                                                ins = e.matmul(pb[:, cs(hh * 128, 128)], lhsT=ptv(h), rhs=Pc[:, h, :], start=True, stop=True)
                                            else:
                                                ins = e.matmul(pb[:, cs(hh * 128, 128)], lhsT=Pc[:, h, :], rhs=ptv(h), start=True, stop=True)
                                        return ins
                                    k.op("pe", f, reads=[Pc, ptb], writes=[pb])
                                    dstb = Pn if which == 0 else PTn
                                    if which == 0 or g % 2 == 0:
                                        k.op("act", lambda e: e.copy(out=dstb[:, g * 4:g * 4 + 4, :].rearrange("p a b -> p (a b)"),
                                                                     in_=pb[:]), reads=[pb], writes=[dstb])
                                    else:
                                        k.op("dve", lambda e: e.tensor_copy(out=dstb[:, g * 4:g * 4 + 4, :].rearrange("p a b -> p (a b)"),
                                                                            in_=pb[:]), reads=[pb], writes=[dstb])
                                if g % 2 == 1:
                                    yield
                            cur = 1 - cur
                    U = Xb[1]
                    pp, b0, b1 = pair()

                    def f(e):
                        for h in range(NHEAD):
                            ct, p0 = h // 2, (h % 2) * 64
                            e.matmul(pp[:, hc(h)], lhsT=FM[p0:p0 + 64, ct, 1, :], rhs=Hb[p0:p0 + 64, ct, :], start=True, stop=False)
                            e.matmul(pp[:, hc(h)], lhsT=MB[:, h, 128:256], rhs=U[:, hc(h)], start=False, stop=False)
                            ins = e.matmul(pp[:, hc(h)], lhsT=MB[:, h, 384:512], rhs=z[:, cs(2 * D + h * 64, 64)],
                                           start=False, stop=True)
                        return ins
                    k.op("pe", f, reads=[FM, Hb, MB, U, z], writes=[b0, b1])
                    k.op("act", lambda e: e.copy(out=ysc[:].rearrange("p (hp h2 i) -> p h2 hp i", h2=2, i=64),
                                                 in_=pp[:, :].rearrange("p (h2 hp i) -> p h2 hp i", hp=8, i=64)), reads=[b0, b1], writes=[ysc])
                    k.dma("sp", ysc_d[d, cs(i * 128, 128), :], ysc[:], reads=[ysc], writes=[R_ysc(d, i)], sembuf=ysc)
                    pp, b0, b1 = pair()

                    def f(e):
                        for ct in range(8):
                            e.matmul(pp[:, cs(ct * 128, 128)], lhsT=TM[:, 2, cs(ct * 128, 128)],
                                     rhs=U[:].rearrange("p (a b c) -> p a b c", a=2, b=8)[:, :, ct, :], start=True, stop=False)
                            ins = e.matmul(pp[:, cs(ct * 128, 128)], lhsT=TM[:, 3, cs(ct * 128, 128)], rhs=z[:, cs(2 * D + ct * 128, 128)],
                                           start=False, stop=True)
                        return ins
                    k.op("pe", f, reads=[TM, U, z], writes=[b0, b1])
                    k.op("dve", lambda e: e.tensor_tensor(out=H[:], in0=H[:], in1=gC[:].unsqueeze(2).to_broadcast([128, 8, 64]),
                                                          op=ALU.mult), reads=[H, gC], writes=[H])
                    ppv = pp[:, :].rearrange("p (a b) -> p a b", b=128)
                    k.op("dve", lambda e: e.tensor_tensor(out=H[0:64, :, :], in0=H[0:64, :, :], in1=ppv[0:64, :, 0:64], op=ALU.add),
                         reads=[H, b0, b1], writes=[H])
                    k.op("dve", lambda e: e.tensor_tensor(out=H[64:128, :, :], in0=H[64:128, :, :], in1=ppv[64:128, :, 64:128], op=ALU.add),
                         reads=[H, b0, b1], writes=[H])
                    if n % 2 == 1:
                        grp = i // 2
                        k.dma("sp", st_d[l, d, grp], H[:].rearrange("p a b -> p (a b)"), reads=[H], writes=[R_st(l, d, grp)], sembuf=H)
                        k.op("dve", lambda e: e.tensor_scalar(out=H[:], in0=H[:], scalar1=cmask[:, 0:1], scalar2=None, op0=ALU.mult),
                             reads=[H, cmask], writes=[H])
                    k.op("act", lambda e: e.copy(out=Hb[:], in_=H[:]), reads=[H], writes=[Hb])
                    yield

            gens = [dir_gen(0), dir_gen(1)]
            next(gens[0])
            next(gens[1])
            for _ in range(6):
                next(gens[0])
            live = list(gens)
            while live:
                for g in list(live):
                    try:
                        next(g)
                    except StopIteration:
                        live.remove(g)
            k.barrier()

    def phaseM(l, xsrc_d, R_xsrc):
        with contextlib.ExitStack() as es:
            wa = sbt(es, "wa", [128, 8, D], BF16)
            wb = sbt(es, "wb", [128, 8, D], BF16)
            wo = sbt(es, "wo", [128, 8, D], BF16)
            gup = sbt(es, "gup", [128, D], BF16)
            wsT = sbt(es, "wsT", [128, 8, 128], BF16)
            bsT = sbt(es, "bsT", [128, 8], F32)
            lnxg = sbt(es, "lnxg", [128, D], F32)
            lnxb = sbt(es, "lnxb", [128, D], F32)
            lnvg = sbt(es, "lnvg", [128, D], F32)
            gate1 = sbt(es, "gate1", [128, D], F32)
            def mkset(pi):
                B = {}
                B['zv'] = sbt(es, "p%d_" % pi + "zv", [128, D + 128], BF16)
                B['zrs'] = sbt(es, "p%d_" % pi + "zrs", [128, 4096], BF16)
                B['yf'] = sbt(es, "p%d_" % pi + "yf", [128, D], F32)
                B['yb'] = sbt(es, "p%d_" % pi + "yb", [128, D], F32)
                B['b0t'] = sbt(es, "p%d_" % pi + "b0t", [128, 16], F32)
                B['b1t'] = sbt(es, "p%d_" % pi + "b1t", [128, 16], F32)
                B['xt'] = sbt(es, "p%d_" % pi + "xtm", [128, D], F32)
                B['W0'] = sbt(es, "p%d_" % pi + "W0", [128, D], F32)
                B['W1'] = sbt(es, "p%d_" % pi + "W1", [128, D], F32)
                B['W2'] = sbt(es, "p%d_" % pi + "W2", [128, D], F32)
                B['s16a'] = sbt(es, "p%d_" % pi + "s16a", [128, 16], F32)
                B['s16b'] = sbt(es, "p%d_" % pi + "s16b", [128, 16], F32)
                B['s16c'] = sbt(es, "p%d_" % pi + "s16c", [128, 16], F32)
                B['bnst'] = sbt(es, "p%d_" % pi + "bnst", [128, 2, 6], F32)
                B['mv'] = sbt(es, "p%d_" % pi + "mv", [128, 2], F32)
                B['rsv'] = sbt(es, "p%d_" % pi + "rsv", [128, 1], F32)
                B['gsb'] = sbt(es, "p%d_" % pi + "gsb", [128, 128], BF16)
                B['gT'] = sbt(es, "p%d_" % pi + "gT", [128, 1, 128], BF16)
                B['actb'] = sbt(es, "p%d_" % pi + "actb", [128, D], BF16)
                B['actT'] = sbt(es, "p%d_" % pi + "actT", [128, 8, 128], BF16)
                B['ub'] = sbt(es, "p%d_" % pi + "ub", [128, D], BF16)
                B['vcb'] = sbt(es, "p%d_" % pi + "vcb", [128, D], BF16)
                return B
            sets = [mkset(0), mkset(1)]
            cast_load_rows(lambda kc: wa[:, kc, :], wa_d[l], 8, D, wa)
            cast_load_rows(lambda kc: wb[:, kc, :], wb_d[l], 8, D, wb)
            cast_load_rows(lambda kc: wo[:, kc, :], wo_d[l], 8, D, wo)
            k.dma("pool", gup[:], gup_d[l], writes=[gup], max_dma_last_dim=4096)
            k.dma("pool", wsT[:].rearrange("p a b -> p (a b)"), wsT_d[l], writes=[wsT], max_dma_last_dim=4096)
            k.dma("sp", bsT[:], bsT_d[l], writes=[bsT])
            bc_load(lnxg, lnxg_d[l])
            bc_load(lnxb, lnxb_d[l])
            bc_load(lnvg, lnvg_d[l])
            load_mod(gate1, l, 2)

            def v3(b):
                return b[:].rearrange("p (h n) -> p h n", n=64)

            def bc16(b):
                return b[:].unsqueeze(2).to_broadcast([128, 16, 64])

            def proj(src_bf, wmat, actT):
                transpose8(src_bf, actT)
                pp, p0, p1 = pair()

                def f(e):
                    for nn in range(2):
                        for kc in range(8):
                            ins = e.matmul(pp[:, cs(nn * 512, 512)], lhsT=actT[:, kc, :], rhs=wmat[:, kc, cs(nn * 512, 512)],
                                           start=(kc == 0), stop=(kc == 7))
                    return ins
                k.op("pe", f, reads=[actT, wmat], writes=[p0, p1])
                return pp, p0, p1

            def tile_gen(i, B):
                zv = B['zv']
                zrs = B['zrs']
                yf = B['yf']
                yb = B['yb']
                b0t = B['b0t']
                b1t = B['b1t']
                xt = B['xt']
                W0 = B['W0']
                W1 = B['W1']
                W2 = B['W2']
                s16a = B['s16a']
                s16b = B['s16b']
                s16c = B['s16c']
                bnst = B['bnst']
                mv = B['mv']
                rsv = B['rsv']
                gsb = B['gsb']
                gT = B['gT']
                actb = B['actb']
                actT = B['actT']
                ub = B['ub']
                vcb = B['vcb']
                rows = cs(i * 128, 128)
                k.dma("sp", zv[:, 0:D], zr_d[rows, 2 * D:3 * D], reads=[R_zr(i)], writes=[zv])
                k.dma("sp", zv[:, D:D + 128], zr_d[rows, cs(3 * D + 256, 128)], reads=[R_zr(i)], writes=[zv])
                k.dma("sp", zrs[:], zrest_d[rows, :], reads=[R_zrest(i, j) for j in range(8)], writes=[zrs])
                k.dma("sp", yf[:], ysc_d[0, rows, :], reads=[R_ysc(0, i)], writes=[yf])
                k.dma("sp", yb[:], ysc_d[1, rows, :], reads=[R_ysc(1, i)], writes=[yb])
                k.dma("sp", b0t[:], bon_d[0, rows, :], reads=[R_bon(0, i)], writes=[b0t])
                k.dma("sp", b1t[:], bon_d[1, rows, :], reads=[R_bon(1, i)], writes=[b1t])
                k.dma("sp", xt[:], xsrc_d[rows, :], reads=[R_xsrc(i)], writes=[xt])
                yield
                k.op("dve", lambda e: e.tensor_tensor(out=yf[:], in0=yf[:], in1=yb[:], op=ALU.add), reads=[yf, yb], writes=[yf])
                k.op("dve", lambda e: e.tensor_reduce(out=s16a[:], in_=v3(yf), axis=AX.X, op=ALU.add), reads=[yf], writes=[s16a])
                k.op("dve", lambda e: e.tensor_scalar(out=s16a[:], in0=s16a[:], scalar1=1.0 / 64, scalar2=None, op0=ALU.mult),
                     reads=[s16a], writes=[s16a])
                k.op("dve", lambda e: e.tensor_tensor(out=v3(yf), in0=v3(yf), in1=bc16(s16a), op=ALU.subtract), reads=[yf, s16a], writes=[yf])
                k.op("dve", lambda e: e.tensor_tensor(out=W0[:], in0=yf[:], in1=yf[:], op=ALU.mult), reads=[yf], writes=[W0])
                k.op("dve", lambda e: e.tensor_reduce(out=s16b[:], in_=v3(W0), axis=AX.X, op=ALU.add), reads=[W0], writes=[s16b])
                rstd_from(s16b, s16c, 1.0 / 64, GN_EPS)
                k.op("dve", lambda e: e.tensor_tensor(out=v3(yf), in0=v3(yf), in1=bc16(s16c), op=ALU.mult), reads=[yf, s16c], writes=[yf])
                k.op("dve", lambda e: e.tensor_tensor(out=yf[:], in0=yf[:], in1=lnxg[:], op=ALU.mult), reads=[yf, lnxg], writes=[yf])
                k.op("dve", lambda e: e.tensor_tensor(out=yf[:], in0=yf[:], in1=lnxb[:], op=ALU.add), reads=[yf, lnxb], writes=[yf])
                yield
                k.op("dve", lambda e: e.tensor_tensor(out=b0t[:], in0=b0t[:], in1=b1t[:], op=ALU.add), reads=[b0t, b1t], writes=[b0t])
                k.op("dve", lambda e: e.tensor_tensor(out=v3(W0), in0=zv[:, 0:D].rearrange("p (h n) -> p h n", n=64), in1=bc16(b0t),
                                                      op=ALU.mult), reads=[zv, b0t], writes=[W0])
                k.op("dve", lambda e: e.tensor_tensor(out=yf[:], in0=yf[:], in1=W0[:], op=ALU.add), reads=[yf, W0], writes=[yf])
                k.op("act", lambda e: e.activation(out=gsb[:], in_=zv[:, D:D + 128], func=AF.Sigmoid), reads=[zv], writes=[gsb])
                transpose8(gsb, gT, nblk=1)
                pp, p0, p1 = pair()

                def f(e):
                    e.matmul(pp[:, 0:512], lhsT=gT[:, 0, :], rhs=gup[:, 0:512], start=True, stop=True)
                    return e.matmul(pp[:, 512:1024], lhsT=gT[:, 0, :], rhs=gup[:, 512:1024], start=True, stop=True)
                k.op("pe", f, reads=[gT, gup], writes=[p0, p1])
                k.op("dve", lambda e: e.tensor_tensor(out=actb[:], in0=pp[:, :], in1=yf[:], op=ALU.mult), reads=[p0, p1, yf], writes=[actb])
                yield
                pp, p0, p1 = proj(actb, wa, actT)
                yield
                k.op("act", lambda e: e.activation(out=W0[:], in_=zrs[:, 2048:3072], func=AF.Sigmoid), reads=[zrs], writes=[W0])
                k.op("dve", lambda e: e.tensor_tensor(out=W2[:], in0=pp[:, :], in1=W0[:], op=ALU.mult), reads=[p0, p1, W0], writes=[W2])
                yield
                k.op("act", lambda e: e.activation(out=ub[:], in_=zrs[:, 0:1024], func=AF.Gelu_apprx_tanh), reads=[zrs], writes=[ub])
                k.op("act", lambda e: e.activation(out=W1[:], in_=zrs[:, 1024:2048], func=AF.Gelu_apprx_tanh), reads=[zrs], writes=[W1])
                for c in range(2):
                    k.op("dve", lambda e: e.bn_stats(out=bnst[:, c, :], in_=W1[:, cs(c * 512, 512)]), reads=[W1], writes=[bnst])
                k.op("dve", lambda e: e.bn_aggr(out=mv[:], in_=bnst[:].rearrange("p a b -> p (a b)")), reads=[bnst], writes=[mv])
                rstd_from_ap(mv, 1, rsv, EPS)
                k.op("dve", lambda e: e.tensor_scalar(out=W1[:], in0=W1[:], scalar1=mv[:, 0:1], scalar2=rsv[:, 0:1], op0=ALU.subtract,
                                                      op1=ALU.mult), reads=[W1, mv, rsv], writes=[W1])
                k.op("dve", lambda e: e.tensor_tensor(out=vcb[:], in0=W1[:], in1=lnvg[:], op=ALU.mult), reads=[W1, lnvg], writes=[vcb])
                yield
                pp, p0, p1 = pair()

                def f(e):
                    for g in range(8):
                        ins = e.matmul(pp[:, cs(g * 128, 128)], lhsT=wsT[:, g, :], rhs=vcb[:, cs(g * 128, 128)], start=True, stop=True)
                    return ins
                k.op("pe", f, reads=[wsT, vcb], writes=[p0, p1])
                k.op("dve", lambda e: e.tensor_tensor(out=W1[:].rearrange("p (g c) -> p g c", c=128),
                                                      in0=pp[:, :].rearrange("p (g c) -> p g c", c=128),
                                                      in1=bsT[:].unsqueeze(2).to_broadcast([128, 8, 128]), op=ALU.add),
                     reads=[p0, p1, bsT], writes=[W1])
                k.op("dve", lambda e: e.tensor_tensor(out=actb[:], in0=W1[:], in1=ub[:], op=ALU.mult), reads=[W1, ub], writes=[actb])
                yield
                pp, p0, p1 = proj(actb, wb, actT)
                yield
                k.op("act", lambda e: e.activation(out=W0[:], in_=zrs[:, 3072:4096], func=AF.Sigmoid), reads=[zrs], writes=[W0])
                k.op("dve", lambda e: e.tensor_tensor(out=W1[:], in0=pp[:, :], in1=W0[:], op=ALU.mult), reads=[p0, p1, W0], writes=[W1])
                k.op("dve", lambda e: e.tensor_tensor(out=actb[:], in0=W1[:], in1=W2[:], op=ALU.add), reads=[W1, W2], writes=[actb])
                yield
                pp, p0, p1 = proj(actb, wo, actT)
                yield
                k.op("dve", lambda e: e.tensor_tensor(out=W0[:], in0=pp[:, :], in1=gate1[:], op=ALU.mult), reads=[p0, p1, gate1], writes=[W0])
                k.op("dve", lambda e: e.tensor_tensor(out=W0[:], in0=W0[:], in1=xt[:], op=ALU.add), reads=[W0, xt], writes=[W0])
                k.dma("sp", x1_d[rows, :], W0[:], reads=[W0], writes=[R_x1(i)], sembuf=W0)
                yield

            pending = list(range(NT))
            live = []
            while pending or live:
                if len(live) < 2 and pending:
                    ti = pending.pop(0)
                    live.append(tile_gen(ti, sets[ti % 2]))
                    if len(live) == 2 and ti == 1:
                        for _ in range(5):
                            next(live[0])
                for g in list(live):
                    try:
                        next(g)
                    except StopIteration:
                        live.remove(g)
            k.barrier()

    def rstd_from_ap(mvb, col, out_rstd, eps):
        k.op("act", lambda e: e.activation(out=out_rstd[:], in_=mvb[:, col:col + 1], func=AF.Ln, bias=eps_t(eps)[:], scale=1.0),
             reads=[mvb, eps_t(eps)], writes=[out_rstd])
        k.op("act", lambda e: e.activation(out=out_rstd[:], in_=out_rstd[:], func=AF.Exp, scale=-0.5),
             reads=[out_rstd], writes=[out_rstd])

    def phaseC(l, last):
        with contextlib.ExitStack() as es:
            w1 = sbt(es, "w1", [128, 8, DFF], BF16)
            w2 = sbt(es, "w2", [128, 32, D], BF16)
            g2 = sbt(es, "g2", [128, D], F32)
            sh2 = sbt(es, "sh2", [128, D], F32)
            gate2 = sbt(es, "gate2", [128, D], F32)
            fg = sbt(es, "fg", [128, D], F32) if last else None
            xt = [sbt(es, "xc%d" % i, [128, D], F32) for i in range(2)]

            def mkset(pi):
                B = {}
                B["W0"] = sbt(es, "Wc0_%d" % pi, [128, D], F32)
                B["hT"] = sbt(es, "hT2_%d" % pi, [128, 8, 128], BF16)
                B["hid"] = sbt(es, "hid_%d" % pi, [128, DFF], BF16)
                B["hidT"] = sbt(es, "hidT_%d" % pi, [128, 32, 128], BF16)
                B["ss"] = sbt(es, "ssc_%d" % pi, [128, 1], F32)
                B["rstd"] = sbt(es, "rstdc_%d" % pi, [128, 1], F32)
                return B
            sets = [mkset(0), mkset(1)]
            cast_load_rows(lambda kc: w1[:, kc, :], w1_d[l], 8, DFF, w1)
            cast_load_rows(lambda kc: w2[:, kc, :], w2_d[l], 32, D, w2)
            load_mod(sh2, l, 3)
            load_mod(g2, l, 4)
            load_mod(gate2, l, 5)
            if last:
                bc_load(fg, fg_d)

            def load_x(i):
                k.dma("sp", xt[i % 2][:], x1_d[cs(i * 128, 128), :], reads=[R_x1(i)], writes=[xt[i % 2]])
            def tile_gen(i, B):
                W0, hT, hid, hidT, ss, rstd = B["W0"], B["hT"], B["hid"], B["hidT"], B["ss"], B["rstd"]
                x = xt[i % 2]
                load_x(i)
                yield
                k.op("act", lambda e: e.activation(out=W0[:], in_=x[:], func=AF.Square), reads=[x], writes=[W0])
                k.op("dve", lambda e: e.tensor_reduce(out=ss[:], in_=W0[:], axis=AX.X, op=ALU.add), reads=[W0], writes=[ss])
                rstd_from(ss, rstd, 1.0 / D, EPS)
                k.op("dve", lambda e: e.scalar_tensor_tensor(out=W0[:], in0=x[:], scalar=rstd[:, 0:1], in1=g2[:], op0=ALU.mult,
                                                             op1=ALU.mult), reads=[x, rstd, g2], writes=[W0])
                k.op("dve", lambda e: e.tensor_tensor(out=hid[:, 0:D], in0=W0[:], in1=sh2[:], op=ALU.add), reads=[W0, sh2], writes=[hid])
                transpose8(hid, hT)
                yield
                for n in range(8):
                    pb = bank()

                    def f(e):
                        for kc in range(8):
                            ins = e.matmul(pb[:], lhsT=hT[:, kc, :], rhs=w1[:, kc, cs(n * 512, 512)], start=(kc == 0), stop=(kc == 7))
                        return ins
                    k.op("pe", f, reads=[hT, w1], writes=[pb])
                    k.op("act", lambda e: e.activation(out=hid[:, cs(n * 512, 512)], in_=pb[:], func=AF.Relu), reads=[pb], writes=[hid])
                    k.op("dve", lambda e: e.tensor_tensor(out=hid[:, cs(n * 512, 512)], in0=hid[:, cs(n * 512, 512)],
                                                          in1=hid[:, cs(n * 512, 512)], op=ALU.mult), reads=[hid], writes=[hid])
                    if n % 4 == 3:
                        yield
                for q in range(4):
                    pb = bank()
                    pv = pb[:].bitcast(BF16)

                    def f(e):
                        for j in range(8):
                            ins = e.transpose(out=pv[:, cs(j * 128, 128)], in_=hid[:, cs((q * 8 + j) * 128, 128)], identity=ident_b[:])
                        return ins
                    k.op("pe", f, reads=[hid, ident_b], writes=[pb])
                    if q % 2 == 0:
                        k.op("act", lambda e: e.copy(out=hidT[:, q * 8:q * 8 + 8, :].rearrange("p a b -> p (a b)"), in_=pv[:, :]),
                             reads=[pb], writes=[hidT])
                    else:
                        k.op("dve", lambda e: e.tensor_copy(out=hidT[:, q * 8:q * 8 + 8, :].rearrange("p a b -> p (a b)"), in_=pv[:, :]),
                             reads=[pb], writes=[hidT])
                yield
                pp, p0, p1 = pair()

                def f(e):
                    for nn in range(2):
                        for kc in range(32):
                            ins = e.matmul(pp[:, cs(nn * 512, 512)], lhsT=hidT[:, kc, :], rhs=w2[:, kc, cs(nn * 512, 512)],
                                           start=(kc == 0), stop=(kc == 31))
                    return ins
                k.op("pe", f, reads=[hidT, w2], writes=[p0, p1])
                k.op("dve", lambda e: e.tensor_tensor(out=W0[:], in0=pp[:, :], in1=gate2[:], op=ALU.mult), reads=[p0, p1, gate2], writes=[W0])
                k.op("dve", lambda e: e.tensor_tensor(out=W0[:], in0=W0[:], in1=x[:], op=ALU.add), reads=[W0, x], writes=[W0])
                rows = cs(i * 128, 128)
                if not last:
                    k.dma("sp", x2_d[rows, :], W0[:], reads=[W0], writes=[R_x2(i)], sembuf=W0)
                else:
                    k.op("act", lambda e: e.activation(out=x[:], in_=W0[:], func=AF.Square), reads=[W0], writes=[x])
                    k.op("dve", lambda e: e.tensor_reduce(out=ss[:], in_=x[:], axis=AX.X, op=ALU.add), reads=[x], writes=[ss])
                    rstd_from(ss, rstd, 1.0 / D, EPS)
                    k.op("dve", lambda e: e.scalar_tensor_tensor(out=W0[:], in0=W0[:], scalar=rstd[:, 0:1], in1=fg[:], op0=ALU.mult,
                                                                 op1=ALU.mult), reads=[W0, rstd, fg], writes=[W0])
                    k.dma("sp", y_d[rows, :], W0[:], reads=[W0], writes=[R_y(i)], sembuf=W0)
                yield

            pending = list(range(NT))
            live = []
            while pending or live:
                if len(live) < 2 and pending:
                    ti = pending.pop(0)
                    live.append(tile_gen(ti, sets[ti % 2]))
                    if len(live) == 2 and ti == 1:
                        for _ in range(3):
                            next(live[0])
                for g in list(live):
                    try:
                        next(g)
                    except StopIteration:
                        live.remove(g)
            k.barrier()

    R_xin = DR("xin")
    steps = [lambda: phaseP(0), lambda: phaseP(1)]
    for l in range(2):
        xs, Rx = (x_d, R_xin) if l == 0 else (x2_d, R_x2)
        steps += [lambda l=l, xs=xs, Rx=Rx: phaseA1(l, xs, Rx), lambda l=l: phaseS2(l),
                  lambda l=l, xs=xs, Rx=Rx: phaseM(l, xs, Rx), lambda l=l: phaseC(l, last=(l == 1))]
    for st_ in steps[:upto]:
        st_()
    k.barrier()
    ges.close()
    return nc, k


def _shift_mats(kind):
    m = np.zeros((4, 3, 128, 128), np.float32)
    eye = np.eye(128, dtype=np.float32)
    t = np.arange(128)
    for cls in range(4):
        cur = np.zeros((128, 128), np.float32)
        nbe = np.zeros((128, 128), np.float32)
        nbo = np.zeros((128, 128), np.float32)
        if kind == "sample":
            if cls == 0:
                for to in t:
                    if to % 64 != 0:
                        cur[to - 1, to] = 1
            elif cls == 1:
                for to in t:
                    if to % 64 != 63:
                        cur[to + 1, to] = 1
            elif cls == 2:
                for to in t:
                    if to >= 64:
                        cur[to - 64, to] = 1
                    else:
                        nbe[to + 64, to] = 1
                        nbo[to + 64, to] = 1
            else:
                for to in t:
                    if to < 64:
                        cur[to + 64, to] = 1
                    else:
                        nbe[to - 64, to] = 1
                        nbo[to - 64, to] = 1
        else:
            if cls in (0, 2):
                for to in t:
                    if to >= 1:
                        cur[to - 1, to] = 1
                nbo[127, 0] = 1
            else:
                for to in t:
                    if to <= 126:
                        cur[to + 1, to] = 1
                nbe[0, 127] = 1
        m[cls, 0] = cur - eye
        m[cls, 1] = nbe
        m[cls, 2] = nbo
    return np.ascontiguousarray(m.reshape(12, 128, 128).transpose(1, 0, 2).reshape(128, 12 * 128))


def _tri4():
    s = np.arange(128)[:, None]
    t = np.arange(128)[None, :]
    m = np.stack([(s <= t), (s >= t), (s < t), (s > t)], axis=1).astype(np.float32)
    return np.ascontiguousarray(m.reshape(128, 512))


def _state_to_H(st):
    a = st.reshape(2, 2, 8, 2, 64, 64)
    a = a.transpose(0, 1, 3, 5, 2, 4)
    return np.ascontiguousarray(a.reshape(2, 2, 128, 512))


def _H_to_state(Hm):
    lead = Hm.shape[:-2]
    a = Hm.reshape(lead + (2, 64, 8, 64))
    nl = len(lead)
    perm = tuple(range(nl)) + (nl + 2, nl + 0, nl + 3, nl + 1)
    a = a.transpose(perm)
    return a.reshape(lead + (16, 64, 64))


def make_core_inputs(kind, x_tokens, cond_vec, state_lh, shared):
    d = dict(shared)
    d["x"] = np.ascontiguousarray(x_tokens, dtype=np.float32)
    d["cond"] = np.ascontiguousarray(cond_vec.reshape(8, 128).T, dtype=np.float32)
    d["state0"] = _state_to_H(state_lh)
    d["cmask"] = np.full((128, 1), 1.0 if kind == "sample" else 0.0, np.float32)
    d["shm"] = _shift_mats(kind)
    return d


def shared_inputs(w_ada, b_ada, norm1_g, norm2_g, w_in, mu_shift, w0, w_up, a0, a_up, g_up, k_k, k_a, r_k, lnx_g,
                  lnx_b, w_branch_a, ln_v_g, w_s, b_s, w_branch_b, w_out, w1, w2, final_g):
    f = lambda a: np.ascontiguousarray(np.asarray(a), dtype=np.float32)
    wup_aug = np.concatenate([np.asarray(w_up), np.asarray(w0)[:, :, None, :]], axis=2)
    aup_aug = np.concatenate([np.asarray(a_up), np.asarray(a0)[:, :, None, :]], axis=2)
    wsT = np.asarray(w_s).transpose(0, 3, 1, 2).reshape(2, 128, 8 * 128)
    bsT = np.asarray(b_s).transpose(0, 2, 1)
    return dict(ident=np.eye(128, dtype=np.float32), tri4=_tri4(), w_ada=f(w_ada), b_ada=f(b_ada), norm1_g=f(norm1_g),
                norm2_g=f(norm2_g), w_in=f(w_in), mu_shift=f(mu_shift), wup_aug=f(wup_aug), aup_aug=f(aup_aug), g_up=f(g_up),
                k_k=f(k_k), k_a=f(k_a), r_k=f(np.asarray(r_k).reshape(2, D)), lnx_g=f(lnx_g), lnx_b=f(lnx_b),
                w_branch_a=f(w_branch_a), ln_v_g=f(ln_v_g), wsT=f(wsT), bsT=f(bsT), w_branch_b=f(w_branch_b), w_out=f(w_out),
                w1=f(w1), w2=f(w2), final_g=f(final_g))


_PROG = {}


def kernel(x_prompt, x_sample, state_rwkv, c, c_ctx, w_ada, b_ada, norm1_g, norm2_g, w_in, mu_shift,
           w0, w_up, a0, a_up, g_up, k_k, k_a, r_k, lnx_g, lnx_b, w_branch_a, ln_v_g, w_s, b_s,
           w_branch_b, w_out, w1, w2, final_g):
    NT = 32
    x_prompt = np.asarray(x_prompt, dtype=np.float32)
    x_sample = np.asarray(x_sample, dtype=np.float32)
    state_rwkv = np.asarray(state_rwkv, dtype=np.float32)
    c = np.asarray(c, dtype=np.float32)
    c_ctx = np.asarray(c_ctx, dtype=np.float32)
    shared = shared_inputs(w_ada, b_ada, norm1_g, norm2_g, w_in, mu_shift, w0, w_up, a0, a_up, g_up, k_k, k_a, r_k,
                           lnx_g, lnx_b, w_branch_a, ln_v_g, w_s, b_s, w_branch_b, w_out, w1, w2, final_g)
    in_maps = []
    for b in range(4):
        in_maps.append(make_core_inputs("sample", x_sample[b], c[b], state_rwkv[b], shared))
    zero_state = np.zeros((2, 2, 16, 64, 64), np.float32)
    for q in range(4):
        xs = np.zeros((NT * 128, D), np.float32)
        xs[:2048] = x_prompt[8 * q:8 * q + 8].reshape(2048, D)
        xs[2048:] = xs[:2048]
        in_maps.append(make_core_inputs("prompt", xs, c_ctx, zero_state, shared))
    if NT not in _PROG:
        _PROG[NT] = build_program(NT)[0]
    res = run_bass_kernel_spmd(_PROG[NT], in_maps, core_ids=list(range(8)))
    r = res.results
    y_sample = np.stack([r[b]["y"] for b in range(4)], axis=0)
    y_prompt = np.concatenate([r[4 + q]["y"][:2048].reshape(8, 256, D) for q in range(4)], axis=0)
    sts = []
    for q in range(4):
        so = r[4 + q]["st_out"]
        so = so[:, :, :8]
        s = _H_to_state(so)
        sts.append(np.transpose(s, (2, 0, 1, 3, 4, 5)))
    new_state = np.ascontiguousarray(np.concatenate(sts, axis=0), dtype=np.float32)
    return (np.ascontiguousarray(y_prompt, dtype=np.float32), np.ascontiguousarray(y_sample, dtype=np.float32), new_state)
```

```python
import contextlib
import os
DBG = int(os.environ.get('KDBG', '99'))
KSKIP = os.environ.get('KSKIP', '')
import numpy as np
import concourse.bass as bass
import concourse.mybir as mybir
from concourse.bass_utils import run_bass_kernel_spmd

F32 = mybir.dt.float32
BF16 = mybir.dt.bfloat16
ALU = mybir.AluOpType
AF = mybir.ActivationFunctionType
AX = mybir.AxisListType

D = 1024
CR = 3456
DIN = 7552
DFF = 4096
NHEAD = 16
EPS = 1e-6
GN_EPS = 64e-5
DSC = float(np.exp(-0.5))


class Buf:
    __slots__ = ("name", "t", "w", "r", "dsem", "dcnt")

    def __init__(self, name, t=None):
        self.name = name
        self.t = t
        self.w = None
        self.r = []
        self.dsem = None
        self.dcnt = 0

    def __getitem__(self, idx):
        return self.t[idx]


class K:
    def __init__(self, nc):
        self.nc = nc
        self.eng = {"pe": nc.tensor, "act": nc.scalar, "dve": nc.vector, "pool": nc.gpsimd, "sp": nc.sync}
        self.sem = {}
        self.cnt = {}
        for e in self.eng:
            self.sem[e] = nc.alloc_semaphore(name="s_" + e)
            self.cnt[e] = 0
        self.waited = {}
        self.dsems = {}
        self.free_dsems = []
        self.ninstr = 0
        self.uid = 0

    def _wait(self, e, tok):
        if tok is None:
            return
        key, val = tok
        if key == e and e == "pe":
            return
        kk = (e, key)
        if self.waited.get(kk, 0) >= val:
            return
        self.waited[kk] = val
        self.eng[e].wait_ge(self.sem[key], val)
        self.ninstr += 1

    def _deps(self, e, reads, writes):
        for b in reads:
            self._wait(e, b.w)
        for b in writes:
            self._wait(e, b.w)
            for tok in b.r:
                self._wait(e, tok)

    def _commit(self, tok, reads, writes):
        for b in reads:
            if b not in writes:
                b.r.append(tok)
                if len(b.r) > 10:
                    best = {}
                    for k_, v_ in b.r:
                        if best.get(k_, -1) < v_:
                            best[k_] = v_
                    b.r = list(best.items())
        for b in writes:
            b.w = tok
            b.r = []

    def op(self, e, fn, reads=(), writes=()):
        reads = [b for b in reads if b is not None]
        writes = [b for b in writes if b is not None]
        self._deps(e, reads, writes)
        ins = fn(self.eng[e])
        self.cnt[e] += 1
        ins.then_inc(self.sem[e], 1)
        self.ninstr += 1
        self._commit((e, self.cnt[e]), reads, writes)

    def dma(self, q, out_ap, in_ap, reads=(), writes=(), sembuf=None, **kw):
        reads = [b for b in reads if b is not None]
        writes = [b for b in writes if b is not None]
        if sembuf is None:
            sembuf = (writes + reads)[0]
        if sembuf.dsem is None:
            if self.free_dsems:
                key, base = self.free_dsems.pop()
                sembuf.dcnt = base
            else:
                key = "d%d" % len(self.sem)
                self.sem[key] = self.nc.alloc_semaphore(name=key)
            self.dsems[key] = sembuf
            sembuf.dsem = key
        self._deps(q, reads, writes)
        ins = self.eng[q].dma_start(out=out_ap, in_=in_ap, **kw)
        sembuf.dcnt += 16
        ins.then_inc(self.sem[sembuf.dsem], 16)
        self.ninstr += 1
        self._commit((sembuf.dsem, sembuf.dcnt), reads, writes)

    def barrier(self):
        toks = [(e, self.cnt[e]) for e in self.eng if self.cnt[e] > 0]
        toks += [(key, b.dcnt) for key, b in self.dsems.items() if b.dcnt > 0]
        for e in self.eng:
            for tok in toks:
                if tok[0] != e:
                    self._wait(e, tok)
        for key, b in list(self.dsems.items()):
            if not getattr(b, "keep", False):
                self.free_dsems.append((key, b.dcnt))
                b.dsem = None
                del self.dsems[key]


def cs(a, n):
    return slice(a, a + n)


def hc(h):
    return slice((h % 2) * 512 + (h // 2) * 64, (h % 2) * 512 + (h // 2) * 64 + 64)


def build_program(NT, upto=99):
    T = NT * 128
    NG = NT // 2
    nc = bass.Bass("TRN2", target_bir_lowering=False)
    k = K(nc)

    def din(name, shape):
        return nc.dram_tensor(name, list(shape), F32, kind="ExternalInput").ap()

    x_d = din("x", [T, D])
    cond_d = din("cond", [128, 8])
    state0_d = din("state0", [2, 2, 128, 512])
    cmask_d = din("cmask", [128, 1])
    shm_d = din("shm", [128, 12 * 128])
    ident_d = din("ident", [128, 128])
    tri4_d = din("tri4", [128, 4 * 128])
    w_ada_d = din("w_ada", [2, D, 6 * D])
    b_ada_d = din("b_ada", [2, 6 * D])
    n1g_d = din("norm1_g", [2, D])
    n2g_d = din("norm2_g", [2, D])
    w_in_d = din("w_in", [2, D, DIN])
    mu_d = din("mu_shift", [2, CR])
    wup_d = din("wup_aug", [2, 2, 65, D])
    aup_d = din("aup_aug", [2, 2, 65, D])
    gup_d = din("g_up", [2, 128, D])
    kk_d = din("k_k", [2, D])
    ka_d = din("k_a", [2, D])
    rk_d = din("r_k", [2, D])
    lnxg_d = din("lnx_g", [2, D])
    lnxb_d = din("lnx_b", [2, D])
    wa_d = din("w_branch_a", [2, D, D])
    lnvg_d = din("ln_v_g", [2, D])
    wsT_d = din("wsT", [2, 128, 8 * 128])
    bsT_d = din("bsT", [2, 128, 8])
    wb_d = din("w_branch_b", [2, D, D])
    wo_d = din("w_out", [2, D, D])
    w1_d = din("w1", [2, D, DFF])
    w2_d = din("w2", [2, DFF, D])
    fg_d = din("final_g", [D])

    y_d = nc.dram_tensor("y", [T, D], F32, kind="ExternalOutput").ap()
    st_d = nc.dram_tensor("st_out", [2, 2, NG, 128, 512], F32, kind="ExternalOutput").ap()

    def dscr(name, shape, dt):
        return nc.dram_tensor(name, list(shape), dt, kind="Internal").ap()

    modbc_d = dscr("modbc", [2, 128, 6 * D], F32)
    zr_d = dscr("zr_s", [T, CR], BF16)
    zrest_d = dscr("zrest_s", [T, 4096], BF16)
    ysc_d = dscr("ysc_s", [2, T, D], F32)
    bon_d = dscr("bon_s", [2, T, 16], F32)
    x1_d = dscr("x1_s", [T, D], F32)
    x2_d = dscr("x2_s", [T, D], F32)

    class DR:
        def __init__(self, nm):
            self.b = {}
            self.nm = nm

        def __call__(self, *key):
            if key not in self.b:
                self.b[key] = Buf(self.nm + str(key))
            return self.b[key]

    R_mod, R_zr, R_zrest, R_ysc, R_bon, R_x1, R_x2, R_y, R_st = [DR(n) for n in
        ("mod", "zr", "zrest", "ysc", "bon", "x1", "x2", "y", "st")]

    PP = [nc.alloc_psum_tensor("psum%d" % i, [128, 1024], F32) for i in range(4)]
    PB = []
    for i in range(8):
        PB.append(Buf("pb%d" % i, PP[i // 2][:, cs((i % 2) * 512, 512)]))
    pst = {"b": 0, "p": 0}

    def bank():
        b = PB[pst["b"] % 8]
        pst["b"] += 1
        return b

    def pair():
        if pst["b"] % 2:
            pst["b"] += 1
        i = (pst["b"] % 8) // 2
        pst["b"] += 2
        return PP[i], PB[2 * i], PB[2 * i + 1]

    def sbt(es, name, shape, dt):
        k.uid += 1
        t = es.enter_context(nc.sbuf_tensor("%s_%d" % (name, k.uid), list(shape), dt))
        return Buf(name, t)

    ges = contextlib.ExitStack()
    ident_f = sbt(ges, "ident_f", [128, 128], F32)
    ident_b = sbt(ges, "ident_b", [128, 128], BF16)
    tri4 = sbt(ges, "tri4", [128, 4, 128], F32)
    ones_f = sbt(ges, "ones_f", [128, 1], F32)
    cmask = sbt(ges, "cmask", [128, 1], F32)
    k.dma("sp", ident_f[:], ident_d, writes=[ident_f])
    k.dma("sp", tri4[:].rearrange("p a b -> p (a b)"), tri4_d, writes=[tri4])
    k.dma("sp", cmask[:], cmask_d, writes=[cmask])
    k.op("dve", lambda e: e.tensor_copy(out=ident_b[:], in_=ident_f[:]), reads=[ident_f], writes=[ident_b])
    k.op("dve", lambda e: e.memset(ones_f[:], 1.0), writes=[ones_f])

    def bc_load(buf, dvec):
        k.dma("sp", buf[:], dvec.partition_broadcast(128), writes=[buf])

    def cast_load_rows(buf_ap_fn, dsrc, nk, ncol, buf):
        for kc in range(nk):
            k.dma("pool", buf_ap_fn(kc), dsrc[cs(kc * 128, 128), :], writes=[buf], max_dma_last_dim=4096)

    def rstd_from(e_ss, out_rstd, scale, eps):
        k.op("act", lambda e: e.activation(out=out_rstd[:], in_=e_ss[:], func=AF.Ln, bias=eps_t(eps)[:], scale=scale),
             reads=[e_ss, eps_t(eps)], writes=[out_rstd])
        k.op("act", lambda e: e.activation(out=out_rstd[:], in_=out_rstd[:], func=AF.Exp, scale=-0.5),
             reads=[out_rstd], writes=[out_rstd])

    eps_tiles = {}

    def eps_t(v):
        if v not in eps_tiles:
            b = sbt(ges, "eps%d" % len(eps_tiles), [128, 1], F32)
            k.op("dve", lambda e: e.memset(b[:], float(v)), writes=[b])
            eps_tiles[v] = b
        return eps_tiles[v]

    for v in (EPS, GN_EPS, 1e-12):
        eps_t(v)

    def phaseP(l):
        with contextlib.ExitStack() as es:
            wad = sbt(es, "wad", [128, 8, 6 * D], BF16)
            ba = sbt(es, "ba", [128, 6 * D], F32)
            mod = sbt(es, "mod", [128, 6 * D], F32)
            n1g = sbt(es, "n1g", [128, D], F32)
            n2g = sbt(es, "n2g", [128, D], F32)
            cnd = sbt(es, "cnd", [128, 8], F32)
            scb = sbt(es, "scb", [128, 8, 128], BF16)
            cast_load_rows(lambda kc: wad[:, kc, :], w_ada_d[l], 8, 6 * D, wad)
            bc_load(ba, b_ada_d[l])
            bc_load(n1g, n1g_d[l])
            bc_load(n2g, n2g_d[l])
            k.dma("sp", cnd[:], cond_d, writes=[cnd])
            k.op("act", lambda e: e.activation(out=cnd[:], in_=cnd[:], func=AF.Silu), reads=[cnd], writes=[cnd])
            k.op("dve", lambda e: e.tensor_copy(out=scb[:], in_=cnd[:].unsqueeze(2).to_broadcast([128, 8, 128])),
                 reads=[cnd], writes=[scb])
            for n in range(12):
                pb = bank()

                def f(e):
                    for kc in range(8):
                        ins = e.matmul(pb[:], lhsT=scb[:, kc, :], rhs=wad[:, kc, cs(n * 512, 512)],
                                       start=(kc == 0), stop=(kc == 7))
                    return ins
                k.op("pe", f, reads=[scb, wad], writes=[pb])
                k.op("dve", lambda e: e.tensor_tensor(out=mod[:, cs(n * 512, 512)], in0=pb[:], in1=ba[:, cs(n * 512, 512)],
                                                      op=ALU.add), reads=[pb, ba], writes=[mod])
            k.op("dve", lambda e: e.scalar_tensor_tensor(out=mod[:, cs(D, D)], in0=mod[:, cs(D, D)], scalar=1.0, in1=n1g[:],
                                                         op0=ALU.add, op1=ALU.mult), reads=[mod, n1g], writes=[mod])
            k.op("dve", lambda e: e.scalar_tensor_tensor(out=mod[:, cs(4 * D, D)], in0=mod[:, cs(4 * D, D)], scalar=1.0,
                                                         in1=n2g[:], op0=ALU.add, op1=ALU.mult), reads=[mod, n2g], writes=[mod])
            k.dma("sp", modbc_d[l], mod[:], reads=[mod], writes=[R_mod(l)], sembuf=mod)
            k.barrier()

    def load_mod(buf, l, j):
        k.dma("sp", buf[:], modbc_d[l][:, cs(j * D, D)], reads=[R_mod(l)], writes=[buf])

    def phaseA1(l, xsrc_d, R_xsrc):
        with contextlib.ExitStack() as es:
            win = sbt(es, "win", [128, 8, DIN], BF16)
            g1 = sbt(es, "g1", [128, D], F32)
            sh1 = sbt(es, "sh1", [128, D], F32)
            mu = sbt(es, "mu", [128, CR], F32)
            shm = sbt(es, "shm", [128, 12, 128], BF16)
            xt = [sbt(es, "xt0", [128, D], F32)]
            xt.append(xt[0])
            sq = sbt(es, "sq", [128, D], F32)
            hb = sbt(es, "hb", [128, D], BF16)
            hT = sbt(es, "hT", [128, 8, 128], BF16)
            ss = sbt(es, "ss", [128, 1], F32)
            rstd = sbt(es, "rstd", [128, 1], F32)
            zb = [sbt(es, "zb%d" % i, [128, CR], BF16) for i in range(2)]
            zm = [sbt(es, "zm%d" % i, [128, CR], BF16) for i in range(3)]
            zst = sbt(es, "zst", [128, CR], BF16)
            rst = [sbt(es, "rst%d" % i, [128, 512], BF16) for i in range(4)]
            cast_load_rows(lambda kc: win[:, kc, :], w_in_d[l], 8, DIN, win)
            k.dma("pool", shm[:].rearrange("p a b -> p (a b)"), shm_d, writes=[shm], max_dma_last_dim=4096)
            load_mod(sh1, l, 0)
            load_mod(g1, l, 1)
            bc_load(mu, mu_d[l])
            rsti = [0]

            def load_x(i):
                k.dma("sp", xt[i % 2][:], xsrc_d[cs(i * 128, 128), :], reads=[R_xsrc(i)], writes=[xt[i % 2]])

            def stage1(i):
                x = xt[i % 2]
                if DBG < 2:
                    if i + 1 < NT:
                        load_x(i + 1)
                    return
                k.op("act", lambda e: e.activation(out=sq[:], in_=x[:], func=AF.Square), reads=[x], writes=[sq])
                k.op("dve", lambda e: e.tensor_reduce(out=ss[:], in_=sq[:], axis=AX.X, op=ALU.add), reads=[sq], writes=[ss])
                rstd_from(ss, rstd, 1.0 / D, EPS)
                k.op("dve", lambda e: e.scalar_tensor_tensor(out=x[:], in0=x[:], scalar=rstd[:, 0:1], in1=g1[:],
                                                             op0=ALU.mult, op1=ALU.mult), reads=[x, rstd, g1], writes=[x])
                k.op("dve", lambda e: e.tensor_tensor(out=hb[:], in0=x[:], in1=sh1[:], op=ALU.add),
                     reads=[x, sh1], writes=[hb])
                if i + 1 < NT:
                    load_x(i + 1)
                if DBG < 3:
                    return
                transpose8(hb, hT)
                if DBG < 4:
                    return
                zbi, zmi = zb[i % 2], zm[i % 3]
                col = 0
                ci = 0
                while col < DIN:
                    if col < CR:
                        n = min(512, CR - col)
                    else:
                        n = 512
                    pb = bank()

                    def f(e):
                        for kc in range(8):
                            ins = e.matmul(pb[:, 0:n], lhsT=hT[:, kc, :], rhs=win[:, kc, cs(col, n)],
                                           start=(kc == 0), stop=(kc == 7))
                        return ins
                    k.op("pe", f, reads=[hT, win], writes=[pb])
                    if 'p' in KSKIP:
                        pass
                    elif col < CR:
                        if 'z' not in KSKIP:
                            k.op("act", lambda e: e.copy(out=zbi[:, cs(col, n)], in_=pb[:, 0:n]), reads=[pb], writes=[zbi])
                        if 'm' not in KSKIP:
                            k.op("dve", lambda e: e.tensor_tensor(out=zmi[:, cs(col, n)], in0=zbi[:, cs(col, n)], in1=mu[:, cs(col, n)],
                                                                  op=ALU.mult), reads=[zbi, mu], writes=[zmi])
                    else:
                        st = rst[rsti[0] % 4]
                        rsti[0] += 1
                        if ci % 2 == 0:
                            k.op("act", lambda e: e.copy(out=st[:], in_=pb[:]), reads=[pb], writes=[st])
                        else:
                            k.op("dve", lambda e: e.tensor_copy(out=st[:], in_=pb[:]), reads=[pb], writes=[st])
                        if 'r' not in KSKIP:
                            k.dma("sp", zrest_d[cs(i * 128, 128), cs(col - CR, 512)], st[:], reads=[st],
                                  writes=[R_zrest(i, (col - CR) // 512)], sembuf=st)
                    col += n
                    ci += 1

            def stage2(i):
                if DBG < 5:
                    return
                par = i % 2
                for cls in range(4):
                    nb = i - 1 if cls in (0, 2) else i + 1
                    for hh in range(2):
                        c0 = cls + 4 * 432 * hh
                        sl = slice(c0, c0 + 4 * 431 + 1, 4)
                        pb = bank()
                        srcs = [(ident_b[:], zb[i % 2], ident_b), (shm[:, 3 * cls, :], zm[i % 3], shm)]
                        if 0 <= nb < NT:
                            srcs.append((shm[:, 3 * cls + 1 + par, :], zm[nb % 3], shm))

                        def f(e):
                            for j, (lt, rb, _) in enumerate(srcs):
                                ins = e.matmul(pb[:, 0:432], lhsT=lt, rhs=rb[:, sl], start=(j == 0), stop=(j == len(srcs) - 1))
                            return ins
                        k.op("pe", f, reads=[s[1] for s in srcs] + [ident_b, shm], writes=[pb])
                        if hh == 0:
                            k.op("act", lambda e: e.copy(out=zst[:, sl], in_=pb[:, 0:432]), reads=[pb], writes=[zst])
                        else:
                            k.op("dve", lambda e: e.tensor_copy(out=zst[:, sl], in_=pb[:, 0:432]), reads=[pb], writes=[zst])
                k.dma("sp", zr_d[cs(i * 128, 128), :], zst[:], reads=[zst], writes=[R_zr(i)], sembuf=zst)

            load_x(0)
            stage1(0)
            for i in range(NT):
                if i + 1 < NT:
                    stage1(i + 1)
                stage2(i)
            k.barrier()

    def transpose8(src, dst, nblk=8, src_off=0):
        pb = bank()
        pv = pb[:].bitcast(BF16)

        def f(e):
            for j in range(nblk):
                ins = e.transpose(out=pv[:, cs(j * 128, 128)], in_=src[:, cs(src_off + j * 128, 128)], identity=ident_b[:])
            return ins
        k.op("pe", f, reads=[src, ident_b], writes=[pb])
        k.op("act", lambda e: e.copy(out=dst[:, 0:nblk, :].rearrange("p a b -> p (a b)"), in_=pv[:, 0:nblk * 128]),
             reads=[pb], writes=[dst])

    def phaseS(l, d):
        with contextlib.ExitStack() as es:
            kkc = sbt(es, "kkc", [128, D], F32)
            kac = sbt(es, "kac", [128, D], F32)
            rkc = sbt(es, "rkc", [128, D], F32)
            wup = sbt(es, "wup", [65, D], BF16)
            aup = sbt(es, "aup", [65, D], BF16)
            maskM = sbt(es, "maskM", [128, 4, 128], F32)
            maskN = sbt(es, "maskN", [128, 4, 128], F32)
            H = sbt(es, "H", [128, 8, 64], F32)
            Hb = sbt(es, "Hb", [128, 8, 64], BF16)
            zr = [sbt(es, "zr%d" % i, [128, CR], BF16) for i in range(2)]
            ldT = sbt(es, "ldT", [65, 2, 128], BF16)
            tw = sbt(es, "tw", [128, 128], BF16)
            SG = sbt(es, "SG", [128, D], F32)
            A = sbt(es, "A", [128, D], F32)
            KX = sbt(es, "KX", [128, D], F32)
            BP = sbt(es, "BP", [128, D], F32)
            KD = sbt(es, "KD", [128, D], F32)
            S0 = sbt(es, "S0", [128, D], F32)
            S1 = sbt(es, "S1", [128, D], F32)
            st16 = sbt(es, "st16", [128, 16], F32)
            rs16 = sbt(es, "rs16", [128, 16], F32)
            bon = sbt(es, "bon", [128, 16], F32)
            gC = sbt(es, "gC", [128, 8], F32)
            TM = sbt(es, "TM", [128, 4, D], BF16)
            Bg = sbt(es, "Bg", [128, D], BF16)
            Kg = sbt(es, "Kg", [128, D], BF16)
            FM = sbt(es, "FM", [128, 8, 4, 128], BF16)
            MB = sbt(es, "MB", [128, 16, 512], BF16)
            Pm = [sbt(es, "Pm%d" % i, [128, 16, 128], BF16) for i in range(2)]
            PT = [sbt(es, "PT%d" % i, [128, 16, 128], BF16) for i in range(2)]
            Xb = [sbt(es, "Xb%d" % i, [128, D], BF16) for i in range(2)]
            ysc = sbt(es, "ysc", [128, D], F32)

            bc_load(kkc, kk_d[l])
            bc_load(kac, ka_d[l])
            bc_load(rkc, rk_d[l])
            k.dma("pool", wup[:], wup_d[l, d], writes=[wup], max_dma_last_dim=4096)
            k.dma("pool", aup[:], aup_d[l, d], writes=[aup], max_dma_last_dim=4096)
            strict_i, incl_i, nmask_i = (2, 0, 3) if d == 0 else (3, 1, 2)
            for j, src in enumerate((strict_i, incl_i, strict_i, incl_i)):
                k.op("dve", lambda e: e.tensor_copy(out=maskM[:, j, :], in_=tri4[:, src, :]), reads=[tri4], writes=[maskM])
            for j in range(4):
                k.op("dve", lambda e: e.tensor_copy(out=maskN[:, j, :], in_=tri4[:, nmask_i, :]), reads=[tri4], writes=[maskN])
            tri_incl = tri4[:, incl_i, :]
            tri_excl = tri4[:, strict_i, :]
            tri_dg = tri4[:, nmask_i, :]
            k.op("dve", lambda e: e.memset(ldT[:], 1.0), writes=[ldT])
            k.dma("sp", H[:].rearrange("p a b -> p (a b)"), state0_d[l, d], writes=[H])
            k.op("act", lambda e: e.copy(out=Hb[:], in_=H[:]), reads=[H], writes=[Hb])

            order = list(range(NT)) if d == 0 else list(range(NT - 1, -1, -1))

            def load_zr(n):
                i = order[n]
                k.dma("sp", zr[n % 2][:], zr_d[cs(i * 128, 128), :], reads=[R_zr(i)], writes=[zr[n % 2]])

            load_zr(0)
            for n, i in enumerate(order):
                z = zr[n % 2]
                if n + 1 < NT:
                    load_zr(n + 1)
                rq = z[:, 0:D]
                kq = z[:, D:2 * D]
                vq = z[:, 2 * D:3 * D]
                k.op("act", lambda e: e.activation(out=tw[:, 0:64], in_=z[:, cs(3 * D + 64 * d, 64)], func=AF.Tanh),
                     reads=[z], writes=[tw])
                k.op("dve", lambda e: e.tensor_copy(out=tw[:, 64:128], in_=z[:, cs(3 * D + 128 + 64 * d, 64)]),
                     reads=[z], writes=[tw])
                pb = bank()
                pv = pb[:].bitcast(BF16)

                def f(e):
                    e.transpose(out=pv[0:64, 0:128], in_=tw[:, 0:64], identity=ident_b[:])
                    return e.transpose(out=pv[0:64, 128:256], in_=tw[:, 64:128], identity=ident_b[:])
                k.op("pe", f, reads=[tw, ident_b], writes=[pb])
                k.op("act", lambda e: e.copy(out=ldT[0:64, :, :].rearrange("p a b -> p (a b)"), in_=pv[0:64, 0:256]),
                     reads=[pb], writes=[ldT])
                for (wmat, src_j, dst) in ((wup, 0, SG), (aup, 1, A)):
                    pp, b0, b1 = pair()

                    def f(e):
                        e.matmul(pp[:, 0:512], lhsT=ldT[:, src_j, :], rhs=wmat[:, 0:512], start=True, stop=True)
                        return e.matmul(pp[:, 512:1024], lhsT=ldT[:, src_j, :], rhs=wmat[:, 512:1024], start=True, stop=True)
                    k.op("pe", f, reads=[ldT, wmat], writes=[b0, b1])
                    k.op("act", lambda e: e.activation(out=dst[:], in_=pp[:, :], func=AF.Sigmoid), reads=[b0, b1], writes=[dst])
                if DBG < 11:
                    continue
                k.op("dve", lambda e: e.tensor_tensor(out=KX[:], in0=kq, in1=kkc[:], op=ALU.mult), reads=[z, kkc], writes=[KX])
                k.op("dve", lambda e: e.tensor_tensor(out=S0[:], in0=KX[:], in1=KX[:], op=ALU.mult), reads=[KX], writes=[S0])
                k.op("dve", lambda e: e.tensor_reduce(out=st16[:], in_=S0[:].rearrange("p (h n) -> p h n", n=64), axis=AX.X,
                                                      op=ALU.add), reads=[S0], writes=[st16])
                rstd_from(st16, rs16, 1.0, 1e-12)
                k.op("dve", lambda e: e.tensor_tensor(out=KX[:].rearrange("p (h n) -> p h n", n=64),
                                                      in0=KX[:].rearrange("p (h n) -> p h n", n=64),
                                                      in1=rs16[:].unsqueeze(2).to_broadcast([128, 16, 64]), op=ALU.mult),
                     reads=[KX, rs16], writes=[KX])
                k.op("dve", lambda e: e.scalar_tensor_tensor(out=BP[:], in0=KX[:], scalar=-1.0, in1=A[:], op0=ALU.mult,
                                                             op1=ALU.mult), reads=[KX, A], writes=[BP])
                k.op("dve", lambda e: e.tensor_tensor(out=S0[:], in0=kq, in1=kac[:], op=ALU.mult), reads=[z, kac], writes=[S0])
                k.op("dve", lambda e: e.scalar_tensor_tensor(out=S0[:], in0=A[:], scalar=-1.0, in1=S0[:], op0=ALU.add,
                                                             op1=ALU.mult), reads=[A, S0], writes=[S0])
                k.op("dve", lambda e: e.tensor_tensor(out=KD[:], in0=S0[:], in1=kq, op=ALU.add), reads=[S0, z], writes=[KD])
                k.op("dve", lambda e: e.tensor_tensor(out=S0[:], in0=KD[:], in1=rkc[:], op=ALU.mult), reads=[KD, rkc], writes=[S0])
                k.op("dve", lambda e: e.tensor_tensor(out=S0[:], in0=S0[:], in1=rq, op=ALU.mult), reads=[S0, z], writes=[S0])
                k.op("dve", lambda e: e.tensor_reduce(out=bon[:], in_=S0[:].rearrange("p (h n) -> p h n", n=64), axis=AX.X,
                                                      op=ALU.add), reads=[S0], writes=[bon])
                k.dma("sp", bon_d[d, cs(i * 128, 128), :], bon[:], reads=[bon], writes=[R_bon(d, i)], sembuf=bon)
                if DBG < 12:
                    continue
                def cum(tri_ap, scale, dstE):
                    pp, b0, b1 = pair()

                    def f(e):
                        e.matmul(pp[:, 0:512], lhsT=tri_ap, rhs=SG[:, 0:512], start=True, stop=True)
                        return e.matmul(pp[:, 512:1024], lhsT=tri_ap, rhs=SG[:, 512:1024], start=True, stop=True)
                    k.op("pe", f, reads=[tri4, SG], writes=[b0, b1])
                    k.op("act", lambda e: e.activation(out=dstE[:], in_=pp[:, :], func=AF.Exp, scale=scale),
                         reads=[b0, b1], writes=[dstE])
                    return pp, b0, b1
                pp, b0, b1 = cum(tri_incl, -DSC, S0)
                k.op("dve", lambda e: e.tensor_tensor(out=TM[:, 1, :], in0=rq, in1=S0[:], op=ALU.mult), reads=[z, S0], writes=[TM])
                k.op("act", lambda e: e.activation(out=S1[:], in_=pp[:, :], func=AF.Exp, scale=DSC), reads=[b0, b1], writes=[S1])
                k.op("dve", lambda e: e.tensor_tensor(out=TM[:, 2, :], in0=BP[:], in1=S1[:], op=ALU.mult), reads=[BP, S1], writes=[TM])
                k.op("dve", lambda e: e.tensor_tensor(out=TM[:, 3, :], in0=KD[:], in1=S1[:], op=ALU.mult), reads=[KD, S1], writes=[TM])
                cum(tri_excl, -DSC, S0)
                k.op("dve", lambda e: e.tensor_tensor(out=TM[:, 0, :], in0=KX[:], in1=S0[:], op=ALU.mult), reads=[KX, S0], writes=[TM])
                cum(tri_dg, -DSC, S1)
                k.op("dve", lambda e: e.tensor_tensor(out=Bg[:], in0=BP[:], in1=S1[:], op=ALU.mult), reads=[BP, S1], writes=[Bg])
                k.op("dve", lambda e: e.tensor_tensor(out=Kg[:], in0=KD[:], in1=S1[:], op=ALU.mult), reads=[KD, S1], writes=[Kg])
                pbg = bank()

                def f(e):
                    for ct in range(8):
                        ins = e.matmul(pbg[:, ct:ct + 1], lhsT=SG[:, cs(ct * 128, 128)], rhs=ones_f[:], start=True, stop=True)
                    return ins
                k.op("pe", f, reads=[SG, ones_f], writes=[pbg])
                k.op("act", lambda e: e.activation(out=gC[:], in_=pbg[:, 0:8], func=AF.Exp, scale=-DSC), reads=[pbg], writes=[gC])
                if DBG < 13:
                    continue
                for g4 in range(4):
                    pb = bank()
                    pv = pb[:].bitcast(BF16)

                    def f(e):
                        for c2 in range(2):
                            ct = g4 * 2 + c2
                            for q in range(4):
                                ins = e.transpose(out=pv[:, cs((c2 * 4 + q) * 128, 128)], in_=TM[:, q, cs(ct * 128, 128)],
                                                  identity=ident_b[:])
                        return ins
                    k.op("pe", f, reads=[TM, ident_b], writes=[pb])
                    eng = "act" if g4 % 2 == 0 else "dve"
                    if eng == "act":
                        k.op("act", lambda e: e.copy(out=FM[:, g4 * 2:g4 * 2 + 2, :, :].rearrange("p a b c -> p (a b c)"),
                                                     in_=pv[:, :]), reads=[pb], writes=[FM])
                    else:
                        k.op("dve", lambda e: e.tensor_copy(out=FM[:, g4 * 2:g4 * 2 + 2, :, :].rearrange("p a b c -> p (a b c)"),
                                                            in_=pv[:, :]), reads=[pb], writes=[FM])
                if DBG < 14:
                    continue
                P0, PT0 = Pm[0], PT[0]
                for h in range(NHEAD):
                    ct, p0 = h // 2, (h % 2) * 64
                    if 'o' in KSKIP and h % 2 == 1:
                        continue
                    pb = bank()

                    def f(e):
                        e.matmul(pb[:, 0:256], lhsT=FM[p0:p0 + 64, ct, 2, :],
                                 rhs=FM[p0:p0 + 64, ct, 0:2, :].rearrange("p a b -> p (a b)"), start=True, stop=True)
                        return e.matmul(pb[:, 256:512], lhsT=FM[p0:p0 + 64, ct, 3, :],
                                        rhs=FM[p0:p0 + 64, ct, 0:2, :].rearrange("p a b -> p (a b)"), start=True, stop=True)
                    k.op("pe", f, reads=[FM], writes=[pb])
                    k.op("dve", lambda e: e.tensor_tensor(out=MB[:, h, :], in0=pb[:], in1=maskM[:].rearrange("p a b -> p (a b)"),
                                                          op=ALU.mult), reads=[pb, maskM], writes=[MB])
                    k.op("act", lambda e: e.copy(out=PT0[:, h, :], in_=MB[:, h, 0:128]), reads=[MB], writes=[PT0])
                for g in range(4):
                    if 'n' in KSKIP:
                        continue
                    pb = bank()

                    def f(e):
                        for hh in range(4):
                            h = (g % 2) + 2 * (4 * (g // 2) + hh)
                            ct, p0 = h // 2, (h % 2) * 64
                            ins = e.matmul(pb[:, cs(hh * 128, 128)], lhsT=FM[p0:p0 + 64, ct, 0, :], rhs=FM[p0:p0 + 64, ct, 2, :],
                                           start=True, stop=True)
                        return ins
                    k.op("pe", f, reads=[FM], writes=[pb])
                    h0 = (g % 2) + 8 * (g // 2)
                    k.op("dve", lambda e: e.tensor_tensor(out=P0[:, h0:h0 + 7:2, :],
                                                          in0=pb[:].rearrange("p (a b) -> p a b", b=128), in1=maskN[:], op=ALU.mult),
                         reads=[pb, maskN], writes=[P0])
                if DBG < 15:
                    continue
                pp, b0, b1 = pair()

                def f(e):
                    for h in range(NHEAD):
                        ct, p0 = h // 2, (h % 2) * 64
                        e.matmul(pp[:, hc(h)], lhsT=FM[p0:p0 + 64, ct, 0, :], rhs=Hb[p0:p0 + 64, ct, :],
                                 start=True, stop=False)
                        ins = e.matmul(pp[:, hc(h)], lhsT=MB[:, h, 256:384], rhs=z[:, cs(2 * D + h * 64, 64)],
                                       start=False, stop=True)
                    return ins
                k.op("pe", f, reads=[FM, Hb, MB, z], writes=[b0, b1])
                xc = Xb[0]
                k.op("act", lambda e: e.copy(out=xc[:], in_=pp[:, :]), reads=[b0, b1], writes=[xc])
                if DBG < 16:
                    continue
                cur = 0
                for lev in range(7):
                    Pc, PTc = Pm[cur], PT[cur]
                    pp, b0, b1 = pair()
                    xc, xn = Xb[lev % 2], Xb[(lev + 1) % 2]

                    def f(e):
                        for h in range(NHEAD):
                            ins = e.matmul(pp[:, hc(h)], lhsT=PTc[:, h, :], rhs=xc[:, hc(h)], start=True, stop=True)
                        return ins
                    k.op("pe", f, reads=[PTc, xc], writes=[b0, b1])
                    k.op("dve", lambda e: e.tensor_tensor(out=xn[:], in0=pp[:, :], in1=xc[:], op=ALU.add),
                         reads=[b0, b1, xc], writes=[xn])
                    if lev < 6:
                        Pn, PTn = Pm[1 - cur], PT[1 - cur]
                        for g in range(4):
                            for which in range(2):
                                pb = bank()

                                def f(e):
                                    for hh in range(4):
                                        h = g * 4 + hh
                                        if which == 0:
                                            ins = e.matmul(pb[:, cs(hh * 128, 128)], lhsT=PTc[:, h, :], rhs=Pc[:, h, :], start=True, stop=True)
                                        else:
                                            ins = e.matmul(pb[:, cs(hh * 128, 128)], lhsT=Pc[:, h, :], rhs=PTc[:, h, :], start=True, stop=True)
                                    return ins
                                k.op("pe", f, reads=[Pc, PTc], writes=[pb])
                                dstb = Pn if which == 0 else PTn
                                if which == 0:
                                    k.op("act", lambda e: e.copy(out=dstb[:, g * 4:g * 4 + 4, :].rearrange("p a b -> p (a b)"),
                                                                 in_=pb[:]), reads=[pb], writes=[dstb])
                                else:
                                    k.op("dve", lambda e: e.tensor_copy(out=dstb[:, g * 4:g * 4 + 4, :].rearrange("p a b -> p (a b)"),
                                                                        in_=pb[:]), reads=[pb], writes=[dstb])
                        cur = 1 - cur
                U = Xb[1]
                if DBG < 17:
                    continue
                pp, b0, b1 = pair()

                def f(e):
                    for h in range(NHEAD):
                        ct, p0 = h // 2, (h % 2) * 64
                        e.matmul(pp[:, hc(h)], lhsT=FM[p0:p0 + 64, ct, 1, :], rhs=Hb[p0:p0 + 64, ct, :], start=True, stop=False)
                        e.matmul(pp[:, hc(h)], lhsT=MB[:, h, 128:256], rhs=U[:, hc(h)], start=False, stop=False)
                        ins = e.matmul(pp[:, hc(h)], lhsT=MB[:, h, 384:512], rhs=z[:, cs(2 * D + h * 64, 64)],
                                       start=False, stop=True)
                    return ins
                k.op("pe", f, reads=[FM, Hb, MB, U, z], writes=[b0, b1])
                k.op("act", lambda e: e.copy(out=ysc[:].rearrange("p (hp h2 i) -> p h2 hp i", h2=2, i=64),
                                             in_=pp[:, :].rearrange("p (h2 hp i) -> p h2 hp i", hp=8, i=64)), reads=[b0, b1], writes=[ysc])
                k.dma("sp", ysc_d[d, cs(i * 128, 128), :], ysc[:], reads=[ysc], writes=[R_ysc(d, i)], sembuf=ysc)
                if DBG < 18:
                    continue
                pp, b0, b1 = pair()

                def f(e):
                    for ct in range(8):
                        e.matmul(pp[:, cs(ct * 128, 128)], lhsT=Bg[:, cs(ct * 128, 128)],
                                 rhs=U[:].rearrange("p (a b c) -> p a b c", a=2, b=8)[:, :, ct, :], start=True, stop=False)
                        ins = e.matmul(pp[:, cs(ct * 128, 128)], lhsT=Kg[:, cs(ct * 128, 128)], rhs=z[:, cs(2 * D + ct * 128, 128)],
                                       start=False, stop=True)
                    return ins
                k.op("pe", f, reads=[Bg, Kg, U, z], writes=[b0, b1])
                k.op("dve", lambda e: e.tensor_tensor(out=H[:], in0=H[:], in1=gC[:].unsqueeze(2).to_broadcast([128, 8, 64]),
                                                      op=ALU.mult), reads=[H, gC], writes=[H])
                ppv = pp[:, :].rearrange("p (a b) -> p a b", b=128)
                k.op("dve", lambda e: e.tensor_tensor(out=H[0:64, :, :], in0=H[0:64, :, :], in1=ppv[0:64, :, 0:64], op=ALU.add),
                     reads=[H, b0, b1], writes=[H])
                k.op("dve", lambda e: e.tensor_tensor(out=H[64:128, :, :], in0=H[64:128, :, :], in1=ppv[64:128, :, 64:128], op=ALU.add),
                     reads=[H, b0, b1], writes=[H])
                if n % 2 == 1:
                    grp = i // 2
                    k.dma("sp", st_d[l, d, grp], H[:].rearrange("p a b -> p (a b)"), reads=[H], writes=[R_st(l, d, grp)], sembuf=H)
                    k.op("dve", lambda e: e.tensor_scalar(out=H[:], in0=H[:], scalar1=cmask[:, 0:1], scalar2=None, op0=ALU.mult),
                         reads=[H, cmask], writes=[H])
                k.op("act", lambda e: e.copy(out=Hb[:], in_=H[:]), reads=[H], writes=[Hb])
            k.barrier()


    def phaseS2(l):
        with contextlib.ExitStack() as es:
            kkc = sbt(es, "kkc", [128, D], F32)
            kac = sbt(es, "kac", [128, D], F32)
            rkc = sbt(es, "rkc", [128, D], F32)
            bc_load(kkc, kk_d[l])
            bc_load(kac, ka_d[l])
            bc_load(rkc, rk_d[l])

            def dir_gen(d):
                sfx = "_%d" % d
                wup = sbt(es, "wup" + sfx, [65, D], BF16)
                aup = sbt(es, "aup" + sfx, [65, D], BF16)
                maskM = sbt(es, "maskM" + sfx, [128, 4, 128], BF16)
                maskN = sbt(es, "maskN" + sfx, [128, 4, 128], BF16)
                H = sbt(es, "H" + sfx, [128, 8, 64], F32)
                Hb = sbt(es, "Hb" + sfx, [128, 8, 64], BF16)
                z = sbt(es, "zr" + sfx, [128, CR], BF16)
                ldT = sbt(es, "ldT" + sfx, [65, 2, 128], BF16)
                tw = sbt(es, "tw" + sfx, [128, 128], BF16)
                SG = sbt(es, "SG" + sfx, [128, D], F32)
                A = sbt(es, "A" + sfx, [128, D], F32)
                KX = sbt(es, "KX" + sfx, [128, D], F32)
                BP = sbt(es, "BP" + sfx, [128, D], F32)
                KD = sbt(es, "KD" + sfx, [128, D], F32)
                S0 = sbt(es, "S0" + sfx, [128, D], F32)
                S1 = A
                ysc = S0
                st16 = sbt(es, "st16" + sfx, [128, 16], F32)
                rs16 = sbt(es, "rs16" + sfx, [128, 16], F32)
                bon = sbt(es, "bon" + sfx, [128, 16], F32)
                gC = sbt(es, "gC" + sfx, [128, 8], F32)
                TM = sbt(es, "TM" + sfx, [128, 4, D], BF16)
                FM = sbt(es, "FM" + sfx, [128, 8, 4, 128], BF16)
                MB = sbt(es, "MB" + sfx, [128, 16, 512], BF16)
                Pm = [sbt(es, "Pm%d" % i + sfx, [128, 16, 128], BF16) for i in range(2)]
                PT = [sbt(es, "PT%d" % i + sfx, [128, 16, 128], BF16) for i in range(2)]
                Xb = [sbt(es, "Xb%d" % i + sfx, [128, D], BF16) for i in range(2)]

                k.dma("pool", wup[:], wup_d[l, d], writes=[wup], max_dma_last_dim=4096)
                k.dma("pool", aup[:], aup_d[l, d], writes=[aup], max_dma_last_dim=4096)
                strict_i, incl_i, nmask_i = (2, 0, 3) if d == 0 else (3, 1, 2)
                for j, src in enumerate((strict_i, incl_i, strict_i, incl_i)):
                    k.op("dve", lambda e: e.tensor_copy(out=maskM[:, j, :], in_=tri4[:, src, :]), reads=[tri4], writes=[maskM])
                for j in range(4):
                    k.op("dve", lambda e: e.tensor_copy(out=maskN[:, j, :], in_=tri4[:, nmask_i, :]), reads=[tri4], writes=[maskN])
                tri_incl = tri4[:, incl_i, :]
                tri_excl = tri4[:, strict_i, :]
                tri_dg = tri4[:, nmask_i, :]
                k.op("dve", lambda e: e.memset(ldT[:], 1.0), writes=[ldT])
                k.dma("sp", H[:].rearrange("p a b -> p (a b)"), state0_d[l, d], writes=[H])
                k.op("act", lambda e: e.copy(out=Hb[:], in_=H[:]), reads=[H], writes=[Hb])
                order = list(range(NT)) if d == 0 else list(range(NT - 1, -1, -1))
                yield

                for n, i in enumerate(order):
                    k.dma("sp", z[:], zr_d[cs(i * 128, 128), :], reads=[R_zr(i)], writes=[z])
                    rq = z[:, 0:D]
                    kq = z[:, D:2 * D]
                    k.op("act", lambda e: e.activation(out=tw[:, 0:64], in_=z[:, cs(3 * D + 64 * d, 64)], func=AF.Tanh),
                         reads=[z], writes=[tw])
                    k.op("pool", lambda e: e.tensor_copy(out=tw[:, 64:128], in_=z[:, cs(3 * D + 128 + 64 * d, 64)]),
                         reads=[z], writes=[tw])
                    pb = bank()
                    pv = pb[:].bitcast(BF16)

                    def f(e):
                        e.transpose(out=pv[0:64, 0:128], in_=tw[:, 0:64], identity=ident_b[:])
                        return e.transpose(out=pv[0:64, 128:256], in_=tw[:, 64:128], identity=ident_b[:])
                    k.op("pe", f, reads=[tw, ident_b], writes=[pb])
                    k.op("act", lambda e: e.copy(out=ldT[0:64, :, :].rearrange("p a b -> p (a b)"), in_=pv[0:64, 0:256]),
                         reads=[pb], writes=[ldT])
                    for (wmat, src_j, dst) in ((wup, 0, SG), (aup, 1, A)):
                        pp, b0, b1 = pair()

                        def f(e):
                            e.matmul(pp[:, 0:512], lhsT=ldT[:, src_j, :], rhs=wmat[:, 0:512], start=True, stop=True)
                            return e.matmul(pp[:, 512:1024], lhsT=ldT[:, src_j, :], rhs=wmat[:, 512:1024], start=True, stop=True)
                        k.op("pe", f, reads=[ldT, wmat], writes=[b0, b1])
                        k.op("act", lambda e: e.activation(out=dst[:], in_=pp[:, :], func=AF.Sigmoid), reads=[b0, b1], writes=[dst])
                    k.op("pool", lambda e: e.tensor_tensor(out=KX[:], in0=kq, in1=kkc[:], op=ALU.mult), reads=[z, kkc], writes=[KX])
                    k.op("pool", lambda e: e.tensor_tensor(out=S0[:], in0=KX[:], in1=KX[:], op=ALU.mult), reads=[KX], writes=[S0])
                    k.op("dve", lambda e: e.tensor_reduce(out=st16[:], in_=S0[:].rearrange("p (h n) -> p h n", n=64), axis=AX.X,
                                                          op=ALU.add), reads=[S0], writes=[st16])
                    rstd_from(st16, rs16, 1.0, 1e-12)
                    k.op("dve", lambda e: e.tensor_tensor(out=KX[:].rearrange("p (h n) -> p h n", n=64),
                                                          in0=KX[:].rearrange("p (h n) -> p h n", n=64),
                                                          in1=rs16[:].unsqueeze(2).to_broadcast([128, 16, 64]), op=ALU.mult),
                         reads=[KX, rs16], writes=[KX])
                    yield
                    k.op("dve", lambda e: e.scalar_tensor_tensor(out=BP[:], in0=KX[:], scalar=-1.0, in1=A[:], op0=ALU.mult,
                                                                 op1=ALU.mult), reads=[KX, A], writes=[BP])
                    k.op("pool", lambda e: e.tensor_tensor(out=S0[:], in0=kq, in1=kac[:], op=ALU.mult), reads=[z, kac], writes=[S0])
                    k.op("dve", lambda e: e.scalar_tensor_tensor(out=S0[:], in0=A[:], scalar=-1.0, in1=S0[:], op0=ALU.add,
                                                                 op1=ALU.mult), reads=[A, S0], writes=[S0])
                    k.op("pool", lambda e: e.tensor_tensor(out=KD[:], in0=S0[:], in1=kq, op=ALU.add), reads=[S0, z], writes=[KD])
                    k.op("pool", lambda e: e.tensor_tensor(out=S0[:], in0=KD[:], in1=rkc[:], op=ALU.mult), reads=[KD, rkc], writes=[S0])
                    k.op("pool", lambda e: e.tensor_tensor(out=S0[:], in0=S0[:], in1=rq, op=ALU.mult), reads=[S0, z], writes=[S0])
                    k.op("dve", lambda e: e.tensor_reduce(out=bon[:], in_=S0[:].rearrange("p (h n) -> p h n", n=64), axis=AX.X,
                                                          op=ALU.add), reads=[S0], writes=[bon])
                    k.dma("sp", bon_d[d, cs(i * 128, 128), :], bon[:], reads=[bon], writes=[R_bon(d, i)], sembuf=bon)
                    yield

                    def cum(tri_ap, scale, dstE):
                        pp, b0, b1 = pair()

                        def f(e):
                            e.matmul(pp[:, 0:512], lhsT=tri_ap, rhs=SG[:, 0:512], start=True, stop=True)
                            return e.matmul(pp[:, 512:1024], lhsT=tri_ap, rhs=SG[:, 512:1024], start=True, stop=True)
                        k.op("pe", f, reads=[tri4, SG], writes=[b0, b1])
                        k.op("act", lambda e: e.activation(out=dstE[:], in_=pp[:, :], func=AF.Exp, scale=scale),
                             reads=[b0, b1], writes=[dstE])
                        return pp, b0, b1
                    pp, b0, b1 = cum(tri_incl, -DSC, S0)
                    k.op("dve", lambda e: e.tensor_tensor(out=TM[:, 1, :], in0=rq, in1=S0[:], op=ALU.mult), reads=[z, S0], writes=[TM])
                    k.op("act", lambda e: e.activation(out=S1[:], in_=pp[:, :], func=AF.Exp, scale=DSC), reads=[b0, b1], writes=[S1])
                    k.op("dve", lambda e: e.tensor_tensor(out=TM[:, 2, :], in0=BP[:], in1=S1[:], op=ALU.mult), reads=[BP, S1], writes=[TM])
                    k.op("pool", lambda e: e.tensor_tensor(out=TM[:, 3, :], in0=KD[:], in1=S1[:], op=ALU.mult), reads=[KD, S1], writes=[TM])
                    cum(tri_excl, -DSC, S0)
                    k.op("dve", lambda e: e.tensor_tensor(out=TM[:, 0, :], in0=KX[:], in1=S0[:], op=ALU.mult), reads=[KX, S0], writes=[TM])
                    pbg = bank()

                    def f(e):
                        for ct in range(8):
                            ins = e.matmul(pbg[:, ct:ct + 1], lhsT=SG[:, cs(ct * 128, 128)], rhs=ones_f[:], start=True, stop=True)
                        return ins
                    k.op("pe", f, reads=[SG, ones_f], writes=[pbg])
                    k.op("act", lambda e: e.activation(out=gC[:], in_=pbg[:, 0:8], func=AF.Exp, scale=-DSC), reads=[pbg], writes=[gC])
                    yield
                    for g4 in range(4):
                        pb = bank()
                        pv = pb[:].bitcast(BF16)

                        def f(e):
                            for c2 in range(2):
                                ct = g4 * 2 + c2
                                for q in range(4):
                                    ins = e.transpose(out=pv[:, cs((c2 * 4 + q) * 128, 128)], in_=TM[:, q, cs(ct * 128, 128)],
                                                      identity=ident_b[:])
                            return ins
                        k.op("pe", f, reads=[TM, ident_b], writes=[pb])
                        if g4 % 2 == 0:
                            k.op("act", lambda e: e.copy(out=FM[:, g4 * 2:g4 * 2 + 2, :, :].rearrange("p a b c -> p (a b c)"),
                                                         in_=pv[:, :]), reads=[pb], writes=[FM])
                        else:
                            k.op("dve", lambda e: e.tensor_copy(out=FM[:, g4 * 2:g4 * 2 + 2, :, :].rearrange("p a b c -> p (a b c)"),
                                                                in_=pv[:, :]), reads=[pb], writes=[FM])
                    cum(tri_dg, -DSC, S1)
                    k.op("dve", lambda e: e.tensor_tensor(out=TM[:, 2, :], in0=BP[:], in1=S1[:], op=ALU.mult), reads=[BP, S1], writes=[TM])
                    k.op("pool", lambda e: e.tensor_tensor(out=TM[:, 3, :], in0=KD[:], in1=S1[:], op=ALU.mult), reads=[KD, S1], writes=[TM])
                    yield
                    P0 = Pm[0]
                    for h in range(NHEAD):
                        ct, p0 = h // 2, (h % 2) * 64
                        pb = bank()

                        def f(e):
                            e.matmul(pb[:, 0:256], lhsT=FM[p0:p0 + 64, ct, 2, :],
                                     rhs=FM[p0:p0 + 64, ct, 0:2, :].rearrange("p a b -> p (a b)"), start=True, stop=True)
                            return e.matmul(pb[:, 256:512], lhsT=FM[p0:p0 + 64, ct, 3, :],
                                            rhs=FM[p0:p0 + 64, ct, 0:2, :].rearrange("p a b -> p (a b)"), start=True, stop=True)
                        k.op("pe", f, reads=[FM], writes=[pb])
                        k.op("dve", lambda e: e.tensor_tensor(out=MB[:, h, :], in0=pb[:], in1=maskM[:].rearrange("p a b -> p (a b)"),
                                                              op=ALU.mult), reads=[pb, maskM], writes=[MB])
                        if h % 4 == 3:
                            yield
                    for g in range(4):
                        pb = bank()

                        def f(e):
                            for hh in range(4):
                                h = (g % 2) + 2 * (4 * (g // 2) + hh)
                                ct, p0 = h // 2, (h % 2) * 64
                                ins = e.matmul(pb[:, cs(hh * 128, 128)], lhsT=FM[p0:p0 + 64, ct, 0, :], rhs=FM[p0:p0 + 64, ct, 2, :],
                                               start=True, stop=True)
                            return ins
                        k.op("pe", f, reads=[FM], writes=[pb])
                        h0 = (g % 2) + 8 * (g // 2)
                        k.op("dve", lambda e: e.tensor_tensor(out=P0[:, h0:h0 + 7:2, :],
                                                              in0=pb[:].rearrange("p (a b) -> p a b", b=128), in1=maskN[:], op=ALU.mult),
                             reads=[pb, maskN], writes=[P0])
                    yield
                    pp, b0, b1 = pair()

                    def f(e):
                        for h in range(NHEAD):
                            ct, p0 = h // 2, (h % 2) * 64
                            e.matmul(pp[:, hc(h)], lhsT=FM[p0:p0 + 64, ct, 0, :], rhs=Hb[p0:p0 + 64, ct, :],
                                     start=True, stop=False)
                            ins = e.matmul(pp[:, hc(h)], lhsT=MB[:, h, 256:384], rhs=z[:, cs(2 * D + h * 64, 64)],
                                           start=False, stop=True)
                        return ins
                    k.op("pe", f, reads=[FM, Hb, MB, z], writes=[b0, b1])
                    xc = Xb[0]
                    k.op("act", lambda e: e.copy(out=xc[:], in_=pp[:, :]), reads=[b0, b1], writes=[xc])
                    yield
                    cur = 0
                    for lev in range(7):
                        Pc = Pm[cur]
                        pp, b0, b1 = pair()
                        xc, xn = Xb[lev % 2], Xb[(lev + 1) % 2]
                        if lev == 0:
                            ptv = lambda h: MB[:, h, 0:128]
                            ptb = MB
                        else:
                            ptv = lambda h, PTc=PT[cur]: PTc[:, h, :]
                            ptb = PT[cur]

                        def f(e):
                            for h in range(NHEAD):
                                ins = e.matmul(pp[:, hc(h)], lhsT=ptv(h), rhs=xc[:, hc(h)], start=True, stop=True)
                            return ins
                        k.op("pe", f, reads=[ptb, xc], writes=[b0, b1])
                        k.op("dve", lambda e: e.tensor_tensor(out=xn[:], in0=pp[:, :], in1=xc[:], op=ALU.add),
                             reads=[b0, b1, xc], writes=[xn])
                        if lev < 6:
                            Pn, PTn = Pm[1 - cur], PT[1 - cur]
                            for g in range(4):
                                for which in range(2):
                                    if which == 0 and lev == 5:
                                        continue
                                    pb = bank()

                                    def f(e):
                                        for hh in range(4):
                                            h = g * 4 + hh
                                            if which == 0:
                                                ins = e.matmul(pb[:, cs(hh * 128, 128)], lhsT=ptv(h), rhs=Pc[:, h, :], start=True, stop=True)
                                            else:
                                                ins = e.matmul(pb[:, cs(hh * 128, 128)], lhsT=Pc[:, h, :], rhs=ptv(h), start=True, stop=True)
                                        return ins
                                    k.op("pe", f, reads=[Pc, ptb], writes=[pb])
                                    dstb = Pn if which == 0 else PTn
                                    if which == 0 or g % 2 == 0:
                                        k.op("act", lambda e: e.copy(out=dstb[:, g * 4:g * 4 + 4, :].rearrange("p a b -> p (a b)"),
                                                                     in_=pb[:]), reads=[pb], writes=[dstb])
                                    else:
                                        k.op("dve", lambda e: e.tensor_copy(out=dstb[:, g * 4:g * 4 + 4, :].rearrange("p a b -> p (a b)"),
                                                                            in_=pb[:]), reads=[pb], writes=[dstb])
                                if g % 2 == 1:
                                    yield
                            cur = 1 - cur
                    U = Xb[1]
                    pp, b0, b1 = pair()

                    def f(e):
                        for h in range(NHEAD):
                            ct, p0 = h // 2, (h % 2) * 64
                            e.matmul(pp[:, hc(h)], lhsT=FM[p0:p0 + 64, ct, 1, :], rhs=Hb[p0:p0 + 64, ct, :], start=True, stop=False)
                            e.matmul(pp[:, hc(h)], lhsT=MB[:, h, 128:256], rhs=U[:, hc(h)], start=False, stop=False)
                            ins = e.matmul(pp[:, hc(h)], lhsT=MB[:, h, 384:512], rhs=z[:, cs(2 * D + h * 64, 64)],
                                           start=False, stop=True)
                        return ins
                    k.op("pe", f, reads=[FM, Hb, MB, U, z], writes=[b0, b1])
                    k.op("act", lambda e: e.copy(out=ysc[:].rearrange("p (hp h2 i) -> p h2 hp i", h2=2, i=64),
                                                 in_=pp[:, :].rearrange("p (h2 hp i) -> p h2 hp i", hp=8, i=64)), reads=[b0, b1], writes=[ysc])
                    k.dma("sp", ysc_d[d, cs(i * 128, 128), :], ysc[:], reads=[ysc], writes=[R_ysc(d, i)], sembuf=ysc)
                    pp, b0, b1 = pair()

                    def f(e):
                        for ct in range(8):
                            e.matmul(pp[:, cs(ct * 128, 128)], lhsT=TM[:, 2, cs(ct * 128, 128)],
                                     rhs=U[:].rearrange("p (a b c) -> p a b c", a=2, b=8)[:, :, ct, :], start=True, stop=False)
                            ins = e.matmul(pp[:, cs(ct * 128, 128)], lhsT=TM[:, 3, cs(ct * 128, 128)], rhs=z[:, cs(2 * D + ct * 128, 128)],
                                           start=False, stop=True)
                        return ins
                    k.op("pe", f, reads=[TM, U, z], writes=[b0, b1])
                    k.op("dve", lambda e: e.tensor_tensor(out=H[:], in0=H[:], in1=gC[:].unsqueeze(2).to_broadcast([128, 8, 64]),
                                                          op=ALU.mult), reads=[H, gC], writes=[H])
                    ppv = pp[:, :].rearrange("p (a b) -> p a b", b=128)
                    k.op("dve", lambda e: e.tensor_tensor(out=H[0:64, :, :], in0=H[0:64, :, :], in1=ppv[0:64, :, 0:64], op=ALU.add),
                         reads=[H, b0, b1], writes=[H])
                    k.op("dve", lambda e: e.tensor_tensor(out=H[64:128, :, :], in0=H[64:128, :, :], in1=ppv[64:128, :, 64:128], op=ALU.add),
                         reads=[H, b0, b1], writes=[H])
                    if n % 2 == 1:
                        grp = i // 2
                        k.dma("sp", st_d[l, d, grp], H[:].rearrange("p a b -> p (a b)"), reads=[H], writes=[R_st(l, d, grp)], sembuf=H)
                        k.op("dve", lambda e: e.tensor_scalar(out=H[:], in0=H[:], scalar1=cmask[:, 0:1], scalar2=None, op0=ALU.mult),
                             reads=[H, cmask], writes=[H])
                    k.op("act", lambda e: e.copy(out=Hb[:], in_=H[:]), reads=[H], writes=[Hb])
                    yield

            gens = [dir_gen(0), dir_gen(1)]
            next(gens[0])
            next(gens[1])
            for _ in range(6):
                next(gens[0])
            live = list(gens)
            while live:
                for g in list(live):
                    try:
                        next(g)
                    except StopIteration:
                        live.remove(g)
            k.barrier()

    def phaseM(l, xsrc_d, R_xsrc):
        with contextlib.ExitStack() as es:
            wa = sbt(es, "wa", [128, 8, D], BF16)
            wb = sbt(es, "wb", [128, 8, D], BF16)
            wo = sbt(es, "wo", [128, 8, D], BF16)
            gup = sbt(es, "gup", [128, D], BF16)
            wsT = sbt(es, "wsT", [128, 8, 128], BF16)
            bsT = sbt(es, "bsT", [128, 8], F32)
            lnxg = sbt(es, "lnxg", [128, D], F32)
            lnxb = sbt(es, "lnxb", [128, D], F32)
            lnvg = sbt(es, "lnvg", [128, D], F32)
            gate1 = sbt(es, "gate1", [128, D], F32)
            def mkset(pi):
                B = {}
                B['zv'] = sbt(es, "p%d_" % pi + "zv", [128, D + 128], BF16)
                B['zrs'] = sbt(es, "p%d_" % pi + "zrs", [128, 4096], BF16)
                B['yf'] = sbt(es, "p%d_" % pi + "yf", [128, D], F32)
                B['yb'] = sbt(es, "p%d_" % pi + "yb", [128, D], F32)
                B['b0t'] = sbt(es, "p%d_" % pi + "b0t", [128, 16], F32)
                B['b1t'] = sbt(es, "p%d_" % pi + "b1t", [128, 16], F32)
                B['xt'] = sbt(es, "p%d_" % pi + "xtm", [128, D], F32)
                B['W0'] = sbt(es, "p%d_" % pi + "W0", [128, D], F32)
                B['W1'] = sbt(es, "p%d_" % pi + "W1", [128, D], F32)
                B['W2'] = sbt(es, "p%d_" % pi + "W2", [128, D], F32)
                B['s16a'] = sbt(es, "p%d_" % pi + "s16a", [128, 16], F32)
                B['s16b'] = sbt(es, "p%d_" % pi + "s16b", [128, 16], F32)
                B['s16c'] = sbt(es, "p%d_" % pi + "s16c", [128, 16], F32)
                B['bnst'] = sbt(es, "p%d_" % pi + "bnst", [128, 2, 6], F32)
                B['mv'] = sbt(es, "p%d_" % pi + "mv", [128, 2], F32)
                B['rsv'] = sbt(es, "p%d_" % pi + "rsv", [128, 1], F32)
                B['gsb'] = sbt(es, "p%d_" % pi + "gsb", [128, 128], BF16)
                B['gT'] = sbt(es, "p%d_" % pi + "gT", [128, 1, 128], BF16)
                B['actb'] = sbt(es, "p%d_" % pi + "actb", [128, D], BF16)
                B['actT'] = sbt(es, "p%d_" % pi + "actT", [128, 8, 128], BF16)
                B['ub'] = sbt(es, "p%d_" % pi + "ub", [128, D], BF16)
                B['vcb'] = sbt(es, "p%d_" % pi + "vcb", [128, D], BF16)
                return B
            sets = [mkset(0), mkset(1)]
            cast_load_rows(lambda kc: wa[:, kc, :], wa_d[l], 8, D, wa)
            cast_load_rows(lambda kc: wb[:, kc, :], wb_d[l], 8, D, wb)
            cast_load_rows(lambda kc: wo[:, kc, :], wo_d[l], 8, D, wo)
            k.dma("pool", gup[:], gup_d[l], writes=[gup], max_dma_last_dim=4096)
            k.dma("pool", wsT[:].rearrange("p a b -> p (a b)"), wsT_d[l], writes=[wsT], max_dma_last_dim=4096)
            k.dma("sp", bsT[:], bsT_d[l], writes=[bsT])
            bc_load(lnxg, lnxg_d[l])
            bc_load(lnxb, lnxb_d[l])
            bc_load(lnvg, lnvg_d[l])
            load_mod(gate1, l, 2)

            def v3(b):
                return b[:].rearrange("p (h n) -> p h n", n=64)

            def bc16(b):
                return b[:].unsqueeze(2).to_broadcast([128, 16, 64])

            def proj(src_bf, wmat, actT):
                transpose8(src_bf, actT)
                pp, p0, p1 = pair()

                def f(e):
                    for nn in range(2):
                        for kc in range(8):
                            ins = e.matmul(pp[:, cs(nn * 512, 512)], lhsT=actT[:, kc, :], rhs=wmat[:, kc, cs(nn * 512, 512)],
                                           start=(kc == 0), stop=(kc == 7))
                    return ins
                k.op("pe", f, reads=[actT, wmat], writes=[p0, p1])
                return pp, p0, p1

            def tile_gen(i, B):
                zv = B['zv']
                zrs = B['zrs']
                yf = B['yf']
                yb = B['yb']
                b0t = B['b0t']
                b1t = B['b1t']
                xt = B['xt']
                W0 = B['W0']
                W1 = B['W1']
                W2 = B['W2']
                s16a = B['s16a']
                s16b = B['s16b']
                s16c = B['s16c']
                bnst = B['bnst']
                mv = B['mv']
                rsv = B['rsv']
                gsb = B['gsb']
                gT = B['gT']
                actb = B['actb']
                actT = B['actT']
                ub = B['ub']
                vcb = B['vcb']
                rows = cs(i * 128, 128)
                k.dma("sp", zv[:, 0:D], zr_d[rows, 2 * D:3 * D], reads=[R_zr(i)], writes=[zv])
                k.dma("sp", zv[:, D:D + 128], zr_d[rows, cs(3 * D + 256, 128)], reads=[R_zr(i)], writes=[zv])
                k.dma("sp", zrs[:], zrest_d[rows, :], reads=[R_zrest(i, j) for j in range(8)], writes=[zrs])
                k.dma("sp", yf[:], ysc_d[0, rows, :], reads=[R_ysc(0, i)], writes=[yf])
                k.dma("sp", yb[:], ysc_d[1, rows, :], reads=[R_ysc(1, i)], writes=[yb])
                k.dma("sp", b0t[:], bon_d[0, rows, :], reads=[R_bon(0, i)], writes=[b0t])
                k.dma("sp", b1t[:], bon_d[1, rows, :], reads=[R_bon(1, i)], writes=[b1t])
                k.dma("sp", xt[:], xsrc_d[rows, :], reads=[R_xsrc(i)], writes=[xt])
                yield
                k.op("dve", lambda e: e.tensor_tensor(out=yf[:], in0=yf[:], in1=yb[:], op=ALU.add), reads=[yf, yb], writes=[yf])
                k.op("dve", lambda e: e.tensor_reduce(out=s16a[:], in_=v3(yf), axis=AX.X, op=ALU.add), reads=[yf], writes=[s16a])
                k.op("dve", lambda e: e.tensor_scalar(out=s16a[:], in0=s16a[:], scalar1=1.0 / 64, scalar2=None, op0=ALU.mult),
                     reads=[s16a], writes=[s16a])
                k.op("dve", lambda e: e.tensor_tensor(out=v3(yf), in0=v3(yf), in1=bc16(s16a), op=ALU.subtract), reads=[yf, s16a], writes=[yf])
                k.op("dve", lambda e: e.tensor_tensor(out=W0[:], in0=yf[:], in1=yf[:], op=ALU.mult), reads=[yf], writes=[W0])
                k.op("dve", lambda e: e.tensor_reduce(out=s16b[:], in_=v3(W0), axis=AX.X, op=ALU.add), reads=[W0], writes=[s16b])
                rstd_from(s16b, s16c, 1.0 / 64, GN_EPS)
                k.op("dve", lambda e: e.tensor_tensor(out=v3(yf), in0=v3(yf), in1=bc16(s16c), op=ALU.mult), reads=[yf, s16c], writes=[yf])
                k.op("dve", lambda e: e.tensor_tensor(out=yf[:], in0=yf[:], in1=lnxg[:], op=ALU.mult), reads=[yf, lnxg], writes=[yf])
                k.op("dve", lambda e: e.tensor_tensor(out=yf[:], in0=yf[:], in1=lnxb[:], op=ALU.add), reads=[yf, lnxb], writes=[yf])
                yield
                k.op("dve", lambda e: e.tensor_tensor(out=b0t[:], in0=b0t[:], in1=b1t[:], op=ALU.add), reads=[b0t, b1t], writes=[b0t])
                k.op("dve", lambda e: e.tensor_tensor(out=v3(W0), in0=zv[:, 0:D].rearrange("p (h n) -> p h n", n=64), in1=bc16(b0t),
                                                      op=ALU.mult), reads=[zv, b0t], writes=[W0])
                k.op("dve", lambda e: e.tensor_tensor(out=yf[:], in0=yf[:], in1=W0[:], op=ALU.add), reads=[yf, W0], writes=[yf])
                k.op("act", lambda e: e.activation(out=gsb[:], in_=zv[:, D:D + 128], func=AF.Sigmoid), reads=[zv], writes=[gsb])
                transpose8(gsb, gT, nblk=1)
                pp, p0, p1 = pair()

                def f(e):
                    e.matmul(pp[:, 0:512], lhsT=gT[:, 0, :], rhs=gup[:, 0:512], start=True, stop=True)
                    return e.matmul(pp[:, 512:1024], lhsT=gT[:, 0, :], rhs=gup[:, 512:1024], start=True, stop=True)
                k.op("pe", f, reads=[gT, gup], writes=[p0, p1])
                k.op("dve", lambda e: e.tensor_tensor(out=actb[:], in0=pp[:, :], in1=yf[:], op=ALU.mult), reads=[p0, p1, yf], writes=[actb])
                yield
                pp, p0, p1 = proj(actb, wa, actT)
                yield
                k.op("act", lambda e: e.activation(out=W0[:], in_=zrs[:, 2048:3072], func=AF.Sigmoid), reads=[zrs], writes=[W0])
                k.op("dve", lambda e: e.tensor_tensor(out=W2[:], in0=pp[:, :], in1=W0[:], op=ALU.mult), reads=[p0, p1, W0], writes=[W2])
                yield
                k.op("act", lambda e: e.activation(out=ub[:], in_=zrs[:, 0:1024], func=AF.Gelu_apprx_tanh), reads=[zrs], writes=[ub])
                k.op("act", lambda e: e.activation(out=W1[:], in_=zrs[:, 1024:2048], func=AF.Gelu_apprx_tanh), reads=[zrs], writes=[W1])
                for c in range(2):
                    k.op("dve", lambda e: e.bn_stats(out=bnst[:, c, :], in_=W1[:, cs(c * 512, 512)]), reads=[W1], writes=[bnst])
                k.op("dve", lambda e: e.bn_aggr(out=mv[:], in_=bnst[:].rearrange("p a b -> p (a b)")), reads=[bnst], writes=[mv])
                rstd_from_ap(mv, 1, rsv, EPS)
                k.op("dve", lambda e: e.tensor_scalar(out=W1[:], in0=W1[:], scalar1=mv[:, 0:1], scalar2=rsv[:, 0:1], op0=ALU.subtract,
                                                      op1=ALU.mult), reads=[W1, mv, rsv], writes=[W1])
                k.op("dve", lambda e: e.tensor_tensor(out=vcb[:], in0=W1[:], in1=lnvg[:], op=ALU.mult), reads=[W1, lnvg], writes=[vcb])
                yield
                pp, p0, p1 = pair()

                def f(e):
                    for g in range(8):
                        ins = e.matmul(pp[:, cs(g * 128, 128)], lhsT=wsT[:, g, :], rhs=vcb[:, cs(g * 128, 128)], start=True, stop=True)
                    return ins
                k.op("pe", f, reads=[wsT, vcb], writes=[p0, p1])
                k.op("dve", lambda e: e.tensor_tensor(out=W1[:].rearrange("p (g c) -> p g c", c=128),
                                                      in0=pp[:, :].rearrange("p (g c) -> p g c", c=128),
                                                      in1=bsT[:].unsqueeze(2).to_broadcast([128, 8, 128]), op=ALU.add),
                     reads=[p0, p1, bsT], writes=[W1])
                k.op("dve", lambda e: e.tensor_tensor(out=actb[:], in0=W1[:], in1=ub[:], op=ALU.mult), reads=[W1, ub], writes=[actb])
                yield
                pp, p0, p1 = proj(actb, wb, actT)
                yield
                k.op("act", lambda e: e.activation(out=W0[:], in_=zrs[:, 3072:4096], func=AF.Sigmoid), reads=[zrs], writes=[W0])
                k.op("dve", lambda e: e.tensor_tensor(out=W1[:], in0=pp[:, :], in1=W0[:], op=ALU.mult), reads=[p0, p1, W0], writes=[W1])
                k.op("dve", lambda e: e.tensor_tensor(out=actb[:], in0=W1[:], in1=W2[:], op=ALU.add), reads=[W1, W2], writes=[actb])
                yield
                pp, p0, p1 = proj(actb, wo, actT)
                yield
                k.op("dve", lambda e: e.tensor_tensor(out=W0[:], in0=pp[:, :], in1=gate1[:], op=ALU.mult), reads=[p0, p1, gate1], writes=[W0])
                k.op("dve", lambda e: e.tensor_tensor(out=W0[:], in0=W0[:], in1=xt[:], op=ALU.add), reads=[W0, xt], writes=[W0])
                k.dma("sp", x1_d[rows, :], W0[:], reads=[W0], writes=[R_x1(i)], sembuf=W0)
                yield

            pending = list(range(NT))
            live = []
            while pending or live:
                if len(live) < 2 and pending:
                    ti = pending.pop(0)
                    live.append(tile_gen(ti, sets[ti % 2]))
                    if len(live) == 2 and ti == 1:
                        for _ in range(5):
                            next(live[0])
                for g in list(live):
                    try:
                        next(g)
                    except StopIteration:
                        live.remove(g)
            k.barrier()

    def rstd_from_ap(mvb, col, out_rstd, eps):
        k.op("act", lambda e: e.activation(out=out_rstd[:], in_=mvb[:, col:col + 1], func=AF.Ln, bias=eps_t(eps)[:], scale=1.0),
             reads=[mvb, eps_t(eps)], writes=[out_rstd])
        k.op("act", lambda e: e.activation(out=out_rstd[:], in_=out_rstd[:], func=AF.Exp, scale=-0.5),
             reads=[out_rstd], writes=[out_rstd])

    def phaseC(l, last):
        with contextlib.ExitStack() as es:
            w1 = sbt(es, "w1", [128, 8, DFF], BF16)
            w2 = sbt(es, "w2", [128, 32, D], BF16)
            g2 = sbt(es, "g2", [128, D], F32)
            sh2 = sbt(es, "sh2", [128, D], F32)
            gate2 = sbt(es, "gate2", [128, D], F32)
            fg = sbt(es, "fg", [128, D], F32) if last else None
            xt = [sbt(es, "xc%d" % i, [128, D], F32) for i in range(2)]

            def mkset(pi):
                B = {}
                B["W0"] = sbt(es, "Wc0_%d" % pi, [128, D], F32)
                B["hT"] = sbt(es, "hT2_%d" % pi, [128, 8, 128], BF16)
                B["hid"] = sbt(es, "hid_%d" % pi, [128, DFF], BF16)
                B["hidT"] = sbt(es, "hidT_%d" % pi, [128, 32, 128], BF16)
                B["ss"] = sbt(es, "ssc_%d" % pi, [128, 1], F32)
                B["rstd"] = sbt(es, "rstdc_%d" % pi, [128, 1], F32)
                return B
            sets = [mkset(0), mkset(1)]
            cast_load_rows(lambda kc: w1[:, kc, :], w1_d[l], 8, DFF, w1)
            cast_load_rows(lambda kc: w2[:, kc, :], w2_d[l], 32, D, w2)
            load_mod(sh2, l, 3)
            load_mod(g2, l, 4)
            load_mod(gate2, l, 5)
            if last:
                bc_load(fg, fg_d)

            def load_x(i):
                k.dma("sp", xt[i % 2][:], x1_d[cs(i * 128, 128), :], reads=[R_x1(i)], writes=[xt[i % 2]])
            def tile_gen(i, B):
                W0, hT, hid, hidT, ss, rstd = B["W0"], B["hT"], B["hid"], B["hidT"], B["ss"], B["rstd"]
                x = xt[i % 2]
                load_x(i)
                yield
                k.op("act", lambda e: e.activation(out=W0[:], in_=x[:], func=AF.Square), reads=[x], writes=[W0])
                k.op("dve", lambda e: e.tensor_reduce(out=ss[:], in_=W0[:], axis=AX.X, op=ALU.add), reads=[W0], writes=[ss])
                rstd_from(ss, rstd, 1.0 / D, EPS)
                k.op("dve", lambda e: e.scalar_tensor_tensor(out=W0[:], in0=x[:], scalar=rstd[:, 0:1], in1=g2[:], op0=ALU.mult,
                                                             op1=ALU.mult), reads=[x, rstd, g2], writes=[W0])
                k.op("dve", lambda e: e.tensor_tensor(out=hid[:, 0:D], in0=W0[:], in1=sh2[:], op=ALU.add), reads=[W0, sh2], writes=[hid])
                transpose8(hid, hT)
                yield
                for n in range(8):
                    pb = bank()

                    def f(e):
                        for kc in range(8):
                            ins = e.matmul(pb[:], lhsT=hT[:, kc, :], rhs=w1[:, kc, cs(n * 512, 512)], start=(kc == 0), stop=(kc == 7))
                        return ins
                    k.op("pe", f, reads=[hT, w1], writes=[pb])
                    k.op("act", lambda e: e.activation(out=hid[:, cs(n * 512, 512)], in_=pb[:], func=AF.Relu), reads=[pb], writes=[hid])
                    k.op("dve", lambda e: e.tensor_tensor(out=hid[:, cs(n * 512, 512)], in0=hid[:, cs(n * 512, 512)],
                                                          in1=hid[:, cs(n * 512, 512)], op=ALU.mult), reads=[hid], writes=[hid])
                    if n % 4 == 3:
                        yield
                for q in range(4):
                    pb = bank()
                    pv = pb[:].bitcast(BF16)

                    def f(e):
                        for j in range(8):
                            ins = e.transpose(out=pv[:, cs(j * 128, 128)], in_=hid[:, cs((q * 8 + j) * 128, 128)], identity=ident_b[:])
                        return ins
                    k.op("pe", f, reads=[hid, ident_b], writes=[pb])
                    if q % 2 == 0:
                        k.op("act", lambda e: e.copy(out=hidT[:, q * 8:q * 8 + 8, :].rearrange("p a b -> p (a b)"), in_=pv[:, :]),
                             reads=[pb], writes=[hidT])
                    else:
                        k.op("dve", lambda e: e.tensor_copy(out=hidT[:, q * 8:q * 8 + 8, :].rearrange("p a b -> p (a b)"), in_=pv[:, :]),
                             reads=[pb], writes=[hidT])
                yield
                pp, p0, p1 = pair()

                def f(e):
                    for nn in range(2):
                        for kc in range(32):
                            ins = e.matmul(pp[:, cs(nn * 512, 512)], lhsT=hidT[:, kc, :], rhs=w2[:, kc, cs(nn * 512, 512)],
                                           start=(kc == 0), stop=(kc == 31))
                    return ins
                k.op("pe", f, reads=[hidT, w2], writes=[p0, p1])
                k.op("dve", lambda e: e.tensor_tensor(out=W0[:], in0=pp[:, :], in1=gate2[:], op=ALU.mult), reads=[p0, p1, gate2], writes=[W0])
                k.op("dve", lambda e: e.tensor_tensor(out=W0[:], in0=W0[:], in1=x[:], op=ALU.add), reads=[W0, x], writes=[W0])
                rows = cs(i * 128, 128)
                if not last:
                    k.dma("sp", x2_d[rows, :], W0[:], reads=[W0], writes=[R_x2(i)], sembuf=W0)
                else:
                    k.op("act", lambda e: e.activation(out=x[:], in_=W0[:], func=AF.Square), reads=[W0], writes=[x])
                    k.op("dve", lambda e: e.tensor_reduce(out=ss[:], in_=x[:], axis=AX.X, op=ALU.add), reads=[x], writes=[ss])
                    rstd_from(ss, rstd, 1.0 / D, EPS)
                    k.op("dve", lambda e: e.scalar_tensor_tensor(out=W0[:], in0=W0[:], scalar=rstd[:, 0:1], in1=fg[:], op0=ALU.mult,
                                                                 op1=ALU.mult), reads=[W0, rstd, fg], writes=[W0])
                    k.dma("sp", y_d[rows, :], W0[:], reads=[W0], writes=[R_y(i)], sembuf=W0)
                yield

            pending = list(range(NT))
            live = []
            while pending or live:
                if len(live) < 2 and pending:
                    ti = pending.pop(0)
                    live.append(tile_gen(ti, sets[ti % 2]))
                    if len(live) == 2 and ti == 1:
                        for _ in range(3):
                            next(live[0])
                for g in list(live):
                    try:
                        next(g)
                    except StopIteration:
                        live.remove(g)
            k.barrier()

    R_xin = DR("xin")
    steps = [lambda: phaseP(0), lambda: phaseP(1)]
    for l in range(2):
        xs, Rx = (x_d, R_xin) if l == 0 else (x2_d, R_x2)
        steps += [lambda l=l, xs=xs, Rx=Rx: phaseA1(l, xs, Rx), lambda l=l: phaseS2(l),
                  lambda l=l, xs=xs, Rx=Rx: phaseM(l, xs, Rx), lambda l=l: phaseC(l, last=(l == 1))]
    for st_ in steps[:upto]:
        st_()
    k.barrier()
    ges.close()
    return nc, k


def _shift_mats(kind):
    m = np.zeros((4, 3, 128, 128), np.float32)
    eye = np.eye(128, dtype=np.float32)
    t = np.arange(128)
    for cls in range(4):
        cur = np.zeros((128, 128), np.float32)
        nbe = np.zeros((128, 128), np.float32)
        nbo = np.zeros((128, 128), np.float32)
        if kind == "sample":
            if cls == 0:
                for to in t:
                    if to % 64 != 0:
                        cur[to - 1, to] = 1
            elif cls == 1:
                for to in t:
                    if to % 64 != 63:
                        cur[to + 1, to] = 1
            elif cls == 2:
                for to in t:
                    if to >= 64:
                        cur[to - 64, to] = 1
                    else:
                        nbe[to + 64, to] = 1
                        nbo[to + 64, to] = 1
            else:
                for to in t:
                    if to < 64:
                        cur[to + 64, to] = 1
                    else:
                        nbe[to - 64, to] = 1
                        nbo[to - 64, to] = 1
        else:
            if cls in (0, 2):
                for to in t:
                    if to >= 1:
                        cur[to - 1, to] = 1
                nbo[127, 0] = 1
            else:
                for to in t:
                    if to <= 126:
                        cur[to + 1, to] = 1
                nbe[0, 127] = 1
        m[cls, 0] = cur - eye
        m[cls, 1] = nbe
        m[cls, 2] = nbo
    return np.ascontiguousarray(m.reshape(12, 128, 128).transpose(1, 0, 2).reshape(128, 12 * 128))


def _tri4():
    s = np.arange(128)[:, None]
    t = np.arange(128)[None, :]
    m = np.stack([(s <= t), (s >= t), (s < t), (s > t)], axis=1).astype(np.float32)
    return np.ascontiguousarray(m.reshape(128, 512))


def _state_to_H(st):
    a = st.reshape(2, 2, 8, 2, 64, 64)
    a = a.transpose(0, 1, 3, 5, 2, 4)
    return np.ascontiguousarray(a.reshape(2, 2, 128, 512))


def _H_to_state(Hm):
    lead = Hm.shape[:-2]
    a = Hm.reshape(lead + (2, 64, 8, 64))
    nl = len(lead)
    perm = tuple(range(nl)) + (nl + 2, nl + 0, nl + 3, nl + 1)
    a = a.transpose(perm)
    return a.reshape(lead + (16, 64, 64))


def make_core_inputs(kind, x_tokens, cond_vec, state_lh, shared):
    d = dict(shared)
    d["x"] = np.ascontiguousarray(x_tokens, dtype=np.float32)
    d["cond"] = np.ascontiguousarray(cond_vec.reshape(8, 128).T, dtype=np.float32)
    d["state0"] = _state_to_H(state_lh)
    d["cmask"] = np.full((128, 1), 1.0 if kind == "sample" else 0.0, np.float32)
    d["shm"] = _shift_mats(kind)
    return d


def shared_inputs(w_ada, b_ada, norm1_g, norm2_g, w_in, mu_shift, w0, w_up, a0, a_up, g_up, k_k, k_a, r_k, lnx_g,
                  lnx_b, w_branch_a, ln_v_g, w_s, b_s, w_branch_b, w_out, w1, w2, final_g):
    f = lambda a: np.ascontiguousarray(np.asarray(a), dtype=np.float32)
    wup_aug = np.concatenate([np.asarray(w_up), np.asarray(w0)[:, :, None, :]], axis=2)
    aup_aug = np.concatenate([np.asarray(a_up), np.asarray(a0)[:, :, None, :]], axis=2)
    wsT = np.asarray(w_s).transpose(0, 3, 1, 2).reshape(2, 128, 8 * 128)
    bsT = np.asarray(b_s).transpose(0, 2, 1)
    return dict(ident=np.eye(128, dtype=np.float32), tri4=_tri4(), w_ada=f(w_ada), b_ada=f(b_ada), norm1_g=f(norm1_g),
                norm2_g=f(norm2_g), w_in=f(w_in), mu_shift=f(mu_shift), wup_aug=f(wup_aug), aup_aug=f(aup_aug), g_up=f(g_up),
                k_k=f(k_k), k_a=f(k_a), r_k=f(np.asarray(r_k).reshape(2, D)), lnx_g=f(lnx_g), lnx_b=f(lnx_b),
                w_branch_a=f(w_branch_a), ln_v_g=f(ln_v_g), wsT=f(wsT), bsT=f(bsT), w_branch_b=f(w_branch_b), w_out=f(w_out),
                w1=f(w1), w2=f(w2), final_g=f(final_g))


_PROG = {}


def kernel(x_prompt, x_sample, state_rwkv, c, c_ctx, w_ada, b_ada, norm1_g, norm2_g, w_in, mu_shift,
           w0, w_up, a0, a_up, g_up, k_k, k_a, r_k, lnx_g, lnx_b, w_branch_a, ln_v_g, w_s, b_s,
           w_branch_b, w_out, w1, w2, final_g):
    NT = 32
    x_prompt = np.asarray(x_prompt, dtype=np.float32)
    x_sample = np.asarray(x_sample, dtype=np.float32)
    state_rwkv = np.asarray(state_rwkv, dtype=np.float32)
    c = np.asarray(c, dtype=np.float32)
    c_ctx = np.asarray(c_ctx, dtype=np.float32)
    shared = shared_inputs(w_ada, b_ada, norm1_g, norm2_g, w_in, mu_shift, w0, w_up, a0, a_up, g_up, k_k, k_a, r_k,
                           lnx_g, lnx_b, w_branch_a, ln_v_g, w_s, b_s, w_branch_b, w_out, w1, w2, final_g)
    in_maps = []
    for b in range(4):
        in_maps.append(make_core_inputs("sample", x_sample[b], c[b], state_rwkv[b], shared))
    zero_state = np.zeros((2, 2, 16, 64, 64), np.float32)
    for q in range(4):
        xs = np.zeros((NT * 128, D), np.float32)
        xs[:2048] = x_prompt[8 * q:8 * q + 8].reshape(2048, D)
        xs[2048:] = xs[:2048]
        in_maps.append(make_core_inputs("prompt", xs, c_ctx, zero_state, shared))
    if NT not in _PROG:
        _PROG[NT] = build_program(NT)[0]
    res = run_bass_kernel_spmd(_PROG[NT], in_maps, core_ids=list(range(8)))
    r = res.results
    y_sample = np.stack([r[b]["y"] for b in range(4)], axis=0)
    y_prompt = np.concatenate([r[4 + q]["y"][:2048].reshape(8, 256, D) for q in range(4)], axis=0)
    sts = []
    for q in range(4):
        so = r[4 + q]["st_out"]
        so = so[:, :, :8]
        s = _H_to_state(so)
        sts.append(np.transpose(s, (2, 0, 1, 3, 4, 5)))
    new_state = np.ascontiguousarray(np.concatenate(sts, axis=0), dtype=np.float32)
    return (np.ascontiguousarray(y_prompt, dtype=np.float32), np.ascontiguousarray(y_sample, dtype=np.float32), new_state)
```

```python
import contextlib
import os
DBG = int(os.environ.get('KDBG', '99'))
KSKIP = os.environ.get('KSKIP', '')
import numpy as np
import concourse.bass as bass
import concourse.mybir as mybir
from concourse.bass_utils import run_bass_kernel_spmd

F32 = mybir.dt.float32
BF16 = mybir.dt.bfloat16
ALU = mybir.AluOpType
AF = mybir.ActivationFunctionType
AX = mybir.AxisListType

D = 1024
CR = 3456
DIN = 7552
DFF = 4096
NHEAD = 16
EPS = 1e-6
GN_EPS = 64e-5
DSC = float(np.exp(-0.5))


class Buf:
    __slots__ = ("name", "t", "w", "r", "dsem", "dcnt")

    def __init__(self, name, t=None):
        self.name = name
        self.t = t
        self.w = None
        self.r = []
        self.dsem = None
        self.dcnt = 0

    def __getitem__(self, idx):
        return self.t[idx]


class K:
    def __init__(self, nc):
        self.nc = nc
        self.eng = {"pe": nc.tensor, "act": nc.scalar, "dve": nc.vector, "pool": nc.gpsimd, "sp": nc.sync}
        self.sem = {}
        self.cnt = {}
        for e in self.eng:
            self.sem[e] = nc.alloc_semaphore(name="s_" + e)
            self.cnt[e] = 0
        self.waited = {}
        self.dsems = {}
        self.free_dsems = []
        self.ninstr = 0
        self.uid = 0

    def _wait(self, e, tok):
        if tok is None:
            return
        key, val = tok
        if key == e and e == "pe":
            return
        kk = (e, key)
        if self.waited.get(kk, 0) >= val:
            return
        self.waited[kk] = val
        self.eng[e].wait_ge(self.sem[key], val)
        self.ninstr += 1

    def _deps(self, e, reads, writes):
        for b in reads:
            self._wait(e, b.w)
        for b in writes:
            self._wait(e, b.w)
            for tok in b.r:
                self._wait(e, tok)

    def _commit(self, tok, reads, writes):
        for b in reads:
            if b not in writes:
                b.r.append(tok)
                if len(b.r) > 10:
                    best = {}
                    for k_, v_ in b.r:
                        if best.get(k_, -1) < v_:
                            best[k_] = v_
                    b.r = list(best.items())
        for b in writes:
            b.w = tok
            b.r = []

    def op(self, e, fn, reads=(), writes=()):
        reads = [b for b in reads if b is not None]
        writes = [b for b in writes if b is not None]
        self._deps(e, reads, writes)
        ins = fn(self.eng[e])
        self.cnt[e] += 1
        ins.then_inc(self.sem[e], 1)
        self.ninstr += 1
        self._commit((e, self.cnt[e]), reads, writes)

    def dma(self, q, out_ap, in_ap, reads=(), writes=(), sembuf=None, **kw):
        reads = [b for b in reads if b is not None]
        writes = [b for b in writes if b is not None]
        if sembuf is None:
            sembuf = (writes + reads)[0]
        if sembuf.dsem is None:
            if self.free_dsems:
                key, base = self.free_dsems.pop()
                sembuf.dcnt = base
            else:
                key = "d%d" % len(self.sem)
                self.sem[key] = self.nc.alloc_semaphore(name=key)
            self.dsems[key] = sembuf
            sembuf.dsem = key
        self._deps(q, reads, writes)
        ins = self.eng[q].dma_start(out=out_ap, in_=in_ap, **kw)
        sembuf.dcnt += 16
        ins.then_inc(self.sem[sembuf.dsem], 16)
        self.ninstr += 1
        self._commit((sembuf.dsem, sembuf.dcnt), reads, writes)

    def barrier(self):
        toks = [(e, self.cnt[e]) for e in self.eng if self.cnt[e] > 0]
        toks += [(key, b.dcnt) for key, b in self.dsems.items() if b.dcnt > 0]
        for e in self.eng:
            for tok in toks:
                if tok[0] != e:
                    self._wait(e, tok)
        for key, b in list(self.dsems.items()):
            if not getattr(b, "keep", False):
                self.free_dsems.append((key, b.dcnt))
                b.dsem = None
                del self.dsems[key]


def cs(a, n):
    return slice(a, a + n)


def hc(h):
    return slice((h % 2) * 512 + (h // 2) * 64, (h % 2) * 512 + (h // 2) * 64 + 64)


def build_program(NT, upto=99):
    T = NT * 128
    NG = NT // 2
    nc = bass.Bass("TRN2", target_bir_lowering=False)
    k = K(nc)

    def din(name, shape):
        return nc.dram_tensor(name, list(shape), F32, kind="ExternalInput").ap()

    x_d = din("x", [T, D])
    cond_d = din("cond", [128, 8])
    state0_d = din("state0", [2, 2, 128, 512])
    cmask_d = din("cmask", [128, 1])
    shm_d = din("shm", [128, 12 * 128])
    ident_d = din("ident", [128, 128])
    tri4_d = din("tri4", [128, 4 * 128])
    w_ada_d = din("w_ada", [2, D, 6 * D])
    b_ada_d = din("b_ada", [2, 6 * D])
    n1g_d = din("norm1_g", [2, D])
    n2g_d = din("norm2_g", [2, D])
    w_in_d = din("w_in", [2, D, DIN])
    mu_d = din("mu_shift", [2, CR])
    wup_d = din("wup_aug", [2, 2, 65, D])
    aup_d = din("aup_aug", [2, 2, 65, D])
    gup_d = din("g_up", [2, 128, D])
    kk_d = din("k_k", [2, D])
    ka_d = din("k_a", [2, D])
    rk_d = din("r_k", [2, D])
    lnxg_d = din("lnx_g", [2, D])
    lnxb_d = din("lnx_b", [2, D])
    wa_d = din("w_branch_a", [2, D, D])
    lnvg_d = din("ln_v_g", [2, D])
    wsT_d = din("wsT", [2, 128, 8 * 128])
    bsT_d = din("bsT", [2, 128, 8])
    wb_d = din("w_branch_b", [2, D, D])
    wo_d = din("w_out", [2, D, D])
    w1_d = din("w1", [2, D, DFF])
    w2_d = din("w2", [2, DFF, D])
    fg_d = din("final_g", [D])

    y_d = nc.dram_tensor("y", [T, D], F32, kind="ExternalOutput").ap()
    st_d = nc.dram_tensor("st_out", [2, 2, NG, 128, 512], F32, kind="ExternalOutput").ap()

    def dscr(name, shape, dt):
        return nc.dram_tensor(name, list(shape), dt, kind="Internal").ap()

    modbc_d = dscr("modbc", [2, 128, 6 * D], F32)
    zr_d = dscr("zr_s", [T, CR], BF16)
    zrest_d = dscr("zrest_s", [T, 4096], BF16)
    ysc_d = dscr("ysc_s", [2, T, D], F32)
    bon_d = dscr("bon_s", [2, T, 16], F32)
    x1_d = dscr("x1_s", [T, D], F32)
    x2_d = dscr("x2_s", [T, D], F32)

    class DR:
        def __init__(self, nm):
            self.b = {}
            self.nm = nm

        def __call__(self, *key):
            if key not in self.b:
                self.b[key] = Buf(self.nm + str(key))
            return self.b[key]

    R_mod, R_zr, R_zrest, R_ysc, R_bon, R_x1, R_x2, R_y, R_st = [DR(n) for n in
        ("mod", "zr", "zrest", "ysc", "bon", "x1", "x2", "y", "st")]

    PP = [nc.alloc_psum_tensor("psum%d" % i, [128, 1024], F32) for i in range(4)]
    PB = []
    for i in range(8):
        PB.append(Buf("pb%d" % i, PP[i // 2][:, cs((i % 2) * 512, 512)]))
    pst = {"b": 0, "p": 0}

    def bank():
        b = PB[pst["b"] % 8]
        pst["b"] += 1
        return b

    def pair():
        if pst["b"] % 2:
            pst["b"] += 1
        i = (pst["b"] % 8) // 2
        pst["b"] += 2
        return PP[i], PB[2 * i], PB[2 * i + 1]

    def sbt(es, name, shape, dt):
        k.uid += 1
        t = es.enter_context(nc.sbuf_tensor("%s_%d" % (name, k.uid), list(shape), dt))
        return Buf(name, t)

    ges = contextlib.ExitStack()
    ident_f = sbt(ges, "ident_f", [128, 128], F32)
    ident_b = sbt(ges, "ident_b", [128, 128], BF16)
    tri4 = sbt(ges, "tri4", [128, 4, 128], F32)
    ones_f = sbt(ges, "ones_f", [128, 1], F32)
    cmask = sbt(ges, "cmask", [128, 1], F32)
    k.dma("sp", ident_f[:], ident_d, writes=[ident_f])
    k.dma("sp", tri4[:].rearrange("p a b -> p (a b)"), tri4_d, writes=[tri4])
    k.dma("sp", cmask[:], cmask_d, writes=[cmask])
    k.op("dve", lambda e: e.tensor_copy(out=ident_b[:], in_=ident_f[:]), reads=[ident_f], writes=[ident_b])
    k.op("dve", lambda e: e.memset(ones_f[:], 1.0), writes=[ones_f])

    def bc_load(buf, dvec):
        k.dma("sp", buf[:], dvec.partition_broadcast(128), writes=[buf])

    def cast_load_rows(buf_ap_fn, dsrc, nk, ncol, buf):
        for kc in range(nk):
            k.dma("pool", buf_ap_fn(kc), dsrc[cs(kc * 128, 128), :], writes=[buf], max_dma_last_dim=4096)

    def rstd_from(e_ss, out_rstd, scale, eps):
        k.op("act", lambda e: e.activation(out=out_rstd[:], in_=e_ss[:], func=AF.Ln, bias=eps_t(eps)[:], scale=scale),
             reads=[e_ss, eps_t(eps)], writes=[out_rstd])
        k.op("act", lambda e: e.activation(out=out_rstd[:], in_=out_rstd[:], func=AF.Exp, scale=-0.5),
             reads=[out_rstd], writes=[out_rstd])

    eps_tiles = {}

    def eps_t(v):
        if v not in eps_tiles:
            b = sbt(ges, "eps%d" % len(eps_tiles), [128, 1], F32)
            k.op("dve", lambda e: e.memset(b[:], float(v)), writes=[b])
            eps_tiles[v] = b
        return eps_tiles[v]

    for v in (EPS, GN_EPS, 1e-12):
        eps_t(v)

    def phaseP(l):
        with contextlib.ExitStack() as es:
            wad = sbt(es, "wad", [128, 8, 6 * D], BF16)
            ba = sbt(es, "ba", [128, 6 * D], F32)
            mod = sbt(es, "mod", [128, 6 * D], F32)
            n1g = sbt(es, "n1g", [128, D], F32)
            n2g = sbt(es, "n2g", [128, D], F32)
            cnd = sbt(es, "cnd", [128, 8], F32)
            scb = sbt(es, "scb", [128, 8, 128], BF16)
            cast_load_rows(lambda kc: wad[:, kc, :], w_ada_d[l], 8, 6 * D, wad)
            bc_load(ba, b_ada_d[l])
            bc_load(n1g, n1g_d[l])
            bc_load(n2g, n2g_d[l])
            k.dma("sp", cnd[:], cond_d, writes=[cnd])
            k.op("act", lambda e: e.activation(out=cnd[:], in_=cnd[:], func=AF.Silu), reads=[cnd], writes=[cnd])
            k.op("dve", lambda e: e.tensor_copy(out=scb[:], in_=cnd[:].unsqueeze(2).to_broadcast([128, 8, 128])),
                 reads=[cnd], writes=[scb])
            for n in range(12):
                pb = bank()

                def f(e):
                    for kc in range(8):
                        ins = e.matmul(pb[:], lhsT=scb[:, kc, :], rhs=wad[:, kc, cs(n * 512, 512)],
                                       start=(kc == 0), stop=(kc == 7))
                    return ins
                k.op("pe", f, reads=[scb, wad], writes=[pb])
                k.op("dve", lambda e: e.tensor_tensor(out=mod[:, cs(n * 512, 512)], in0=pb[:], in1=ba[:, cs(n * 512, 512)],
                                                      op=ALU.add), reads=[pb, ba], writes=[mod])
            k.op("dve", lambda e: e.scalar_tensor_tensor(out=mod[:, cs(D, D)], in0=mod[:, cs(D, D)], scalar=1.0, in1=n1g[:],
                                                         op0=ALU.add, op1=ALU.mult), reads=[mod, n1g], writes=[mod])
            k.op("dve", lambda e: e.scalar_tensor_tensor(out=mod[:, cs(4 * D, D)], in0=mod[:, cs(4 * D, D)], scalar=1.0,
                                                         in1=n2g[:], op0=ALU.add, op1=ALU.mult), reads=[mod, n2g], writes=[mod])
            k.dma("sp", modbc_d[l], mod[:], reads=[mod], writes=[R_mod(l)], sembuf=mod)
            k.barrier()

    def load_mod(buf, l, j):
        k.dma("sp", buf[:], modbc_d[l][:, cs(j * D, D)], reads=[R_mod(l)], writes=[buf])

    def phaseA1(l, xsrc_d, R_xsrc):
        with contextlib.ExitStack() as es:
            win = sbt(es, "win", [128, 8, DIN], BF16)
            g1 = sbt(es, "g1", [128, D], F32)
            sh1 = sbt(es, "sh1", [128, D], F32)
            mu = sbt(es, "mu", [128, CR], F32)
            shm = sbt(es, "shm", [128, 12, 128], BF16)
            xt = [sbt(es, "xt0", [128, D], F32)]
            xt.append(xt[0])
            sq = sbt(es, "sq", [128, D], F32)
            hb = sbt(es, "hb", [128, D], BF16)
            hT = sbt(es, "hT", [128, 8, 128], BF16)
            ss = sbt(es, "ss", [128, 1], F32)
            rstd = sbt(es, "rstd", [128, 1], F32)
            zb = [sbt(es, "zb%d" % i, [128, CR], BF16) for i in range(2)]
            zm = [sbt(es, "zm%d" % i, [128, CR], BF16) for i in range(3)]
            zst = sbt(es, "zst", [128, CR], BF16)
            rst = [sbt(es, "rst%d" % i, [128, 512], BF16) for i in range(4)]
            cast_load_rows(lambda kc: win[:, kc, :], w_in_d[l], 8, DIN, win)
            k.dma("pool", shm[:].rearrange("p a b -> p (a b)"), shm_d, writes=[shm], max_dma_last_dim=4096)
            load_mod(sh1, l, 0)
            load_mod(g1, l, 1)
            bc_load(mu, mu_d[l])
            rsti = [0]

            def load_x(i):
                k.dma("sp", xt[i % 2][:], xsrc_d[cs(i * 128, 128), :], reads=[R_xsrc(i)], writes=[xt[i % 2]])

            def stage1(i):
                x = xt[i % 2]
                if DBG < 2:
                    if i + 1 < NT:
                        load_x(i + 1)
                    return
                k.op("act", lambda e: e.activation(out=sq[:], in_=x[:], func=AF.Square), reads=[x], writes=[sq])
                k.op("dve", lambda e: e.tensor_reduce(out=ss[:], in_=sq[:], axis=AX.X, op=ALU.add), reads=[sq], writes=[ss])
                rstd_from(ss, rstd, 1.0 / D, EPS)
                k.op("dve", lambda e: e.scalar_tensor_tensor(out=x[:], in0=x[:], scalar=rstd[:, 0:1], in1=g1[:],
                                                             op0=ALU.mult, op1=ALU.mult), reads=[x, rstd, g1], writes=[x])
                k.op("dve", lambda e: e.tensor_tensor(out=hb[:], in0=x[:], in1=sh1[:], op=ALU.add),
                     reads=[x, sh1], writes=[hb])
                if i + 1 < NT:
                    load_x(i + 1)
                if DBG < 3:
                    return
                transpose8(hb, hT)
                if DBG < 4:
                    return
                zbi, zmi = zb[i % 2], zm[i % 3]
                col = 0
                ci = 0
                while col < DIN:
                    if col < CR:
                        n = min(512, CR - col)
                    else:
                        n = 512
                    pb = bank()

                    def f(e):
                        for kc in range(8):
                            ins = e.matmul(pb[:, 0:n], lhsT=hT[:, kc, :], rhs=win[:, kc, cs(col, n)],
                                           start=(kc == 0), stop=(kc == 7))
                        return ins
                    k.op("pe", f, reads=[hT, win], writes=[pb])
                    if 'p' in KSKIP:
                        pass
                    elif col < CR:
                        if 'z' not in KSKIP:
                            k.op("act", lambda e: e.copy(out=zbi[:, cs(col, n)], in_=pb[:, 0:n]), reads=[pb], writes=[zbi])
                        if 'm' not in KSKIP:
                            k.op("dve", lambda e: e.tensor_tensor(out=zmi[:, cs(col, n)], in0=zbi[:, cs(col, n)], in1=mu[:, cs(col, n)],
                                                                  op=ALU.mult), reads=[zbi, mu], writes=[zmi])
                    else:
                        st = rst[rsti[0] % 4]
                        rsti[0] += 1
                        if ci % 2 == 0:
                            k.op("act", lambda e: e.copy(out=st[:], in_=pb[:]), reads=[pb], writes=[st])
                        else:
                            k.op("dve", lambda e: e.tensor_copy(out=st[:], in_=pb[:]), reads=[pb], writes=[st])
                        if 'r' not in KSKIP:
                            k.dma("sp", zrest_d[cs(i * 128, 128), cs(col - CR, 512)], st[:], reads=[st],
                                  writes=[R_zrest(i, (col - CR) // 512)], sembuf=st)
                    col += n
                    ci += 1

            def stage2(i):
                if DBG < 5:
                    return
                par = i % 2
                for cls in range(4):
                    nb = i - 1 if cls in (0, 2) else i + 1
                    for hh in range(2):
                        c0 = cls + 4 * 432 * hh
                        sl = slice(c0, c0 + 4 * 431 + 1, 4)
                        pb = bank()
                        srcs = [(ident_b[:], zb[i % 2], ident_b), (shm[:, 3 * cls, :], zm[i % 3], shm)]
                        if 0 <= nb < NT:
                            srcs.append((shm[:, 3 * cls + 1 + par, :], zm[nb % 3], shm))

                        def f(e):
                            for j, (lt, rb, _) in enumerate(srcs):
                                ins = e.matmul(pb[:, 0:432], lhsT=lt, rhs=rb[:, sl], start=(j == 0), stop=(j == len(srcs) - 1))
                            return ins
                        k.op("pe", f, reads=[s[1] for s in srcs] + [ident_b, shm], writes=[pb])
                        if hh == 0:
                            k.op("act", lambda e: e.copy(out=zst[:, sl], in_=pb[:, 0:432]), reads=[pb], writes=[zst])
                        else:
                            k.op("dve", lambda e: e.tensor_copy(out=zst[:, sl], in_=pb[:, 0:432]), reads=[pb], writes=[zst])
                k.dma("sp", zr_d[cs(i * 128, 128), :], zst[:], reads=[zst], writes=[R_zr(i)], sembuf=zst)

            load_x(0)
            stage1(0)
            for i in range(NT):
                if i + 1 < NT:
                    stage1(i + 1)
                stage2(i)
            k.barrier()

    def transpose8(src, dst, nblk=8, src_off=0):
        pb = bank()
        pv = pb[:].bitcast(BF16)

        def f(e):
            for j in range(nblk):
                ins = e.transpose(out=pv[:, cs(j * 128, 128)], in_=src[:, cs(src_off + j * 128, 128)], identity=ident_b[:])
            return ins
        k.op("pe", f, reads=[src, ident_b], writes=[pb])
        k.op("act", lambda e: e.copy(out=dst[:, 0:nblk, :].rearrange("p a b -> p (a b)"), in_=pv[:, 0:nblk * 128]),
             reads=[pb], writes=[dst])

    def phaseS(l, d):
        with contextlib.ExitStack() as es:
            kkc = sbt(es, "kkc", [128, D], F32)
            kac = sbt(es, "kac", [128, D], F32)
            rkc = sbt(es, "rkc", [128, D], F32)
            wup = sbt(es, "wup", [65, D], BF16)
            aup = sbt(es, "aup", [65, D], BF16)
            maskM = sbt(es, "maskM", [128, 4, 128], F32)
            maskN = sbt(es, "maskN", [128, 4, 128], F32)
            H = sbt(es, "H", [128, 8, 64], F32)
            Hb = sbt(es, "Hb", [128, 8, 64], BF16)
            zr = [sbt(es, "zr%d" % i, [128, CR], BF16) for i in range(2)]
            ldT = sbt(es, "ldT", [65, 2, 128], BF16)
            tw = sbt(es, "tw", [128, 128], BF16)
            SG = sbt(es, "SG", [128, D], F32)
            A = sbt(es, "A", [128, D], F32)
            KX = sbt(es, "KX", [128, D], F32)
            BP = sbt(es, "BP", [128, D], F32)
            KD = sbt(es, "KD", [128, D], F32)
            S0 = sbt(es, "S0", [128, D], F32)
            S1 = sbt(es, "S1", [128, D], F32)
            st16 = sbt(es, "st16", [128, 16], F32)
            rs16 = sbt(es, "rs16", [128, 16], F32)
            bon = sbt(es, "bon", [128, 16], F32)
            gC = sbt(es, "gC", [128, 8], F32)
            TM = sbt(es, "TM", [128, 4, D], BF16)
            Bg = sbt(es, "Bg", [128, D], BF16)
            Kg = sbt(es, "Kg", [128, D], BF16)
            FM = sbt(es, "FM", [128, 8, 4, 128], BF16)
            MB = sbt(es, "MB", [128, 16, 512], BF16)
            Pm = [sbt(es, "Pm%d" % i, [128, 16, 128], BF16) for i in range(2)]
            PT = [sbt(es, "PT%d" % i, [128, 16, 128], BF16) for i in range(2)]
            Xb = [sbt(es, "Xb%d" % i, [128, D], BF16) for i in range(2)]
            ysc = sbt(es, "ysc", [128, D], F32)

            bc_load(kkc, kk_d[l])
            bc_load(kac, ka_d[l])
            bc_load(rkc, rk_d[l])
            k.dma("pool", wup[:], wup_d[l, d], writes=[wup], max_dma_last_dim=4096)
            k.dma("pool", aup[:], aup_d[l, d], writes=[aup], max_dma_last_dim=4096)
            strict_i, incl_i, nmask_i = (2, 0, 3) if d == 0 else (3, 1, 2)
            for j, src in enumerate((strict_i, incl_i, strict_i, incl_i)):
                k.op("dve", lambda e: e.tensor_copy(out=maskM[:, j, :], in_=tri4[:, src, :]), reads=[tri4], writes=[maskM])
            for j in range(4):
                k.op("dve", lambda e: e.tensor_copy(out=maskN[:, j, :], in_=tri4[:, nmask_i, :]), reads=[tri4], writes=[maskN])
            tri_incl = tri4[:, incl_i, :]
            tri_excl = tri4[:, strict_i, :]
            tri_dg = tri4[:, nmask_i, :]
            k.op("dve", lambda e: e.memset(ldT[:], 1.0), writes=[ldT])
            k.dma("sp", H[:].rearrange("p a b -> p (a b)"), state0_d[l, d], writes=[H])
            k.op("act", lambda e: e.copy(out=Hb[:], in_=H[:]), reads=[H], writes=[Hb])

            order = list(range(NT)) if d == 0 else list(range(NT - 1, -1, -1))

            def load_zr(n):
                i = order[n]
                k.dma("sp", zr[n % 2][:], zr_d[cs(i * 128, 128), :], reads=[R_zr(i)], writes=[zr[n % 2]])

            load_zr(0)
            for n, i in enumerate(order):
                z = zr[n % 2]
                if n + 1 < NT:
                    load_zr(n + 1)
                rq = z[:, 0:D]
                kq = z[:, D:2 * D]
                vq = z[:, 2 * D:3 * D]
                k.op("act", lambda e: e.activation(out=tw[:, 0:64], in_=z[:, cs(3 * D + 64 * d, 64)], func=AF.Tanh),
                     reads=[z], writes=[tw])
                k.op("dve", lambda e: e.tensor_copy(out=tw[:, 64:128], in_=z[:, cs(3 * D + 128 + 64 * d, 64)]),
                     reads=[z], writes=[tw])
                pb = bank()
                pv = pb[:].bitcast(BF16)

                def f(e):
                    e.transpose(out=pv[0:64, 0:128], in_=tw[:, 0:64], identity=ident_b[:])
                    return e.transpose(out=pv[0:64, 128:256], in_=tw[:, 64:128], identity=ident_b[:])
                k.op("pe", f, reads=[tw, ident_b], writes=[pb])
                k.op("act", lambda e: e.copy(out=ldT[0:64, :, :].rearrange("p a b -> p (a b)"), in_=pv[0:64, 0:256]),
                     reads=[pb], writes=[ldT])
                for (wmat, src_j, dst) in ((wup, 0, SG), (aup, 1, A)):
                    pp, b0, b1 = pair()

                    def f(e):
                        e.matmul(pp[:, 0:512], lhsT=ldT[:, src_j, :], rhs=wmat[:, 0:512], start=True, stop=True)
                        return e.matmul(pp[:, 512:1024], lhsT=ldT[:, src_j, :], rhs=wmat[:, 512:1024], start=True, stop=True)
                    k.op("pe", f, reads=[ldT, wmat], writes=[b0, b1])
                    k.op("act", lambda e: e.activation(out=dst[:], in_=pp[:, :], func=AF.Sigmoid), reads=[b0, b1], writes=[dst])
                if DBG < 11:
                    continue
                k.op("dve", lambda e: e.tensor_tensor(out=KX[:], in0=kq, in1=kkc[:], op=ALU.mult), reads=[z, kkc], writes=[KX])
                k.op("dve", lambda e: e.tensor_tensor(out=S0[:], in0=KX[:], in1=KX[:], op=ALU.mult), reads=[KX], writes=[S0])
                k.op("dve", lambda e: e.tensor_reduce(out=st16[:], in_=S0[:].rearrange("p (h n) -> p h n", n=64), axis=AX.X,
                                                      op=ALU.add), reads=[S0], writes=[st16])
                rstd_from(st16, rs16, 1.0, 1e-12)
                k.op("dve", lambda e: e.tensor_tensor(out=KX[:].rearrange("p (h n) -> p h n", n=64),
                                                      in0=KX[:].rearrange("p (h n) -> p h n", n=64),
                                                      in1=rs16[:].unsqueeze(2).to_broadcast([128, 16, 64]), op=ALU.mult),
                     reads=[KX, rs16], writes=[KX])
                k.op("dve", lambda e: e.scalar_tensor_tensor(out=BP[:], in0=KX[:], scalar=-1.0, in1=A[:], op0=ALU.mult,
                                                             op1=ALU.mult), reads=[KX, A], writes=[BP])
                k.op("dve", lambda e: e.tensor_tensor(out=S0[:], in0=kq, in1=kac[:], op=ALU.mult), reads=[z, kac], writes=[S0])
                k.op("dve", lambda e: e.scalar_tensor_tensor(out=S0[:], in0=A[:], scalar=-1.0, in1=S0[:], op0=ALU.add,
                                                             op1=ALU.mult), reads=[A, S0], writes=[S0])
                k.op("dve", lambda e: e.tensor_tensor(out=KD[:], in0=S0[:], in1=kq, op=ALU.add), reads=[S0, z], writes=[KD])
                k.op("dve", lambda e: e.tensor_tensor(out=S0[:], in0=KD[:], in1=rkc[:], op=ALU.mult), reads=[KD, rkc], writes=[S0])
                k.op("dve", lambda e: e.tensor_tensor(out=S0[:], in0=S0[:], in1=rq, op=ALU.mult), reads=[S0, z], writes=[S0])
                k.op("dve", lambda e: e.tensor_reduce(out=bon[:], in_=S0[:].rearrange("p (h n) -> p h n", n=64), axis=AX.X,
                                                      op=ALU.add), reads=[S0], writes=[bon])
                k.dma("sp", bon_d[d, cs(i * 128, 128), :], bon[:], reads=[bon], writes=[R_bon(d, i)], sembuf=bon)
                if DBG < 12:
                    continue
                def cum(tri_ap, scale, dstE):
                    pp, b0, b1 = pair()

                    def f(e):
                        e.matmul(pp[:, 0:512], lhsT=tri_ap, rhs=SG[:, 0:512], start=True, stop=True)
                        return e.matmul(pp[:, 512:1024], lhsT=tri_ap, rhs=SG[:, 512:1024], start=True, stop=True)
                    k.op("pe", f, reads=[tri4, SG], writes=[b0, b1])
                    k.op("act", lambda e: e.activation(out=dstE[:], in_=pp[:, :], func=AF.Exp, scale=scale),
                         reads=[b0, b1], writes=[dstE])
                    return pp, b0, b1
                pp, b0, b1 = cum(tri_incl, -DSC, S0)
                k.op("dve", lambda e: e.tensor_tensor(out=TM[:, 1, :], in0=rq, in1=S0[:], op=ALU.mult), reads=[z, S0], writes=[TM])
                k.op("act", lambda e: e.activation(out=S1[:], in_=pp[:, :], func=AF.Exp, scale=DSC), reads=[b0, b1], writes=[S1])
                k.op("dve", lambda e: e.tensor_tensor(out=TM[:, 2, :], in0=BP[:], in1=S1[:], op=ALU.mult), reads=[BP, S1], writes=[TM])
                k.op("dve", lambda e: e.tensor_tensor(out=TM[:, 3, :], in0=KD[:], in1=S1[:], op=ALU.mult), reads=[KD, S1], writes=[TM])
                cum(tri_excl, -DSC, S0)
                k.op("dve", lambda e: e.tensor_tensor(out=TM[:, 0, :], in0=KX[:], in1=S0[:], op=ALU.mult), reads=[KX, S0], writes=[TM])
                cum(tri_dg, -DSC, S1)
                k.op("dve", lambda e: e.tensor_tensor(out=Bg[:], in0=BP[:], in1=S1[:], op=ALU.mult), reads=[BP, S1], writes=[Bg])
                k.op("dve", lambda e: e.tensor_tensor(out=Kg[:], in0=KD[:], in1=S1[:], op=ALU.mult), reads=[KD, S1], writes=[Kg])
                pbg = bank()

                def f(e):
                    for ct in range(8):
                        ins = e.matmul(pbg[:, ct:ct + 1], lhsT=SG[:, cs(ct * 128, 128)], rhs=ones_f[:], start=True, stop=True)
                    return ins
                k.op("pe", f, reads=[SG, ones_f], writes=[pbg])
                k.op("act", lambda e: e.activation(out=gC[:], in_=pbg[:, 0:8], func=AF.Exp, scale=-DSC), reads=[pbg], writes=[gC])
                if DBG < 13:
                    continue
                for g4 in range(4):
                    pb = bank()
                    pv = pb[:].bitcast(BF16)

                    def f(e):
                        for c2 in range(2):
                            ct = g4 * 2 + c2
                            for q in range(4):
                                ins = e.transpose(out=pv[:, cs((c2 * 4 + q) * 128, 128)], in_=TM[:, q, cs(ct * 128, 128)],
                                                  identity=ident_b[:])
                        return ins
                    k.op("pe", f, reads=[TM, ident_b], writes=[pb])
                    eng = "act" if g4 % 2 == 0 else "dve"
                    if eng == "act":
                        k.op("act", lambda e: e.copy(out=FM[:, g4 * 2:g4 * 2 + 2, :, :].rearrange("p a b c -> p (a b c)"),
                                                     in_=pv[:, :]), reads=[pb], writes=[FM])
                    else:
                        k.op("dve", lambda e: e.tensor_copy(out=FM[:, g4 * 2:g4 * 2 + 2, :, :].rearrange("p a b c -> p (a b c)"),
                                                            in_=pv[:, :]), reads=[pb], writes=[FM])
                if DBG < 14:
                    continue
                P0, PT0 = Pm[0], PT[0]
                for h in range(NHEAD):
                    ct, p0 = h // 2, (h % 2) * 64
                    if 'o' in KSKIP and h % 2 == 1:
                        continue
                    pb = bank()

                    def f(e):
                        e.matmul(pb[:, 0:256], lhsT=FM[p0:p0 + 64, ct, 2, :],
                                 rhs=FM[p0:p0 + 64, ct, 0:2, :].rearrange("p a b -> p (a b)"), start=True, stop=True)
                        return e.matmul(pb[:, 256:512], lhsT=FM[p0:p0 + 64, ct, 3, :],
                                        rhs=FM[p0:p0 + 64, ct, 0:2, :].rearrange("p a b -> p (a b)"), start=True, stop=True)
                    k.op("pe", f, reads=[FM], writes=[pb])
                    k.op("dve", lambda e: e.tensor_tensor(out=MB[:, h, :], in0=pb[:], in1=maskM[:].rearrange("p a b -> p (a b)"),
                                                          op=ALU.mult), reads=[pb, maskM], writes=[MB])
                    k.op("act", lambda e: e.copy(out=PT0[:, h, :], in_=MB[:, h, 0:128]), reads=[MB], writes=[PT0])
                for g in range(4):
                    if 'n' in KSKIP:
                        continue
                    pb = bank()

                    def f(e):
                        for hh in range(4):
                            h = (g % 2) + 2 * (4 * (g // 2) + hh)
                            ct, p0 = h // 2, (h % 2) * 64
                            ins = e.matmul(pb[:, cs(hh * 128, 128)], lhsT=FM[p0:p0 + 64, ct, 0, :], rhs=FM[p0:p0 + 64, ct, 2, :],
                                           start=True, stop=True)
                        return ins
                    k.op("pe", f, reads=[FM], writes=[pb])
                    h0 = (g % 2) + 8 * (g // 2)
                    k.op("dve", lambda e: e.tensor_tensor(out=P0[:, h0:h0 + 7:2, :],
                                                          in0=pb[:].rearrange("p (a b) -> p a b", b=128), in1=maskN[:], op=ALU.mult),
                         reads=[pb, maskN], writes=[P0])
                if DBG < 15:
                    continue
                pp, b0, b1 = pair()

                def f(e):
                    for h in range(NHEAD):
                        ct, p0 = h // 2, (h % 2) * 64
                        e.matmul(pp[:, hc(h)], lhsT=FM[p0:p0 + 64, ct, 0, :], rhs=Hb[p0:p0 + 64, ct, :],
                                 start=True, stop=False)
                        ins = e.matmul(pp[:, hc(h)], lhsT=MB[:, h, 256:384], rhs=z[:, cs(2 * D + h * 64, 64)],
                                       start=False, stop=True)
                    return ins
                k.op("pe", f, reads=[FM, Hb, MB, z], writes=[b0, b1])
                xc = Xb[0]
                k.op("act", lambda e: e.copy(out=xc[:], in_=pp[:, :]), reads=[b0, b1], writes=[xc])
                if DBG < 16:
                    continue
                cur = 0
                for lev in range(7):
                    Pc, PTc = Pm[cur], PT[cur]
                    pp, b0, b1 = pair()
                    xc, xn = Xb[lev % 2], Xb[(lev + 1) % 2]

                    def f(e):
                        for h in range(NHEAD):
                            ins = e.matmul(pp[:, hc(h)], lhsT=PTc[:, h, :], rhs=xc[:, hc(h)], start=True, stop=True)
                        return ins
                    k.op("pe", f, reads=[PTc, xc], writes=[b0, b1])
                    k.op("dve", lambda e: e.tensor_tensor(out=xn[:], in0=pp[:, :], in1=xc[:], op=ALU.add),
                         reads=[b0, b1, xc], writes=[xn])
                    if lev < 6:
                        Pn, PTn = Pm[1 - cur], PT[1 - cur]
                        for g in range(4):
                            for which in range(2):
                                pb = bank()

                                def f(e):
                                    for hh in range(4):
                                        h = g * 4 + hh
                                        if which == 0:
                                            ins = e.matmul(pb[:, cs(hh * 128, 128)], lhsT=PTc[:, h, :], rhs=Pc[:, h, :], start=True, stop=True)
                                        else:
                                            ins = e.matmul(pb[:, cs(hh * 128, 128)], lhsT=Pc[:, h, :], rhs=PTc[:, h, :], start=True, stop=True)
                                    return ins
                                k.op("pe", f, reads=[Pc, PTc], writes=[pb])
                                dstb = Pn if which == 0 else PTn
                                if which == 0:
                                    k.op("act", lambda e: e.copy(out=dstb[:, g * 4:g * 4 + 4, :].rearrange("p a b -> p (a b)"),
                                                                 in_=pb[:]), reads=[pb], writes=[dstb])
                                else:
                                    k.op("dve", lambda e: e.tensor_copy(out=dstb[:, g * 4:g * 4 + 4, :].rearrange("p a b -> p (a b)"),
                                                                        in_=pb[:]), reads=[pb], writes=[dstb])
                        cur = 1 - cur
                U = Xb[1]
                if DBG < 17:
                    continue
                pp, b0, b1 = pair()

                def f(e):
                    for h in range(NHEAD):
                        ct, p0 = h // 2, (h % 2) * 64
                        e.matmul(pp[:, hc(h)], lhsT=FM[p0:p0 + 64, ct, 1, :], rhs=Hb[p0:p0 + 64, ct, :], start=True, stop=False)
                        e.matmul(pp[:, hc(h)], lhsT=MB[:, h, 128:256], rhs=U[:, hc(h)], start=False, stop=False)
                        ins = e.matmul(pp[:, hc(h)], lhsT=MB[:, h, 384:512], rhs=z[:, cs(2 * D + h * 64, 64)],
                                       start=False, stop=True)
                    return ins
                k.op("pe", f, reads=[FM, Hb, MB, U, z], writes=[b0, b1])
                k.op("act", lambda e: e.copy(out=ysc[:].rearrange("p (hp h2 i) -> p h2 hp i", h2=2, i=64),
                                             in_=pp[:, :].rearrange("p (h2 hp i) -> p h2 hp i", hp=8, i=64)), reads=[b0, b1], writes=[ysc])
                k.dma("sp", ysc_d[d, cs(i * 128, 128), :], ysc[:], reads=[ysc], writes=[R_ysc(d, i)], sembuf=ysc)
                if DBG < 18:
                    continue
                pp, b0, b1 = pair()

                def f(e):
                    for ct in range(8):
                        e.matmul(pp[:, cs(ct * 128, 128)], lhsT=Bg[:, cs(ct * 128, 128)],
                                 rhs=U[:].rearrange("p (a b c) -> p a b c", a=2, b=8)[:, :, ct, :], start=True, stop=False)
                        ins = e.matmul(pp[:, cs(ct * 128, 128)], lhsT=Kg[:, cs(ct * 128, 128)], rhs=z[:, cs(2 * D + ct * 128, 128)],
                                       start=False, stop=True)
                    return ins
                k.op("pe", f, reads=[Bg, Kg, U, z], writes=[b0, b1])
                k.op("dve", lambda e: e.tensor_tensor(out=H[:], in0=H[:], in1=gC[:].unsqueeze(2).to_broadcast([128, 8, 64]),
                                                      op=ALU.mult), reads=[H, gC], writes=[H])
                ppv = pp[:, :].rearrange("p (a b) -> p a b", b=128)
                k.op("dve", lambda e: e.tensor_tensor(out=H[0:64, :, :], in0=H[0:64, :, :], in1=ppv[0:64, :, 0:64], op=ALU.add),
                     reads=[H, b0, b1], writes=[H])
                k.op("dve", lambda e: e.tensor_tensor(out=H[64:128, :, :], in0=H[64:128, :, :], in1=ppv[64:128, :, 64:128], op=ALU.add),
                     reads=[H, b0, b1], writes=[H])
                if n % 2 == 1:
                    grp = i // 2
                    k.dma("sp", st_d[l, d, grp], H[:].rearrange("p a b -> p (a b)"), reads=[H], writes=[R_st(l, d, grp)], sembuf=H)
                    k.op("dve", lambda e: e.tensor_scalar(out=H[:], in0=H[:], scalar1=cmask[:, 0:1], scalar2=None, op0=ALU.mult),
                         reads=[H, cmask], writes=[H])
                k.op("act", lambda e: e.copy(out=Hb[:], in_=H[:]), reads=[H], writes=[Hb])
            k.barrier()


    def phaseS2(l):
        with contextlib.ExitStack() as es:
            kkc = sbt(es, "kkc", [128, D], F32)
            kac = sbt(es, "kac", [128, D], F32)
            rkc = sbt(es, "rkc", [128, D], F32)
            bc_load(kkc, kk_d[l])
            bc_load(kac, ka_d[l])
            bc_load(rkc, rk_d[l])

            def dir_gen(d):
                sfx = "_%d" % d
                wup = sbt(es, "wup" + sfx, [65, D], BF16)
                aup = sbt(es, "aup" + sfx, [65, D], BF16)
                maskM = sbt(es, "maskM" + sfx, [128, 4, 128], BF16)
                maskN = sbt(es, "maskN" + sfx, [128, 4, 128], BF16)
                H = sbt(es, "H" + sfx, [128, 8, 64], F32)
                Hb = sbt(es, "Hb" + sfx, [128, 8, 64], BF16)
                z = sbt(es, "zr" + sfx, [128, CR], BF16)
                ldT = sbt(es, "ldT" + sfx, [65, 2, 128], BF16)
                tw = sbt(es, "tw" + sfx, [128, 128], BF16)
                SG = sbt(es, "SG" + sfx, [128, D], F32)
                A = sbt(es, "A" + sfx, [128, D], F32)
                KX = sbt(es, "KX" + sfx, [128, D], F32)
                BP = sbt(es, "BP" + sfx, [128, D], F32)
                KD = sbt(es, "KD" + sfx, [128, D], F32)
                S0 = sbt(es, "S0" + sfx, [128, D], F32)
                S1 = A
                ysc = S0
                st16 = sbt(es, "st16" + sfx, [128, 16], F32)
                rs16 = sbt(es, "rs16" + sfx, [128, 16], F32)
                bon = sbt(es, "bon" + sfx, [128, 16], F32)
                gC = sbt(es, "gC" + sfx, [128, 8], F32)
                TM = sbt(es, "TM" + sfx, [128, 4, D], BF16)
                FM = sbt(es, "FM" + sfx, [128, 8, 4, 128], BF16)
                MB = sbt(es, "MB" + sfx, [128, 16, 512], BF16)
                Pm = [sbt(es, "Pm%d" % i + sfx, [128, 16, 128], BF16) for i in range(2)]
                PT = [sbt(es, "PT%d" % i + sfx, [128, 16, 128], BF16) for i in range(2)]
                Xb = [sbt(es, "Xb%d" % i + sfx, [128, D], BF16) for i in range(2)]

                k.dma("pool", wup[:], wup_d[l, d], writes=[wup], max_dma_last_dim=4096)
                k.dma("pool", aup[:], aup_d[l, d], writes=[aup], max_dma_last_dim=4096)
                strict_i, incl_i, nmask_i = (2, 0, 3) if d == 0 else (3, 1, 2)
                for j, src in enumerate((strict_i, incl_i, strict_i, incl_i)):
                    k.op("dve", lambda e: e.tensor_copy(out=maskM[:, j, :], in_=tri4[:, src, :]), reads=[tri4], writes=[maskM])
                for j in range(4):
                    k.op("dve", lambda e: e.tensor_copy(out=maskN[:, j, :], in_=tri4[:, nmask_i, :]), reads=[tri4], writes=[maskN])
                tri_incl = tri4[:, incl_i, :]
                tri_excl = tri4[:, strict_i, :]
                tri_dg = tri4[:, nmask_i, :]
                k.op("dve", lambda e: e.memset(ldT[:], 1.0), writes=[ldT])
                k.dma("sp", H[:].rearrange("p a b -> p (a b)"), state0_d[l, d], writes=[H])
                k.op("act", lambda e: e.copy(out=Hb[:], in_=H[:]), reads=[H], writes=[Hb])
                order = list(range(NT)) if d == 0 else list(range(NT - 1, -1, -1))
                yield

                for n, i in enumerate(order):
                    k.dma("sp", z[:], zr_d[cs(i * 128, 128), :], reads=[R_zr(i)], writes=[z])
                    rq = z[:, 0:D]
                    kq = z[:, D:2 * D]
                    k.op("act", lambda e: e.activation(out=tw[:, 0:64], in_=z[:, cs(3 * D + 64 * d, 64)], func=AF.Tanh),
                         reads=[z], writes=[tw])
                    k.op("pool", lambda e: e.tensor_copy(out=tw[:, 64:128], in_=z[:, cs(3 * D + 128 + 64 * d, 64)]),
                         reads=[z], writes=[tw])
                    pb = bank()
                    pv = pb[:].bitcast(BF16)

                    def f(e):
                        e.transpose(out=pv[0:64, 0:128], in_=tw[:, 0:64], identity=ident_b[:])
                        return e.transpose(out=pv[0:64, 128:256], in_=tw[:, 64:128], identity=ident_b[:])
                    k.op("pe", f, reads=[tw, ident_b], writes=[pb])
                    k.op("act", lambda e: e.copy(out=ldT[0:64, :, :].rearrange("p a b -> p (a b)"), in_=pv[0:64, 0:256]),
                         reads=[pb], writes=[ldT])
                    for (wmat, src_j, dst) in ((wup, 0, SG), (aup, 1, A)):
                        pp, b0, b1 = pair()

                        def f(e):
                            e.matmul(pp[:, 0:512], lhsT=ldT[:, src_j, :], rhs=wmat[:, 0:512], start=True, stop=True)
                            return e.matmul(pp[:, 512:1024], lhsT=ldT[:, src_j, :], rhs=wmat[:, 512:1024], start=True, stop=True)
                        k.op("pe", f, reads=[ldT, wmat], writes=[b0, b1])
                        k.op("act", lambda e: e.activation(out=dst[:], in_=pp[:, :], func=AF.Sigmoid), reads=[b0, b1], writes=[dst])
                    k.op("pool", lambda e: e.tensor_tensor(out=KX[:], in0=kq, in1=kkc[:], op=ALU.mult), reads=[z, kkc], writes=[KX])
                    k.op("pool", lambda e: e.tensor_tensor(out=S0[:], in0=KX[:], in1=KX[:], op=ALU.mult), reads=[KX], writes=[S0])
                    k.op("dve", lambda e: e.tensor_reduce(out=st16[:], in_=S0[:].rearrange("p (h n) -> p h n", n=64), axis=AX.X,
                                                          op=ALU.add), reads=[S0], writes=[st16])
                    rstd_from(st16, rs16, 1.0, 1e-12)
                    k.op("dve", lambda e: e.tensor_tensor(out=KX[:].rearrange("p (h n) -> p h n", n=64),
                                                          in0=KX[:].rearrange("p (h n) -> p h n", n=64),
                                                          in1=rs16[:].unsqueeze(2).to_broadcast([128, 16, 64]), op=ALU.mult),
                         reads=[KX, rs16], writes=[KX])
                    yield
                    k.op("dve", lambda e: e.scalar_tensor_tensor(out=BP[:], in0=KX[:], scalar=-1.0, in1=A[:], op0=ALU.mult,
                                                                 op1=ALU.mult), reads=[KX, A], writes=[BP])
                    k.op("pool", lambda e: e.tensor_tensor(out=S0[:], in0=kq, in1=kac[:], op=ALU.mult), reads=[z, kac], writes=[S0])
                    k.op("dve", lambda e: e.scalar_tensor_tensor(out=S0[:], in0=A[:], scalar=-1.0, in1=S0[:], op0=ALU.add,
                                                                 op1=ALU.mult), reads=[A, S0], writes=[S0])
                    k.op("pool", lambda e: e.tensor_tensor(out=KD[:], in0=S0[:], in1=kq, op=ALU.add), reads=[S0, z], writes=[KD])
                    k.op("pool", lambda e: e.tensor_tensor(out=S0[:], in0=KD[:], in1=rkc[:], op=ALU.mult), reads=[KD, rkc], writes=[S0])
                    k.op("pool", lambda e: e.tensor_tensor(out=S0[:], in0=S0[:], in1=rq, op=ALU.mult), reads=[S0, z], writes=[S0])
                    k.op("dve", lambda e: e.tensor_reduce(out=bon[:], in_=S0[:].rearrange("p (h n) -> p h n", n=64), axis=AX.X,
                                                          op=ALU.add), reads=[S0], writes=[bon])
                    k.dma("sp", bon_d[d, cs(i * 128, 128), :], bon[:], reads=[bon], writes=[R_bon(d, i)], sembuf=bon)
                    yield

                    def cum(tri_ap, scale, dstE):
                        pp, b0, b1 = pair()

                        def f(e):
                            e.matmul(pp[:, 0:512], lhsT=tri_ap, rhs=SG[:, 0:512], start=True, stop=True)
                            return e.matmul(pp[:, 512:1024], lhsT=tri_ap, rhs=SG[:, 512:1024], start=True, stop=True)
                        k.op("pe", f, reads=[tri4, SG], writes=[b0, b1])
                        k.op("act", lambda e: e.activation(out=dstE[:], in_=pp[:, :], func=AF.Exp, scale=scale),
                             reads=[b0, b1], writes=[dstE])
                        return pp, b0, b1
                    pp, b0, b1 = cum(tri_incl, -DSC, S0)
                    k.op("dve", lambda e: e.tensor_tensor(out=TM[:, 1, :], in0=rq, in1=S0[:], op=ALU.mult), reads=[z, S0], writes=[TM])
                    k.op("act", lambda e: e.activation(out=S1[:], in_=pp[:, :], func=AF.Exp, scale=DSC), reads=[b0, b1], writes=[S1])
                    k.op("dve", lambda e: e.tensor_tensor(out=TM[:, 2, :], in0=BP[:], in1=S1[:], op=ALU.mult), reads=[BP, S1], writes=[TM])
                    k.op("pool", lambda e: e.tensor_tensor(out=TM[:, 3, :], in0=KD[:], in1=S1[:], op=ALU.mult), reads=[KD, S1], writes=[TM])
                    cum(tri_excl, -DSC, S0)
                    k.op("dve", lambda e: e.tensor_tensor(out=TM[:, 0, :], in0=KX[:], in1=S0[:], op=ALU.mult), reads=[KX, S0], writes=[TM])
                    pbg = bank()

                    def f(e):
                        for ct in range(8):
                            ins = e.matmul(pbg[:, ct:ct + 1], lhsT=SG[:, cs(ct * 128, 128)], rhs=ones_f[:], start=True, stop=True)
                        return ins
                    k.op("pe", f, reads=[SG, ones_f], writes=[pbg])
                    k.op("act", lambda e: e.activation(out=gC[:], in_=pbg[:, 0:8], func=AF.Exp, scale=-DSC), reads=[pbg], writes=[gC])
                    yield
                    for g4 in range(4):
                        pb = bank()
                        pv = pb[:].bitcast(BF16)

                        def f(e):
                            for c2 in range(2):
                                ct = g4 * 2 + c2
                                for q in range(4):
                                    ins = e.transpose(out=pv[:, cs((c2 * 4 + q) * 128, 128)], in_=TM[:, q, cs(ct * 128, 128)],
                                                      identity=ident_b[:])
                            return ins
                        k.op("pe", f, reads=[TM, ident_b], writes=[pb])
                        if g4 % 2 == 0:
                            k.op("act", lambda e: e.copy(out=FM[:, g4 * 2:g4 * 2 + 2, :, :].rearrange("p a b c -> p (a b c)"),
                                                         in_=pv[:, :]), reads=[pb], writes=[FM])
                        else:
                            k.op("dve", lambda e: e.tensor_copy(out=FM[:, g4 * 2:g4 * 2 + 2, :, :].rearrange("p a b c -> p (a b c)"),
                                                                in_=pv[:, :]), reads=[pb], writes=[FM])
                    cum(tri_dg, -DSC, S1)
                    k.op("dve", lambda e: e.tensor_tensor(out=TM[:, 2, :], in0=BP[:], in1=S1[:], op=ALU.mult), reads=[BP, S1], writes=[TM])
                    k.op("pool", lambda e: e.tensor_tensor(out=TM[:, 3, :], in0=KD[:], in1=S1[:], op=ALU.mult), reads=[KD, S1], writes=[TM])
                    yield
                    P0 = Pm[0]
                    for h in range(NHEAD):
                        ct, p0 = h // 2, (h % 2) * 64
                        pb = bank()

                        def f(e):
                            e.matmul(pb[:, 0:256], lhsT=FM[p0:p0 + 64, ct, 2, :],
                                     rhs=FM[p0:p0 + 64, ct, 0:2, :].rearrange("p a b -> p (a b)"), start=True, stop=True)
                            return e.matmul(pb[:, 256:512], lhsT=FM[p0:p0 + 64, ct, 3, :],
                                            rhs=FM[p0:p0 + 64, ct, 0:2, :].rearrange("p a b -> p (a b)"), start=True, stop=True)
                        k.op("pe", f, reads=[FM], writes=[pb])
                        k.op("dve", lambda e: e.tensor_tensor(out=MB[:, h, :], in0=pb[:], in1=maskM[:].rearrange("p a b -> p (a b)"),
                                                              op=ALU.mult), reads=[pb, maskM], writes=[MB])
                        if h % 4 == 3:
                            yield
                    for g in range(4):
                        pb = bank()

                        def f(e):
                            for hh in range(4):
                                h = (g % 2) + 2 * (4 * (g // 2) + hh)
                                ct, p0 = h // 2, (h % 2) * 64
                                ins = e.matmul(pb[:, cs(hh * 128, 128)], lhsT=FM[p0:p0 + 64, ct, 0, :], rhs=FM[p0:p0 + 64, ct, 2, :],
                                               start=True, stop=True)
                            return ins
                        k.op("pe", f, reads=[FM], writes=[pb])
                        h0 = (g % 2) + 8 * (g // 2)
                        k.op("dve", lambda e: e.tensor_tensor(out=P0[:, h0:h0 + 7:2, :],
                                                              in0=pb[:].rearrange("p (a b) -> p a b", b=128), in1=maskN[:], op=ALU.mult),
                             reads=[pb, maskN], writes=[P0])
                    yield
                    pp, b0, b1 = pair()

                    def f(e):
                        for h in range(NHEAD):
                            ct, p0 = h // 2, (h % 2) * 64
                            e.matmul(pp[:, hc(h)], lhsT=FM[p0:p0 + 64, ct, 0, :], rhs=Hb[p0:p0 + 64, ct, :],
                                     start=True, stop=False)
                            ins = e.matmul(pp[:, hc(h)], lhsT=MB[:, h, 256:384], rhs=z[:, cs(2 * D + h * 64, 64)],
                                           start=False, stop=True)
                        return ins
                    k.op("pe", f, reads=[FM, Hb, MB, z], writes=[b0, b1])
                    xc = Xb[0]
                    k.op("act", lambda e: e.copy(out=xc[:], in_=pp[:, :]), reads=[b0, b1], writes=[xc])
                    yield
                    cur = 0
                    for lev in range(7):
                        Pc = Pm[cur]
                        pp, b0, b1 = pair()
                        xc, xn = Xb[lev % 2], Xb[(lev + 1) % 2]
                        if lev == 0:
                            ptv = lambda h: MB[:, h, 0:128]
                            ptb = MB
                        else:
                            ptv = lambda h, PTc=PT[cur]: PTc[:, h, :]
                            ptb = PT[cur]

                        def f(e):
                            for h in range(NHEAD):
                                ins = e.matmul(pp[:, hc(h)], lhsT=ptv(h), rhs=xc[:, hc(h)], start=True, stop=True)
                            return ins
                        k.op("pe", f, reads=[ptb, xc], writes=[b0, b1])
                        k.op("dve", lambda e: e.tensor_tensor(out=xn[:], in0=pp[:, :], in1=xc[:], op=ALU.add),
                             reads=[b0, b1, xc], writes=[xn])
                        if lev < 6:
                            Pn, PTn = Pm[1 - cur], PT[1 - cur]
                            for g in range(4):
                                for which in range(2):
                                    if which == 0 and lev == 5:
                                        continue
                                    pb = bank()

                                    def f(e):
                                        for hh in range(4):
                                            h = g * 4 + hh
                                            if which == 0:
                                                ins = e.matmul(pb[:, cs(hh * 128, 128)], lhsT=ptv(h), rhs=Pc[:, h, :], start=True, stop=True)
                                            else:
                                                ins = e.matmul(pb[:, cs(hh * 128, 128)], lhsT=Pc[:, h, :], rhs=ptv(h), start=True, stop=True)
                                        return ins
                                    k.op("pe", f, reads=[Pc, ptb], writes=[pb])
                                    dstb = Pn if which == 0 else PTn
                                    if which == 0 or g % 2 == 0:
                                        k.op("act", lambda e: e.copy(out=dstb[:, g * 4:g * 4 + 4, :].rearrange("p a b -> p (a b)"),
                                                                     in_=pb[:]), reads=[pb], writes=[dstb])
                                    else:
                                        k.op("dve", lambda e: e.tensor_copy(out=dstb[:, g * 4:g * 4 + 4, :].rearrange("p a b -> p (a b)"),
                                                                            in_=pb[:]), reads=[pb], writes=[dstb])
                                if g % 2 == 1:
                                    yield
                            cur = 1 - cur
                    U = Xb[1]
                    pp, b0, b1 = pair()

                    def f(e):
                        for h in range(NHEAD):
                            ct, p0 = h // 2, (h % 2) * 64
                            e.matmul(pp[:, hc(h)], lhsT=FM[p0:p0 + 64, ct, 1, :], rhs=Hb[p0:p0 + 64, ct, :], start=True, stop=False)
                            e.matmul(pp[:, hc(h)], lhsT=MB[:, h, 128:256], rhs=U[:, hc(h)], start=False, stop=False)
                            ins = e.matmul(pp[:, hc(h)], lhsT=MB[:, h, 384:512], rhs=z[:, cs(2 * D + h * 64, 64)],
                                           start=False, stop=True)
                        return ins
                    k.op("pe", f, reads=[FM, Hb, MB, U, z], writes=[b0, b1])
                    k.op("act", lambda e: e.copy(out=ysc[:].rearrange("p (hp h2 i) -> p h2 hp i", h2=2, i=64),
                                                 in_=pp[:, :].rearrange("p (h2 hp i) -> p h2 hp i", hp=8, i=64)), reads=[b0, b1], writes=[ysc])
                    k.dma("sp", ysc_d[d, cs(i * 128, 128), :], ysc[:], reads=[ysc], writes=[R_ysc(d, i)], sembuf=ysc)
                    pp, b0, b1 = pair()

                    def f(e):
                        for ct in range(8):
                            e.matmul(pp[:, cs(ct * 128, 128)], lhsT=TM[:, 2, cs(ct * 128, 128)],
                                     rhs=U[:].rearrange("p (a b c) -> p a b c", a=2, b=8)[:, :, ct, :], start=True, stop=False)
                            ins = e.matmul(pp[:, cs(ct * 128, 128)], lhsT=TM[:, 3, cs(ct * 128, 128)], rhs=z[:, cs(2 * D + ct * 128, 128)],
                                           start=False, stop=True)
                        return ins
                    k.op("pe", f, reads=[TM, U, z], writes=[b0, b1])
                    k.op("dve", lambda e: e.tensor_tensor(out=H[:], in0=H[:], in1=gC[:].unsqueeze(2).to_broadcast([128, 8, 64]),
                                                          op=ALU.mult), reads=[H, gC], writes=[H])
                    ppv = pp[:, :].rearrange("p (a b) -> p a b", b=128)
                    k.op("dve", lambda e: e.tensor_tensor(out=H[0:64, :, :], in0=H[0:64, :, :], in1=ppv[0:64, :, 0:64], op=ALU.add),
                         reads=[H, b0, b1], writes=[H])
                    k.op("dve", lambda e: e.tensor_tensor(out=H[64:128, :, :], in0=H[64:128, :, :], in1=ppv[64:128, :, 64:128], op=ALU.add),
                         reads=[H, b0, b1], writes=[H])
                    if n % 2 == 1:
                        grp = i // 2
                        k.dma("sp", st_d[l, d, grp], H[:].rearrange("p a b -> p (a b)"), reads=[H], writes=[R_st(l, d, grp)], sembuf=H)
                        k.op("dve", lambda e: e.tensor_scalar(out=H[:], in0=H[:], scalar1=cmask[:, 0:1], scalar2=None, op0=ALU.mult),
                             reads=[H, cmask], writes=[H])
                    k.op("act", lambda e: e.copy(out=Hb[:], in_=H[:]), reads=[H], writes=[Hb])
                    yield

            gens = [dir_gen(0), dir_gen(1)]
            next(gens[0])
            next(gens[1])
            for _ in range(6):
                next(gens[0])
            live = list(gens)
            while live:
                for g in list(live):
                    try:
                        next(g)
                    except StopIteration:
                        live.remove(g)
            k.barrier()

    def phaseM(l, xsrc_d, R_xsrc):
        with contextlib.ExitStack() as es:
            wa = sbt(es, "wa", [128, 8, D], BF16)
            wb = sbt(es, "wb", [128, 8, D], BF16)
            wo = sbt(es, "wo", [128, 8, D], BF16)
            gup = sbt(es, "gup", [128, D], BF16)
            wsT = sbt(es, "wsT", [128, 8, 128], BF16)
            bsT = sbt(es, "bsT", [128, 8], F32)
            lnxg = sbt(es, "lnxg", [128, D], F32)
            lnxb = sbt(es, "lnxb", [128, D], F32)
            lnvg = sbt(es, "lnvg", [128, D], F32)
            gate1 = sbt(es, "gate1", [128, D], F32)
            def mkset(pi):
                B = {}
                B['zv'] = sbt(es, "p%d_" % pi + "zv", [128, D + 128], BF16)
                B['zrs'] = sbt(es, "p%d_" % pi + "zrs", [128, 4096], BF16)
                B['yf'] = sbt(es, "p%d_" % pi + "yf", [128, D], F32)
                B['yb'] = sbt(es, "p%d_" % pi + "yb", [128, D], F32)
                B['b0t'] = sbt(es, "p%d_" % pi + "b0t", [128, 16], F32)
                B['b1t'] = sbt(es, "p%d_" % pi + "b1t", [128, 16], F32)
                B['xt'] = sbt(es, "p%d_" % pi + "xtm", [128, D], F32)
                B['W0'] = sbt(es, "p%d_" % pi + "W0", [128, D], F32)
                B['W1'] = sbt(es, "p%d_" % pi + "W1", [128, D], F32)
                B['W2'] = sbt(es, "p%d_" % pi + "W2", [128, D], F32)
                B['s16a'] = sbt(es, "p%d_" % pi + "s16a", [128, 16], F32)
                B['s16b'] = sbt(es, "p%d_" % pi + "s16b", [128, 16], F32)
                B['s16c'] = sbt(es, "p%d_" % pi + "s16c", [128, 16], F32)
                B['bnst'] = sbt(es, "p%d_" % pi + "bnst", [128, 2, 6], F32)
                B['mv'] = sbt(es, "p%d_" % pi + "mv", [128, 2], F32)
                B['rsv'] = sbt(es, "p%d_" % pi + "rsv", [128, 1], F32)
                B['gsb'] = sbt(es, "p%d_" % pi + "gsb", [128, 128], BF16)
                B['gT'] = sbt(es, "p%d_" % pi + "gT", [128, 1, 128], BF16)
                B['actb'] = sbt(es, "p%d_" % pi + "actb", [128, D], BF16)
                B['actT'] = sbt(es, "p%d_" % pi + "actT", [128, 8, 128], BF16)
                B['ub'] = sbt(es, "p%d_" % pi + "ub", [128, D], BF16)
                B['vcb'] = sbt(es, "p%d_" % pi + "vcb", [128, D], BF16)
                return B
            sets = [mkset(0), mkset(1)]
            cast_load_rows(lambda kc: wa[:, kc, :], wa_d[l], 8, D, wa)
            cast_load_rows(lambda kc: wb[:, kc, :], wb_d[l], 8, D, wb)
            cast_load_rows(lambda kc: wo[:, kc, :], wo_d[l], 8, D, wo)
            k.dma("pool", gup[:], gup_d[l], writes=[gup], max_dma_last_dim=4096)
            k.dma("pool", wsT[:].rearrange("p a b -> p (a b)"), wsT_d[l], writes=[wsT], max_dma_last_dim=4096)
            k.dma("sp", bsT[:], bsT_d[l], writes=[bsT])
            bc_load(lnxg, lnxg_d[l])
            bc_load(lnxb, lnxb_d[l])
            bc_load(lnvg, lnvg_d[l])
            load_mod(gate1, l, 2)

            def v3(b):
                return b[:].rearrange("p (h n) -> p h n", n=64)

            def bc16(b):
                return b[:].unsqueeze(2).to_broadcast([128, 16, 64])

            def proj(src_bf, wmat, actT):
                transpose8(src_bf, actT)
                pp, p0, p1 = pair()

                def f(e):
                    for nn in range(2):
                        for kc in range(8):
                            ins = e.matmul(pp[:, cs(nn * 512, 512)], lhsT=actT[:, kc, :], rhs=wmat[:, kc, cs(nn * 512, 512)],
                                           start=(kc == 0), stop=(kc == 7))
                    return ins
                k.op("pe", f, reads=[actT, wmat], writes=[p0, p1])
                return pp, p0, p1

            def tile_gen(i, B):
                zv = B['zv']
                zrs = B['zrs']
                yf = B['yf']
                yb = B['yb']
                b0t = B['b0t']
                b1t = B['b1t']
                xt = B['xt']
                W0 = B['W0']
                W1 = B['W1']
                W2 = B['W2']
                s16a = B['s16a']
                s16b = B['s16b']
                s16c = B['s16c']
                bnst = B['bnst']
                mv = B['mv']
                rsv = B['rsv']
                gsb = B['gsb']
                gT = B['gT']
                actb = B['actb']
                actT = B['actT']
                ub = B['ub']
                vcb = B['vcb']
                rows = cs(i * 128, 128)
                k.dma("sp", zv[:, 0:D], zr_d[rows, 2 * D:3 * D], reads=[R_zr(i)], writes=[zv])
                k.dma("sp", zv[:, D:D + 128], zr_d[rows, cs(3 * D + 256, 128)], reads=[R_zr(i)], writes=[zv])
                k.dma("sp", zrs[:], zrest_d[rows, :], reads=[R_zrest(i, j) for j in range(8)], writes=[zrs])
                k.dma("sp", yf[:], ysc_d[0, rows, :], reads=[R_ysc(0, i)], writes=[yf])
                k.dma("sp", yb[:], ysc_d[1, rows, :], reads=[R_ysc(1, i)], writes=[yb])
                k.dma("sp", b0t[:], bon_d[0, rows, :], reads=[R_bon(0, i)], writes=[b0t])
                k.dma("sp", b1t[:], bon_d[1, rows, :], reads=[R_bon(1, i)], writes=[b1t])
                k.dma("sp", xt[:], xsrc_d[rows, :], reads=[R_xsrc(i)], writes=[xt])
                yield
                k.op("dve", lambda e: e.tensor_tensor(out=yf[:], in0=yf[:], in1=yb[:], op=ALU.add), reads=[yf, yb], writes=[yf])
                k.op("dve", lambda e: e.tensor_reduce(out=s16a[:], in_=v3(yf), axis=AX.X, op=ALU.add), reads=[yf], writes=[s16a])
                k.op("dve", lambda e: e.tensor_scalar(out=s16a[:], in0=s16a[:], scalar1=1.0 / 64, scalar2=None, op0=ALU.mult),
                     reads=[s16a], writes=[s16a])
                k.op("dve", lambda e: e.tensor_tensor(out=v3(yf), in0=v3(yf), in1=bc16(s16a), op=ALU.subtract), reads=[yf, s16a], writes=[yf])
                k.op("dve", lambda e: e.tensor_tensor(out=W0[:], in0=yf[:], in1=yf[:], op=ALU.mult), reads=[yf], writes=[W0])
                k.op("dve", lambda e: e.tensor_reduce(out=s16b[:], in_=v3(W0), axis=AX.X, op=ALU.add), reads=[W0], writes=[s16b])
                rstd_from(s16b, s16c, 1.0 / 64, GN_EPS)
                k.op("dve", lambda e: e.tensor_tensor(out=v3(yf), in0=v3(yf), in1=bc16(s16c), op=ALU.mult), reads=[yf, s16c], writes=[yf])
                k.op("dve", lambda e: e.tensor_tensor(out=yf[:], in0=yf[:], in1=lnxg[:], op=ALU.mult), reads=[yf, lnxg], writes=[yf])
                k.op("dve", lambda e: e.tensor_tensor(out=yf[:], in0=yf[:], in1=lnxb[:], op=ALU.add), reads=[yf, lnxb], writes=[yf])
                yield
                k.op("dve", lambda e: e.tensor_tensor(out=b0t[:], in0=b0t[:], in1=b1t[:], op=ALU.add), reads=[b0t, b1t], writes=[b0t])
                k.op("dve", lambda e: e.tensor_tensor(out=v3(W0), in0=zv[:, 0:D].rearrange("p (h n) -> p h n", n=64), in1=bc16(b0t),
                                                      op=ALU.mult), reads=[zv, b0t], writes=[W0])
                k.op("dve", lambda e: e.tensor_tensor(out=yf[:], in0=yf[:], in1=W0[:], op=ALU.add), reads=[yf, W0], writes=[yf])
                k.op("act", lambda e: e.activation(out=gsb[:], in_=zv[:, D:D + 128], func=AF.Sigmoid), reads=[zv], writes=[gsb])
                transpose8(gsb, gT, nblk=1)
                pp, p0, p1 = pair()

                def f(e):
                    e.matmul(pp[:, 0:512], lhsT=gT[:, 0, :], rhs=gup[:, 0:512], start=True, stop=True)
                    return e.matmul(pp[:, 512:1024], lhsT=gT[:, 0, :], rhs=gup[:, 512:1024], start=True, stop=True)
                k.op("pe", f, reads=[gT, gup], writes=[p0, p1])
                k.op("dve", lambda e: e.tensor_tensor(out=actb[:], in0=pp[:, :], in1=yf[:], op=ALU.mult), reads=[p0, p1, yf], writes=[actb])
                yield
                pp, p0, p1 = proj(actb, wa, actT)
                yield
                k.op("act", lambda e: e.activation(out=W0[:], in_=zrs[:, 2048:3072], func=AF.Sigmoid), reads=[zrs], writes=[W0])
                k.op("dve", lambda e: e.tensor_tensor(out=W2[:], in0=pp[:, :], in1=W0[:], op=ALU.mult), reads=[p0, p1, W0], writes=[W2])
                yield
                k.op("act", lambda e: e.activation(out=ub[:], in_=zrs[:, 0:1024], func=AF.Gelu_apprx_tanh), reads=[zrs], writes=[ub])
                k.op("act", lambda e: e.activation(out=W1[:], in_=zrs[:, 1024:2048], func=AF.Gelu_apprx_tanh), reads=[zrs], writes=[W1])
                for c in range(2):
                    k.op("dve", lambda e: e.bn_stats(out=bnst[:, c, :], in_=W1[:, cs(c * 512, 512)]), reads=[W1], writes=[bnst])
                k.op("dve", lambda e: e.bn_aggr(out=mv[:], in_=bnst[:].rearrange("p a b -> p (a b)")), reads=[bnst], writes=[mv])
                rstd_from_ap(mv, 1, rsv, EPS)
                k.op("dve", lambda e: e.tensor_scalar(out=W1[:], in0=W1[:], scalar1=mv[:, 0:1], scalar2=rsv[:, 0:1], op0=ALU.subtract,
                                                      op1=ALU.mult), reads=[W1, mv, rsv], writes=[W1])
                k.op("dve", lambda e: e.tensor_tensor(out=vcb[:], in0=W1[:], in1=lnvg[:], op=ALU.mult), reads=[W1, lnvg], writes=[vcb])
                yield
                pp, p0, p1 = pair()

                def f(e):
                    for g in range(8):
                        ins = e.matmul(pp[:, cs(g * 128, 128)], lhsT=wsT[:, g, :], rhs=vcb[:, cs(g * 128, 128)], start=True, stop=True)
                    return ins
                k.op("pe", f, reads=[wsT, vcb], writes=[p0, p1])
                k.op("dve", lambda e: e.tensor_tensor(out=W1[:].rearrange("p (g c) -> p g c", c=128),
                                                      in0=pp[:, :].rearrange("p (g c) -> p g c", c=128),
                                                      in1=bsT[:].unsqueeze(2).to_broadcast([128, 8, 128]), op=ALU.add),
                     reads=[p0, p1, bsT], writes=[W1])
                k.op("dve", lambda e: e.tensor_tensor(out=actb[:], in0=W1[:], in1=ub[:], op=ALU.mult), reads=[W1, ub], writes=[actb])
                yield
                pp, p0, p1 = proj(actb, wb, actT)
                yield
                k.op("act", lambda e: e.activation(out=W0[:], in_=zrs[:, 3072:4096], func=AF.Sigmoid), reads=[zrs], writes=[W0])
                k.op("dve", lambda e: e.tensor_tensor(out=W1[:], in0=pp[:, :], in1=W0[:], op=ALU.mult), reads=[p0, p1, W0], writes=[W1])
                k.op("dve", lambda e: e.tensor_tensor(out=actb[:], in0=W1[:], in1=W2[:], op=ALU.add), reads=[W1, W2], writes=[actb])
                yield
                pp, p0, p1 = proj(actb, wo, actT)
                yield
                k.op("dve", lambda e: e.tensor_tensor(out=W0[:], in0=pp[:, :], in1=gate1[:], op=ALU.mult), reads=[p0, p1, gate1], writes=[W0])
                k.op("dve", lambda e: e.tensor_tensor(out=W0[:], in0=W0[:], in1=xt[:], op=ALU.add), reads=[W0, xt], writes=[W0])
                k.dma("pool", x1_d[rows, :], W0[:], reads=[W0], writes=[R_x1(i)], sembuf=W0)
                yield

            pending = list(range(NT))
            live = []
            while pending or live:
                if len(live) < 2 and pending:
                    ti = pending.pop(0)
                    live.append(tile_gen(ti, sets[ti % 2]))
                    if len(live) == 2 and ti == 1:
                        for _ in range(5):
                            next(live[0])
                for g in list(live):
                    try:
                        next(g)
                    except StopIteration:
                        live.remove(g)
            k.barrier()

    def rstd_from_ap(mvb, col, out_rstd, eps):
        k.op("act", lambda e: e.activation(out=out_rstd[:], in_=mvb[:, col:col + 1], func=AF.Ln, bias=eps_t(eps)[:], scale=1.0),
             reads=[mvb, eps_t(eps)], writes=[out_rstd])
        k.op("act", lambda e: e.activation(out=out_rstd[:], in_=out_rstd[:], func=AF.Exp, scale=-0.5),
             reads=[out_rstd], writes=[out_rstd])

    def phaseC(l, last):
        with contextlib.ExitStack() as es:
            w1 = sbt(es, "w1", [128, 8, DFF], BF16)
            w2 = sbt(es, "w2", [128, 32, D], BF16)
            g2 = sbt(es, "g2", [128, D], F32)
            sh2 = sbt(es, "sh2", [128, D], F32)
            gate2 = sbt(es, "gate2", [128, D], F32)
            fg = sbt(es, "fg", [128, D], F32) if last else None
            xt = [sbt(es, "xc%d" % i, [128, D], F32) for i in range(2)]

            def mkset(pi):
                B = {}
                B["W0"] = sbt(es, "Wc0_%d" % pi, [128, D], F32)
                B["hT"] = sbt(es, "hT2_%d" % pi, [128, 8, 128], BF16)
                B["hid"] = sbt(es, "hid_%d" % pi, [128, DFF], BF16)
                B["hidT"] = sbt(es, "hidT_%d" % pi, [128, 32, 128], BF16)
                B["ss"] = sbt(es, "ssc_%d" % pi, [128, 1], F32)
                B["rstd"] = sbt(es, "rstdc_%d" % pi, [128, 1], F32)
                return B
            sets = [mkset(0), mkset(1)]
            cast_load_rows(lambda kc: w1[:, kc, :], w1_d[l], 8, DFF, w1)
            cast_load_rows(lambda kc: w2[:, kc, :], w2_d[l], 32, D, w2)
            load_mod(sh2, l, 3)
            load_mod(g2, l, 4)
            load_mod(gate2, l, 5)
            if last:
                bc_load(fg, fg_d)

            def load_x(i):
                k.dma("sp", xt[i % 2][:], x1_d[cs(i * 128, 128), :], reads=[R_x1(i)], writes=[xt[i % 2]])
            def tile_gen(i, B):
                W0, hT, hid, hidT, ss, rstd = B["W0"], B["hT"], B["hid"], B["hidT"], B["ss"], B["rstd"]
                x = xt[i % 2]
                load_x(i)
                yield
                k.op("act", lambda e: e.activation(out=W0[:], in_=x[:], func=AF.Square), reads=[x], writes=[W0])
                k.op("dve", lambda e: e.tensor_reduce(out=ss[:], in_=W0[:], axis=AX.X, op=ALU.add), reads=[W0], writes=[ss])
                rstd_from(ss, rstd, 1.0 / D, EPS)
                k.op("dve", lambda e: e.scalar_tensor_tensor(out=W0[:], in0=x[:], scalar=rstd[:, 0:1], in1=g2[:], op0=ALU.mult,
                                                             op1=ALU.mult), reads=[x, rstd, g2], writes=[W0])
                k.op("dve", lambda e: e.tensor_tensor(out=hid[:, 0:D], in0=W0[:], in1=sh2[:], op=ALU.add), reads=[W0, sh2], writes=[hid])
                transpose8(hid, hT)
                yield
                for n in range(8):
                    pb = bank()

                    def f(e):
                        for kc in range(8):
                            ins = e.matmul(pb[:], lhsT=hT[:, kc, :], rhs=w1[:, kc, cs(n * 512, 512)], start=(kc == 0), stop=(kc == 7))
                        return ins
                    k.op("pe", f, reads=[hT, w1], writes=[pb])
                    k.op("act", lambda e: e.activation(out=hid[:, cs(n * 512, 512)], in_=pb[:], func=AF.Relu), reads=[pb], writes=[hid])
                    k.op("dve", lambda e: e.tensor_tensor(out=hid[:, cs(n * 512, 512)], in0=hid[:, cs(n * 512, 512)],
                                                          in1=hid[:, cs(n * 512, 512)], op=ALU.mult), reads=[hid], writes=[hid])
                    if n % 4 == 3:
                        yield
                for q in range(4):
                    pb = bank()
                    pv = pb[:].bitcast(BF16)

                    def f(e):
                        for j in range(8):
                            ins = e.transpose(out=pv[:, cs(j * 128, 128)], in_=hid[:, cs((q * 8 + j) * 128, 128)], identity=ident_b[:])
                        return ins
                    k.op("pe", f, reads=[hid, ident_b], writes=[pb])
                    if q % 2 == 0:
                        k.op("act", lambda e: e.copy(out=hidT[:, q * 8:q * 8 + 8, :].rearrange("p a b -> p (a b)"), in_=pv[:, :]),
                             reads=[pb], writes=[hidT])
                    else:
                        k.op("dve", lambda e: e.tensor_copy(out=hidT[:, q * 8:q * 8 + 8, :].rearrange("p a b -> p (a b)"), in_=pv[:, :]),
                             reads=[pb], writes=[hidT])
                yield
                pp, p0, p1 = pair()

                def f(e):
                    for nn in range(2):
                        for kc in range(32):
                            ins = e.matmul(pp[:, cs(nn * 512, 512)], lhsT=hidT[:, kc, :], rhs=w2[:, kc, cs(nn * 512, 512)],
                                           start=(kc == 0), stop=(kc == 31))
                    return ins
                k.op("pe", f, reads=[hidT, w2], writes=[p0, p1])
                k.op("dve", lambda e: e.tensor_tensor(out=W0[:], in0=pp[:, :], in1=gate2[:], op=ALU.mult), reads=[p0, p1, gate2], writes=[W0])
                k.op("dve", lambda e: e.tensor_tensor(out=W0[:], in0=W0[:], in1=x[:], op=ALU.add), reads=[W0, x], writes=[W0])
                rows = cs(i * 128, 128)
                if not last:
                    k.dma("pool", x2_d[rows, :], W0[:], reads=[W0], writes=[R_x2(i)], sembuf=W0)
                else:
                    k.op("act", lambda e: e.activation(out=x[:], in_=W0[:], func=AF.Square), reads=[W0], writes=[x])
                    k.op("dve", lambda e: e.tensor_reduce(out=ss[:], in_=x[:], axis=AX.X, op=ALU.add), reads=[x], writes=[ss])
                    rstd_from(ss, rstd, 1.0 / D, EPS)
                    k.op("dve", lambda e: e.scalar_tensor_tensor(out=W0[:], in0=W0[:], scalar=rstd[:, 0:1], in1=fg[:], op0=ALU.mult,
                                                                 op1=ALU.mult), reads=[W0, rstd, fg], writes=[W0])
                    k.dma("pool", y_d[rows, :], W0[:], reads=[W0], writes=[R_y(i)], sembuf=W0)
                yield

            pending = list(range(NT))
            live = []
            while pending or live:
                if len(live) < 2 and pending:
                    ti = pending.pop(0)
                    live.append(tile_gen(ti, sets[ti % 2]))
                    if len(live) == 2 and ti == 1:
                        for _ in range(3):
                            next(live[0])
                for g in list(live):
                    try:
                        next(g)
                    except StopIteration:
                        live.remove(g)
            k.barrier()

    R_xin = DR("xin")
    steps = [lambda: phaseP(0), lambda: phaseP(1)]
    for l in range(2):
        xs, Rx = (x_d, R_xin) if l == 0 else (x2_d, R_x2)
        steps += [lambda l=l, xs=xs, Rx=Rx: phaseA1(l, xs, Rx), lambda l=l: phaseS2(l),
                  lambda l=l, xs=xs, Rx=Rx: phaseM(l, xs, Rx), lambda l=l: phaseC(l, last=(l == 1))]
    for st_ in steps[:upto]:
        st_()
    k.barrier()
    ges.close()
    return nc, k


def _shift_mats(kind):
    m = np.zeros((4, 3, 128, 128), np.float32)
    eye = np.eye(128, dtype=np.float32)
    t = np.arange(128)
    for cls in range(4):
        cur = np.zeros((128, 128), np.float32)
        nbe = np.zeros((128, 128), np.float32)
        nbo = np.zeros((128, 128), np.float32)
        if kind == "sample":
            if cls == 0:
                for to in t:
                    if to % 64 != 0:
                        cur[to - 1, to] = 1
            elif cls == 1:
                for to in t:
                    if to % 64 != 63:
                        cur[to + 1, to] = 1
            elif cls == 2:
                for to in t:
                    if to >= 64:
                        cur[to - 64, to] = 1
                    else:
                        nbe[to + 64, to] = 1
                        nbo[to + 64, to] = 1
            else:
                for to in t:
                    if to < 64:
                        cur[to + 64, to] = 1
                    else:
                        nbe[to - 64, to] = 1
                        nbo[to - 64, to] = 1
        else:
            if cls in (0, 2):
                for to in t:
                    if to >= 1:
                        cur[to - 1, to] = 1
                nbo[127, 0] = 1
            else:
                for to in t:
                    if to <= 126:
                        cur[to + 1, to] = 1
                nbe[0, 127] = 1
        m[cls, 0] = cur - eye
        m[cls, 1] = nbe
        m[cls, 2] = nbo
    return np.ascontiguousarray(m.reshape(12, 128, 128).transpose(1, 0, 2).reshape(128, 12 * 128))


def _tri4():
    s = np.arange(128)[:, None]
    t = np.arange(128)[None, :]
    m = np.stack([(s <= t), (s >= t), (s < t), (s > t)], axis=1).astype(np.float32)
    return np.ascontiguousarray(m.reshape(128, 512))


def _state_to_H(st):
    a = st.reshape(2, 2, 8, 2, 64, 64)
    a = a.transpose(0, 1, 3, 5, 2, 4)
    return np.ascontiguousarray(a.reshape(2, 2, 128, 512))


def _H_to_state(Hm):
    lead = Hm.shape[:-2]
    a = Hm.reshape(lead + (2, 64, 8, 64))
    nl = len(lead)
    perm = tuple(range(nl)) + (nl + 2, nl + 0, nl + 3, nl + 1)
    a = a.transpose(perm)
    return a.reshape(lead + (16, 64, 64))


def make_core_inputs(kind, x_tokens, cond_vec, state_lh, shared):
    d = dict(shared)
    d["x"] = np.ascontiguousarray(x_tokens, dtype=np.float32)
    d["cond"] = np.ascontiguousarray(cond_vec.reshape(8, 128).T, dtype=np.float32)
    d["state0"] = _state_to_H(state_lh)
    d["cmask"] = np.full((128, 1), 1.0 if kind == "sample" else 0.0, np.float32)
    d["shm"] = _shift_mats(kind)
    return d


def shared_inputs(w_ada, b_ada, norm1_g, norm2_g, w_in, mu_shift, w0, w_up, a0, a_up, g_up, k_k, k_a, r_k, lnx_g,
                  lnx_b, w_branch_a, ln_v_g, w_s, b_s, w_branch_b, w_out, w1, w2, final_g):
    f = lambda a: np.ascontiguousarray(np.asarray(a), dtype=np.float32)
    wup_aug = np.concatenate([np.asarray(w_up), np.asarray(w0)[:, :, None, :]], axis=2)
    aup_aug = np.concatenate([np.asarray(a_up), np.asarray(a0)[:, :, None, :]], axis=2)
    wsT = np.asarray(w_s).transpose(0, 3, 1, 2).reshape(2, 128, 8 * 128)
    bsT = np.asarray(b_s).transpose(0, 2, 1)
    return dict(ident=np.eye(128, dtype=np.float32), tri4=_tri4(), w_ada=f(w_ada), b_ada=f(b_ada), norm1_g=f(norm1_g),
                norm2_g=f(norm2_g), w_in=f(w_in), mu_shift=f(mu_shift), wup_aug=f(wup_aug), aup_aug=f(aup_aug), g_up=f(g_up),
                k_k=f(k_k), k_a=f(k_a), r_k=f(np.asarray(r_k).reshape(2, D)), lnx_g=f(lnx_g), lnx_b=f(lnx_b),
                w_branch_a=f(w_branch_a), ln_v_g=f(ln_v_g), wsT=f(wsT), bsT=f(bsT), w_branch_b=f(w_branch_b), w_out=f(w_out),
                w1=f(w1), w2=f(w2), final_g=f(final_g))


_PROG = {}


def kernel(x_prompt, x_sample, state_rwkv, c, c_ctx, w_ada, b_ada, norm1_g, norm2_g, w_in, mu_shift,
           w0, w_up, a0, a_up, g_up, k_k, k_a, r_k, lnx_g, lnx_b, w_branch_a, ln_v_g, w_s, b_s,
           w_branch_b, w_out, w1, w2, final_g):
    NT = 32
    x_prompt = np.asarray(x_prompt, dtype=np.float32)
    x_sample = np.asarray(x_sample, dtype=np.float32)
    state_rwkv = np.asarray(state_rwkv, dtype=np.float32)
    c = np.asarray(c, dtype=np.float32)
    c_ctx = np.asarray(c_ctx, dtype=np.float32)
    shared = shared_inputs(w_ada, b_ada, norm1_g, norm2_g, w_in, mu_shift, w0, w_up, a0, a_up, g_up, k_k, k_a, r_k,
                           lnx_g, lnx_b, w_branch_a, ln_v_g, w_s, b_s, w_branch_b, w_out, w1, w2, final_g)
    in_maps = []
    for b in range(4):
        in_maps.append(make_core_inputs("sample", x_sample[b], c[b], state_rwkv[b], shared))
    zero_state = np.zeros((2, 2, 16, 64, 64), np.float32)
    for q in range(4):
        xs = np.zeros((NT * 128, D), np.float32)
        xs[:2048] = x_prompt[8 * q:8 * q + 8].reshape(2048, D)
        xs[2048:] = xs[:2048]
        in_maps.append(make_core_inputs("prompt", xs, c_ctx, zero_state, shared))
    if NT not in _PROG:
        _PROG[NT] = build_program(NT)[0]
    res = run_bass_kernel_spmd(_PROG[NT], in_maps, core_ids=list(range(8)))
    r = res.results
    y_sample = np.stack([r[b]["y"] for b in range(4)], axis=0)
    y_prompt = np.concatenate([r[4 + q]["y"][:2048].reshape(8, 256, D) for q in range(4)], axis=0)
    sts = []
    for q in range(4):
        so = r[4 + q]["st_out"]
        so = so[:, :, :8]
        s = _H_to_state(so)
        sts.append(np.transpose(s, (2, 0, 1, 3, 4, 5)))
    new_state = np.ascontiguousarray(np.concatenate(sts, axis=0), dtype=np.float32)
    return (np.ascontiguousarray(y_prompt, dtype=np.float32), np.ascontiguousarray(y_sample, dtype=np.float32), new_state)
```

```python
import contextlib
import os
DBG = int(os.environ.get('KDBG', '99'))
KSKIP = os.environ.get('KSKIP', '')
import numpy as np
import concourse.bass as bass
import concourse.mybir as mybir
from concourse.bass_utils import run_bass_kernel_spmd

F32 = mybir.dt.float32
BF16 = mybir.dt.bfloat16
ALU = mybir.AluOpType
AF = mybir.ActivationFunctionType
AX = mybir.AxisListType

D = 1024
CR = 3456
DIN = 7552
DFF = 4096
NHEAD = 16
EPS = 1e-6
GN_EPS = 64e-5
DSC = float(np.exp(-0.5))


class Buf:
    __slots__ = ("name", "t", "w", "r", "dsem", "dcnt")

    def __init__(self, name, t=None):
        self.name = name
        self.t = t
        self.w = None
        self.r = []
        self.dsem = None
        self.dcnt = 0

    def __getitem__(self, idx):
        return self.t[idx]


class K:
    def __init__(self, nc):
        self.nc = nc
        self.eng = {"pe": nc.tensor, "act": nc.scalar, "dve": nc.vector, "pool": nc.gpsimd, "sp": nc.sync}
        self.sem = {}
        self.cnt = {}
        for e in self.eng:
            self.sem[e] = nc.alloc_semaphore(name="s_" + e)
            self.cnt[e] = 0
        self.waited = {}
        self.dsems = {}
        self.free_dsems = []
        self.ninstr = 0
        self.uid = 0

    def _wait(self, e, tok):
        if tok is None:
            return
        key, val = tok
        if key == e and e == "pe":
            return
        kk = (e, key)
        if self.waited.get(kk, 0) >= val:
            return
        self.waited[kk] = val
        self.eng[e].wait_ge(self.sem[key], val)
        self.ninstr += 1

    def _deps(self, e, reads, writes):
        for b in reads:
            self._wait(e, b.w)
        for b in writes:
            self._wait(e, b.w)
            for tok in b.r:
                self._wait(e, tok)

    def _commit(self, tok, reads, writes):
        for b in reads:
            if b not in writes:
                b.r.append(tok)
                if len(b.r) > 10:
                    best = {}
                    for k_, v_ in b.r:
                        if best.get(k_, -1) < v_:
                            best[k_] = v_
                    b.r = list(best.items())
        for b in writes:
            b.w = tok
            b.r = []

    def op(self, e, fn, reads=(), writes=()):
        reads = [b for b in reads if b is not None]
        writes = [b for b in writes if b is not None]
        self._deps(e, reads, writes)
        ins = fn(self.eng[e])
        self.cnt[e] += 1
        ins.then_inc(self.sem[e], 1)
        self.ninstr += 1
        self._commit((e, self.cnt[e]), reads, writes)

    def dma(self, q, out_ap, in_ap, reads=(), writes=(), sembuf=None, **kw):
        reads = [b for b in reads if b is not None]
        writes = [b for b in writes if b is not None]
        if sembuf is None:
            sembuf = (writes + reads)[0]
        if sembuf.dsem is None:
            if self.free_dsems:
                key, base = self.free_dsems.pop()
                sembuf.dcnt = base
            else:
                key = "d%d" % len(self.sem)
                self.sem[key] = self.nc.alloc_semaphore(name=key)
            self.dsems[key] = sembuf
            sembuf.dsem = key
        self._deps(q, reads, writes)
        ins = self.eng[q].dma_start(out=out_ap, in_=in_ap, **kw)
        sembuf.dcnt += 16
        ins.then_inc(self.sem[sembuf.dsem], 16)
        self.ninstr += 1
        self._commit((sembuf.dsem, sembuf.dcnt), reads, writes)

    def barrier(self):
        toks = [(e, self.cnt[e]) for e in self.eng if self.cnt[e] > 0]
        toks += [(key, b.dcnt) for key, b in self.dsems.items() if b.dcnt > 0]
        for e in self.eng:
            for tok in toks:
                if tok[0] != e:
                    self._wait(e, tok)
        for key, b in list(self.dsems.items()):
            if not getattr(b, "keep", False):
                self.free_dsems.append((key, b.dcnt))
                b.dsem = None
                del self.dsems[key]


def cs(a, n):
    return slice(a, a + n)


def hc(h):
    return slice((h % 2) * 512 + (h // 2) * 64, (h % 2) * 512 + (h // 2) * 64 + 64)


def build_program(NT, upto=99):
    T = NT * 128
    NG = NT // 2
    nc = bass.Bass("TRN2", target_bir_lowering=False)
    k = K(nc)

    def din(name, shape):
        return nc.dram_tensor(name, list(shape), F32, kind="ExternalInput").ap()

    x_d = din("x", [T, D])
    cond_d = din("cond", [128, 8])
    state0_d = din("state0", [2, 2, 128, 512])
    cmask_d = din("cmask", [128, 1])
    shm_d = din("shm", [128, 12 * 128])
    ident_d = din("ident", [128, 128])
    tri4_d = din("tri4", [128, 4 * 128])
    w_ada_d = din("w_ada", [2, D, 6 * D])
    b_ada_d = din("b_ada", [2, 6 * D])
    n1g_d = din("norm1_g", [2, D])
    n2g_d = din("norm2_g", [2, D])
    w_in_d = din("w_in", [2, D, DIN])
    mu_d = din("mu_shift", [2, CR])
    wup_d = din("wup_aug", [2, 2, 65, D])
    aup_d = din("aup_aug", [2, 2, 65, D])
    gup_d = din("g_up", [2, 128, D])
    kk_d = din("k_k", [2, D])
    ka_d = din("k_a", [2, D])
    rk_d = din("r_k", [2, D])
    lnxg_d = din("lnx_g", [2, D])
    lnxb_d = din("lnx_b", [2, D])
    wa_d = din("w_branch_a", [2, D, D])
    lnvg_d = din("ln_v_g", [2, D])
    wsT_d = din("wsT", [2, 128, 8 * 128])
    bsT_d = din("bsT", [2, 128, 8])
    wb_d = din("w_branch_b", [2, D, D])
    wo_d = din("w_out", [2, D, D])
    w1_d = din("w1", [2, D, DFF])
    w2_d = din("w2", [2, DFF, D])
    fg_d = din("final_g", [D])

    y_d = nc.dram_tensor("y", [T, D], F32, kind="ExternalOutput").ap()
    st_d = nc.dram_tensor("st_out", [2, 2, NG, 128, 512], F32, kind="ExternalOutput").ap()

    def dscr(name, shape, dt):
        return nc.dram_tensor(name, list(shape), dt, kind="Internal").ap()

    modbc_d = dscr("modbc", [2, 128, 6 * D], F32)
    zr_d = dscr("zr_s", [T, CR], BF16)
    zrest_d = dscr("zrest_s", [T, 4096], BF16)
    ysc_d = dscr("ysc_s", [2, T, D], F32)
    bon_d = dscr("bon_s", [2, T, 16], F32)
    x1_d = dscr("x1_s", [T, D], F32)
    x2_d = dscr("x2_s", [T, D], F32)

    class DR:
        def __init__(self, nm):
            self.b = {}
            self.nm = nm

        def __call__(self, *key):
            if key not in self.b:
                self.b[key] = Buf(self.nm + str(key))
            return self.b[key]

    R_mod, R_zr, R_zrest, R_ysc, R_bon, R_x1, R_x2, R_y, R_st = [DR(n) for n in
        ("mod", "zr", "zrest", "ysc", "bon", "x1", "x2", "y", "st")]

    PP = [nc.alloc_psum_tensor("psum%d" % i, [128, 1024], F32) for i in range(4)]
    PB = []
    for i in range(8):
        PB.append(Buf("pb%d" % i, PP[i // 2][:, cs((i % 2) * 512, 512)]))
    pst = {"b": 0, "p": 0}

    def bank():
        b = PB[pst["b"] % 8]
        pst["b"] += 1
        return b

    def pair():
        if pst["b"] % 2:
            pst["b"] += 1
        i = (pst["b"] % 8) // 2
        pst["b"] += 2
        return PP[i], PB[2 * i], PB[2 * i + 1]

    def sbt(es, name, shape, dt):
        k.uid += 1
        t = es.enter_context(nc.sbuf_tensor("%s_%d" % (name, k.uid), list(shape), dt))
        return Buf(name, t)

    ges = contextlib.ExitStack()
    ident_f = sbt(ges, "ident_f", [128, 128], F32)
    ident_b = sbt(ges, "ident_b", [128, 128], BF16)
    tri4 = sbt(ges, "tri4", [128, 4, 128], F32)
    ones_f = sbt(ges, "ones_f", [128, 1], F32)
    cmask = sbt(ges, "cmask", [128, 1], F32)
    k.dma("sp", ident_f[:], ident_d, writes=[ident_f])
    k.dma("sp", tri4[:].rearrange("p a b -> p (a b)"), tri4_d, writes=[tri4])
    k.dma("sp", cmask[:], cmask_d, writes=[cmask])
    k.op("dve", lambda e: e.tensor_copy(out=ident_b[:], in_=ident_f[:]), reads=[ident_f], writes=[ident_b])
    k.op("dve", lambda e: e.memset(ones_f[:], 1.0), writes=[ones_f])

    def bc_load(buf, dvec):
        k.dma("sp", buf[:], dvec.partition_broadcast(128), writes=[buf])

    def cast_load_rows(buf_ap_fn, dsrc, nk, ncol, buf):
        for kc in range(nk):
            k.dma("pool", buf_ap_fn(kc), dsrc[cs(kc * 128, 128), :], writes=[buf], max_dma_last_dim=8192)

    def rstd_from(e_ss, out_rstd, scale, eps):
        k.op("act", lambda e: e.activation(out=out_rstd[:], in_=e_ss[:], func=AF.Ln, bias=eps_t(eps)[:], scale=scale),
             reads=[e_ss, eps_t(eps)], writes=[out_rstd])
        k.op("act", lambda e: e.activation(out=out_rstd[:], in_=out_rstd[:], func=AF.Exp, scale=-0.5),
             reads=[out_rstd], writes=[out_rstd])

    eps_tiles = {}

    def eps_t(v):
        if v not in eps_tiles:
            b = sbt(ges, "eps%d" % len(eps_tiles), [128, 1], F32)
            k.op("dve", lambda e: e.memset(b[:], float(v)), writes=[b])
            eps_tiles[v] = b
        return eps_tiles[v]

    for v in (EPS, GN_EPS, 1e-12):
        eps_t(v)

    def phaseP(l):
        with contextlib.ExitStack() as es:
            wad = sbt(es, "wad", [128, 8, 6 * D], BF16)
            ba = sbt(es, "ba", [128, 6 * D], F32)
            mod = sbt(es, "mod", [128, 6 * D], F32)
            n1g = sbt(es, "n1g", [128, D], F32)
            n2g = sbt(es, "n2g", [128, D], F32)
            cnd = sbt(es, "cnd", [128, 8], F32)
            scb = sbt(es, "scb", [128, 8, 128], BF16)
            cast_load_rows(lambda kc: wad[:, kc, :], w_ada_d[l], 8, 6 * D, wad)
            bc_load(ba, b_ada_d[l])
            bc_load(n1g, n1g_d[l])
            bc_load(n2g, n2g_d[l])
            k.dma("sp", cnd[:], cond_d, writes=[cnd])
            k.op("act", lambda e: e.activation(out=cnd[:], in_=cnd[:], func=AF.Silu), reads=[cnd], writes=[cnd])
            k.op("dve", lambda e: e.tensor_copy(out=scb[:], in_=cnd[:].unsqueeze(2).to_broadcast([128, 8, 128])),
                 reads=[cnd], writes=[scb])
            for n in range(12):
                pb = bank()

                def f(e):
                    for kc in range(8):
                        ins = e.matmul(pb[:], lhsT=scb[:, kc, :], rhs=wad[:, kc, cs(n * 512, 512)],
                                       start=(kc == 0), stop=(kc == 7))
                    return ins
                k.op("pe", f, reads=[scb, wad], writes=[pb])
                k.op("dve", lambda e: e.tensor_tensor(out=mod[:, cs(n * 512, 512)], in0=pb[:], in1=ba[:, cs(n * 512, 512)],
                                                      op=ALU.add), reads=[pb, ba], writes=[mod])
            k.op("dve", lambda e: e.scalar_tensor_tensor(out=mod[:, cs(D, D)], in0=mod[:, cs(D, D)], scalar=1.0, in1=n1g[:],
                                                         op0=ALU.add, op1=ALU.mult), reads=[mod, n1g], writes=[mod])
            k.op("dve", lambda e: e.scalar_tensor_tensor(out=mod[:, cs(4 * D, D)], in0=mod[:, cs(4 * D, D)], scalar=1.0,
                                                         in1=n2g[:], op0=ALU.add, op1=ALU.mult), reads=[mod, n2g], writes=[mod])
            k.dma("sp", modbc_d[l], mod[:], reads=[mod], writes=[R_mod(l)], sembuf=mod)
            k.barrier()

    def load_mod(buf, l, j):
        k.dma("sp", buf[:], modbc_d[l][:, cs(j * D, D)], reads=[R_mod(l)], writes=[buf])

    def phaseA1(l, xsrc_d, R_xsrc):
        with contextlib.ExitStack() as es:
            win = sbt(es, "win", [128, 8, DIN], BF16)
            g1 = sbt(es, "g1", [128, D], F32)
            sh1 = sbt(es, "sh1", [128, D], F32)
            mu = sbt(es, "mu", [128, CR], F32)
            shm = sbt(es, "shm", [128, 12, 128], BF16)
            xt = [sbt(es, "xt0", [128, D], F32)]
            xt.append(xt[0])
            sq = sbt(es, "sq", [128, D], F32)
            hb = sbt(es, "hb", [128, D], BF16)
            hT = sbt(es, "hT", [128, 8, 128], BF16)
            ss = sbt(es, "ss", [128, 1], F32)
            rstd = sbt(es, "rstd", [128, 1], F32)
            zb = [sbt(es, "zb%d" % i, [128, CR], BF16) for i in range(2)]
            zm = [sbt(es, "zm%d" % i, [128, CR], BF16) for i in range(3)]
            zst = sbt(es, "zst", [128, CR], BF16)
            rst = [sbt(es, "rst%d" % i, [128, 512], BF16) for i in range(4)]
            cast_load_rows(lambda kc: win[:, kc, :], w_in_d[l], 8, DIN, win)
            k.dma("pool", shm[:].rearrange("p a b -> p (a b)"), shm_d, writes=[shm], max_dma_last_dim=8192)
            load_mod(sh1, l, 0)
            load_mod(g1, l, 1)
            bc_load(mu, mu_d[l])
            rsti = [0]

            def load_x(i):
                k.dma("sp", xt[i % 2][:], xsrc_d[cs(i * 128, 128), :], reads=[R_xsrc(i)], writes=[xt[i % 2]])

            def stage1(i):
                x = xt[i % 2]
                if DBG < 2:
                    if i + 1 < NT:
                        load_x(i + 1)
                    return
                k.op("act", lambda e: e.activation(out=sq[:], in_=x[:], func=AF.Square), reads=[x], writes=[sq])
                k.op("dve", lambda e: e.tensor_reduce(out=ss[:], in_=sq[:], axis=AX.X, op=ALU.add), reads=[sq], writes=[ss])
                rstd_from(ss, rstd, 1.0 / D, EPS)
                k.op("dve", lambda e: e.scalar_tensor_tensor(out=x[:], in0=x[:], scalar=rstd[:, 0:1], in1=g1[:],
                                                             op0=ALU.mult, op1=ALU.mult), reads=[x, rstd, g1], writes=[x])
                k.op("dve", lambda e: e.tensor_tensor(out=hb[:], in0=x[:], in1=sh1[:], op=ALU.add),
                     reads=[x, sh1], writes=[hb])
                if i + 1 < NT:
                    load_x(i + 1)
                if DBG < 3:
                    return
                transpose8(hb, hT)
                if DBG < 4:
                    return
                zbi, zmi = zb[i % 2], zm[i % 3]
                col = 0
                ci = 0
                while col < DIN:
                    if col < CR:
                        n = min(512, CR - col)
                    else:
                        n = 512
                    pb = bank()

                    def f(e):
                        for kc in range(8):
                            ins = e.matmul(pb[:, 0:n], lhsT=hT[:, kc, :], rhs=win[:, kc, cs(col, n)],
                                           start=(kc == 0), stop=(kc == 7))
                        return ins
                    k.op("pe", f, reads=[hT, win], writes=[pb])
                    if 'p' in KSKIP:
                        pass
                    elif col < CR:
                        if 'z' not in KSKIP:
                            k.op("act", lambda e: e.copy(out=zbi[:, cs(col, n)], in_=pb[:, 0:n]), reads=[pb], writes=[zbi])
                        if 'm' not in KSKIP:
                            k.op("dve", lambda e: e.tensor_tensor(out=zmi[:, cs(col, n)], in0=zbi[:, cs(col, n)], in1=mu[:, cs(col, n)],
                                                                  op=ALU.mult), reads=[zbi, mu], writes=[zmi])
                    else:
                        st = rst[rsti[0] % 4]
                        rsti[0] += 1
                        if ci % 2 == 0:
                            k.op("act", lambda e: e.copy(out=st[:], in_=pb[:]), reads=[pb], writes=[st])
                        else:
                            k.op("dve", lambda e: e.tensor_copy(out=st[:], in_=pb[:]), reads=[pb], writes=[st])
                        if 'r' not in KSKIP:
                            k.dma("sp", zrest_d[cs(i * 128, 128), cs(col - CR, 512)], st[:], reads=[st],
                                  writes=[R_zrest(i, (col - CR) // 512)], sembuf=st)
                    col += n
                    ci += 1

            def stage2(i):
                if DBG < 5:
                    return
                par = i % 2
                for cls in range(4):
                    nb = i - 1 if cls in (0, 2) else i + 1
                    for hh in range(2):
                        c0 = cls + 4 * 432 * hh
                        sl = slice(c0, c0 + 4 * 431 + 1, 4)
                        pb = bank()
                        srcs = [(ident_b[:], zb[i % 2], ident_b), (shm[:, 3 * cls, :], zm[i % 3], shm)]
                        if 0 <= nb < NT:
                            srcs.append((shm[:, 3 * cls + 1 + par, :], zm[nb % 3], shm))

                        def f(e):
                            for j, (lt, rb, _) in enumerate(srcs):
                                ins = e.matmul(pb[:, 0:432], lhsT=lt, rhs=rb[:, sl], start=(j == 0), stop=(j == len(srcs) - 1))
                            return ins
                        k.op("pe", f, reads=[s[1] for s in srcs] + [ident_b, shm], writes=[pb])
                        if hh == 0:
                            k.op("act", lambda e: e.copy(out=zst[:, sl], in_=pb[:, 0:432]), reads=[pb], writes=[zst])
                        else:
                            k.op("dve", lambda e: e.tensor_copy(out=zst[:, sl], in_=pb[:, 0:432]), reads=[pb], writes=[zst])
                k.dma("sp", zr_d[cs(i * 128, 128), :], zst[:], reads=[zst], writes=[R_zr(i)], sembuf=zst)

            load_x(0)
            stage1(0)
            for i in range(NT):
                if i + 1 < NT:
                    stage1(i + 1)
                stage2(i)
            k.barrier()

    def transpose8(src, dst, nblk=8, src_off=0):
        pb = bank()
        pv = pb[:].bitcast(BF16)

        def f(e):
            for j in range(nblk):
                ins = e.transpose(out=pv[:, cs(j * 128, 128)], in_=src[:, cs(src_off + j * 128, 128)], identity=ident_b[:])
            return ins
        k.op("pe", f, reads=[src, ident_b], writes=[pb])
        k.op("act", lambda e: e.copy(out=dst[:, 0:nblk, :].rearrange("p a b -> p (a b)"), in_=pv[:, 0:nblk * 128]),
             reads=[pb], writes=[dst])

    def phaseS(l, d):
        with contextlib.ExitStack() as es:
            kkc = sbt(es, "kkc", [128, D], F32)
            kac = sbt(es, "kac", [128, D], F32)
            rkc = sbt(es, "rkc", [128, D], F32)
            wup = sbt(es, "wup", [65, D], BF16)
            aup = sbt(es, "aup", [65, D], BF16)
            maskM = sbt(es, "maskM", [128, 4, 128], F32)
            maskN = sbt(es, "maskN", [128, 4, 128], F32)
            H = sbt(es, "H", [128, 8, 64], F32)
            Hb = sbt(es, "Hb", [128, 8, 64], BF16)
            zr = [sbt(es, "zr%d" % i, [128, CR], BF16) for i in range(2)]
            ldT = sbt(es, "ldT", [65, 2, 128], BF16)
            tw = sbt(es, "tw", [128, 128], BF16)
            SG = sbt(es, "SG", [128, D], F32)
            A = sbt(es, "A", [128, D], F32)
            KX = sbt(es, "KX", [128, D], F32)
            BP = sbt(es, "BP", [128, D], F32)
            KD = sbt(es, "KD", [128, D], F32)
            S0 = sbt(es, "S0", [128, D], F32)
            S1 = sbt(es, "S1", [128, D], F32)
            st16 = sbt(es, "st16", [128, 16], F32)
            rs16 = sbt(es, "rs16", [128, 16], F32)
            bon = sbt(es, "bon", [128, 16], F32)
            gC = sbt(es, "gC", [128, 8], F32)
            TM = sbt(es, "TM", [128, 4, D], BF16)
            Bg = sbt(es, "Bg", [128, D], BF16)
            Kg = sbt(es, "Kg", [128, D], BF16)
            FM = sbt(es, "FM", [128, 8, 4, 128], BF16)
            MB = sbt(es, "MB", [128, 16, 512], BF16)
            Pm = [sbt(es, "Pm%d" % i, [128, 16, 128], BF16) for i in range(2)]
            PT = [sbt(es, "PT%d" % i, [128, 16, 128], BF16) for i in range(2)]
            Xb = [sbt(es, "Xb%d" % i, [128, D], BF16) for i in range(2)]
            ysc = sbt(es, "ysc", [128, D], F32)

            bc_load(kkc, kk_d[l])
            bc_load(kac, ka_d[l])
            bc_load(rkc, rk_d[l])
            k.dma("pool", wup[:], wup_d[l, d], writes=[wup], max_dma_last_dim=8192)
            k.dma("pool", aup[:], aup_d[l, d], writes=[aup], max_dma_last_dim=8192)
            strict_i, incl_i, nmask_i = (2, 0, 3) if d == 0 else (3, 1, 2)
            for j, src in enumerate((strict_i, incl_i, strict_i, incl_i)):
                k.op("dve", lambda e: e.tensor_copy(out=maskM[:, j, :], in_=tri4[:, src, :]), reads=[tri4], writes=[maskM])
            for j in range(4):
                k.op("dve", lambda e: e.tensor_copy(out=maskN[:, j, :], in_=tri4[:, nmask_i, :]), reads=[tri4], writes=[maskN])
            tri_incl = tri4[:, incl_i, :]
            tri_excl = tri4[:, strict_i, :]
            tri_dg = tri4[:, nmask_i, :]
            k.op("dve", lambda e: e.memset(ldT[:], 1.0), writes=[ldT])
            k.dma("sp", H[:].rearrange("p a b -> p (a b)"), state0_d[l, d], writes=[H])
            k.op("act", lambda e: e.copy(out=Hb[:], in_=H[:]), reads=[H], writes=[Hb])

            order = list(range(NT)) if d == 0 else list(range(NT - 1, -1, -1))

            def load_zr(n):
                i = order[n]
                k.dma("sp", zr[n % 2][:], zr_d[cs(i * 128, 128), :], reads=[R_zr(i)], writes=[zr[n % 2]])

            load_zr(0)
            for n, i in enumerate(order):
                z = zr[n % 2]
                if n + 1 < NT:
                    load_zr(n + 1)
                rq = z[:, 0:D]
                kq = z[:, D:2 * D]
                vq = z[:, 2 * D:3 * D]
                k.op("act", lambda e: e.activation(out=tw[:, 0:64], in_=z[:, cs(3 * D + 64 * d, 64)], func=AF.Tanh),
                     reads=[z], writes=[tw])
                k.op("dve", lambda e: e.tensor_copy(out=tw[:, 64:128], in_=z[:, cs(3 * D + 128 + 64 * d, 64)]),
                     reads=[z], writes=[tw])
                pb = bank()
                pv = pb[:].bitcast(BF16)

                def f(e):
                    e.transpose(out=pv[0:64, 0:128], in_=tw[:, 0:64], identity=ident_b[:])
                    return e.transpose(out=pv[0:64, 128:256], in_=tw[:, 64:128], identity=ident_b[:])
                k.op("pe", f, reads=[tw, ident_b], writes=[pb])
                k.op("act", lambda e: e.copy(out=ldT[0:64, :, :].rearrange("p a b -> p (a b)"), in_=pv[0:64, 0:256]),
                     reads=[pb], writes=[ldT])
                for (wmat, src_j, dst) in ((wup, 0, SG), (aup, 1, A)):
                    pp, b0, b1 = pair()

                    def f(e):
                        e.matmul(pp[:, 0:512], lhsT=ldT[:, src_j, :], rhs=wmat[:, 0:512], start=True, stop=True)
                        return e.matmul(pp[:, 512:1024], lhsT=ldT[:, src_j, :], rhs=wmat[:, 512:1024], start=True, stop=True)
                    k.op("pe", f, reads=[ldT, wmat], writes=[b0, b1])
                    k.op("act", lambda e: e.activation(out=dst[:], in_=pp[:, :], func=AF.Sigmoid), reads=[b0, b1], writes=[dst])
                if DBG < 11:
                    continue
                k.op("dve", lambda e: e.tensor_tensor(out=KX[:], in0=kq, in1=kkc[:], op=ALU.mult), reads=[z, kkc], writes=[KX])
                k.op("dve", lambda e: e.tensor_tensor(out=S0[:], in0=KX[:], in1=KX[:], op=ALU.mult), reads=[KX], writes=[S0])
                k.op("dve", lambda e: e.tensor_reduce(out=st16[:], in_=S0[:].rearrange("p (h n) -> p h n", n=64), axis=AX.X,
                                                      op=ALU.add), reads=[S0], writes=[st16])
                rstd_from(st16, rs16, 1.0, 1e-12)
                k.op("dve", lambda e: e.tensor_tensor(out=KX[:].rearrange("p (h n) -> p h n", n=64),
                                                      in0=KX[:].rearrange("p (h n) -> p h n", n=64),
                                                      in1=rs16[:].unsqueeze(2).to_broadcast([128, 16, 64]), op=ALU.mult),
                     reads=[KX, rs16], writes=[KX])
                k.op("dve", lambda e: e.scalar_tensor_tensor(out=BP[:], in0=KX[:], scalar=-1.0, in1=A[:], op0=ALU.mult,
                                                             op1=ALU.mult), reads=[KX, A], writes=[BP])
                k.op("dve", lambda e: e.tensor_tensor(out=S0[:], in0=kq, in1=kac[:], op=ALU.mult), reads=[z, kac], writes=[S0])
                k.op("dve", lambda e: e.scalar_tensor_tensor(out=S0[:], in0=A[:], scalar=-1.0, in1=S0[:], op0=ALU.add,
                                                             op1=ALU.mult), reads=[A, S0], writes=[S0])
                k.op("dve", lambda e: e.tensor_tensor(out=KD[:], in0=S0[:], in1=kq, op=ALU.add), reads=[S0, z], writes=[KD])
                k.op("dve", lambda e: e.tensor_tensor(out=S0[:], in0=KD[:], in1=rkc[:], op=ALU.mult), reads=[KD, rkc], writes=[S0])
                k.op("dve", lambda e: e.tensor_tensor(out=S0[:], in0=S0[:], in1=rq, op=ALU.mult), reads=[S0, z], writes=[S0])
                k.op("dve", lambda e: e.tensor_reduce(out=bon[:], in_=S0[:].rearrange("p (h n) -> p h n", n=64), axis=AX.X,
                                                      op=ALU.add), reads=[S0], writes=[bon])
                k.dma("sp", bon_d[d, cs(i * 128, 128), :], bon[:], reads=[bon], writes=[R_bon(d, i)], sembuf=bon)
                if DBG < 12:
                    continue
                def cum(tri_ap, scale, dstE):
                    pp, b0, b1 = pair()

                    def f(e):
                        e.matmul(pp[:, 0:512], lhsT=tri_ap, rhs=SG[:, 0:512], start=True, stop=True)
                        return e.matmul(pp[:, 512:1024], lhsT=tri_ap, rhs=SG[:, 512:1024], start=True, stop=True)
                    k.op("pe", f, reads=[tri4, SG], writes=[b0, b1])
                    k.op("act", lambda e: e.activation(out=dstE[:], in_=pp[:, :], func=AF.Exp, scale=scale),
                         reads=[b0, b1], writes=[dstE])
                    return pp, b0, b1
                pp, b0, b1 = cum(tri_incl, -DSC, S0)
                k.op("dve", lambda e: e.tensor_tensor(out=TM[:, 1, :], in0=rq, in1=S0[:], op=ALU.mult), reads=[z, S0], writes=[TM])
                k.op("act", lambda e: e.activation(out=S1[:], in_=pp[:, :], func=AF.Exp, scale=DSC), reads=[b0, b1], writes=[S1])
                k.op("dve", lambda e: e.tensor_tensor(out=TM[:, 2, :], in0=BP[:], in1=S1[:], op=ALU.mult), reads=[BP, S1], writes=[TM])
                k.op("dve", lambda e: e.tensor_tensor(out=TM[:, 3, :], in0=KD[:], in1=S1[:], op=ALU.mult), reads=[KD, S1], writes=[TM])
                cum(tri_excl, -DSC, S0)
                k.op("dve", lambda e: e.tensor_tensor(out=TM[:, 0, :], in0=KX[:], in1=S0[:], op=ALU.mult), reads=[KX, S0], writes=[TM])
                cum(tri_dg, -DSC, S1)
                k.op("dve", lambda e: e.tensor_tensor(out=Bg[:], in0=BP[:], in1=S1[:], op=ALU.mult), reads=[BP, S1], writes=[Bg])
                k.op("dve", lambda e: e.tensor_tensor(out=Kg[:], in0=KD[:], in1=S1[:], op=ALU.mult), reads=[KD, S1], writes=[Kg])
                pbg = bank()

                def f(e):
                    for ct in range(8):
                        ins = e.matmul(pbg[:, ct:ct + 1], lhsT=SG[:, cs(ct * 128, 128)], rhs=ones_f[:], start=True, stop=True)
                    return ins
                k.op("pe", f, reads=[SG, ones_f], writes=[pbg])
                k.op("act", lambda e: e.activation(out=gC[:], in_=pbg[:, 0:8], func=AF.Exp, scale=-DSC), reads=[pbg], writes=[gC])
                if DBG < 13:
                    continue
                for g4 in range(4):
                    pb = bank()
                    pv = pb[:].bitcast(BF16)

                    def f(e):
                        for c2 in range(2):
                            ct = g4 * 2 + c2
                            for q in range(4):
                                ins = e.transpose(out=pv[:, cs((c2 * 4 + q) * 128, 128)], in_=TM[:, q, cs(ct * 128, 128)],
                                                  identity=ident_b[:])
                        return ins
                    k.op("pe", f, reads=[TM, ident_b], writes=[pb])
                    eng = "act" if g4 % 2 == 0 else "dve"
                    if eng == "act":
                        k.op("act", lambda e: e.copy(out=FM[:, g4 * 2:g4 * 2 + 2, :, :].rearrange("p a b c -> p (a b c)"),
                                                     in_=pv[:, :]), reads=[pb], writes=[FM])
                    else:
                        k.op("dve", lambda e: e.tensor_copy(out=FM[:, g4 * 2:g4 * 2 + 2, :, :].rearrange("p a b c -> p (a b c)"),
                                                            in_=pv[:, :]), reads=[pb], writes=[FM])
                if DBG < 14:
                    continue
                P0, PT0 = Pm[0], PT[0]
                for h in range(NHEAD):
                    ct, p0 = h // 2, (h % 2) * 64
                    if 'o' in KSKIP and h % 2 == 1:
                        continue
                    pb = bank()

                    def f(e):
                        e.matmul(pb[:, 0:256], lhsT=FM[p0:p0 + 64, ct, 2, :],
                                 rhs=FM[p0:p0 + 64, ct, 0:2, :].rearrange("p a b -> p (a b)"), start=True, stop=True)
                        return e.matmul(pb[:, 256:512], lhsT=FM[p0:p0 + 64, ct, 3, :],
                                        rhs=FM[p0:p0 + 64, ct, 0:2, :].rearrange("p a b -> p (a b)"), start=True, stop=True)
                    k.op("pe", f, reads=[FM], writes=[pb])
                    k.op("dve", lambda e: e.tensor_tensor(out=MB[:, h, :], in0=pb[:], in1=maskM[:].rearrange("p a b -> p (a b)"),
                                                          op=ALU.mult), reads=[pb, maskM], writes=[MB])
                    k.op("act", lambda e: e.copy(out=PT0[:, h, :], in_=MB[:, h, 0:128]), reads=[MB], writes=[PT0])
                for g in range(4):
                    if 'n' in KSKIP:
                        continue
                    pb = bank()

                    def f(e):
                        for hh in range(4):
                            h = (g % 2) + 2 * (4 * (g // 2) + hh)
                            ct, p0 = h // 2, (h % 2) * 64
                            ins = e.matmul(pb[:, cs(hh * 128, 128)], lhsT=FM[p0:p0 + 64, ct, 0, :], rhs=FM[p0:p0 + 64, ct, 2, :],
                                           start=True, stop=True)
                        return ins
                    k.op("pe", f, reads=[FM], writes=[pb])
                    h0 = (g % 2) + 8 * (g // 2)
                    k.op("dve", lambda e: e.tensor_tensor(out=P0[:, h0:h0 + 7:2, :],
                                                          in0=pb[:].rearrange("p (a b) -> p a b", b=128), in1=maskN[:], op=ALU.mult),
                         reads=[pb, maskN], writes=[P0])
                if DBG < 15:
                    continue
                pp, b0, b1 = pair()

                def f(e):
                    for h in range(NHEAD):
                        ct, p0 = h // 2, (h % 2) * 64
                        e.matmul(pp[:, hc(h)], lhsT=FM[p0:p0 + 64, ct, 0, :], rhs=Hb[p0:p0 + 64, ct, :],
                                 start=True, stop=False)
                        ins = e.matmul(pp[:, hc(h)], lhsT=MB[:, h, 256:384], rhs=z[:, cs(2 * D + h * 64, 64)],
                                       start=False, stop=True)
                    return ins
                k.op("pe", f, reads=[FM, Hb, MB, z], writes=[b0, b1])
                xc = Xb[0]
                k.op("act", lambda e: e.copy(out=xc[:], in_=pp[:, :]), reads=[b0, b1], writes=[xc])
                if DBG < 16:
                    continue
                cur = 0
                for lev in range(7):
                    Pc, PTc = Pm[cur], PT[cur]
                    pp, b0, b1 = pair()
                    xc, xn = Xb[lev % 2], Xb[(lev + 1) % 2]

                    def f(e):
                        for h in range(NHEAD):
                            ins = e.matmul(pp[:, hc(h)], lhsT=PTc[:, h, :], rhs=xc[:, hc(h)], start=True, stop=True)
                        return ins
                    k.op("pe", f, reads=[PTc, xc], writes=[b0, b1])
                    k.op("dve", lambda e: e.tensor_tensor(out=xn[:], in0=pp[:, :], in1=xc[:], op=ALU.add),
                         reads=[b0, b1, xc], writes=[xn])
                    if lev < 6:
                        Pn, PTn = Pm[1 - cur], PT[1 - cur]
                        for g in range(4):
                            for which in range(2):
                                pb = bank()

                                def f(e):
                                    for hh in range(4):
                                        h = g * 4 + hh
                                        if which == 0:
                                            ins = e.matmul(pb[:, cs(hh * 128, 128)], lhsT=PTc[:, h, :], rhs=Pc[:, h, :], start=True, stop=True)
                                        else:
                                            ins = e.matmul(pb[:, cs(hh * 128, 128)], lhsT=Pc[:, h, :], rhs=PTc[:, h, :], start=True, stop=True)
                                    return ins
                                k.op("pe", f, reads=[Pc, PTc], writes=[pb])
                                dstb = Pn if which == 0 else PTn
                                if which == 0:
                                    k.op("act", lambda e: e.copy(out=dstb[:, g * 4:g * 4 + 4, :].rearrange("p a b -> p (a b)"),
                                                                 in_=pb[:]), reads=[pb], writes=[dstb])
                                else:
                                    k.op("dve", lambda e: e.tensor_copy(out=dstb[:, g * 4:g * 4 + 4, :].rearrange("p a b -> p (a b)"),
                                                                        in_=pb[:]), reads=[pb], writes=[dstb])
                        cur = 1 - cur
                U = Xb[1]
                if DBG < 17:
                    continue
                pp, b0, b1 = pair()

                def f(e):
                    for h in range(NHEAD):
                        ct, p0 = h // 2, (h % 2) * 64
                        e.matmul(pp[:, hc(h)], lhsT=FM[p0:p0 + 64, ct, 1, :], rhs=Hb[p0:p0 + 64, ct, :], start=True, stop=False)
                        e.matmul(pp[:, hc(h)], lhsT=MB[:, h, 128:256], rhs=U[:, hc(h)], start=False, stop=False)
                        ins = e.matmul(pp[:, hc(h)], lhsT=MB[:, h, 384:512], rhs=z[:, cs(2 * D + h * 64, 64)],
                                       start=False, stop=True)
                    return ins
                k.op("pe", f, reads=[FM, Hb, MB, U, z], writes=[b0, b1])
                k.op("act", lambda e: e.copy(out=ysc[:].rearrange("p (hp h2 i) -> p h2 hp i", h2=2, i=64),
                                             in_=pp[:, :].rearrange("p (h2 hp i) -> p h2 hp i", hp=8, i=64)), reads=[b0, b1], writes=[ysc])
                k.dma("sp", ysc_d[d, cs(i * 128, 128), :], ysc[:], reads=[ysc], writes=[R_ysc(d, i)], sembuf=ysc)
                if DBG < 18:
                    continue
                pp, b0, b1 = pair()

                def f(e):
                    for ct in range(8):
                        e.matmul(pp[:, cs(ct * 128, 128)], lhsT=Bg[:, cs(ct * 128, 128)],
                                 rhs=U[:].rearrange("p (a b c) -> p a b c", a=2, b=8)[:, :, ct, :], start=True, stop=False)
                        ins = e.matmul(pp[:, cs(ct * 128, 128)], lhsT=Kg[:, cs(ct * 128, 128)], rhs=z[:, cs(2 * D + ct * 128, 128)],
                                       start=False, stop=True)
                    return ins
                k.op("pe", f, reads=[Bg, Kg, U, z], writes=[b0, b1])
                k.op("dve", lambda e: e.tensor_tensor(out=H[:], in0=H[:], in1=gC[:].unsqueeze(2).to_broadcast([128, 8, 64]),
                                                      op=ALU.mult), reads=[H, gC], writes=[H])
                ppv = pp[:, :].rearrange("p (a b) -> p a b", b=128)
                k.op("dve", lambda e: e.tensor_tensor(out=H[0:64, :, :], in0=H[0:64, :, :], in1=ppv[0:64, :, 0:64], op=ALU.add),
                     reads=[H, b0, b1], writes=[H])
                k.op("dve", lambda e: e.tensor_tensor(out=H[64:128, :, :], in0=H[64:128, :, :], in1=ppv[64:128, :, 64:128], op=ALU.add),
                     reads=[H, b0, b1], writes=[H])
                if n % 2 == 1:
                    grp = i // 2
                    k.dma("sp", st_d[l, d, grp], H[:].rearrange("p a b -> p (a b)"), reads=[H], writes=[R_st(l, d, grp)], sembuf=H)
                    k.op("dve", lambda e: e.tensor_scalar(out=H[:], in0=H[:], scalar1=cmask[:, 0:1], scalar2=None, op0=ALU.mult),
                         reads=[H, cmask], writes=[H])
                k.op("act", lambda e: e.copy(out=Hb[:], in_=H[:]), reads=[H], writes=[Hb])
            k.barrier()


    def phaseS2(l):
        with contextlib.ExitStack() as es:
            kkc = sbt(es, "kkc", [128, D], F32)
            kac = sbt(es, "kac", [128, D], F32)
            rkc = sbt(es, "rkc", [128, D], F32)
            bc_load(kkc, kk_d[l])
            bc_load(kac, ka_d[l])
            bc_load(rkc, rk_d[l])

            def dir_gen(d):
                sfx = "_%d" % d
                wup = sbt(es, "wup" + sfx, [65, D], BF16)
                aup = sbt(es, "aup" + sfx, [65, D], BF16)
                maskM = sbt(es, "maskM" + sfx, [128, 4, 128], BF16)
                maskN = sbt(es, "maskN" + sfx, [128, 4, 128], BF16)
                H = sbt(es, "H" + sfx, [128, 8, 64], F32)
                Hb = sbt(es, "Hb" + sfx, [128, 8, 64], BF16)
                z = sbt(es, "zr" + sfx, [128, CR], BF16)
                ldT = sbt(es, "ldT" + sfx, [65, 2, 128], BF16)
                tw = sbt(es, "tw" + sfx, [128, 128], BF16)
                SG = sbt(es, "SG" + sfx, [128, D], F32)
                A = sbt(es, "A" + sfx, [128, D], F32)
                KX = sbt(es, "KX" + sfx, [128, D], F32)
                BP = sbt(es, "BP" + sfx, [128, D], F32)
                KD = sbt(es, "KD" + sfx, [128, D], F32)
                S0 = sbt(es, "S0" + sfx, [128, D], F32)
                S1 = A
                ysc = S0
                st16 = sbt(es, "st16" + sfx, [128, 16], F32)
                rs16 = sbt(es, "rs16" + sfx, [128, 16], F32)
                bon = sbt(es, "bon" + sfx, [128, 16], F32)
                gC = sbt(es, "gC" + sfx, [128, 8], F32)
                TM = sbt(es, "TM" + sfx, [128, 4, D], BF16)
                FM = sbt(es, "FM" + sfx, [128, 8, 4, 128], BF16)
                MB = sbt(es, "MB" + sfx, [128, 16, 512], BF16)
                Pm = [sbt(es, "Pm%d" % i + sfx, [128, 16, 128], BF16) for i in range(2)]
                PT = [sbt(es, "PT%d" % i + sfx, [128, 16, 128], BF16) for i in range(2)]
                Xb = [sbt(es, "Xb%d" % i + sfx, [128, D], BF16) for i in range(2)]

                k.dma("pool", wup[:], wup_d[l, d], writes=[wup], max_dma_last_dim=8192)
                k.dma("pool", aup[:], aup_d[l, d], writes=[aup], max_dma_last_dim=8192)
                strict_i, incl_i, nmask_i = (2, 0, 3) if d == 0 else (3, 1, 2)
                for j, src in enumerate((strict_i, incl_i, strict_i, incl_i)):
                    k.op("dve", lambda e: e.tensor_copy(out=maskM[:, j, :], in_=tri4[:, src, :]), reads=[tri4], writes=[maskM])
                for j in range(4):
                    k.op("dve", lambda e: e.tensor_copy(out=maskN[:, j, :], in_=tri4[:, nmask_i, :]), reads=[tri4], writes=[maskN])
                tri_incl = tri4[:, incl_i, :]
                tri_excl = tri4[:, strict_i, :]
                tri_dg = tri4[:, nmask_i, :]
                k.op("dve", lambda e: e.memset(ldT[:], 1.0), writes=[ldT])
                k.dma("sp", H[:].rearrange("p a b -> p (a b)"), state0_d[l, d], writes=[H])
                k.op("act", lambda e: e.copy(out=Hb[:], in_=H[:]), reads=[H], writes=[Hb])
                order = list(range(NT)) if d == 0 else list(range(NT - 1, -1, -1))
                yield

                for n, i in enumerate(order):
                    k.dma("sp", z[:], zr_d[cs(i * 128, 128), :], reads=[R_zr(i)], writes=[z])
                    rq = z[:, 0:D]
                    kq = z[:, D:2 * D]
                    k.op("act", lambda e: e.activation(out=tw[:, 0:64], in_=z[:, cs(3 * D + 64 * d, 64)], func=AF.Tanh),
                         reads=[z], writes=[tw])
                    k.op("pool", lambda e: e.tensor_copy(out=tw[:, 64:128], in_=z[:, cs(3 * D + 128 + 64 * d, 64)]),
                         reads=[z], writes=[tw])
                    pb = bank()
                    pv = pb[:].bitcast(BF16)

                    def f(e):
                        e.transpose(out=pv[0:64, 0:128], in_=tw[:, 0:64], identity=ident_b[:])
                        return e.transpose(out=pv[0:64, 128:256], in_=tw[:, 64:128], identity=ident_b[:])
                    k.op("pe", f, reads=[tw, ident_b], writes=[pb])
                    k.op("act", lambda e: e.copy(out=ldT[0:64, :, :].rearrange("p a b -> p (a b)"), in_=pv[0:64, 0:256]),
                         reads=[pb], writes=[ldT])
                    for (wmat, src_j, dst) in ((wup, 0, SG), (aup, 1, A)):
                        pp, b0, b1 = pair()

                        def f(e):
                            e.matmul(pp[:, 0:512], lhsT=ldT[:, src_j, :], rhs=wmat[:, 0:512], start=True, stop=True)
                            return e.matmul(pp[:, 512:1024], lhsT=ldT[:, src_j, :], rhs=wmat[:, 512:1024], start=True, stop=True)
                        k.op("pe", f, reads=[ldT, wmat], writes=[b0, b1])
                        k.op("act", lambda e: e.activation(out=dst[:], in_=pp[:, :], func=AF.Sigmoid), reads=[b0, b1], writes=[dst])
                    k.op("pool", lambda e: e.tensor_tensor(out=KX[:], in0=kq, in1=kkc[:], op=ALU.mult), reads=[z, kkc], writes=[KX])
                    k.op("pool", lambda e: e.tensor_tensor(out=S0[:], in0=KX[:], in1=KX[:], op=ALU.mult), reads=[KX], writes=[S0])
                    k.op("dve", lambda e: e.tensor_reduce(out=st16[:], in_=S0[:].rearrange("p (h n) -> p h n", n=64), axis=AX.X,
                                                          op=ALU.add), reads=[S0], writes=[st16])
                    rstd_from(st16, rs16, 1.0, 1e-12)
                    k.op("dve", lambda e: e.tensor_tensor(out=KX[:].rearrange("p (h n) -> p h n", n=64),
                                                          in0=KX[:].rearrange("p (h n) -> p h n", n=64),
                                                          in1=rs16[:].unsqueeze(2).to_broadcast([128, 16, 64]), op=ALU.mult),
                         reads=[KX, rs16], writes=[KX])
                    yield
                    k.op("dve", lambda e: e.scalar_tensor_tensor(out=BP[:], in0=KX[:], scalar=-1.0, in1=A[:], op0=ALU.mult,
                                                                 op1=ALU.mult), reads=[KX, A], writes=[BP])
                    k.op("pool", lambda e: e.tensor_tensor(out=S0[:], in0=kq, in1=kac[:], op=ALU.mult), reads=[z, kac], writes=[S0])
                    k.op("dve", lambda e: e.scalar_tensor_tensor(out=S0[:], in0=A[:], scalar=-1.0, in1=S0[:], op0=ALU.add,
                                                                 op1=ALU.mult), reads=[A, S0], writes=[S0])
                    k.op("pool", lambda e: e.tensor_tensor(out=KD[:], in0=S0[:], in1=kq, op=ALU.add), reads=[S0, z], writes=[KD])
                    k.op("pool", lambda e: e.tensor_tensor(out=S0[:], in0=KD[:], in1=rkc[:], op=ALU.mult), reads=[KD, rkc], writes=[S0])
                    k.op("pool", lambda e: e.tensor_tensor(out=S0[:], in0=S0[:], in1=rq, op=ALU.mult), reads=[S0, z], writes=[S0])
                    k.op("dve", lambda e: e.tensor_reduce(out=bon[:], in_=S0[:].rearrange("p (h n) -> p h n", n=64), axis=AX.X,
                                                          op=ALU.add), reads=[S0], writes=[bon])
                    k.dma("sp", bon_d[d, cs(i * 128, 128), :], bon[:], reads=[bon], writes=[R_bon(d, i)], sembuf=bon)
                    yield

                    def cum(tri_ap, scale, dstE):
                        pp, b0, b1 = pair()

                        def f(e):
                            e.matmul(pp[:, 0:512], lhsT=tri_ap, rhs=SG[:, 0:512], start=True, stop=True)
                            return e.matmul(pp[:, 512:1024], lhsT=tri_ap, rhs=SG[:, 512:1024], start=True, stop=True)
                        k.op("pe", f, reads=[tri4, SG], writes=[b0, b1])
                        k.op("act", lambda e: e.activation(out=dstE[:], in_=pp[:, :], func=AF.Exp, scale=scale),
                             reads=[b0, b1], writes=[dstE])
                        return pp, b0, b1
                    pp, b0, b1 = cum(tri_incl, -DSC, S0)
                    k.op("dve", lambda e: e.tensor_tensor(out=TM[:, 1, :], in0=rq, in1=S0[:], op=ALU.mult), reads=[z, S0], writes=[TM])
                    k.op("act", lambda e: e.activation(out=S1[:], in_=pp[:, :], func=AF.Exp, scale=DSC), reads=[b0, b1], writes=[S1])
                    k.op("dve", lambda e: e.tensor_tensor(out=TM[:, 2, :], in0=BP[:], in1=S1[:], op=ALU.mult), reads=[BP, S1], writes=[TM])
                    k.op("pool", lambda e: e.tensor_tensor(out=TM[:, 3, :], in0=KD[:], in1=S1[:], op=ALU.mult), reads=[KD, S1], writes=[TM])
                    cum(tri_excl, -DSC, S0)
                    k.op("dve", lambda e: e.tensor_tensor(out=TM[:, 0, :], in0=KX[:], in1=S0[:], op=ALU.mult), reads=[KX, S0], writes=[TM])
                    pbg = bank()

                    def f(e):
                        for ct in range(8):
                            ins = e.matmul(pbg[:, ct:ct + 1], lhsT=SG[:, cs(ct * 128, 128)], rhs=ones_f[:], start=True, stop=True)
                        return ins
                    k.op("pe", f, reads=[SG, ones_f], writes=[pbg])
                    k.op("act", lambda e: e.activation(out=gC[:], in_=pbg[:, 0:8], func=AF.Exp, scale=-DSC), reads=[pbg], writes=[gC])
                    yield
                    for g4 in range(4):
                        pb = bank()
                        pv = pb[:].bitcast(BF16)

                        def f(e):
                            for c2 in range(2):
                                ct = g4 * 2 + c2
                                for q in range(4):
                                    ins = e.transpose(out=pv[:, cs((c2 * 4 + q) * 128, 128)], in_=TM[:, q, cs(ct * 128, 128)],
                                                      identity=ident_b[:])
                            return ins
                        k.op("pe", f, reads=[TM, ident_b], writes=[pb])
                        if g4 % 2 == 0:
                            k.op("act", lambda e: e.copy(out=FM[:, g4 * 2:g4 * 2 + 2, :, :].rearrange("p a b c -> p (a b c)"),
                                                         in_=pv[:, :]), reads=[pb], writes=[FM])
                        else:
                            k.op("dve", lambda e: e.tensor_copy(out=FM[:, g4 * 2:g4 * 2 + 2, :, :].rearrange("p a b c -> p (a b c)"),
                                                                in_=pv[:, :]), reads=[pb], writes=[FM])
                    cum(tri_dg, -DSC, S1)
                    k.op("dve", lambda e: e.tensor_tensor(out=TM[:, 2, :], in0=BP[:], in1=S1[:], op=ALU.mult), reads=[BP, S1], writes=[TM])
                    k.op("pool", lambda e: e.tensor_tensor(out=TM[:, 3, :], in0=KD[:], in1=S1[:], op=ALU.mult), reads=[KD, S1], writes=[TM])
                    yield
                    P0 = Pm[0]
                    for h in range(NHEAD):
                        ct, p0 = h // 2, (h % 2) * 64
                        pb = bank()

                        def f(e):
                            e.matmul(pb[:, 0:256], lhsT=FM[p0:p0 + 64, ct, 2, :],
                                     rhs=FM[p0:p0 + 64, ct, 0:2, :].rearrange("p a b -> p (a b)"), start=True, stop=True)
                            return e.matmul(pb[:, 256:512], lhsT=FM[p0:p0 + 64, ct, 3, :],
                                            rhs=FM[p0:p0 + 64, ct, 0:2, :].rearrange("p a b -> p (a b)"), start=True, stop=True)
                        k.op("pe", f, reads=[FM], writes=[pb])
                        k.op("dve", lambda e: e.tensor_tensor(out=MB[:, h, :], in0=pb[:], in1=maskM[:].rearrange("p a b -> p (a b)"),
                                                              op=ALU.mult), reads=[pb, maskM], writes=[MB])
                        if h % 4 == 3:
                            yield
                    for g in range(4):
                        pb = bank()

                        def f(e):
                            for hh in range(4):
                                h = (g % 2) + 2 * (4 * (g // 2) + hh)
                                ct, p0 = h // 2, (h % 2) * 64
                                ins = e.matmul(pb[:, cs(hh * 128, 128)], lhsT=FM[p0:p0 + 64, ct, 0, :], rhs=FM[p0:p0 + 64, ct, 2, :],
                                               start=True, stop=True)
                            return ins
                        k.op("pe", f, reads=[FM], writes=[pb])
                        h0 = (g % 2) + 8 * (g // 2)
                        k.op("dve", lambda e: e.tensor_tensor(out=P0[:, h0:h0 + 7:2, :],
                                                              in0=pb[:].rearrange("p (a b) -> p a b", b=128), in1=maskN[:], op=ALU.mult),
                             reads=[pb, maskN], writes=[P0])
                    yield
                    pp, b0, b1 = pair()

                    def f(e):
                        for h in range(NHEAD):
                            ct, p0 = h // 2, (h % 2) * 64
                            e.matmul(pp[:, hc(h)], lhsT=FM[p0:p0 + 64, ct, 0, :], rhs=Hb[p0:p0 + 64, ct, :],
                                     start=True, stop=False)
                            ins = e.matmul(pp[:, hc(h)], lhsT=MB[:, h, 256:384], rhs=z[:, cs(2 * D + h * 64, 64)],
                                           start=False, stop=True)
                        return ins
                    k.op("pe", f, reads=[FM, Hb, MB, z], writes=[b0, b1])
                    xc = Xb[0]
                    k.op("act", lambda e: e.copy(out=xc[:], in_=pp[:, :]), reads=[b0, b1], writes=[xc])
                    yield
                    cur = 0
                    for lev in range(7):
                        Pc = Pm[cur]
                        pp, b0, b1 = pair()
                        xc, xn = Xb[lev % 2], Xb[(lev + 1) % 2]
                        if lev == 0:
                            ptv = lambda h: MB[:, h, 0:128]
                            ptb = MB
                        else:
                            ptv = lambda h, PTc=PT[cur]: PTc[:, h, :]
                            ptb = PT[cur]

                        def f(e):
                            for h in range(NHEAD):
                                ins = e.matmul(pp[:, hc(h)], lhsT=ptv(h), rhs=xc[:, hc(h)], start=True, stop=True)
                            return ins
                        k.op("pe", f, reads=[ptb, xc], writes=[b0, b1])
                        k.op("dve", lambda e: e.tensor_tensor(out=xn[:], in0=pp[:, :], in1=xc[:], op=ALU.add),
                             reads=[b0, b1, xc], writes=[xn])
                        if lev < 6:
                            Pn, PTn = Pm[1 - cur], PT[1 - cur]
                            for g in range(4):
                                for which in range(2):
                                    if which == 0 and lev == 5:
                                        continue
                                    pb = bank()

                                    def f(e):
                                        for hh in range(4):
                                            h = g * 4 + hh
                                            if which == 0:
                                                ins = e.matmul(pb[:, cs(hh * 128, 128)], lhsT=ptv(h), rhs=Pc[:, h, :], start=True, stop=True)
                                            else:
                                                ins = e.matmul(pb[:, cs(hh * 128, 128)], lhsT=Pc[:, h, :], rhs=ptv(h), start=True, stop=True)
                                        return ins
                                    k.op("pe", f, reads=[Pc, ptb], writes=[pb])
                                    dstb = Pn if which == 0 else PTn
                                    if which == 0 or g % 2 == 0:
                                        k.op("act", lambda e: e.copy(out=dstb[:, g * 4:g * 4 + 4, :].rearrange("p a b -> p (a b)"),
                                                                     in_=pb[:]), reads=[pb], writes=[dstb])
                                    else:
                                        k.op("dve", lambda e: e.tensor_copy(out=dstb[:, g * 4:g * 4 + 4, :].rearrange("p a b -> p (a b)"),
                                                                            in_=pb[:]), reads=[pb], writes=[dstb])
                                if g % 2 == 1:
                                    yield
                            cur = 1 - cur
                    U = Xb[1]
                    pp, b0, b1 = pair()

                    def f(e):
                        for h in range(NHEAD):
                            ct, p0 = h // 2, (h % 2) * 64
                            e.matmul(pp[:, hc(h)], lhsT=FM[p0:p0 + 64, ct, 1, :], rhs=Hb[p0:p0 + 64, ct, :], start=True, stop=False)
                            e.matmul(pp[:, hc(h)], lhsT=MB[:, h, 128:256], rhs=U[:, hc(h)], start=False, stop=False)
                            ins = e.matmul(pp[:, hc(h)], lhsT=MB[:, h, 384:512], rhs=z[:, cs(2 * D + h * 64, 64)],
                                           start=False, stop=True)
                        return ins
                    k.op("pe", f, reads=[FM, Hb, MB, U, z], writes=[b0, b1])
                    k.op("act", lambda e: e.copy(out=ysc[:].rearrange("p (hp h2 i) -> p h2 hp i", h2=2, i=64),
                                                 in_=pp[:, :].rearrange("p (h2 hp i) -> p h2 hp i", hp=8, i=64)), reads=[b0, b1], writes=[ysc])
                    k.dma("sp", ysc_d[d, cs(i * 128, 128), :], ysc[:], reads=[ysc], writes=[R_ysc(d, i)], sembuf=ysc)
                    pp, b0, b1 = pair()

                    def f(e):
                        for ct in range(8):
                            e.matmul(pp[:, cs(ct * 128, 128)], lhsT=TM[:, 2, cs(ct * 128, 128)],
                                     rhs=U[:].rearrange("p (a b c) -> p a b c", a=2, b=8)[:, :, ct, :], start=True, stop=False)
                            ins = e.matmul(pp[:, cs(ct * 128, 128)], lhsT=TM[:, 3, cs(ct * 128, 128)], rhs=z[:, cs(2 * D + ct * 128, 128)],
                                           start=False, stop=True)
                        return ins
                    k.op("pe", f, reads=[TM, U, z], writes=[b0, b1])
                    k.op("dve", lambda e: e.tensor_tensor(out=H[:], in0=H[:], in1=gC[:].unsqueeze(2).to_broadcast([128, 8, 64]),
                                                          op=ALU.mult), reads=[H, gC], writes=[H])
                    ppv = pp[:, :].rearrange("p (a b) -> p a b", b=128)
                    k.op("dve", lambda e: e.tensor_tensor(out=H[0:64, :, :], in0=H[0:64, :, :], in1=ppv[0:64, :, 0:64], op=ALU.add),
                         reads=[H, b0, b1], writes=[H])
                    k.op("dve", lambda e: e.tensor_tensor(out=H[64:128, :, :], in0=H[64:128, :, :], in1=ppv[64:128, :, 64:128], op=ALU.add),
                         reads=[H, b0, b1], writes=[H])
                    if n % 2 == 1:
                        grp = i // 2
                        k.dma("sp", st_d[l, d, grp], H[:].rearrange("p a b -> p (a b)"), reads=[H], writes=[R_st(l, d, grp)], sembuf=H)
                        k.op("dve", lambda e: e.tensor_scalar(out=H[:], in0=H[:], scalar1=cmask[:, 0:1], scalar2=None, op0=ALU.mult),
                             reads=[H, cmask], writes=[H])
                    k.op("act", lambda e: e.copy(out=Hb[:], in_=H[:]), reads=[H], writes=[Hb])
                    yield

            gens = [dir_gen(0), dir_gen(1)]
            next(gens[0])
            next(gens[1])
            for _ in range(6):
                next(gens[0])
            live = list(gens)
            while live:
                for g in list(live):
                    try:
                        next(g)
                    except StopIteration:
                        live.remove(g)
            k.barrier()

    def phaseM(l, xsrc_d, R_xsrc):
        with contextlib.ExitStack() as es:
            wa = sbt(es, "wa", [128, 8, D], BF16)
            wb = sbt(es, "wb", [128, 8, D], BF16)
            wo = sbt(es, "wo", [128, 8, D], BF16)
            gup = sbt(es, "gup", [128, D], BF16)
            wsT = sbt(es, "wsT", [128, 8, 128], BF16)
            bsT = sbt(es, "bsT", [128, 8], F32)
            lnxg = sbt(es, "lnxg", [128, D], F32)
            lnxb = sbt(es, "lnxb", [128, D], F32)
            lnvg = sbt(es, "lnvg", [128, D], F32)
            gate1 = sbt(es, "gate1", [128, D], F32)
            def mkset(pi):
                B = {}
                B['zv'] = sbt(es, "p%d_" % pi + "zv", [128, D + 128], BF16)
                B['zrs'] = sbt(es, "p%d_" % pi + "zrs", [128, 4096], BF16)
                B['yf'] = sbt(es, "p%d_" % pi + "yf", [128, D], F32)
                B['yb'] = sbt(es, "p%d_" % pi + "yb", [128, D], F32)
                B['b0t'] = sbt(es, "p%d_" % pi + "b0t", [128, 16], F32)
                B['b1t'] = sbt(es, "p%d_" % pi + "b1t", [128, 16], F32)
                B['xt'] = sbt(es, "p%d_" % pi + "xtm", [128, D], F32)
                B['W0'] = sbt(es, "p%d_" % pi + "W0", [128, D], F32)
                B['W1'] = sbt(es, "p%d_" % pi + "W1", [128, D], F32)
                B['W2'] = sbt(es, "p%d_" % pi + "W2", [128, D], F32)
                B['s16a'] = sbt(es, "p%d_" % pi + "s16a", [128, 16], F32)
                B['s16b'] = sbt(es, "p%d_" % pi + "s16b", [128, 16], F32)
                B['s16c'] = sbt(es, "p%d_" % pi + "s16c", [128, 16], F32)
                B['bnst'] = sbt(es, "p%d_" % pi + "bnst", [128, 2, 6], F32)
                B['mv'] = sbt(es, "p%d_" % pi + "mv", [128, 2], F32)
                B['rsv'] = sbt(es, "p%d_" % pi + "rsv", [128, 1], F32)
                B['gsb'] = sbt(es, "p%d_" % pi + "gsb", [128, 128], BF16)
                B['gT'] = sbt(es, "p%d_" % pi + "gT", [128, 1, 128], BF16)
                B['actb'] = sbt(es, "p%d_" % pi + "actb", [128, D], BF16)
                B['actT'] = sbt(es, "p%d_" % pi + "actT", [128, 8, 128], BF16)
                B['ub'] = sbt(es, "p%d_" % pi + "ub", [128, D], BF16)
                B['vcb'] = sbt(es, "p%d_" % pi + "vcb", [128, D], BF16)
                return B
            sets = [mkset(0), mkset(1)]
            cast_load_rows(lambda kc: wa[:, kc, :], wa_d[l], 8, D, wa)
            cast_load_rows(lambda kc: wb[:, kc, :], wb_d[l], 8, D, wb)
            cast_load_rows(lambda kc: wo[:, kc, :], wo_d[l], 8, D, wo)
            k.dma("pool", gup[:], gup_d[l], writes=[gup], max_dma_last_dim=8192)
            k.dma("pool", wsT[:].rearrange("p a b -> p (a b)"), wsT_d[l], writes=[wsT], max_dma_last_dim=8192)
            k.dma("sp", bsT[:], bsT_d[l], writes=[bsT])
            bc_load(lnxg, lnxg_d[l])
            bc_load(lnxb, lnxb_d[l])
            bc_load(lnvg, lnvg_d[l])
            load_mod(gate1, l, 2)

            def v3(b):
                return b[:].rearrange("p (h n) -> p h n", n=64)

            def bc16(b):
                return b[:].unsqueeze(2).to_broadcast([128, 16, 64])

            def proj(src_bf, wmat, actT):
                transpose8(src_bf, actT)
                pp, p0, p1 = pair()

                def f(e):
                    for nn in range(2):
                        for kc in range(8):
                            ins = e.matmul(pp[:, cs(nn * 512, 512)], lhsT=actT[:, kc, :], rhs=wmat[:, kc, cs(nn * 512, 512)],
                                           start=(kc == 0), stop=(kc == 7))
                    return ins
                k.op("pe", f, reads=[actT, wmat], writes=[p0, p1])
                return pp, p0, p1

            def tile_gen(i, B):
                zv = B['zv']
                zrs = B['zrs']
                yf = B['yf']
                yb = B['yb']
                b0t = B['b0t']
                b1t = B['b1t']
                xt = B['xt']
                W0 = B['W0']
                W1 = B['W1']
                W2 = B['W2']
                s16a = B['s16a']
                s16b = B['s16b']
                s16c = B['s16c']
                bnst = B['bnst']
                mv = B['mv']
                rsv = B['rsv']
                gsb = B['gsb']
                gT = B['gT']
                actb = B['actb']
                actT = B['actT']
                ub = B['ub']
                vcb = B['vcb']
                rows = cs(i * 128, 128)
                k.dma("sp", zv[:, 0:D], zr_d[rows, 2 * D:3 * D], reads=[R_zr(i)], writes=[zv])
                k.dma("sp", zv[:, D:D + 128], zr_d[rows, cs(3 * D + 256, 128)], reads=[R_zr(i)], writes=[zv])
                k.dma("sp", zrs[:], zrest_d[rows, :], reads=[R_zrest(i, j) for j in range(8)], writes=[zrs])
                k.dma("sp", yf[:], ysc_d[0, rows, :], reads=[R_ysc(0, i)], writes=[yf])
                k.dma("sp", yb[:], ysc_d[1, rows, :], reads=[R_ysc(1, i)], writes=[yb])
                k.dma("sp", b0t[:], bon_d[0, rows, :], reads=[R_bon(0, i)], writes=[b0t])
                k.dma("sp", b1t[:], bon_d[1, rows, :], reads=[R_bon(1, i)], writes=[b1t])
                k.dma("sp", xt[:], xsrc_d[rows, :], reads=[R_xsrc(i)], writes=[xt])
                yield
                k.op("dve", lambda e: e.tensor_tensor(out=yf[:], in0=yf[:], in1=yb[:], op=ALU.add), reads=[yf, yb], writes=[yf])
                k.op("dve", lambda e: e.tensor_reduce(out=s16a[:], in_=v3(yf), axis=AX.X, op=ALU.add), reads=[yf], writes=[s16a])
                k.op("dve", lambda e: e.tensor_scalar(out=s16a[:], in0=s16a[:], scalar1=1.0 / 64, scalar2=None, op0=ALU.mult),
                     reads=[s16a], writes=[s16a])
                k.op("dve", lambda e: e.tensor_tensor(out=v3(yf), in0=v3(yf), in1=bc16(s16a), op=ALU.subtract), reads=[yf, s16a], writes=[yf])
                k.op("dve", lambda e: e.tensor_tensor(out=W0[:], in0=yf[:], in1=yf[:], op=ALU.mult), reads=[yf], writes=[W0])
                k.op("dve", lambda e: e.tensor_reduce(out=s16b[:], in_=v3(W0), axis=AX.X, op=ALU.add), reads=[W0], writes=[s16b])
                rstd_from(s16b, s16c, 1.0 / 64, GN_EPS)
                k.op("dve", lambda e: e.tensor_tensor(out=v3(yf), in0=v3(yf), in1=bc16(s16c), op=ALU.mult), reads=[yf, s16c], writes=[yf])
                k.op("dve", lambda e: e.tensor_tensor(out=yf[:], in0=yf[:], in1=lnxg[:], op=ALU.mult), reads=[yf, lnxg], writes=[yf])
                k.op("dve", lambda e: e.tensor_tensor(out=yf[:], in0=yf[:], in1=lnxb[:], op=ALU.add), reads=[yf, lnxb], writes=[yf])
                yield
                k.op("dve", lambda e: e.tensor_tensor(out=b0t[:], in0=b0t[:], in1=b1t[:], op=ALU.add), reads=[b0t, b1t], writes=[b0t])
                k.op("dve", lambda e: e.tensor_tensor(out=v3(W0), in0=zv[:, 0:D].rearrange("p (h n) -> p h n", n=64), in1=bc16(b0t),
                                                      op=ALU.mult), reads=[zv, b0t], writes=[W0])
                k.op("dve", lambda e: e.tensor_tensor(out=yf[:], in0=yf[:], in1=W0[:], op=ALU.add), reads=[yf, W0], writes=[yf])
                k.op("act", lambda e: e.activation(out=gsb[:], in_=zv[:, D:D + 128], func=AF.Sigmoid), reads=[zv], writes=[gsb])
                transpose8(gsb, gT, nblk=1)
                pp, p0, p1 = pair()

                def f(e):
                    e.matmul(pp[:, 0:512], lhsT=gT[:, 0, :], rhs=gup[:, 0:512], start=True, stop=True)
                    return e.matmul(pp[:, 512:1024], lhsT=gT[:, 0, :], rhs=gup[:, 512:1024], start=True, stop=True)
                k.op("pe", f, reads=[gT, gup], writes=[p0, p1])
                k.op("dve", lambda e: e.tensor_tensor(out=actb[:], in0=pp[:, :], in1=yf[:], op=ALU.mult), reads=[p0, p1, yf], writes=[actb])
                yield
                pp, p0, p1 = proj(actb, wa, actT)
                yield
                k.op("act", lambda e: e.activation(out=W0[:], in_=zrs[:, 2048:3072], func=AF.Sigmoid), reads=[zrs], writes=[W0])
                k.op("dve", lambda e: e.tensor_tensor(out=W2[:], in0=pp[:, :], in1=W0[:], op=ALU.mult), reads=[p0, p1, W0], writes=[W2])
                yield
                k.op("act", lambda e: e.activation(out=ub[:], in_=zrs[:, 0:1024], func=AF.Gelu_apprx_tanh), reads=[zrs], writes=[ub])
                k.op("act", lambda e: e.activation(out=W1[:], in_=zrs[:, 1024:2048], func=AF.Gelu_apprx_tanh), reads=[zrs], writes=[W1])
                for c in range(2):
                    k.op("dve", lambda e: e.bn_stats(out=bnst[:, c, :], in_=W1[:, cs(c * 512, 512)]), reads=[W1], writes=[bnst])
                k.op("dve", lambda e: e.bn_aggr(out=mv[:], in_=bnst[:].rearrange("p a b -> p (a b)")), reads=[bnst], writes=[mv])
                rstd_from_ap(mv, 1, rsv, EPS)
                k.op("dve", lambda e: e.tensor_scalar(out=W1[:], in0=W1[:], scalar1=mv[:, 0:1], scalar2=rsv[:, 0:1], op0=ALU.subtract,
                                                      op1=ALU.mult), reads=[W1, mv, rsv], writes=[W1])
                k.op("dve", lambda e: e.tensor_tensor(out=vcb[:], in0=W1[:], in1=lnvg[:], op=ALU.mult), reads=[W1, lnvg], writes=[vcb])
                yield
                pp, p0, p1 = pair()

                def f(e):
                    for g in range(8):
                        ins = e.matmul(pp[:, cs(g * 128, 128)], lhsT=wsT[:, g, :], rhs=vcb[:, cs(g * 128, 128)], start=True, stop=True)
                    return ins
                k.op("pe", f, reads=[wsT, vcb], writes=[p0, p1])
                k.op("dve", lambda e: e.tensor_tensor(out=W1[:].rearrange("p (g c) -> p g c", c=128),
                                                      in0=pp[:, :].rearrange("p (g c) -> p g c", c=128),
                                                      in1=bsT[:].unsqueeze(2).to_broadcast([128, 8, 128]), op=ALU.add),
                     reads=[p0, p1, bsT], writes=[W1])
                k.op("dve", lambda e: e.tensor_tensor(out=actb[:], in0=W1[:], in1=ub[:], op=ALU.mult), reads=[W1, ub], writes=[actb])
                yield
                pp, p0, p1 = proj(actb, wb, actT)
                yield
                k.op("act", lambda e: e.activation(out=W0[:], in_=zrs[:, 3072:4096], func=AF.Sigmoid), reads=[zrs], writes=[W0])
                k.op("dve", lambda e: e.tensor_tensor(out=W1[:], in0=pp[:, :], in1=W0[:], op=ALU.mult), reads=[p0, p1, W0], writes=[W1])
                k.op("dve", lambda e: e.tensor_tensor(out=actb[:], in0=W1[:], in1=W2[:], op=ALU.add), reads=[W1, W2], writes=[actb])
                yield
                pp, p0, p1 = proj(actb, wo, actT)
                yield
                k.op("dve", lambda e: e.tensor_tensor(out=W0[:], in0=pp[:, :], in1=gate1[:], op=ALU.mult), reads=[p0, p1, gate1], writes=[W0])
                k.op("dve", lambda e: e.tensor_tensor(out=W0[:], in0=W0[:], in1=xt[:], op=ALU.add), reads=[W0, xt], writes=[W0])
                k.dma("sp", x1_d[rows, :], W0[:], reads=[W0], writes=[R_x1(i)], sembuf=W0)
                yield

            pending = list(range(NT))
            live = []
            while pending or live:
                if len(live) < 2 and pending:
                    ti = pending.pop(0)
                    live.append(tile_gen(ti, sets[ti % 2]))
                    if len(live) == 2 and ti == 1:
                        for _ in range(5):
                            next(live[0])
                for g in list(live):
                    try:
                        next(g)
                    except StopIteration:
                        live.remove(g)
            k.barrier()

    def rstd_from_ap(mvb, col, out_rstd, eps):
        k.op("act", lambda e: e.activation(out=out_rstd[:], in_=mvb[:, col:col + 1], func=AF.Ln, bias=eps_t(eps)[:], scale=1.0),
             reads=[mvb, eps_t(eps)], writes=[out_rstd])
        k.op("act", lambda e: e.activation(out=out_rstd[:], in_=out_rstd[:], func=AF.Exp, scale=-0.5),
             reads=[out_rstd], writes=[out_rstd])

    def phaseC(l, last):
        with contextlib.ExitStack() as es:
            w1 = sbt(es, "w1", [128, 8, DFF], BF16)
            w2 = sbt(es, "w2", [128, 32, D], BF16)
            g2 = sbt(es, "g2", [128, D], F32)
            sh2 = sbt(es, "sh2", [128, D], F32)
            gate2 = sbt(es, "gate2", [128, D], F32)
            fg = sbt(es, "fg", [128, D], F32) if last else None
            xt = [sbt(es, "xc%d" % i, [128, D], F32) for i in range(2)]

            def mkset(pi):
                B = {}
                B["W0"] = sbt(es, "Wc0_%d" % pi, [128, D], F32)
                B["hT"] = sbt(es, "hT2_%d" % pi, [128, 8, 128], BF16)
                B["hid"] = sbt(es, "hid_%d" % pi, [128, DFF], BF16)
                B["hidT"] = sbt(es, "hidT_%d" % pi, [128, 32, 128], BF16)
                B["ss"] = sbt(es, "ssc_%d" % pi, [128, 1], F32)
                B["rstd"] = sbt(es, "rstdc_%d" % pi, [128, 1], F32)
                return B
            sets = [mkset(0), mkset(1)]
            cast_load_rows(lambda kc: w1[:, kc, :], w1_d[l], 8, DFF, w1)
            cast_load_rows(lambda kc: w2[:, kc, :], w2_d[l], 32, D, w2)
            load_mod(sh2, l, 3)
            load_mod(g2, l, 4)
            load_mod(gate2, l, 5)
            if last:
                bc_load(fg, fg_d)

            def load_x(i):
                k.dma("sp", xt[i % 2][:], x1_d[cs(i * 128, 128), :], reads=[R_x1(i)], writes=[xt[i % 2]])
            def tile_gen(i, B):
                W0, hT, hid, hidT, ss, rstd = B["W0"], B["hT"], B["hid"], B["hidT"], B["ss"], B["rstd"]
                x = xt[i % 2]
                load_x(i)
                yield
                k.op("act", lambda e: e.activation(out=W0[:], in_=x[:], func=AF.Square), reads=[x], writes=[W0])
                k.op("dve", lambda e: e.tensor_reduce(out=ss[:], in_=W0[:], axis=AX.X, op=ALU.add), reads=[W0], writes=[ss])
                rstd_from(ss, rstd, 1.0 / D, EPS)
                k.op("dve", lambda e: e.scalar_tensor_tensor(out=W0[:], in0=x[:], scalar=rstd[:, 0:1], in1=g2[:], op0=ALU.mult,
                                                             op1=ALU.mult), reads=[x, rstd, g2], writes=[W0])
                k.op("dve", lambda e: e.tensor_tensor(out=hid[:, 0:D], in0=W0[:], in1=sh2[:], op=ALU.add), reads=[W0, sh2], writes=[hid])
                transpose8(hid, hT)
                yield
                for n in range(8):
                    pb = bank()

                    def f(e):
                        for kc in range(8):
                            ins = e.matmul(pb[:], lhsT=hT[:, kc, :], rhs=w1[:, kc, cs(n * 512, 512)], start=(kc == 0), stop=(kc == 7))
                        return ins
                    k.op("pe", f, reads=[hT, w1], writes=[pb])
                    k.op("act", lambda e: e.activation(out=hid[:, cs(n * 512, 512)], in_=pb[:], func=AF.Relu), reads=[pb], writes=[hid])
                    k.op("dve", lambda e: e.tensor_tensor(out=hid[:, cs(n * 512, 512)], in0=hid[:, cs(n * 512, 512)],
                                                          in1=hid[:, cs(n * 512, 512)], op=ALU.mult), reads=[hid], writes=[hid])
                    if n % 4 == 3:
                        yield
                for q in range(4):
                    pb = bank()
                    pv = pb[:].bitcast(BF16)

                    def f(e):
                        for j in range(8):
                            ins = e.transpose(out=pv[:, cs(j * 128, 128)], in_=hid[:, cs((q * 8 + j) * 128, 128)], identity=ident_b[:])
                        return ins
                    k.op("pe", f, reads=[hid, ident_b], writes=[pb])
                    if q % 2 == 0:
                        k.op("act", lambda e: e.copy(out=hidT[:, q * 8:q * 8 + 8, :].rearrange("p a b -> p (a b)"), in_=pv[:, :]),
                             reads=[pb], writes=[hidT])
                    else:
                        k.op("dve", lambda e: e.tensor_copy(out=hidT[:, q * 8:q * 8 + 8, :].rearrange("p a b -> p (a b)"), in_=pv[:, :]),
                             reads=[pb], writes=[hidT])
                yield
                pp, p0, p1 = pair()

                def f(e):
                    for nn in range(2):
                        for kc in range(32):
                            ins = e.matmul(pp[:, cs(nn * 512, 512)], lhsT=hidT[:, kc, :], rhs=w2[:, kc, cs(nn * 512, 512)],
                                           start=(kc == 0), stop=(kc == 31))
                    return ins
                k.op("pe", f, reads=[hidT, w2], writes=[p0, p1])
                k.op("dve", lambda e: e.tensor_tensor(out=W0[:], in0=pp[:, :], in1=gate2[:], op=ALU.mult), reads=[p0, p1, gate2], writes=[W0])
                k.op("dve", lambda e: e.tensor_tensor(out=W0[:], in0=W0[:], in1=x[:], op=ALU.add), reads=[W0, x], writes=[W0])
                rows = cs(i * 128, 128)
                if not last:
                    k.dma("sp", x2_d[rows, :], W0[:], reads=[W0], writes=[R_x2(i)], sembuf=W0)
                else:
                    k.op("act", lambda e: e.activation(out=x[:], in_=W0[:], func=AF.Square), reads=[W0], writes=[x])
                    k.op("dve", lambda e: e.tensor_reduce(out=ss[:], in_=x[:], axis=AX.X, op=ALU.add), reads=[x], writes=[ss])
                    rstd_from(ss, rstd, 1.0 / D, EPS)
                    k.op("dve", lambda e: e.scalar_tensor_tensor(out=W0[:], in0=W0[:], scalar=rstd[:, 0:1], in1=fg[:], op0=ALU.mult,
                                                                 op1=ALU.mult), reads=[W0, rstd, fg], writes=[W0])
                    k.dma("sp", y_d[rows, :], W0[:], reads=[W0], writes=[R_y(i)], sembuf=W0)
                yield

            pending = list(range(NT))
            live = []
            while pending or live:
                if len(live) < 2 and pending:
                    ti = pending.pop(0)
                    live.append(tile_gen(ti, sets[ti % 2]))
                    if len(live) == 2 and ti == 1:
                        for _ in range(3):
                            next(live[0])
                for g in list(live):
                    try:
                        next(g)
                    except StopIteration:
                        live.remove(g)
            k.barrier()

    R_xin = DR("xin")
    steps = [lambda: phaseP(0), lambda: phaseP(1)]
    for l in range(2):
        xs, Rx = (x_d, R_xin) if l == 0 else (x2_d, R_x2)
        steps += [lambda l=l, xs=xs, Rx=Rx: phaseA1(l, xs, Rx), lambda l=l: phaseS2(l),
                  lambda l=l, xs=xs, Rx=Rx: phaseM(l, xs, Rx), lambda l=l: phaseC(l, last=(l == 1))]
    for st_ in steps[:upto]:
        st_()
    k.barrier()
    ges.close()
    return nc, k


def _shift_mats(kind):
    m = np.zeros((4, 3, 128, 128), np.float32)
    eye = np.eye(128, dtype=np.float32)
    t = np.arange(128)
    for cls in range(4):
        cur = np.zeros((128, 128), np.float32)
        nbe = np.zeros((128, 128), np.float32)
        nbo = np.zeros((128, 128), np.float32)
        if kind == "sample":
            if cls == 0:
                for to in t:
                    if to % 64 != 0:
                        cur[to - 1, to] = 1
            elif cls == 1:
                for to in t:
                    if to % 64 != 63:
                        cur[to + 1, to] = 1
            elif cls == 2:
                for to in t:
                    if to >= 64:
                        cur[to - 64, to] = 1
                    else:
                        nbe[to + 64, to] = 1
                        nbo[to + 64, to] = 1
            else:
                for to in t:
                    if to < 64:
                        cur[to + 64, to] = 1
                    else:
                        nbe[to - 64, to] = 1
                        nbo[to - 64, to] = 1
        else:
            if cls in (0, 2):
                for to in t:
                    if to >= 1:
                        cur[to - 1, to] = 1
                nbo[127, 0] = 1
            else:
                for to in t:
                    if to <= 126:
                        cur[to + 1, to] = 1
                nbe[0, 127] = 1
        m[cls, 0] = cur - eye
        m[cls, 1] = nbe
        m[cls, 2] = nbo
    return np.ascontiguousarray(m.reshape(12, 128, 128).transpose(1, 0, 2).reshape(128, 12 * 128))


def _tri4():
    s = np.arange(128)[:, None]
    t = np.arange(128)[None, :]
    m = np.stack([(s <= t), (s >= t), (s < t), (s > t)], axis=1).astype(np.float32)
    return np.ascontiguousarray(m.reshape(128, 512))


def _state_to_H(st):
    a = st.reshape(2, 2, 8, 2, 64, 64)
    a = a.transpose(0, 1, 3, 5, 2, 4)
    return np.ascontiguousarray(a.reshape(2, 2, 128, 512))


def _H_to_state(Hm):
    lead = Hm.shape[:-2]
    a = Hm.reshape(lead + (2, 64, 8, 64))
    nl = len(lead)
    perm = tuple(range(nl)) + (nl + 2, nl + 0, nl + 3, nl + 1)
    a = a.transpose(perm)
    return a.reshape(lead + (16, 64, 64))


def make_core_inputs(kind, x_tokens, cond_vec, state_lh, shared):
    d = dict(shared)
    d["x"] = np.ascontiguousarray(x_tokens, dtype=np.float32)
    d["cond"] = np.ascontiguousarray(cond_vec.reshape(8, 128).T, dtype=np.float32)
    d["state0"] = _state_to_H(state_lh)
    d["cmask"] = np.full((128, 1), 1.0 if kind == "sample" else 0.0, np.float32)
    d["shm"] = _shift_mats(kind)
    return d


def shared_inputs(w_ada, b_ada, norm1_g, norm2_g, w_in, mu_shift, w0, w_up, a0, a_up, g_up, k_k, k_a, r_k, lnx_g,
                  lnx_b, w_branch_a, ln_v_g, w_s, b_s, w_branch_b, w_out, w1, w2, final_g):
    f = lambda a: np.ascontiguousarray(np.asarray(a), dtype=np.float32)
    wup_aug = np.concatenate([np.asarray(w_up), np.asarray(w0)[:, :, None, :]], axis=2)
    aup_aug = np.concatenate([np.asarray(a_up), np.asarray(a0)[:, :, None, :]], axis=2)
    wsT = np.asarray(w_s).transpose(0, 3, 1, 2).reshape(2, 128, 8 * 128)
    bsT = np.asarray(b_s).transpose(0, 2, 1)
    return dict(ident=np.eye(128, dtype=np.float32), tri4=_tri4(), w_ada=f(w_ada), b_ada=f(b_ada), norm1_g=f(norm1_g),
                norm2_g=f(norm2_g), w_in=f(w_in), mu_shift=f(mu_shift), wup_aug=f(wup_aug), aup_aug=f(aup_aug), g_up=f(g_up),
                k_k=f(k_k), k_a=f(k_a), r_k=f(np.asarray(r_k).reshape(2, D)), lnx_g=f(lnx_g), lnx_b=f(lnx_b),
                w_branch_a=f(w_branch_a), ln_v_g=f(ln_v_g), wsT=f(wsT), bsT=f(bsT), w_branch_b=f(w_branch_b), w_out=f(w_out),
                w1=f(w1), w2=f(w2), final_g=f(final_g))


_PROG = {}


def kernel(x_prompt, x_sample, state_rwkv, c, c_ctx, w_ada, b_ada, norm1_g, norm2_g, w_in, mu_shift,
           w0, w_up, a0, a_up, g_up, k_k, k_a, r_k, lnx_g, lnx_b, w_branch_a, ln_v_g, w_s, b_s,
           w_branch_b, w_out, w1, w2, final_g):
    NT = 32
    x_prompt = np.asarray(x_prompt, dtype=np.float32)
    x_sample = np.asarray(x_sample, dtype=np.float32)
    state_rwkv = np.asarray(state_rwkv, dtype=np.float32)
    c = np.asarray(c, dtype=np.float32)
    c_ctx = np.asarray(c_ctx, dtype=np.float32)
    shared = shared_inputs(w_ada, b_ada, norm1_g, norm2_g, w_in, mu_shift, w0, w_up, a0, a_up, g_up, k_k, k_a, r_k,
                           lnx_g, lnx_b, w_branch_a, ln_v_g, w_s, b_s, w_branch_b, w_out, w1, w2, final_g)
    in_maps = []
    for b in range(4):
        in_maps.append(make_core_inputs("sample", x_sample[b], c[b], state_rwkv[b], shared))
    zero_state = np.zeros((2, 2, 16, 64, 64), np.float32)
    for q in range(4):
        xs = np.zeros((NT * 128, D), np.float32)
        xs[:2048] = x_prompt[8 * q:8 * q + 8].reshape(2048, D)
        xs[2048:] = xs[:2048]
        in_maps.append(make_core_inputs("prompt", xs, c_ctx, zero_state, shared))
    if NT not in _PROG:
        _PROG[NT] = build_program(NT)[0]
    res = run_bass_kernel_spmd(_PROG[NT], in_maps, core_ids=list(range(8)))
    r = res.results
    y_sample = np.stack([r[b]["y"] for b in range(4)], axis=0)
    y_prompt = np.concatenate([r[4 + q]["y"][:2048].reshape(8, 256, D) for q in range(4)], axis=0)
    sts = []
    for q in range(4):
        so = r[4 + q]["st_out"]
        so = so[:, :, :8]
        s = _H_to_state(so)
        sts.append(np.transpose(s, (2, 0, 1, 3, 4, 5)))
    new_state = np.ascontiguousarray(np.concatenate(sts, axis=0), dtype=np.float32)
    return (np.ascontiguousarray(y_prompt, dtype=np.float32), np.ascontiguousarray(y_sample, dtype=np.float32), new_state)
```
